# Optimizing a Trainium2 kernel written in Bass

```python
import math
import jax
import jax.numpy as jnp
from jax import lax
import numpy as np

D_MODEL = 1024
BATCH = 8
SEQ = 2048
DEPTH = 2

HEAD_DIM = 64
GRID_W = 64
N_MEM = 256
A_HEADS = 8
A_WIDTH = A_HEADS * HEAD_DIM
A_GROUPS = ((128, 1), (512, 4), (2048, 16))
BAND_BLOCK = 64
B_HEADS = 8
B_WIDTH = B_HEADS * HEAD_DIM
B_DECAY_LORA = 64
B_ICLR_LORA = 64
B_GN_EPS = 64e-5
B_MIX_COLS = 3 * B_WIDTH + 2 * B_DECAY_LORA + 2 * B_ICLR_LORA
C_HEADS = 8
C_KV_HEADS = 2
C_REP = C_HEADS // C_KV_HEADS
C_WIDTH = C_HEADS * HEAD_DIM
C_KV_WIDTH = C_KV_HEADS * HEAD_DIM
ROPE_THETA = 10000.0
D_HEADS = 4
D_VDIM = 2 * HEAD_DIM
D_QK_WIDTH = D_HEADS * 2 * HEAD_DIM
D_V_WIDTH = D_HEADS * D_VDIM
M_HEADS = 4
M_WIDTH = M_HEADS * HEAD_DIM
NUM_BUCKETS = 32
REL_MAX_DISTANCE = 1024
QBLK = 128
N_BRANCHES = 5

A_SPLIT = (A_WIDTH, A_WIDTH, A_WIDTH, A_WIDTH)
B_SPLIT = (B_WIDTH, B_WIDTH, B_WIDTH, 2 * B_DECAY_LORA, 2 * B_ICLR_LORA, B_WIDTH)
C_SPLIT = (C_WIDTH, C_KV_WIDTH, C_KV_WIDTH, C_WIDTH)
D_SPLIT = (D_QK_WIDTH, D_QK_WIDTH, D_V_WIDTH, D_V_WIDTH)
M_SPLIT = (M_WIDTH, M_WIDTH)
BRANCH_WIDTHS = (A_WIDTH, B_WIDTH, C_WIDTH, D_V_WIDTH, M_WIDTH)

F32 = jnp.float32
NEG_INF = -1e30

kernel_name = 'hybrid_gated_parallel_encoder'


def split_last(t, sizes):
    parts, start = [], 0
    for size in sizes:
        parts.append(t[..., start:start + size])
        start += size
    return parts


def to_heads(t, n_heads):
    b, s, _ = t.shape
    return t.reshape(b, s, n_heads, -1).transpose(0, 2, 1, 3)


def from_heads(t):
    b, h, s, d = t.shape
    return t.transpose(0, 2, 1, 3).reshape(b, s, h * d)


def layer_norm(t, g, b, eps=1e-5):
    tf = t.astype(F32)
    mu = jnp.mean(tf, -1, keepdims=True)
    var = jnp.mean(jnp.square(tf - mu), -1, keepdims=True)
    return ((tf - mu) * lax.rsqrt(var + eps) * g + b).astype(t.dtype)


def rms_norm(t, g, eps=1e-6):
    tf = t.astype(F32)
    return (tf * lax.rsqrt(jnp.mean(jnp.square(tf), -1, keepdims=True) + eps) * g).astype(t.dtype)


def rel_bucket(rel):
    nb = NUM_BUCKETS // 2
    max_exact = nb // 2
    n = jnp.abs(rel)
    nf = jnp.maximum(n, 1).astype(F32)
    large = max_exact + (jnp.log(nf / max_exact) / math.log(REL_MAX_DISTANCE / max_exact)
                         * (nb - max_exact)).astype(jnp.int32)
    large = jnp.minimum(large, nb - 1)
    return jnp.where(rel > 0, nb, 0) + jnp.where(n < max_exact, n, large)


def dilated_branch(q, k, v, table, window, dil):
    b, h, s, d = q.shape
    radius = window // (2 * dil)
    L = s // dil
    nblk = -(-L // BAND_BLOCK)
    lp = nblk * BAND_BLOCK
    kw = BAND_BLOCK + 2 * radius

    def strided(t):
        return t.reshape(b, h, L, dil, d).swapaxes(2, 3)

    qs = jnp.pad(strided(q), ((0, 0), (0, 0), (0, 0), (0, lp - L), (0, 0)))
    kpad = ((0, 0), (0, 0), (0, 0), (radius, radius + lp - L), (0, 0))
    ks = jnp.pad(strided(k), kpad)
    vs = jnp.pad(strided(v), kpad)
    qb = qs.reshape(b, h, dil, nblk, BAND_BLOCK, d)
    idx = (jnp.arange(nblk) * BAND_BLOCK)[:, None] + jnp.arange(kw)[None, :]
    kb = ks[:, :, :, idx]
    vb = vs[:, :, :, idx]
    off = jnp.arange(kw)[None, :] - radius - jnp.arange(BAND_BLOCK)[:, None]
    bias = jnp.moveaxis(table[rel_bucket(off * dil)], -1, 0).astype(F32)
    kpos = idx - radius
    valid = (jnp.abs(off) <= radius)[None] & ((kpos >= 0) & (kpos < L))[:, None, :]
    logits = jnp.einsum('bhrnqd,bhrnkd->bhrnqk', qb, kb).astype(F32) * (d ** -0.5) + bias[:, None, None]
    logits = jnp.where(valid, logits, NEG_INF)
    m = jnp.max(logits, -1, keepdims=True)
    e = jnp.exp(logits - m)
    den = jnp.sum(e, -1, keepdims=True)
    o = jnp.einsum('bhrnqk,bhrnkd->bhrnqd', e, vb.astype(F32)) / den
    lse = (m + jnp.log(den))[..., 0]
    o = o.reshape(b, h, dil, lp, d)[:, :, :, :L].swapaxes(2, 3).reshape(b, h, s, d)
    lse = lse.reshape(b, h, dil, lp)[..., :L].swapaxes(2, 3).reshape(b, h, s)
    return o, lse


def mixer_dilated(pa, table_a):
    q, k, v, g = split_last(pa, A_SPLIT)
    q, k, v = (to_heads(t, A_HEADS) for t in (q, k, v))
    outs, lses = [], []
    for window, dil in A_GROUPS:
        o, lse = dilated_branch(q, k, v, table_a, window, dil)
        outs.append(o)
        lses.append(lse)
    wts = jax.nn.softmax(jnp.stack(lses), axis=0)
    o = jnp.sum(wts[..., None] * jnp.stack(outs), axis=0)
    return from_heads(o).astype(pa.dtype) * jax.nn.silu(g)


def rwkv7_step(state, inp):
    r, w, k, v, a, bb = inp
    sa = jnp.einsum('bhij,bhj->bhi', state, a)
    state = state * w[:, :, None, :] + sa[..., None] * bb[:, :, None, :] + v[..., None] * k[:, :, None, :]
    return state, jnp.einsum('bhij,bhj->bhi', state, r)


def mixer_rwkv(pb, mu, w0, w_up, a0, a_up, k_k, k_a, r_k, ln_g, ln_b):
    b, s, _ = pb.shape
    xm, g = pb[..., :B_MIX_COLS], pb[..., B_MIX_COLS:]
    prev = jnp.pad(xm[:, :-1], ((0, 0), (1, 0), (0, 0)))
    nxt = jnp.pad(xm[:, 1:], ((0, 0), (0, 1), (0, 0)))
    xm = xm + mu[0] * (prev - xm) + mu[1] * (nxt - xm)
    r, k, v, wd, ad = (t.astype(F32) for t in split_last(xm, B_SPLIT[:5]))
    wd = wd.reshape(b, s, 2, B_DECAY_LORA)
    ad = ad.reshape(b, s, 2, B_ICLR_LORA)
    w_log = -jax.nn.softplus(-(w0 + jnp.einsum('bser,erc->bsec', jnp.tanh(wd), w_up))) - 0.5
    decay = jnp.exp(-jnp.exp(w_log.astype(F32)))
    a = jax.nn.sigmoid((a0 + jnp.einsum('bser,erc->bsec', ad, a_up)).astype(F32))
    kk = (k * k_k).reshape(b, s, B_HEADS, HEAD_DIM)
    kk = (kk / jnp.maximum(jnp.linalg.norm(kk, axis=-1, keepdims=True), 1e-12)).reshape(b, s, B_WIDTH)

    def seq_heads(t):
        return t.reshape(b, s, B_HEADS, HEAD_DIM).transpose(1, 0, 2, 3)

    rh, vh, ah = seq_heads(r), seq_heads(v), seq_heads(-kk)
    ys, bonuses = [], []
    for e, rev in ((0, False), (1, True)):
        ke = k * (1.0 + (a[:, :, e] - 1.0) * k_a)
        be = kk * a[:, :, e]
        xs = (rh, seq_heads(decay[:, :, e]), seq_heads(ke), vh, ah, seq_heads(be))
        _, ye = lax.scan(rwkv7_step, jnp.zeros((b, B_HEADS, HEAD_DIM, HEAD_DIM), F32), xs, reverse=rev)
        ys.append(ye)
        bonuses.append(jnp.sum((r * ke * r_k.reshape(-1)).reshape(b, s, B_HEADS, HEAD_DIM), -1, keepdims=True))
    y = (ys[0] + ys[1]).transpose(1, 0, 2, 3)
    mu_y = jnp.mean(y, -1, keepdims=True)
    var_y = jnp.mean(jnp.square(y - mu_y), -1, keepdims=True)
    gn = ((y - mu_y) * lax.rsqrt(var_y + B_GN_EPS)).reshape(b, s, B_WIDTH) * ln_g + ln_b
    bonus = ((bonuses[0] + bonuses[1]) * v.reshape(b, s, B_HEADS, HEAD_DIM)).reshape(b, s, B_WIDTH)
    return (gn + bonus).astype(pb.dtype) * jax.nn.silu(g)


def axial_rope(t, row, col):
    d = t.shape[-1]
    half = d // 2
    qtr = half // 2
    freqs = ROPE_THETA ** (-(jnp.arange(qtr, dtype=F32) / qtr))

    def rot(u, pos):
        ang = pos.astype(F32)[:, None] * freqs[None, :]
        c, sn = jnp.cos(ang), jnp.sin(ang)
        u1, u2 = u[..., :qtr], u[..., qtr:]
        return jnp.concatenate([u1 * c - u2 * sn, u1 * sn + u2 * c], -1)

    tf = t.astype(F32)
    return jnp.concatenate([rot(tf[..., :half], row), rot(tf[..., half:], col)], -1).astype(t.dtype)


def mixer_axial_gqa(pc, qn_g, kn_g, row, col):
    b, s, _ = pc.shape
    q, k, v, g = split_last(pc, C_SPLIT)
    q = axial_rope(rms_norm(q.reshape(b, s, C_HEADS, HEAD_DIM), qn_g).transpose(0, 2, 1, 3), row, col)
    k = axial_rope(rms_norm(k.reshape(b, s, C_KV_HEADS, HEAD_DIM), kn_g).transpose(0, 2, 1, 3), row, col)
    v = to_heads(v, C_KV_HEADS)
    q = q.reshape(b, C_KV_HEADS, C_REP, s, HEAD_DIM)
    scale = HEAD_DIM ** -0.5

    def block(i):
        qb = lax.dynamic_slice_in_dim(q, i * QBLK, QBLK, axis=3)
        p = jax.nn.softmax(jnp.einsum('bgrqd,bgkd->bgrqk', qb, k).astype(F32) * scale, axis=-1)
        return jnp.einsum('bgrqk,bgkd->bgrqd', p, v.astype(F32))

    o = lax.map(block, jnp.arange(s // QBLK))
    o = jnp.moveaxis(o, 0, 3).reshape(b, C_HEADS, s, HEAD_DIM)
    return from_heads(o).astype(pc.dtype) * jax.nn.silu(g)


def mixer_diff(pd, lam_params, subln_g, table_d, layer_idx):
    b, s, _ = pd.shape
    q, k, v, g = split_last(pd, D_SPLIT)
    q = q.reshape(b, s, D_HEADS, 2, HEAD_DIM).transpose(3, 0, 2, 1, 4)
    k = k.reshape(b, s, D_HEADS, 2, HEAD_DIM).transpose(3, 0, 2, 1, 4)
    v = to_heads(v, D_HEADS).astype(F32)
    lam_init = 0.8 - 0.6 * math.exp(-0.3 * layer_idx)
    lq1, lk1, lq2, lk2 = lam_params[0], lam_params[1], lam_params[2], lam_params[3]
    lam = (jnp.exp(jnp.sum(lq1 * lk1)) - jnp.exp(jnp.sum(lq2 * lk2)) + lam_init).astype(F32)
    scale = HEAD_DIM ** -0.5
    kpos = jnp.arange(s)

    def block(i):
        start = i * QBLK
        qb = lax.dynamic_slice_in_dim(q, start, QBLK, axis=3)
        rel = kpos[None, :] - (start + jnp.arange(QBLK))[:, None]
        bias = jnp.moveaxis(table_d[rel_bucket(rel)], -1, 0).astype(F32)
        p = jax.nn.softmax(jnp.einsum('cbhqd,cbhkd->cbhqk', qb, k).astype(F32) * scale + bias, axis=-1)
        return jnp.einsum('bhqk,bhkd->bhqd', p[0] - lam * p[1], v)

    o = lax.map(block, jnp.arange(s // QBLK))
    o = jnp.moveaxis(o, 0, 2).reshape(b, D_HEADS, s, D_VDIM)
    o = rms_norm(o, subln_g, eps=1e-5) * (1.0 - lam_init)
    return from_heads(o).astype(pd.dtype) * jax.nn.silu(g)


def mixer_memory(pm, mem, w_mem_kv):
    q, g = split_last(pm, M_SPLIT)
    q = to_heads(q, M_HEADS)
    km, vm = split_last(jnp.einsum('bmd,dc->bmc', mem, w_mem_kv), (M_WIDTH, M_WIDTH))
    km, vm = to_heads(km, M_HEADS), to_heads(vm, M_HEADS)
    p = jax.nn.softmax(jnp.einsum('bhqd,bhmd->bhqm', q, km).astype(F32) * (HEAD_DIM ** -0.5), axis=-1)
    o = jnp.einsum('bhqm,bhmd->bhqd', p, vm.astype(F32))
    return from_heads(o).astype(pm.dtype) * jax.nn.silu(g)


def gated_merge(h, branch_outs, w_branch, w_gate, b_gate):
    d = h.shape[-1]
    terms, start = [], 0
    for i, (o, wdt) in enumerate(zip(branch_outs, BRANCH_WIDTHS)):
        proj = jnp.einsum('bsc,cd->bsd', o, w_branch[start:start + wdt])
        gate = jax.nn.sigmoid(jnp.einsum('bsd,de->bse', h, w_gate[:, i * d:(i + 1) * d]) + b_gate[i * d:(i + 1) * d])
        terms.append(gate * proj)
        start += wdt
    return sum(terms[1:], terms[0])


def setup_inputs(seed: int = 0) -> dict:
    key = jax.random.key(seed)
    ks = jax.random.split(key, 28)
    d = D_MODEL
    in_cols = sum(A_SPLIT) + sum(B_SPLIT) + sum(C_SPLIT) + sum(D_SPLIT) + sum(M_SPLIT)
    beta = (8 * DEPTH) ** -0.25

    def nrm(k, shape, scale):
        return scale * jax.random.normal(k, shape, F32)

    w_branch = jnp.concatenate(
        [nrm(jax.random.fold_in(ks[21], i), (DEPTH, wdt, d), beta * wdt ** -0.5)
         for i, wdt in enumerate(BRANCH_WIDTHS)], axis=1)
    return {
        'x': nrm(ks[0], (BATCH, SEQ, d), 1.0),
        'mem': nrm(ks[1], (BATCH, N_MEM, d), 1.0),
        'ln_in_g': 1.0 + nrm(ks[2], (d,), 0.02),
        'ln_in_b': nrm(ks[3], (d,), 0.02),
        'rel_bias': nrm(ks[4], (NUM_BUCKETS, A_HEADS + D_HEADS), 0.5),
        'w_in': nrm(ks[5], (DEPTH, d, in_cols), d ** -0.5),
        'shift_mu': jax.random.uniform(ks[6], (DEPTH, 2, B_MIX_COLS), F32, 0.0, 0.5),
        'rwkv_w0': jax.random.uniform(ks[7], (DEPTH, 2, B_WIDTH), F32, -6.5, -1.0),
        'rwkv_w_up': nrm(ks[8], (DEPTH, 2, B_DECAY_LORA, B_WIDTH), 0.1),
        'rwkv_a0': nrm(ks[9], (DEPTH, 2, B_WIDTH), 0.1),
        'rwkv_a_up': nrm(ks[10], (DEPTH, 2, B_ICLR_LORA, B_WIDTH), 0.5 * B_ICLR_LORA ** -0.5),
        'rwkv_k_k': 0.85 + nrm(ks[11], (DEPTH, B_WIDTH), 0.05),
        'rwkv_k_a': 1.0 + nrm(ks[12], (DEPTH, B_WIDTH), 0.05),
        'rwkv_r_k': nrm(ks[13], (DEPTH, B_HEADS, HEAD_DIM), 0.1),
        'rwkv_ln_g': 1.0 + nrm(ks[14], (DEPTH, B_WIDTH), 0.02),
        'rwkv_ln_b': nrm(ks[15], (DEPTH, B_WIDTH), 0.02),
        'c_qnorm_g': 1.0 + nrm(ks[16], (DEPTH, HEAD_DIM), 0.02),
        'c_knorm_g': 1.0 + nrm(ks[17], (DEPTH, HEAD_DIM), 0.02),
        'd_lambda': nrm(ks[18], (DEPTH, 4, HEAD_DIM), 0.1),
        'd_subln_g': 1.0 + nrm(ks[19], (DEPTH, D_VDIM), 0.02),
        'w_mem_kv': nrm(ks[20], (DEPTH, d, 2 * M_WIDTH), d ** -0.5),
        'w_branch': w_branch,
        'w_gate': nrm(ks[22], (DEPTH, d, N_BRANCHES * d), d ** -0.5),
        'b_gate': nrm(ks[23], (DEPTH, N_BRANCHES * d), 0.1),
        'w_out': nrm(ks[24], (DEPTH, d, d), beta * d ** -0.5),
        'ln_g': 1.0 + nrm(ks[25], (DEPTH, d), 0.02),
        'ln_b': nrm(ks[26], (DEPTH, d), 0.02),
    }


def reference(x, mem, ln_in_g, ln_in_b, rel_bias, w_in, shift_mu, rwkv_w0, rwkv_w_up, rwkv_a0,
              rwkv_a_up, rwkv_k_k, rwkv_k_a, rwkv_r_k, rwkv_ln_g, rwkv_ln_b, c_qnorm_g, c_knorm_g,
              d_lambda, d_subln_g, w_mem_kv, w_branch, w_gate, b_gate, w_out, ln_g, ln_b):
    s = x.shape[1]
    rows = s // GRID_W
    row = jnp.repeat(jnp.arange(rows, dtype=jnp.int32), GRID_W)
    col = jnp.tile(jnp.arange(GRID_W, dtype=jnp.int32), rows)
    table_a = rel_bias[:, :A_HEADS]
    table_d = rel_bias[:, A_HEADS:]
    alpha = (2 * DEPTH) ** 0.25
    mixer_sizes = (sum(A_SPLIT), sum(B_SPLIT), sum(C_SPLIT), sum(D_SPLIT), sum(M_SPLIT))
    h = layer_norm(x, ln_in_g, ln_in_b)
    for l in range(DEPTH):
        p = jnp.einsum('bsd,dc->bsc', h, w_in[l])
        pa, pb, pc, pd, pm = split_last(p, mixer_sizes)
        o_a = mixer_dilated(pa, table_a)
        o_b = mixer_rwkv(pb, shift_mu[l], rwkv_w0[l], rwkv_w_up[l], rwkv_a0[l], rwkv_a_up[l],
                         rwkv_k_k[l], rwkv_k_a[l], rwkv_r_k[l], rwkv_ln_g[l], rwkv_ln_b[l])
        o_c = mixer_axial_gqa(pc, c_qnorm_g[l], c_knorm_g[l], row, col)
        o_d = mixer_diff(pd, d_lambda[l], d_subln_g[l], table_d, l)
        o_m = mixer_memory(pm, mem, w_mem_kv[l])
        y = gated_merge(h, (o_a, o_b, o_c, o_d, o_m), w_branch[l], w_gate[l], b_gate[l])
        out = jnp.einsum('bsd,de->bse', y, w_out[l])
        h = layer_norm(alpha * h + out, ln_g[l], ln_b[l])
    return h
```

```python
import math
from concourse.ap import AP
import contextlib
import numpy as np
import concourse.bass as bass
import concourse.mybir as mybir
from concourse.bass_utils import run_bass_kernel_spmd

F32 = mybir.dt.float32
BF16 = mybir.dt.bfloat16
I32 = mybir.dt.int32
AF = mybir.ActivationFunctionType
ALU = mybir.AluOpType
AX = mybir.AxisListType

ENGS = ("pe", "act", "dve", "pool", "sp")
DMA_SEMS = 8


class Op:
    __slots__ = ("eng", "fn", "waits", "is_dma", "idx", "marked", "dma_slot", "dma_val", "prewait")

    def __init__(self, eng, fn, is_dma):
        self.eng = eng
        self.fn = fn
        self.is_dma = is_dma
        self.waits = []
        self.marked = False
        self.idx = None
        self.dma_slot = None
        self.dma_val = None
        self.prewait = None


class Prog:
    def __init__(self, nc, same_engine_sync=True):
        self.nc = nc
        self.ops = {e: [] for e in ENGS}
        self.last_write = {}
        self.readers = {}
        self.same_engine_sync = same_engine_sync
        self.dma_count = {e: 0 for e in ENGS}
        self.dma_hist = {e: [] for e in ENGS}
        self.all_dma_out = []
        self.stack = contextlib.ExitStack()
        self.n_ops = 0

    def sb(self, name, shape, dt):
        return self.stack.enter_context(self.nc.sbuf_tensor("s_" + name, list(shape), dt))

    def ps(self, name, shape, dt):
        return self.stack.enter_context(self.nc.psum_tensor("p_" + name, list(shape), dt))

    def _deps(self, op, reads, writes):
        deps = []
        for k in reads:
            w = self.last_write.get(k)
            if w is not None:
                deps.append(w)
        for k in writes:
            w = self.last_write.get(k)
            if w is not None:
                deps.append(w)
            for r in self.readers.get(k, ()):
                deps.append(r)
        best = {}
        for d in deps:
            if d is op:
                continue
            key = (d.eng, d.is_dma, d.dma_slot if d.is_dma else None)
            cur = best.get(key)
            if cur is None or d.idx > cur.idx:
                best[key] = d
        for d in best.values():
            if (not d.is_dma) and d.eng == op.eng and not op.is_dma:
                if op.eng == "pe" or not self.same_engine_sync:
                    continue
            op.waits.append(d)
            d.marked = True
        for k in reads:
            self.readers.setdefault(k, []).append(op)
        for k in writes:
            self.last_write[k] = op
            self.readers[k] = []

    def barrier(self, fn):
        o = Op("pool", fn, False)
        o.idx = len(self.ops["pool"])
        self.ops["pool"].append(o)
        self._deps(o, [], ["__phase__"])
        return o

    def op(self, eng, fn, reads=(), writes=()):
        reads = list(reads) + ["__phase__"]
        o = Op(eng, fn, False)
        o.idx = len(self.ops[eng])
        self.ops[eng].append(o)
        self._deps(o, reads, writes)
        self.n_ops += 1
        return o

    def dma(self, eng, out, in_, reads=(), writes=(), is_output=False, **kw):
        def fn(e, out=out, in_=in_, kw=kw):
            return e.dma_start(out=out, in_=in_, **kw)
        reads = list(reads) + ["__phase__"]
        o = Op(eng, fn, True)
        o.idx = len(self.ops[eng])
        n = self.dma_count[eng]
        self.dma_count[eng] += 1
        o.dma_slot = n % DMA_SEMS
        o.dma_val = 16 * (n // DMA_SEMS + 1)
        if n >= DMA_SEMS:
            o.prewait = self.dma_hist[eng][n - DMA_SEMS]
        self.dma_hist[eng].append(o)
        self.ops[eng].append(o)
        self._deps(o, reads, writes)
        if is_output:
            self.all_dma_out.append(o)
        self.n_ops += 1
        return o

    def emit(self):
        nc = self.nc
        st = self.stack
        fin = Op("sp", None, False)
        fin.idx = len(self.ops["sp"])
        for o in self.all_dma_out:
            fin.waits.append(o)
        self.ops["sp"].append(fin)
        csem = {e: st.enter_context(nc.semaphore("c_" + e)) for e in ENGS}
        dsem = {e: [st.enter_context(nc.semaphore("d_%s_%d" % (e, i))) for i in range(DMA_SEMS)]
                for e in ENGS if self.dma_count[e] > 0}
        for e in ENGS:
            c = 0
            for o in self.ops[e]:
                if o.is_dma:
                    continue
                if o.marked:
                    c += 1
                    o.dma_val = c
        block = st.enter_context(nc.Block())
        prog = self

        def run(e, eng):
            seen = {}
            for o in prog.ops[e]:
                ws = list(o.waits)
                if o.prewait is not None:
                    ws.append(o.prewait)
                for d in ws:
                    if d.is_dma:
                        sem, val = dsem[d.eng][d.dma_slot], d.dma_val
                    else:
                        sem, val = csem[d.eng], d.dma_val
                    k = id(sem)
                    if seen.get(k, 0) >= val:
                        continue
                    seen[k] = val
                    eng.wait_ge(sem, val)
                if o.fn is None:
                    continue
                ins = o.fn(eng)
                if o.is_dma:
                    ins.then_inc(dsem[e][o.dma_slot], 16)
                elif o.marked:
                    ins.then_inc(csem[e], 1)

        @block.tensor
        def _(eng):
            run("pe", eng)

        @block.scalar
        def _(eng):
            run("act", eng)

        @block.vector
        def _(eng):
            run("dve", eng)

        @block.gpsimd
        def _(eng):
            run("pool", eng)

        @block.sync
        def _(eng):
            run("sp", eng)

    def close(self):
        self.stack.close()


S = 2048
D = 1024
NT = 16
DEPTH = 2
WC = 256
XC = 2047
GW = 4096
ALPHA = (2 * DEPTH) ** 0.25
A0, B0, C0, D0, M0 = 0, 2048, 4352, 5632, 7680
OA, OB, OC, OD, OM = 0, 512, 1024, 1536, 2048
BROWS = [(0, 512), (512, 512), (1024, 512), (1536, 512), (2048, 256)]


def rel_bucket_np(rel):
    nb = 16
    max_exact = 8
    n = np.abs(rel)
    nf = np.maximum(n, 1).astype(np.float32)
    large = max_exact + (np.log(nf / max_exact) / np.float32(math.log(1024 / max_exact)) * (nb - max_exact)).astype(np.int32)
    large = np.minimum(large, nb - 1)
    return np.where(rel > 0, nb, 0) + np.where(n < max_exact, n, large)


def host_consts():
    c = {}
    c["ident"] = np.eye(128, dtype=np.float32)
    rel = np.arange(4096) - XC
    bkt = rel_bucket_np(rel)
    oh = np.zeros((32, 4096), np.float32)
    oh[bkt, np.arange(4096)] = 1.0
    c["onehot"] = oh
    n = np.abs(rel)
    mA = (n <= 64).astype(np.float32) + ((rel % 4 == 0) & (n <= 256)) + ((rel % 16 == 0) & (n <= 1024))
    mt = np.ones((12, 4096), np.float32)
    mt[:8] = mA[None, :]
    c["multab"] = mt
    t = np.arange(S)
    row = (t // 64).astype(np.float32)
    col = (t % 64).astype(np.float32)
    freqs = (10000.0 ** (-(np.arange(16, dtype=np.float32) / 16))).astype(np.float32)
    ar = row[:, None] * freqs[None, :]
    ac = col[:, None] * freqs[None, :]
    c["ropec"] = np.concatenate([np.cos(ar), np.cos(ar), np.cos(ac), np.cos(ac)], 1).astype(np.float32)
    c["ropes"] = np.concatenate([-np.sin(ar), np.sin(ar), -np.sin(ac), np.sin(ac)], 1).astype(np.float32)
    tri = np.zeros((2, 3, 128, 128), np.float32)
    sg = np.arange(128)[:, None]
    tt = np.arange(128)[None, :]
    same = (sg // 64) == (tt // 64)
    tri[0, 0] = same & (sg <= tt)
    tri[0, 1] = same & (sg < tt)
    tri[0, 2] = same & (sg > tt)
    tri[1, 0] = same & (sg >= tt)
    tri[1, 1] = same & (sg > tt)
    tri[1, 2] = same & (sg < tt)
    c["tri"] = tri.reshape(6 * 128, 128)
    mk_ = np.zeros((2, 64, 192), np.float32)
    a = np.arange(64)[:, None]
    b = np.arange(64)[None, :]
    mk_[0, :, 0:64] = a < b
    mk_[0, :, 64:128] = a <= b
    mk_[0, :, 128:192] = b < a
    mk_[1, :, 0:64] = a > b
    mk_[1, :, 64:128] = a >= b
    mk_[1, :, 128:192] = b > a
    c["rmask"] = mk_.reshape(128, 192)
    return c


IN_SPECS = [("ln_in_g", [1, D]), ("ln_in_b", [1, D]), ("rel_bias", [32, 12]), ("w_in", [DEPTH * D, 8192]),
            ("shift_mu", [DEPTH * 2, 1792]), ("rwkv_w0", [DEPTH * 2, 512]), ("rwkv_w_up", [DEPTH * 2 * 64, 512]),
            ("rwkv_a0", [DEPTH * 2, 512]), ("rwkv_a_up", [DEPTH * 2 * 64, 512]), ("rwkv_k_k", [DEPTH, 512]),
            ("rwkv_k_a", [DEPTH, 512]), ("rwkv_r_k", [DEPTH, 512]), ("rwkv_ln_g", [DEPTH, 512]),
            ("rwkv_ln_b", [DEPTH, 512]), ("c_qnorm_g", [DEPTH, 64]), ("c_knorm_g", [DEPTH, 64]),
            ("d_lambda", [DEPTH, 256]), ("d_subln_g", [DEPTH, 128]), ("w_mem_kv", [DEPTH * D, 512]),
            ("w_branch", [DEPTH * 2304, D]), ("w_gate", [DEPTH * D, 5120]), ("b_gate", [DEPTH, 5120]),
            ("w_out", [DEPTH * D, D]), ("ln_g", [DEPTH, D]), ("ln_b", [DEPTH, D]),
            ("ident", [128, 128]), ("onehot", [32, 4096]), ("multab", [12, 4096]), ("ropec", [S, 64]),
            ("ropes", [S, 64]), ("tri", [768, 128]), ("rmask", [128, 192])]


class KB:
    def __init__(self, debug=False, mixers="MCADB", layers=DEPTH):
        self.debug = debug
        self.mixers = mixers
        nc = bass.Bass("TRN2", target_bir_lowering=False)
        self.nc = nc
        P = Prog(nc)
        self.P = P
        I = {}
        I["x"] = nc.dram_tensor("x", [S, D], F32, kind="ExternalInput").ap()
        I["mem"] = nc.dram_tensor("mem", [256, D], F32, kind="ExternalInput").ap()
        for nm, shp in IN_SPECS:
            I[nm] = nc.dram_tensor(nm, list(shp), F32, kind="ExternalInput").ap()
        self.I = I
        self.out = nc.dram_tensor("out", [S, D], F32, kind="ExternalOutput").ap()
        self.hres = [nc.dram_tensor("hres%d" % i, [S, D], F32, kind="ExternalOutput" if debug else "Internal").ap() for i in range(2)]
        self.dbg_outs = []
        self.o_scr = nc.dram_tensor("o_scr", [S, 2304], F32, kind="ExternalOutput" if debug else "Internal").ap()
        self.xtab = nc.dram_tensor("xtab", [12, 4096], F32).ap()
        self.rk_scr = nc.dram_tensor("rk_scr", [1024, S], F32).ap()
        self.wa_scr = nc.dram_tensor("wa_scr", [256, S], F32).ap()
        self.v_scr = nc.dram_tensor("v_scr", [S, 512], F32).ap()
        self.y_scr = nc.dram_tensor("y_scr", [2 * S, 520], F32, kind="ExternalOutput" if debug else "Internal").ap()
        self.sg_scr = nc.dram_tensor("sg_scr", [S, 512], F32).ap()
        self.ident = P.sb("ident", [128, 128], F32)
        self.hT = P.sb("hT", [128, 8, S + 2], BF16)
        self.BIG = [P.sb("BIG%d" % i, [128, 9216], BF16) for i in range(4)]
        self.G = P.sb("G", [128, GW], F32)
        self.wst = [P.sb("wst%d" % i, [128, 8, WC], F32) for i in range(2)]
        self.wbf = [P.sb("wbf%d" % i, [128, 8, WC], BF16) for i in range(4)]
        self.ropec = P.sb("ropec", [128, NT, 64], F32)
        self.ropes = P.sb("ropes", [128, NT, 64], F32)
        self.lnx = [P.sb("lnx%d" % i, [128, D], F32) for i in range(2)]
        self.junk = P.sb("junk", [128, D], F32)
        self.lng = P.sb("lng", [128, D], F32)
        self.lnb = P.sb("lnb", [128, D], F32)
        self.pt = [P.sb("pt%d" % i, [128, 512], BF16) for i in range(4)]
        self.pe_ = [P.sb("pe%d" % i, [128, 512], BF16) for i in range(2)]
        self.ost = [P.sb("ost%d" % i, [128, 128], F32) for i in range(4)]
        self.sm = P.sb("sm", [128, 64], F32)
        self.tmp = [P.sb("tmp%d" % i, [128, 512], F32) for i in range(3)]
        self.onesf = P.sb("onesf", [128, 128], F32)
        self.brow = [P.sb("brow%d" % i, [1, WC], F32) for i in range(2)]
        self.gq = P.sb("gq", [128, 128], F32)
        self.subg = P.sb("subg", [128, 128], F32)
        self.lamt = P.sb("lamt", [128, 264], F32)
        self.pbar = P.sb("pbar", [1, 8], F32)
        self.memT = P.sb("memT", [128, 8, 256], BF16)
        self.bank = [P.ps("bank%d" % i, [128, 512], F32) for i in range(8)]
        self.cnt = {}
        self.pbi = 0
        B0_, B1_, B2_, B3_ = [b[:] for b in self.BIG]
        self.qT = B0_[:, 0:8192].rearrange("p (c t) -> p c t", t=S)
        self.kT = B1_[:, 0:8192].rearrange("p (c t) -> p c t", t=S)
        self.sg = B3_[:, 0:8192].rearrange("p (t c) -> p t c", c=512)
        self.prelude()
        for l in range(layers):
            self.layer(l, last=(l == layers - 1))
        P.emit()
        P.close()

    def nxt(self, name, n):
        v = self.cnt.get(name, 0)
        self.cnt[name] = (v + 1) % n
        return v

    def bk(self, i):
        return "bank%d" % i

    def pbank(self):
        self.pbi ^= 1
        return 2 + self.pbi

    def barrier(self):
        pbar = self.pbar
        self.P.barrier(lambda e: e.memset(pbar[:], 0.0))

    def mm(self, out, lhsT, rhs, start, stop, reads, writes):
        self.P.op("pe", lambda e: e.matmul(out, lhsT=lhsT, rhs=rhs, start=start, stop=stop), reads=reads, writes=writes)

    def tr(self, out, in_, reads, writes, np_=128):
        ident = self.ident
        self.P.op("pe", lambda e: e.transpose(out, in_, ident[0:np_, 0:np_]), reads=list(reads) + ["ident"], writes=writes)

    def cp(self, eng, out, in_, reads, writes):
        if eng == "act":
            self.P.op("act", lambda e: e.copy(out=out, in_=in_), reads=reads, writes=writes)
        else:
            self.P.op(eng, lambda e: e.tensor_copy(out=out, in_=in_), reads=reads, writes=writes)

    def act(self, out, in_, func, reads, writes, **kw):
        self.P.op("act", lambda e: e.activation(out=out, in_=in_, func=func, **kw), reads=reads, writes=writes)

    def tt(self, eng, out, in0, in1, op, reads, writes):
        self.P.op(eng, lambda e: e.tensor_tensor(out=out, in0=in0, in1=in1, op=op), reads=reads, writes=writes)

    def ts(self, eng, out, in0, s1, s2, op0, op1, reads, writes):
        if s2 is None:
            self.P.op(eng, lambda e: e.tensor_scalar(out=out, in0=in0, scalar1=s1, scalar2=None, op0=op0), reads=reads, writes=writes)
        else:
            self.P.op(eng, lambda e: e.tensor_scalar(out=out, in0=in0, scalar1=s1, scalar2=s2, op0=op0, op1=op1), reads=reads, writes=writes)

    def stt(self, eng, out, in0, scalar, in1, op0, op1, reads, writes):
        self.P.op(eng, lambda e: e.scalar_tensor_tensor(out=out, in0=in0, scalar=scalar, in1=in1, op0=op0, op1=op1), reads=reads, writes=writes)

    def memset(self, eng, ap, val, writes):
        self.P.op(eng, lambda e: e.memset(ap, val), writes=writes)

    def rsqrt_cols(self, src, dst, scale, eps, key="sm"):
        self.ts("dve", dst, src, scale, eps, ALU.mult, ALU.add, [key], [key])
        self.P.op("act", lambda e: e.sqrt(out=dst, in_=dst), reads=[key], writes=[key])
        self.P.op("dve", lambda e: e.reciprocal(out=dst, in_=dst), reads=[key], writes=[key])

    def wload(self, src2d, n, kc=8, variants=None):
        P = self.P
        i = self.nxt("w", 2)
        wst = self.wst[i]
        P.dma("sp", wst[:, 0:kc, 0:n], src2d.rearrange("(c p) n -> p c n", p=128), writes=["wst%d" % i])
        res = []
        if variants is None:
            j = self.nxt("wb", 4)
            self.cp("pool", self.wbf[j][:, 0:kc, 0:n], wst[:, 0:kc, 0:n], ["wst%d" % i], ["wbf%d" % j])
            return [(self.wbf[j], "wbf%d" % j)]
        for (vap, vkey) in variants:
            j = self.nxt("wb", 4)
            self.tt("pool", self.wbf[j][:, 0:kc, 0:n], wst[:, 0:kc, 0:n], vap.unsqueeze(1).broadcast_to([128, kc, n]), ALU.mult,
                    ["wst%d" % i, vkey], ["wbf%d" % j])
            res.append((self.wbf[j], "wbf%d" % j))
        return res

    def proj_T(self, wl, n, consume, shifts=(0,), rhs_fn=None, rkey="hT", ntb=4, tbw=512):
        hT = self.hT
        for ct in range(n // 128):
            for tb in range(ntb):
                b = self.pbank()
                nmm = 8 * len(shifts)
                m = 0
                for (wap, wkey), s in zip(wl, shifts):
                    for c in range(8):
                        if rhs_fn is None:
                            lo = 1 + tb * 512 + s
                            rhs = hT[:, c, lo:lo + 512]
                        else:
                            rhs = rhs_fn(c, tb)
                        self.mm(self.bank[b][:, 0:tbw], wap[:, c, ct * 128:(ct + 1) * 128], rhs, m == 0, m == nmm - 1,
                                [wkey, rkey], [self.bk(b)])
                        m += 1
                consume(b, ct, tb)

    def proj_N(self, wl, n, consume, shifts=(0,), lhs_fn=None, lkey="hT", ntt=NT, kc=8):
        hT = self.hT
        for tt in range(ntt):
            b = self.pbank()
            nmm = kc * len(shifts)
            m = 0
            for (wap, wkey), s in zip(wl, shifts):
                for c in range(kc):
                    if lhs_fn is None:
                        lo = 1 + tt * 128 + s
                        lh = hT[:, c, lo:lo + 128]
                    else:
                        lh = lhs_fn(c, tt)
                    self.mm(self.bank[b][:, 0:n], lh, wap[:, c, 0:n], m == 0, m == nmm - 1, [wkey, lkey], [self.bk(b)])
                    m += 1
            consume(b, tt)

    def ln_inplace(self, xt, xkey, eps=1e-5):
        sm, junk = self.sm, self.junk
        P = self.P
        P.op("dve", lambda e: e.reduce_sum(out=sm[:, 0:1], in_=xt, axis=AX.X), reads=[xkey], writes=["sm"])
        self.ts("dve", sm[:, 1:2], sm[:, 0:1], -1.0 / D, None, ALU.mult, None, ["sm"], ["sm"])
        self.ts("dve", xt, xt, sm[:, 1:2], None, ALU.add, None, [xkey, "sm"], [xkey])
        self.memset("dve", sm[:, 2:3], 0.0, ["sm"])
        self.act(junk[:], xt, AF.Square, [xkey, "sm"], ["junk", "sm"], accum_out=sm[:, 2:3])
        self.rsqrt_cols(sm[:, 2:3], sm[:, 3:4], 1.0 / D, eps)
        self.stt("dve", xt, xt, sm[:, 3:4], self.lng[:], ALU.mult, ALU.mult, [xkey, "sm", "lng"], [xkey])
        self.tt("dve", xt, xt, self.lnb[:], ALU.add, [xkey, "lnb"], [xkey])

    def to_hT(self, src, skey, tt):
        hT = self.hT
        for half in range(2):
            b = self.pbank()
            for c4 in range(4):
                c = half * 4 + c4
                self.tr(self.bank[b][:, c4 * 128:(c4 + 1) * 128], src[:, c * 128:(c + 1) * 128], [skey], [self.bk(b)])
            self.cp("act" if half else "dve", hT[:, half * 4:half * 4 + 4, 1 + tt * 128:1 + (tt + 1) * 128],
                    self.bank[b][:, :].rearrange("p (c t) -> p c t", t=128), [self.bk(b)], ["hT"])

    def prelude(self):
        P, I = self.P, self.I
        P.dma("sp", self.ident[:], I["ident"], writes=["ident"])
        P.dma("sp", self.ropec[:], I["ropec"].rearrange("(t p) c -> p t c", p=128), writes=["ropec"])
        P.dma("sp", self.ropes[:], I["ropes"].rearrange("(t p) c -> p t c", p=128), writes=["ropes"])
        self.memset("pool", self.onesf[:], 1.0, ["onesf"])
        self.memset("pool", self.hT[:, :, 0:1], 0.0, ["hT"])
        self.memset("pool", self.hT[:, :, S + 1:S + 2], 0.0, ["hT"])
        tmpA = self.tmp[0]
        rb = tmpA[0:32, 0:12]
        P.dma("sp", rb, I["rel_bias"], writes=["tmp0"])
        ohs = self.BIG[0][:].bitcast(F32)
        P.dma("sp", ohs[0:32, 0:4096], I["onehot"], writes=["BIG0"])
        mts = self.BIG[1][:].bitcast(F32)
        P.dma("sp", mts[0:12, 0:4096], I["multab"], writes=["BIG1"])
        xts = self.BIG[2][:].bitcast(F32)
        for j in range(8):
            b = self.pbank()
            self.mm(self.bank[b][0:12, :], rb, ohs[0:32, j * 512:(j + 1) * 512], True, True, ["tmp0", "BIG0"], [self.bk(b)])
            self.act(xts[0:12, j * 512:(j + 1) * 512], self.bank[b][0:12, :], AF.Exp, [self.bk(b)], ["BIG2"])
        self.tt("dve", xts[0:12, 0:4096], xts[0:12, 0:4096], mts[0:12, 0:4096], ALU.mult, ["BIG2", "BIG1"], ["BIG2"])
        P.dma("sp", self.xtab, xts[0:12, 0:4096], reads=["BIG2"], writes=["xtab"])
        self.barrier()
        for mt_ in range(2):
            i = self.nxt("ln", 2)
            P.dma("sp", self.lnx[i][:], I["mem"][mt_ * 128:(mt_ + 1) * 128, :], writes=["lnx%d" % i])
            for half in range(2):
                b = self.pbank()
                for c4 in range(4):
                    c = half * 4 + c4
                    self.tr(self.bank[b][:, c4 * 128:(c4 + 1) * 128], self.lnx[i][:, c * 128:(c + 1) * 128], ["lnx%d" % i], [self.bk(b)])
                self.cp("dve", self.memT[:, half * 4:half * 4 + 4, mt_ * 128:(mt_ + 1) * 128],
                        self.bank[b][:, :].rearrange("p (c t) -> p c t", t=128), [self.bk(b)], ["memT"])
        P.dma("sp", self.lng[:], I["ln_in_g"].partition_broadcast(128), writes=["lng"])
        P.dma("sp", self.lnb[:], I["ln_in_b"].partition_broadcast(128), writes=["lnb"])
        for tt in range(NT):
            i = self.nxt("ln", 2)
            P.dma("sp", self.lnx[i][:], I["x"][tt * 128:(tt + 1) * 128, :], writes=["lnx%d" % i])
            self.ln_inplace(self.lnx[i][:], "lnx%d" % i)
            P.dma("sp", self.hres[0][tt * 128:(tt + 1) * 128, :], self.lnx[i][:], reads=["lnx%d" % i], writes=["hres0"])
        self.barrier()

    def layer(self, l, last):
        P, I = self.P, self.I
        hin = self.hres[l % 2]
        for tt in range(NT):
            i = self.nxt("ln", 2)
            P.dma("sp", self.lnx[i][:], hin[tt * 128:(tt + 1) * 128, :], reads=["hres%d" % (l % 2)], writes=["lnx%d" % i])
            self.to_hT(self.lnx[i], "lnx%d" % i, tt)
        self.win = I["w_in"][l * D:(l + 1) * D, :]
        for mx in "MCADB":
            if mx in self.mixers:
                getattr(self, "mixer_" + mx)(l)
            else:
                self.zero_o(mx)
            self.barrier()
        self.merge(l, last)
        self.barrier()

    def zero_o(self, mx):
        c0, w = {"M": (OM, 256), "C": (OC, 512), "A": (OA, 512), "D": (OD, 512), "B": (OB, 512)}[mx]
        t = self.tmp[2]
        self.memset("pool", t[:, :], 0.0, ["tmp2"])
        for tt in range(NT):
            self.P.dma("sp", self.o_scr[tt * 128:(tt + 1) * 128, c0:c0 + w], t[:, 0:w], reads=["tmp2"], writes=["o_scr"])

    def attn_head(self, maps, vfn, vkey, nkt, dv1, table, band, post):
        nm = len(maps)
        G = self.G

        nqt = 4 if nm == 1 else 2
        QB = nqt * 128

        def accap(m, qt):
            bi = 4 + m * nqt + qt
            return self.bank[bi][:, 0:dv1], bi

        for qb in range(S // QB):
            q0 = qb * QB
            kts = []
            for kt in range(nkt):
                dk = kt * 128 - q0
                if band and (dk - (QB - 1) > 1024 or dk + 127 < -1024):
                    continue
                kts.append(kt)
            for idx, kt in enumerate(kts):
                for m, (q_ap, kfn) in enumerate(maps):
                    sb_ = m if nm == 2 else (idx % 2)
                    self.mm(self.bank[sb_][:, 0:QB], kfn(kt), q_ap[:, q0:q0 + QB], True, True, ["BIG0", "BIG1"], [self.bk(sb_)])
                    pti = self.nxt("pt", 4)
                    ptile = self.pt[pti]
                    if table:
                        pei = self.nxt("pe", 2)
                        self.act(self.pe_[pei][:, 0:QB], self.bank[sb_][:, 0:QB], AF.Exp, [self.bk(sb_)], ["pe%d" % pei], scale=0.125)
                        j0 = kt * 128 - q0 + XC
                        gs = G[:, j0 - (QB - 1):j0 + 1][:, ::-1]
                        self.tt("dve", ptile[:, 0:QB], self.pe_[pei][:, 0:QB], gs, ALU.mult, ["pe%d" % pei, "G"], ["pt%d" % pti])
                    else:
                        self.act(ptile[:, 0:QB], self.bank[sb_][:, 0:QB], AF.Exp, [self.bk(sb_)], ["pt%d" % pti], scale=0.125)
                    for qt in range(nqt):
                        acc, bi = accap(m, qt)
                        self.mm(acc, ptile[:, qt * 128:(qt + 1) * 128], vfn(kt), idx == 0, idx == len(kts) - 1,
                                ["pt%d" % pti, vkey], [self.bk(bi)])
            for qt in range(nqt):
                accs = [accap(m, qt) for m in range(nm)]
                post(qb * nqt + qt, [a for a, _ in accs], [self.bk(bi) for _, bi in accs])

    def post_simple(self, h, hd, ocol):
        def post(tt, accs, keys):
            acc = accs[0]
            sm = self.sm
            i = self.nxt("ost", 4)
            self.P.op("dve", lambda e: e.reciprocal(out=sm[:, 8:9], in_=acc[:, hd:hd + 1]), reads=keys, writes=["sm"])
            self.stt("dve", self.ost[i][:, 0:hd], acc[:, 0:hd], sm[:, 8:9], self.sg[:, tt, h * hd:(h + 1) * hd], ALU.mult, ALU.mult,
                     keys + ["sm", "BIG3"], ["ost%d" % i])
            self.P.dma("sp", self.o_scr[tt * 128:(tt + 1) * 128, ocol + h * hd:ocol + (h + 1) * hd], self.ost[i][:, 0:hd],
                       reads=["ost%d" % i], writes=["o_scr"])
        return post

    def cons_T(self, dst, dkey):
        def consume(b, ct, tb):
            self.cp("dve" if (ct + tb) % 2 else "act", dst[:, ct, tb * 512:(tb + 1) * 512], self.bank[b][:, :], [self.bk(b)], [dkey])
        return consume

    def cons_sg(self, c0, n):
        def consume(b, tt):
            self.act(self.sg[:, tt, c0:c0 + n], self.bank[b][:, 0:n], AF.Silu, [self.bk(b)], ["BIG3"])
        return consume

    def cons_v(self, V1, h0, nh, hd):
        def consume(b, tt):
            self.cp("dve", V1[:, tt, h0:h0 + nh, 0:hd], self.bank[b][:, 0:nh * hd].rearrange("p (h d) -> p h d", d=hd), [self.bk(b)], ["BIG2"])
        return consume

    def mixer_M(self, l):
        P, I = self.P, self.I
        wkv = I["w_mem_kv"][l * D:(l + 1) * D, :]
        V1 = self.BIG[2][:, 0:520].rearrange("p (k h d) -> p k h d", k=2, h=4, d=65)
        self.memset("pool", self.BIG[2][:, 0:520], 1.0, ["BIG2"])
        memT = self.memT
        wl = self.wload(wkv[:, 0:256], 256)
        self.proj_T(wl, 256, lambda b, ct, tb: self.cp("dve", self.kT[:, ct, 0:256], self.bank[b][:, 0:256], [self.bk(b)], ["BIG1"]),
                    rhs_fn=lambda c, tb: memT[:, c, 0:256], rkey="memT", ntb=1, tbw=256)
        wl = self.wload(wkv[:, 256:512], 256)
        self.proj_N(wl, 256, self.cons_v(V1, 0, 4, 64), lhs_fn=lambda c, tt: memT[:, c, tt * 128:(tt + 1) * 128], lkey="memT", ntt=2)
        wl = self.wload(self.win[:, M0:M0 + 256], 256)
        self.proj_T(wl, 256, self.cons_T(self.qT, "BIG0"))
        wl = self.wload(self.win[:, M0 + 256:M0 + 512], 256)
        self.proj_N(wl, 256, self.cons_sg(0, 256))
        if l == 0 and "m" in self.mixers:
            self.dbg("qT", self.BIG[0][:, 0:8192], ["BIG0"], BF16)
            self.dbg("kT", self.BIG[1][:, 0:8192], ["BIG1"], BF16)
            self.dbg("V1", self.BIG[2][:, 0:520], ["BIG2"], BF16)
            self.dbg("sg", self.BIG[3][:, 0:8192], ["BIG3"], BF16)
            self.dbg("hT", self.hT[:, :, :].rearrange("p c t -> p (c t)"), ["hT"], BF16)
        for h in range(4):
            ct, pb = h // 2, (h % 2) * 64
            q_ap = self.qT[pb:pb + 64, ct, :]
            self.attn_head([(q_ap, lambda kt, ct=ct, pb=pb: self.kT[pb:pb + 64, ct, kt * 128:(kt + 1) * 128])],
                           lambda kt, h=h: V1[:, kt, h, :], "BIG2", 2, 65, False, False, self.post_simple(h, 64, OM))

    def mixer_A(self, l):
        P = self.P
        V1 = self.BIG[2][:, 0:8320].rearrange("p (k h d) -> p k h d", k=16, h=8, d=65)
        self.memset("pool", self.BIG[2][:, 0:8320], 1.0, ["BIG2"])
        for j in range(2):
            wl = self.wload(self.win[:, A0 + j * 256:A0 + (j + 1) * 256], 256)
            self.proj_T(wl, 256, lambda b, ct, tb, j=j: self.cons_T(self.qT, "BIG0")(b, ct + 2 * j, tb))
        for j in range(2):
            wl = self.wload(self.win[:, A0 + 512 + j * 256:A0 + 512 + (j + 1) * 256], 256)
            self.proj_T(wl, 256, lambda b, ct, tb, j=j: self.cons_T(self.kT, "BIG1")(b, ct + 2 * j, tb))
        for j in range(2):
            wl = self.wload(self.win[:, A0 + 1024 + j * 256:A0 + 1024 + (j + 1) * 256], 256)
            self.proj_N(wl, 256, self.cons_v(V1, 4 * j, 4, 64))
        for j in range(2):
            wl = self.wload(self.win[:, A0 + 1536 + j * 256:A0 + 1536 + (j + 1) * 256], 256)
            self.proj_N(wl, 256, self.cons_sg(j * 256, 256))
        for h in range(8):
            ct, pb = h // 2, (h % 2) * 64
            P.dma("sp", self.G[:, 0:3968], AP(self.xtab.tensor, h * 4096, [[1, 128], [1, 3968]]), reads=["xtab"], writes=["G"])
            q_ap = self.qT[pb:pb + 64, ct, :]
            self.attn_head([(q_ap, lambda kt, ct=ct, pb=pb: self.kT[pb:pb + 64, ct, kt * 128:(kt + 1) * 128])],
                           lambda kt, h=h: V1[:, kt, h, :], "BIG2", 16, 65, True, True, self.post_simple(h, 64, OA))

    def normrope(self, b, tt, nh, gcol):
        n = nh * 64
        sm = self.sm
        t0, t1, t2 = self.tmp
        ps = self.bank[b][:, 0:n]
        self.act(t0[:, 0:n], ps, AF.Square, [self.bk(b)], ["tmp0"])
        self.P.op("dve", lambda e: e.reduce_sum(out=sm[:, 16:16 + nh], in_=t0[:, 0:n].rearrange("p (h d) -> p h d", d=64), axis=AX.X),
                  reads=["tmp0"], writes=["sm"])
        self.rsqrt_cols(sm[:, 16:16 + nh], sm[:, 24:24 + nh], 1.0 / 64, 1e-6)
        v3 = lambda ap: ap.rearrange("p (h d) -> p h d", d=64)
        self.tt("dve", v3(t0[:, 0:n]), v3(ps), sm[:, 24:24 + nh].unsqueeze(2).broadcast_to([128, nh, 64]), ALU.mult,
                [self.bk(b), "sm"], ["tmp0"])
        self.tt("dve", v3(t0[:, 0:n]), v3(t0[:, 0:n]), self.gq[:, gcol:gcol + 64].unsqueeze(1).broadcast_to([128, nh, 64]), ALU.mult,
                ["tmp0", "gq"], ["tmp0"])
        self.tt("pool", v3(t1[:, 0:n]), v3(t0[:, 0:n]), self.ropec[:, tt, :].unsqueeze(1).broadcast_to([128, nh, 64]), ALU.mult,
                ["tmp0", "ropec"], ["tmp1"])
        v5 = lambda ap: ap.rearrange("p (h a b c) -> p h a b c", a=2, b=2, c=16)
        rs = self.ropes[:, tt, :].rearrange("p (a b c) -> p a b c", a=2, b=2, c=16)
        for bb in range(2):
            self.tt("dve", v5(t2[:, 0:n])[:, :, :, bb, :], v5(t0[:, 0:n])[:, :, :, 1 - bb, :],
                    rs[:, :, bb, :].unsqueeze(1).broadcast_to([128, nh, 2, 16]), ALU.mult, ["tmp0", "ropes"], ["tmp2"])
        self.tt("dve", t1[:, 0:n], t1[:, 0:n], t2[:, 0:n], ALU.add, ["tmp1", "tmp2"], ["tmp1"])

    def mixer_C(self, l):
        P, I = self.P, self.I
        V1 = self.BIG[2][:, 0:2080].rearrange("p (k h d) -> p k h d", k=16, h=2, d=65)
        self.memset("pool", self.BIG[2][:, 0:2080], 1.0, ["BIG2"])
        P.dma("sp", self.gq[:, 0:64], I["c_qnorm_g"][l:l + 1, :].partition_broadcast(128), writes=["gq"])
        P.dma("sp", self.gq[:, 64:128], I["c_knorm_g"][l:l + 1, :].partition_broadcast(128), writes=["gq"])
        t1 = self.tmp[1]
        for j in range(2):
            wl = self.wload(self.win[:, C0 + j * 256:C0 + (j + 1) * 256], 256)

            def cons_q(b, tt, j=j):
                self.normrope(b, tt, 4, 0)
                b2 = self.pbank()
                for u in range(2):
                    self.tr(self.bank[b2][:, u * 128:(u + 1) * 128], t1[:, u * 128:(u + 1) * 128], ["tmp1"], [self.bk(b2)])
                self.cp("act", self.qT[:, 2 * j:2 * j + 2, tt * 128:(tt + 1) * 128],
                        self.bank[b2][:, 0:256].rearrange("p (c t) -> p c t", t=128), [self.bk(b2)], ["BIG0"])
            self.proj_N(wl, 256, cons_q)
        wl = self.wload(self.win[:, C0 + 512:C0 + 768], 256)

        def cons_kv(b, tt):
            self.cp("act", V1[:, tt, :, 0:64], self.bank[b][:, 128:256].rearrange("p (h d) -> p h d", d=64), [self.bk(b)], ["BIG2"])
            self.normrope(b, tt, 2, 64)
            t2 = self.tmp[2]
            self.cp("dve", t2[:, 0:256].rearrange("p (g r d) -> p g r d", g=2, r=2, d=64),
                    t1[:, 0:128].rearrange("p (g d) -> p g d", d=64).unsqueeze(2).broadcast_to([128, 2, 2, 64]), ["tmp1"], ["tmp2"])
            b2 = self.pbank()
            for u in range(2):
                self.tr(self.bank[b2][:, u * 128:(u + 1) * 128], t2[:, u * 128:(u + 1) * 128], ["tmp2"], [self.bk(b2)])
            self.cp("act", self.kT[:, 0:2, tt * 128:(tt + 1) * 128],
                    self.bank[b2][:, 0:256].rearrange("p (c t) -> p c t", t=128), [self.bk(b2)], ["BIG1"])
        self.proj_N(wl, 256, cons_kv)
        for j in range(2):
            wl = self.wload(self.win[:, C0 + 768 + j * 256:C0 + 768 + (j + 1) * 256], 256)
            self.proj_N(wl, 256, self.cons_sg(j * 256, 256))
        for h in range(8):
            ct, pb = h // 2, (h % 2) * 64
            g = h // 4
            q_ap = self.qT[pb:pb + 64, ct, :]
            self.attn_head([(q_ap, lambda kt, g=g, pb=pb: self.kT[pb:pb + 64, g, kt * 128:(kt + 1) * 128])],
                           lambda kt, g=g: V1[:, kt, g, :], "BIG2", 16, 65, False, False, self.post_simple(h, 64, OC))

    def mixer_D(self, l):
        P, I = self.P, self.I
        V1 = self.BIG[2][:, 0:8256].rearrange("p (k h d) -> p k h d", k=16, h=4, d=129)
        self.memset("pool", self.BIG[2][:, 0:8256], 1.0, ["BIG2"])
        lam_init = 0.8 - 0.6 * math.exp(-0.3 * l)
        lamt, sm = self.lamt, self.sm
        P.dma("sp", lamt[:, 0:256], I["d_lambda"][l:l + 1, :].partition_broadcast(128), writes=["lamt"])
        P.dma("sp", self.subg[:], I["d_subln_g"][l:l + 1, :].partition_broadcast(128), writes=["subg"])
        self.ts("pool", self.subg[:], self.subg[:], 1.0 - lam_init, None, ALU.mult, None, ["subg"], ["subg"])
        lv = lamt[:, 0:256].rearrange("p (a b c) -> p a b c", a=2, b=2, c=64)
        lp = self.tmp[2][:, 0:128].rearrange("p (a c) -> p a c", c=64)
        self.tt("dve", lp, lv[:, :, 0, :], lv[:, :, 1, :], ALU.mult, ["lamt"], ["tmp2"])
        P.op("dve", lambda e: e.reduce_sum(out=lamt[:, 256:258], in_=lp, axis=AX.X), reads=["tmp2"], writes=["lamt"])
        self.act(lamt[:, 258:260], lamt[:, 256:258], AF.Exp, ["lamt"], ["lamt"])
        self.tt("dve", lamt[:, 260:261], lamt[:, 259:260], lamt[:, 258:259], ALU.subtract, ["lamt"], ["lamt"])
        self.ts("dve", lamt[:, 260:261], lamt[:, 260:261], -lam_init, None, ALU.add, None, ["lamt"], ["lamt"])
        for j in range(2):
            wl = self.wload(self.win[:, D0 + j * 256:D0 + (j + 1) * 256], 256)
            self.proj_T(wl, 256, lambda b, ct, tb, j=j: self.cons_T(self.qT, "BIG0")(b, ct + 2 * j, tb))
        for j in range(2):
            wl = self.wload(self.win[:, D0 + 512 + j * 256:D0 + 512 + (j + 1) * 256], 256)
            self.proj_T(wl, 256, lambda b, ct, tb, j=j: self.cons_T(self.kT, "BIG1")(b, ct + 2 * j, tb))
        for j in range(2):
            wl = self.wload(self.win[:, D0 + 1024 + j * 256:D0 + 1024 + (j + 1) * 256], 256)
            self.proj_N(wl, 256, self.cons_v(V1, 2 * j, 2, 128))
        for j in range(2):
            wl = self.wload(self.win[:, D0 + 1536 + j * 256:D0 + 1536 + (j + 1) * 256], 256)
            self.proj_N(wl, 256, self.cons_sg(j * 256, 256))
        t0 = self.tmp[0]
        for h in range(4):
            P.dma("sp", self.G[:, 0:3968], AP(self.xtab.tensor, (8 + h) * 4096, [[1, 128], [1, 3968]]), reads=["xtab"], writes=["G"])

            def post(tt, accs, keys, h=h):
                a1, a2 = accs
                i = self.nxt("ost", 4)
                P.op("dve", lambda e: e.reciprocal(out=sm[:, 8:9], in_=a1[:, 128:129]), reads=keys, writes=["sm"])
                P.op("dve", lambda e: e.reciprocal(out=sm[:, 9:10], in_=a2[:, 128:129]), reads=keys, writes=["sm"])
                self.tt("dve", sm[:, 9:10], sm[:, 9:10], lamt[:, 260:261], ALU.mult, ["sm", "lamt"], ["sm"])
                self.ts("dve", t0[:, 0:128], a1[:, 0:128], sm[:, 8:9], None, ALU.mult, None, keys + ["sm"], ["tmp0"])
                self.stt("dve", t0[:, 128:256], a2[:, 0:128], sm[:, 9:10], t0[:, 0:128], ALU.mult, ALU.add, keys + ["sm", "tmp0"], ["tmp0"])
                self.memset("dve", sm[:, 10:11], 0.0, ["sm"])
                self.act(t0[:, 256:384], t0[:, 128:256], AF.Square, ["tmp0", "sm"], ["tmp0", "sm"], accum_out=sm[:, 10:11])
                self.rsqrt_cols(sm[:, 10:11], sm[:, 11:12], 1.0 / 128, 1e-5)
                self.stt("dve", t0[:, 128:256], t0[:, 128:256], sm[:, 11:12], self.subg[:], ALU.mult, ALU.mult, ["tmp0", "sm", "subg"], ["tmp0"])
                self.tt("dve", self.ost[i][:], t0[:, 128:256], self.sg[:, tt, h * 128:(h + 1) * 128], ALU.mult, ["tmp0", "BIG3"], ["ost%d" % i])
                P.dma("sp", self.o_scr[tt * 128:(tt + 1) * 128, OD + h * 128:OD + (h + 1) * 128], self.ost[i][:],
                      reads=["ost%d" % i], writes=["o_scr"])
            maps = [(self.qT[c * 64:(c + 1) * 64, h, :], (lambda kt, c=c, h=h: self.kT[c * 64:(c + 1) * 64, h, kt * 128:(kt + 1) * 128]))
                    for c in range(2)]
            self.attn_head(maps, lambda kt, h=h: V1[:, kt, h, :], "BIG2", 16, 129, True, False, post)

    def dbg(self, name, ap, reads, dt=F32):
        if not self.debug:
            return
        t = self.nc.dram_tensor("dbg_" + name, list(ap.shape), dt, kind="ExternalOutput").ap()
        self.P.dma("sp", t, ap, reads=reads, is_output=True)
        self.dbg_outs.append("dbg_" + name)

    def mixer_B(self, l):
        P, I = self.P, self.I
        CW = 0.6065306597126334
        t_ring = self.tmp
        mub = self.lnx[0][:, 0:768].rearrange("p (v n) -> p v n", n=256)

        def load_mu(c0):
            for v in range(2):
                P.dma("sp", mub[:, 1 + v, :], I["shift_mu"][l * 2 + v:l * 2 + v + 1, c0:c0 + 256].partition_broadcast(128), writes=["lnx0"])
            self.tt("dve", mub[:, 0, :], mub[:, 1, :], mub[:, 2, :], ALU.add, ["lnx0"], ["lnx0"])
            self.ts("dve", mub[:, 0, :], mub[:, 0, :], -1.0, 1.0, ALU.mult, ALU.add, ["lnx0"], ["lnx0"])
            return [(mub[:, 0, :], "lnx0"), (mub[:, 1, :], "lnx0"), (mub[:, 2, :], "lnx0")]

        def stage_out(dst_ap, dkey, func=None):
            def f(b, n_part=128, ncol=512):
                i = self.nxt("tmp", 3)
                if func is None:
                    self.cp("dve", t_ring[i][0:n_part, 0:ncol], self.bank[b][0:n_part, 0:ncol], [self.bk(b)], ["tmp%d" % i])
                else:
                    self.act(t_ring[i][0:n_part, 0:ncol], self.bank[b][0:n_part, 0:ncol], func, [self.bk(b)], ["tmp%d" % i])
                P.dma("sp", dst_ap, t_ring[i][0:n_part, 0:ncol], reads=["tmp%d" % i], writes=[dkey])
            return f

        for j in range(4):
            c0 = j * 256
            wl = self.wload(self.win[:, B0 + c0:B0 + c0 + 256], 256, variants=load_mu(c0))
            self.proj_T(wl, 256, lambda b, ct, tb, c0=c0: stage_out(self.rk_scr[c0 + ct * 128:c0 + (ct + 1) * 128, tb * 512:(tb + 1) * 512], "rk_scr")(b),
                        shifts=(0, -1, 1))
        for j in range(2):
            c0 = 1024 + j * 256
            wl = self.wload(self.win[:, B0 + c0:B0 + c0 + 256], 256, variants=load_mu(c0))
            self.proj_N(wl, 256, lambda b, tt, j=j: stage_out(self.v_scr[tt * 128:(tt + 1) * 128, j * 256:(j + 1) * 256], "v_scr")(b, 128, 256),
                        shifts=(0, -1, 1))
        wl = self.wload(self.win[:, B0 + 1536:B0 + 1792], 256, variants=load_mu(1536))
        self.proj_T(wl, 256, lambda b, ct, tb: stage_out(self.wa_scr[ct * 128:(ct + 1) * 128, tb * 512:(tb + 1) * 512], "wa_scr",
                                                         AF.Tanh if ct == 0 else AF.Copy)(b), shifts=(0, -1, 1))
        for j in range(2):
            wl = self.wload(self.win[:, B0 + 1792 + j * 256:B0 + 1792 + (j + 1) * 256], 256)
            self.proj_N(wl, 256, lambda b, tt, j=j: stage_out(self.sg_scr[tt * 128:(tt + 1) * 128, j * 256:(j + 1) * 256], "sg_scr", AF.Silu)(b, 128, 256))
        self.barrier()
        slots = []
        for bi in range(4):
            a = self.BIG[bi][:].bitcast(F32)
            for q in range(4):
                slots.append(a[:, q * 1024:(q + 1) * 1024])
        for q in range(4):
            slots.append(self.G[:, q * 1024:(q + 1) * 1024])
        for wi in range(2):
            a = self.wst[wi][:, :, :].rearrange("p c n -> p (c n)")
            for q in range(2):
                slots.append(a[:, q * 1024:(q + 1) * 1024])
        si = [0]

        def slot(full=True):
            if full:
                if si[0] % 2:
                    si[0] += 1
                a = slots[si[0] // 2]
                si[0] += 2
                return a
            a = slots[si[0] // 2][:, (si[0] % 2) * 512:(si[0] % 2) * 512 + 512]
            si[0] += 1
            return a

        def v3(ap, w):
            return ap[0:64, 0:8 * w].rearrange("p (h t) -> p h t", t=w)

        w_upS = slot()[0:64, :].rearrange("p (e c) -> p e c", c=512)
        a_upS = slot()[0:64, :].rearrange("p (e c) -> p e c", c=512)
        w0B = slot()[0:64, :].rearrange("p (e c) -> p e c", c=512)
        rkT = slot()[0:64, :].rearrange("p (g t) -> p g t", t=64)
        AR = slot()[0:64, :].rearrange("p (h t) -> p h t", t=128)
        NP = [slot()[0:64, :].rearrange("p (h t) -> p h t", t=128) for _ in range(2)]
        ysb = slot()[0:64, 0:520]
        rmaskS = slot(False)[0:64, 0:384].rearrange("p (e n) -> p e n", n=192)
        waT = slot(False)[0:64, 0:256].rearrange("p (g t) -> p g t", t=64)
        vtok = slot(False)[0:64, :]
        sgw = slot(False)[0:64, :]
        asT, kkn, ke, be, tE0, tE1, bch, kch, z = [v3(slot(False), 64) for _ in range(9)]
        eLs, Bt, Kt = [slot(False)[0:64, :] for _ in range(3)]
        Mm = [v3(slot(False), 64) for _ in range(2)]
        Mrb, Mak, Mrk, Xs, Us, tmpS = [v3(slot(False), 64) for _ in range(6)]
        Sst = [v3(slot(False), 64) for _ in range(2)]
        assert si[0] <= 2 * len(slots), si[0]
        rwp = self.gq[0:64, 0:40]
        omka = self.gq[0:64, 40:48]
        ident64 = self.ident[0:64, 0:64]
        ones64 = self.onesf[0:64, 0:64]
        P.dma("sp", w_upS, I["rwkv_w_up"][l * 128:(l + 1) * 128, :].rearrange("(e r) c -> r e c", r=64), writes=["w_upS"])
        P.dma("sp", a_upS, I["rwkv_a_up"][l * 128:(l + 1) * 128, :].rearrange("(e r) c -> r e c", r=64), writes=["a_upS"])
        for e in range(2):
            P.dma("sp", w0B[:, e, :], I["rwkv_w0"][l * 2 + e:l * 2 + e + 1, :].partition_broadcast(64), writes=["w0B"])
        P.dma("sp", rmaskS, I["rmask"].rearrange("(e p) n -> p e n", p=64), writes=["rmaskS"])
        pm = self.tmp[0]
        P.dma("sp", pm[0:16, 0:64], I["rwkv_a0"][l * 2:(l + 1) * 2, :].rearrange("e (h c) -> (e h) c", c=64), writes=["tmp0"])
        P.dma("sp", pm[16:24, 0:64], I["rwkv_k_k"][l:l + 1, :].rearrange("e (h c) -> (e h) c", c=64), writes=["tmp0"])
        P.dma("sp", pm[24:32, 0:64], I["rwkv_k_a"][l:l + 1, :].rearrange("e (h c) -> (e h) c", c=64), writes=["tmp0"])
        P.dma("sp", pm[32:40, 0:64], I["rwkv_r_k"][l:l + 1, :].rearrange("e (h c) -> (e h) c", c=64), writes=["tmp0"])
        b = self.pbank()
        self.P.op("pe", lambda e_: e_.transpose(self.bank[b][0:64, 0:40], pm[0:40, 0:64], self.ident[0:40, 0:40]), reads=["tmp0", "ident"], writes=[self.bk(b)])
        self.cp("dve", rwp, self.bank[b][0:64, 0:40], [self.bk(b)], ["gq"])
        self.ts("dve", omka, rwp[:, 24:32], -1.0, 1.0, ALU.mult, ALU.add, ["gq"], ["gq"])
        bc3 = lambda ap: ap.unsqueeze(2).broadcast_to([64, 8, 64])
        hb = lambda b_, h, w=64: self.bank[b_][0:64, h * w:(h + 1) * w]
        b3 = lambda b_, w=64: self.bank[b_][0:64, 0:8 * w].rearrange("p (h t) -> p h t", t=w)

        for e in range(2):
            Scur = 0
            self.memset("dve", Sst[0], 0.0, ["S0"])
            order = range(32) if e == 0 else range(31, -1, -1)
            tl = 63 if e == 0 else 0
            mS, mI, mT = rmaskS[:, e, 0:64], rmaskS[:, e, 64:128], rmaskS[:, e, 128:192]
            for ch in order:
                t0 = ch * 64
                P.dma("sp", rkT, self.rk_scr.rearrange("(g p) t -> p g t", p=64)[:, :, t0:t0 + 64], reads=["rk_scr"], writes=["rkT"])
                P.dma("sp", waT, self.wa_scr.rearrange("(g p) t -> p g t", p=64)[:, :, t0:t0 + 64], reads=["wa_scr"], writes=["waT"])
                P.dma("sp", vtok, self.v_scr[t0:t0 + 64, :], reads=["v_scr"], writes=["vtok"])
                rT, kT_ = rkT[:, 0:8, :], rkT[:, 8:16, :]
                b = self.pbank()
                self.mm(self.bank[b][0:64, :], waT[:, e, :], w_upS[:, e, :], True, True, ["waT", "w_upS"], [self.bk(b)])
                self.tt("dve", sgw, self.bank[b][0:64, :], w0B[:, e, :], ALU.add, [self.bk(b), "w0B"], ["sgw"])
                self.act(sgw, sgw, AF.Sigmoid, ["sgw"], ["sgw"])
                b = self.pbank()
                for h in range(8):
                    self.mm(hb(b, h), a_upS[:, e, h * 64:(h + 1) * 64], waT[:, 2 + e, :], True, True, ["waT", "a_upS"], [self.bk(b)])
                self.tt("dve", asT, b3(b), bc3(rwp[:, e * 8:(e + 1) * 8]), ALU.add, [self.bk(b), "gq"], ["asT"])
                self.act(asT, asT, AF.Sigmoid, ["asT"], ["asT"])
                self.tt("dve", kkn, kT_, bc3(rwp[:, 16:24]), ALU.mult, ["rkT", "gq"], ["kkn"])
                self.act(tE0, kkn, AF.Square, ["kkn"], ["tE0"])
                b = self.pbank()
                self.mm(self.bank[b][0:64, :], ones64, tE0.rearrange("p h t -> p (h t)"), True, True, ["tE0", "onesf"], [self.bk(b)])
                self.act(tE0, b3(b), AF.Sqrt, [self.bk(b)], ["tE0"])
                self.ts("dve", tE0, tE0, 1e-12, None, ALU.max, None, ["tE0"], ["tE0"])
                self.P.op("dve", lambda e_: e_.reciprocal(out=tE0, in_=tE0), reads=["tE0"], writes=["tE0"])
                self.tt("dve", kkn, kkn, tE0, ALU.mult, ["kkn", "tE0"], ["kkn"])
                self.tt("pool", ke, asT, bc3(rwp[:, 24:32]), ALU.mult, ["asT", "gq"], ["ke"])
                self.tt("pool", ke, ke, bc3(omka), ALU.add, ["ke", "gq"], ["ke"])
                self.tt("pool", ke, ke, kT_, ALU.mult, ["ke", "rkT"], ["ke"])
                self.tt("pool", be, kkn, asT, ALU.mult, ["kkn", "asT"], ["be"])
                self.tt("pool", z, rT, ke, ALU.mult, ["rkT", "ke"], ["z"])
                bLi = self.pbank()
                for h in range(8):
                    self.mm(hb(bLi, h), sgw[:, h * 64:(h + 1) * 64], mI, True, True, ["sgw", "rmaskS"], [self.bk(bLi)])
                self.act(tE0, b3(bLi), AF.Exp, [self.bk(bLi)], ["tE0"], scale=-CW)
                self.act(tE1, b3(bLi), AF.Exp, [self.bk(bLi)], ["tE1"], scale=CW)
                self.tt("dve", AR[:, :, 64:128], rT, tE0, ALU.mult, ["rkT", "tE0"], ["AR"])
                self.cp("dve", self.sm[0:64, 32:40], tE0[:, :, tl], ["tE0"], ["sm"])
                self.tt("dve", bch, be, tE1, ALU.mult, ["be", "tE1"], ["bch"])
                self.tt("pool", kch, ke, tE1, ALU.mult, ["ke", "tE1"], ["kch"])
                bLe = self.pbank()
                for h in range(8):
                    self.mm(hb(bLe, h), sgw[:, h * 64:(h + 1) * 64], mS, True, True, ["sgw", "rmaskS"], [self.bk(bLe)])
                self.act(tE0, b3(bLe), AF.Exp, [self.bk(bLe)], ["tE0"], scale=-CW)
                self.stt("dve", AR[:, :, 0:64], kkn, -1.0, tE0, ALU.mult, ALU.mult, ["kkn", "tE0"], ["AR"])
                b = self.pbank()
                self.mm(self.bank[b][0:64, :], mT, sgw, True, True, ["sgw", "rmaskS"], [self.bk(b)])
                self.act(eLs, self.bank[b][0:64, :], AF.Exp, [self.bk(b)], ["eLs"], scale=-CW)
                for src, skey, dst, dkey in ((be, "be", Bt, "Bt"), (ke, "ke", Kt, "Kt")):
                    b = self.pbank()
                    for h in range(8):
                        self.P.op("pe", lambda e_, b=b, h=h, src=src: e_.transpose(hb(b, h), src[:, h, :], ident64), reads=[skey, "ident"], writes=[self.bk(b)])
                    self.tt("dve", dst, self.bank[b][0:64, :], eLs, ALU.mult, [self.bk(b), "eLs"], [dkey])
                b = self.pbank()
                for h in range(8):
                    self.mm(self.bank[b][0:64, h:h + 1], z[:, h, :], rwp[:, 32 + h:33 + h], True, True, ["z", "gq"], [self.bk(b)])
                self.cp("act", ysb[:, 512:520], self.bank[b][0:64, 0:8], [self.bk(b)], ["ysb"])
                for h in range(8):
                    self.mm(self.bank[h // 4][0:64, (h % 4) * 128:(h % 4 + 1) * 128], bch[:, h, :], AR[:, h, :], True, True, ["bch", "AR"], [self.bk(h // 4)])
                for h in range(8):
                    self.mm(self.bank[4 + h // 4][0:64, (h % 4) * 128:(h % 4 + 1) * 128], kch[:, h, :], AR[:, h, :], True, True, ["kch", "AR"], [self.bk(4 + h // 4)])
                for h in range(8):
                    self.mm(hb(6, h), AR[:, h, 0:64], bch[:, h, :], True, True, ["bch", "AR"], [self.bk(6)])
                m4 = lambda m_: m_.unsqueeze(1).broadcast_to([64, 4, 64])
                for g in range(2):
                    bb = self.bank[g][0:64, :].rearrange("p (h t) -> p h t", t=128)
                    kb = self.bank[4 + g][0:64, :].rearrange("p (h t) -> p h t", t=128)
                    self.tt("dve", NP[0][:, 4 * g:4 * g + 4, 0:64], bb[:, :, 0:64], m4(mS), ALU.mult, [self.bk(g), "rmaskS"], ["NP0"])
                    self.tt("dve", Mrb[:, 4 * g:4 * g + 4, :], bb[:, :, 64:128], m4(mI), ALU.mult, [self.bk(g), "rmaskS"], ["Mrb"])
                    self.tt("dve", Mak[:, 4 * g:4 * g + 4, :], kb[:, :, 0:64], m4(mS), ALU.mult, [self.bk(4 + g), "rmaskS"], ["Mak"])
                    self.tt("dve", Mrk[:, 4 * g:4 * g + 4, :], kb[:, :, 64:128], m4(mI), ALU.mult, [self.bk(4 + g), "rmaskS"], ["Mrk"])
                self.tt("dve", Mm[0], b3(6), mT.unsqueeze(1).broadcast_to([64, 8, 64]), ALU.mult, [self.bk(6), "rmaskS"], ["Mm0"])
                self.tt("pool", NP[0][:, :, 64:128], NP[0][:, :, 0:64], ident64.unsqueeze(1).broadcast_to([64, 8, 64]), ALU.add, ["NP0", "ident"], ["NP0"])
                cur = 0
                for step in range(6):
                    nx = 1 - cur
                    pbk = (0, 1) if step % 2 == 0 else (4, 5)
                    mbk = 6 if step % 2 else 7
                    ncur, nnx, mcur, mnx = "NP%d" % cur, "NP%d" % nx, "Mm%d" % cur, "Mm%d" % nx
                    if step == 0:
                        for h in range(8):
                            self.mm(self.bank[pbk[h // 4]][0:64, (h % 4) * 128:(h % 4) * 128 + 64], Mm[cur][:, h, :], NP[cur][:, h, 0:64], True, True,
                                    [mcur, ncur], [self.bk(pbk[h // 4])])
                    elif step < 5:
                        for h in range(8):
                            self.mm(self.bank[pbk[h // 4]][0:64, (h % 4) * 128:(h % 4 + 1) * 128], Mm[cur][:, h, :], NP[cur][:, h, :], True, True,
                                    [mcur, ncur], [self.bk(pbk[h // 4])])
                    else:
                        for h in range(8):
                            self.mm(self.bank[pbk[h // 4]][0:64, (h % 4) * 128 + 64:(h % 4 + 1) * 128], Mm[cur][:, h, :], NP[cur][:, h, 64:128], True, True,
                                    [mcur, ncur], [self.bk(pbk[h // 4])])
                    if step < 5:
                        for h in range(8):
                            self.mm(hb(mbk, h), NP[cur][:, h, 0:64], Mm[cur][:, h, :], True, True, [mcur, ncur], [self.bk(mbk)])
                    for g in range(2):
                        pv = self.bank[pbk[g]][0:64, :].rearrange("p (h t) -> p h t", t=128)
                        if step < 5:
                            self.cp("act", NP[nx][:, 4 * g:4 * g + 4, 0:64], pv[:, :, 0:64], [self.bk(pbk[g])], [nnx])
                        if step == 0:
                            self.cp("dve", NP[nx][:, 4 * g:4 * g + 4, 64:128], NP[cur][:, 4 * g:4 * g + 4, 64:128], [ncur], [nnx])
                        else:
                            self.tt("dve", NP[nx][:, 4 * g:4 * g + 4, 64:128], pv[:, :, 64:128], NP[cur][:, 4 * g:4 * g + 4, 64:128], ALU.add,
                                    [self.bk(pbk[g]), ncur], [nnx])
                    if step < 5:
                        self.cp("act", Mm[nx], b3(mbk), [self.bk(mbk)], [mnx])
                    cur = nx
                TT, tkey = NP[cur], "NP%d" % cur
                S0, skey = Sst[Scur], "S%d" % Scur
                S1, s1key = Sst[1 - Scur], "S%d" % (1 - Scur)
                bX = self.pbank()
                for h in range(8):
                    self.mm(hb(bX, h), AR[:, h, 0:64], S0[:, h, :], True, False, ["AR", skey], [self.bk(bX)])
                    self.mm(hb(bX, h), Mak[:, h, :], vtok[:, h * 64:(h + 1) * 64], False, True, ["Mak", "vtok"], [self.bk(bX)])
                self.cp("dve", Xs, b3(bX), [self.bk(bX)], ["Xs"])
                bU = self.pbank()
                for h in range(8):
                    self.mm(hb(bU, h), TT[:, h, 64:128], Xs[:, h, :], True, True, [tkey, "Xs"], [self.bk(bU)])
                self.cp("act", Us, b3(bU), [self.bk(bU)], ["Us"])
                bY = self.pbank()
                for h in range(8):
                    self.mm(hb(bY, h), AR[:, h, 64:128], S0[:, h, :], True, False, ["AR", skey], [self.bk(bY)])
                    self.mm(hb(bY, h), Mrb[:, h, :], Us[:, h, :], False, False, ["Mrb", "Us"], [self.bk(bY)])
                    self.mm(hb(bY, h), Mrk[:, h, :], vtok[:, h * 64:(h + 1) * 64], False, True, ["Mrk", "vtok"], [self.bk(bY)])
                self.cp("act", ysb[:, 0:512], self.bank[bY][0:64, :], [self.bk(bY)], ["ysb"])
                P.dma("sp", self.y_scr[e * S + t0:e * S + t0 + 64, :], ysb, reads=["ysb"], writes=["y_scr"])
                bS = self.pbank()
                for h in range(8):
                    self.mm(hb(bS, h), Bt[:, h * 64:(h + 1) * 64], Us[:, h, :], True, False, ["Bt", "Us"], [self.bk(bS)])
                    self.mm(hb(bS, h), Kt[:, h * 64:(h + 1) * 64], vtok[:, h * 64:(h + 1) * 64], False, True, ["Kt", "vtok"], [self.bk(bS)])
                self.tt("pool", tmpS, S0, bc3(self.sm[0:64, 32:40]), ALU.mult, [skey, "sm"], ["tmpS"])
                self.tt("dve", S1, tmpS, b3(bS), ALU.add, ["tmpS", self.bk(bS)], [s1key])
                Scur = 1 - Scur
        self.barrier()
        P.dma("sp", self.lng[:, 0:512], I["rwkv_ln_g"][l:l + 1, :].partition_broadcast(128), writes=["lng"])
        P.dma("sp", self.lnb[:, 0:512], I["rwkv_ln_b"][l:l + 1, :].partition_broadcast(128), writes=["lnb"])
        yf, yb, vt = self.lnx[0], self.lnx[1], self.junk
        sm = self.sm
        t0_, t1_, t2_ = self.tmp
        for tt in range(NT):
            P.dma("sp", yf[:, 0:520], self.y_scr[tt * 128:(tt + 1) * 128, :], reads=["y_scr"], writes=["lnx0"])
            P.dma("sp", yb[:, 0:520], self.y_scr[S + tt * 128:S + (tt + 1) * 128, :], reads=["y_scr"], writes=["lnx1"])
            P.dma("sp", vt[:, 0:512], self.v_scr[tt * 128:(tt + 1) * 128, :], reads=["v_scr"], writes=["junk"])
            P.dma("sp", vt[:, 512:1024], self.sg_scr[tt * 128:(tt + 1) * 128, :], reads=["sg_scr"], writes=["junk"])
            self.tt("dve", yf[:, 0:520], yf[:, 0:520], yb[:, 0:520], ALU.add, ["lnx0", "lnx1"], ["lnx0"])
            y3 = yf[:, 0:512].rearrange("p (h d) -> p h d", d=64)
            P.op("dve", lambda e_, y3=y3: e_.reduce_sum(out=sm[:, 40:48], in_=y3, axis=AX.X), reads=["lnx0"], writes=["sm"])
            self.ts("dve", sm[:, 40:48], sm[:, 40:48], -1.0 / 64, None, ALU.mult, None, ["sm"], ["sm"])
            self.tt("dve", y3, y3, sm[:, 40:48].unsqueeze(2).broadcast_to([128, 8, 64]), ALU.add, ["lnx0", "sm"], ["lnx0"])
            self.act(t0_[:, 0:512], yf[:, 0:512], AF.Square, ["lnx0"], ["tmp0"])
            P.op("dve", lambda e_: e_.reduce_sum(out=sm[:, 48:56], in_=t0_[:, 0:512].rearrange("p (h d) -> p h d", d=64), axis=AX.X), reads=["tmp0"], writes=["sm"])
            self.rsqrt_cols(sm[:, 48:56], sm[:, 56:64], 1.0 / 64, 64e-5)
            self.tt("dve", y3, y3, sm[:, 56:64].unsqueeze(2).broadcast_to([128, 8, 64]), ALU.mult, ["lnx0", "sm"], ["lnx0"])
            self.tt("dve", yf[:, 0:512], yf[:, 0:512], self.lng[:, 0:512], ALU.mult, ["lnx0", "lng"], ["lnx0"])
            self.tt("pool", yf[:, 0:512], yf[:, 0:512], self.lnb[:, 0:512], ALU.add, ["lnx0", "lnb"], ["lnx0"])
            self.tt("pool", t1_[:, 0:512].rearrange("p (h d) -> p h d", d=64), vt[:, 0:512].rearrange("p (h d) -> p h d", d=64),
                    yf[:, 512:520].unsqueeze(2).broadcast_to([128, 8, 64]), ALU.mult, ["junk", "lnx0"], ["tmp1"])
            self.tt("dve", t1_[:, 0:512], t1_[:, 0:512], yf[:, 0:512], ALU.add, ["tmp1", "lnx0"], ["tmp1"])
            self.tt("dve", t2_[:, 0:512], t1_[:, 0:512], vt[:, 512:1024], ALU.mult, ["tmp1", "junk"], ["tmp2"])
            P.dma("sp", self.o_scr[tt * 128:(tt + 1) * 128, OB:OB + 512], t2_[:, 0:512], reads=["tmp2"], writes=["o_scr"])

    def merge(self, l, last):
        P, I = self.P, self.I
        wg_all = I["w_gate"][l * D:(l + 1) * D, :]
        wb_all = I["w_branch"][l * 2304:(l + 1) * 2304, :]
        wo_all = I["w_out"][l * D:(l + 1) * D, :]
        P.dma("sp", self.lng[:], I["ln_g"][l:l + 1, :].partition_broadcast(128), writes=["lng"])
        P.dma("sp", self.lnb[:], I["ln_b"][l:l + 1, :].partition_broadcast(128), writes=["lnb"])
        oT = self.BIG[0][:, 0:9216].rearrange("p (j t) -> p j t", t=512)
        ygrp = self.BIG[1][:].bitcast(F32)[:, 0:4096].rearrange("p (q c) -> p q c", c=1024)
        otile = self.BIG[2][:].bitcast(F32)[:, 0:2304]
        yT = self.BIG[3][:, 0:4096].rearrange("p (c t) -> p c t", t=512)
        hgrp = self.G[:, 0:4096].rearrange("p (q c) -> p q c", c=1024)
        hin = self.hres[l % 2]
        hout = self.out if last else self.hres[(l + 1) % 2]
        t0, t1 = self.tmp[0], self.tmp[1]
        for grp in range(4):
            for tq in range(4):
                tt = grp * 4 + tq
                P.dma("sp", otile, self.o_scr[tt * 128:(tt + 1) * 128, :], reads=["o_scr"], writes=["BIG2"])
                P.dma("sp", hgrp[:, tq, :], hin[tt * 128:(tt + 1) * 128, :], reads=["hres%d" % (l % 2)], writes=["G"])
                for j4 in range(5):
                    nj = min(4, 18 - j4 * 4)
                    b = self.pbank()
                    for u in range(nj):
                        j = j4 * 4 + u
                        self.tr(self.bank[b][:, u * 128:(u + 1) * 128], otile[:, j * 128:(j + 1) * 128], ["BIG2"], [self.bk(b)])
                    self.cp("act" if j4 % 2 else "dve", oT[:, j4 * 4:j4 * 4 + nj, tq * 128:(tq + 1) * 128],
                            self.bank[b][:, 0:nj * 128].rearrange("p (c t) -> p c t", t=128), [self.bk(b)], ["BIG0"])
            for i, (r0, rw) in enumerate(BROWS):
                kci = rw // 128
                for cc in range(4):
                    wg, wgk = self.wload(wg_all[:, i * 1024 + cc * 256:i * 1024 + (cc + 1) * 256], 256)[0]
                    wb, wbk = self.wload(wb_all[r0:r0 + rw, cc * 256:(cc + 1) * 256], 256, kc=kci)[0]
                    bi = self.nxt("brow", 2)
                    P.dma("sp", self.brow[bi][0:1, :], I["b_gate"][l:l + 1, i * 1024 + cc * 256:i * 1024 + (cc + 1) * 256], writes=["brow%d" % bi])
                    for tq in range(4):
                        tt = grp * 4 + tq
                        b1 = self.pbank()
                        self.mm(self.bank[b1][:, 0:256], self.onesf[0:1, 0:128], self.brow[bi][0:1, :], True, False,
                                ["onesf", "brow%d" % bi], [self.bk(b1)])
                        for c in range(8):
                            self.mm(self.bank[b1][:, 0:256], self.hT[:, c, 1 + tt * 128:1 + (tt + 1) * 128], wg[:, c, :], False, c == 7,
                                    [wgk, "hT"], [self.bk(b1)])
                        self.act(t0[:, 0:256], self.bank[b1][:, 0:256], AF.Sigmoid, [self.bk(b1)], ["tmp0"])
                        b2 = self.pbank()
                        for c in range(kci):
                            self.mm(self.bank[b2][:, 0:256], oT[:, r0 // 128 + c, tq * 128:(tq + 1) * 128], wb[:, c, :], c == 0, c == kci - 1,
                                    [wbk, "BIG0"], [self.bk(b2)])
                        ysl = ygrp[:, tq, cc * 256:(cc + 1) * 256]
                        if i == 0:
                            self.tt("dve", ysl, self.bank[b2][:, 0:256], t0[:, 0:256], ALU.mult, [self.bk(b2), "tmp0"], ["BIG1"])
                        else:
                            self.tt("dve", t1[:, 0:256], self.bank[b2][:, 0:256], t0[:, 0:256], ALU.mult, [self.bk(b2), "tmp0"], ["tmp1"])
                            self.tt("pool", ysl, ysl, t1[:, 0:256], ALU.add, ["BIG1", "tmp1"], ["BIG1"])
            for tq in range(4):
                for half in range(2):
                    b = self.pbank()
                    for c4 in range(4):
                        c = half * 4 + c4
                        self.tr(self.bank[b][:, c4 * 128:(c4 + 1) * 128], ygrp[:, tq, c * 128:(c + 1) * 128], ["BIG1"], [self.bk(b)])
                    self.cp("act" if half else "dve", yT[:, half * 4:half * 4 + 4, tq * 128:(tq + 1) * 128],
                            self.bank[b][:, :].rearrange("p (c t) -> p c t", t=128), [self.bk(b)], ["BIG3"])
            for cc in range(4):
                wo, wok = self.wload(wo_all[:, cc * 256:(cc + 1) * 256], 256)[0]
                for tq in range(4):
                    b = self.pbank()
                    for c in range(8):
                        self.mm(self.bank[b][:, 0:256], yT[:, c, tq * 128:(tq + 1) * 128], wo[:, c, :], c == 0, c == 7, [wok, "BIG3"], [self.bk(b)])
                    hs = hgrp[:, tq, cc * 256:(cc + 1) * 256]
                    self.stt("dve", hs, hs, ALPHA, self.bank[b][:, 0:256], ALU.mult, ALU.add, ["G", self.bk(b)], ["G"])
            for tq in range(4):
                tt = grp * 4 + tq
                self.ln_inplace(hgrp[:, tq, :], "G")
                P.dma("sp", hout[tt * 128:(tt + 1) * 128, :], hgrp[:, tq, :], reads=["G"],
                      writes=["out" if last else "hres%d" % ((l + 1) % 2)], is_output=last)


def make_in_map(inputs, b, consts):
    m = {"x": np.ascontiguousarray(inputs["x"][b]), "mem": np.ascontiguousarray(inputs["mem"][b])}
    for nm, shp in IN_SPECS:
        if nm in consts:
            m[nm] = consts[nm]
        else:
            m[nm] = np.ascontiguousarray(np.asarray(inputs[nm], dtype=np.float32).reshape(shp))
    return m


def kernel(**inputs):
    consts = host_consts()
    kb = KB(debug=False)
    nb = inputs["x"].shape[0]
    in_maps = [make_in_map(inputs, b, consts) for b in range(nb)]
    res = run_bass_kernel_spmd(kb.nc, in_maps, core_ids=list(range(nb)))
    out = np.stack([np.asarray(r["out"], dtype=np.float32).reshape(S, D) for r in res.results], axis=0)
    return out
```

```python
import math
from concourse.ap import AP
import contextlib
import numpy as np
import concourse.bass as bass
import concourse.mybir as mybir
from concourse.bass_utils import run_bass_kernel_spmd

F32 = mybir.dt.float32
BF16 = mybir.dt.bfloat16
I32 = mybir.dt.int32
AF = mybir.ActivationFunctionType
ALU = mybir.AluOpType
AX = mybir.AxisListType

ENGS = ("pe", "act", "dve", "pool", "sp")
DMA_SEMS = 8


class Op:
    __slots__ = ("eng", "fn", "waits", "is_dma", "idx", "marked", "dma_slot", "dma_val", "prewait")

    def __init__(self, eng, fn, is_dma):
        self.eng = eng
        self.fn = fn
        self.is_dma = is_dma
        self.waits = []
        self.marked = False
        self.idx = None
        self.dma_slot = None
        self.dma_val = None
        self.prewait = None


class Prog:
    def __init__(self, nc, same_engine_sync=True):
        self.nc = nc
        self.ops = {e: [] for e in ENGS}
        self.last_write = {}
        self.readers = {}
        self.same_engine_sync = same_engine_sync
        self.dma_count = {e: 0 for e in ENGS}
        self.dma_hist = {e: [] for e in ENGS}
        self.all_dma_out = []
        self.stack = contextlib.ExitStack()
        self.n_ops = 0

    def sb(self, name, shape, dt):
        return self.stack.enter_context(self.nc.sbuf_tensor("s_" + name, list(shape), dt))

    def ps(self, name, shape, dt):
        return self.stack.enter_context(self.nc.psum_tensor("p_" + name, list(shape), dt))

    def _deps(self, op, reads, writes):
        deps = []
        for k in reads:
            w = self.last_write.get(k)
            if w is not None:
                deps.append(w)
        for k in writes:
            w = self.last_write.get(k)
            if w is not None:
                deps.append(w)
            for r in self.readers.get(k, ()):
                deps.append(r)
        best = {}
        for d in deps:
            if d is op:
                continue
            key = (d.eng, d.is_dma, d.dma_slot if d.is_dma else None)
            cur = best.get(key)
            if cur is None or d.idx > cur.idx:
                best[key] = d
        for d in best.values():
            if (not d.is_dma) and d.eng == op.eng and not op.is_dma:
                if op.eng == "pe" or not self.same_engine_sync:
                    continue
            op.waits.append(d)
            d.marked = True
        for k in reads:
            self.readers.setdefault(k, []).append(op)
        for k in writes:
            self.last_write[k] = op
            self.readers[k] = []

    def barrier(self, fn):
        o = Op("pool", fn, False)
        o.idx = len(self.ops["pool"])
        self.ops["pool"].append(o)
        self._deps(o, [], ["__phase__"])
        return o

    def op(self, eng, fn, reads=(), writes=()):
        reads = list(reads) + ["__phase__"]
        o = Op(eng, fn, False)
        o.idx = len(self.ops[eng])
        self.ops[eng].append(o)
        self._deps(o, reads, writes)
        self.n_ops += 1
        return o

    def dma(self, eng, out, in_, reads=(), writes=(), is_output=False, **kw):
        def fn(e, out=out, in_=in_, kw=kw):
            return e.dma_start(out=out, in_=in_, **kw)
        reads = list(reads) + ["__phase__"]
        o = Op(eng, fn, True)
        o.idx = len(self.ops[eng])
        n = self.dma_count[eng]
        self.dma_count[eng] += 1
        o.dma_slot = n % DMA_SEMS
        o.dma_val = 16 * (n // DMA_SEMS + 1)
        if n >= DMA_SEMS:
            o.prewait = self.dma_hist[eng][n - DMA_SEMS]
        self.dma_hist[eng].append(o)
        self.ops[eng].append(o)
        self._deps(o, reads, writes)
        if is_output:
            self.all_dma_out.append(o)
        self.n_ops += 1
        return o

    def emit(self):
        nc = self.nc
        st = self.stack
        fin = Op("sp", None, False)
        fin.idx = len(self.ops["sp"])
        for o in self.all_dma_out:
            fin.waits.append(o)
        self.ops["sp"].append(fin)
        csem = {e: st.enter_context(nc.semaphore("c_" + e)) for e in ENGS}
        dsem = {e: [st.enter_context(nc.semaphore("d_%s_%d" % (e, i))) for i in range(DMA_SEMS)]
                for e in ENGS if self.dma_count[e] > 0}
        for e in ENGS:
            c = 0
            for o in self.ops[e]:
                if o.is_dma:
                    continue
                if o.marked:
                    c += 1
                    o.dma_val = c
        block = st.enter_context(nc.Block())
        prog = self

        def run(e, eng):
            seen = {}
            for o in prog.ops[e]:
                ws = list(o.waits)
                if o.prewait is not None:
                    ws.append(o.prewait)
                for d in ws:
                    if d.is_dma:
                        sem, val = dsem[d.eng][d.dma_slot], d.dma_val
                    else:
                        sem, val = csem[d.eng], d.dma_val
                    k = id(sem)
                    if seen.get(k, 0) >= val:
                        continue
                    seen[k] = val
                    eng.wait_ge(sem, val)
                if o.fn is None:
                    continue
                ins = o.fn(eng)
                if o.is_dma:
                    ins.then_inc(dsem[e][o.dma_slot], 16)
                elif o.marked:
                    ins.then_inc(csem[e], 1)

        @block.tensor
        def _(eng):
            run("pe", eng)

        @block.scalar
        def _(eng):
            run("act", eng)

        @block.vector
        def _(eng):
            run("dve", eng)

        @block.gpsimd
        def _(eng):
            run("pool", eng)

        @block.sync
        def _(eng):
            run("sp", eng)

    def close(self):
        self.stack.close()


S = 2048
D = 1024
NT = 16
DEPTH = 2
WC = 256
XC = 2047
GW = 4096
ALPHA = (2 * DEPTH) ** 0.25
A0, B0, C0, D0, M0 = 0, 2048, 4352, 5632, 7680
OA, OB, OC, OD, OM = 0, 512, 1024, 1536, 2048
BROWS = [(0, 512), (512, 512), (1024, 512), (1536, 512), (2048, 256)]


def rel_bucket_np(rel):
    nb = 16
    max_exact = 8
    n = np.abs(rel)
    nf = np.maximum(n, 1).astype(np.float32)
    large = max_exact + (np.log(nf / max_exact) / np.float32(math.log(1024 / max_exact)) * (nb - max_exact)).astype(np.int32)
    large = np.minimum(large, nb - 1)
    return np.where(rel > 0, nb, 0) + np.where(n < max_exact, n, large)


def host_consts():
    c = {}
    c["ident"] = np.eye(128, dtype=np.float32)
    rel = np.arange(4096) - XC
    bkt = rel_bucket_np(rel)
    oh = np.zeros((32, 4096), np.float32)
    oh[bkt, np.arange(4096)] = 1.0
    c["onehot"] = oh
    n = np.abs(rel)
    mA = (n <= 64).astype(np.float32) + ((rel % 4 == 0) & (n <= 256)) + ((rel % 16 == 0) & (n <= 1024))
    mt = np.ones((12, 4096), np.float32)
    mt[:8] = mA[None, :]
    c["multab"] = mt
    t = np.arange(S)
    row = (t // 64).astype(np.float32)
    col = (t % 64).astype(np.float32)
    freqs = (10000.0 ** (-(np.arange(16, dtype=np.float32) / 16))).astype(np.float32)
    ar = row[:, None] * freqs[None, :]
    ac = col[:, None] * freqs[None, :]
    c["ropec"] = np.concatenate([np.cos(ar), np.cos(ar), np.cos(ac), np.cos(ac)], 1).astype(np.float32)
    c["ropes"] = np.concatenate([-np.sin(ar), np.sin(ar), -np.sin(ac), np.sin(ac)], 1).astype(np.float32)
    tri = np.zeros((2, 3, 128, 128), np.float32)
    sg = np.arange(128)[:, None]
    tt = np.arange(128)[None, :]
    same = (sg // 64) == (tt // 64)
    tri[0, 0] = same & (sg <= tt)
    tri[0, 1] = same & (sg < tt)
    tri[0, 2] = same & (sg > tt)
    tri[1, 0] = same & (sg >= tt)
    tri[1, 1] = same & (sg > tt)
    tri[1, 2] = same & (sg < tt)
    c["tri"] = tri.reshape(6 * 128, 128)
    mk_ = np.zeros((2, 64, 192), np.float32)
    a = np.arange(64)[:, None]
    b = np.arange(64)[None, :]
    mk_[0, :, 0:64] = a < b
    mk_[0, :, 64:128] = a <= b
    mk_[0, :, 128:192] = b < a
    mk_[1, :, 0:64] = a > b
    mk_[1, :, 64:128] = a >= b
    mk_[1, :, 128:192] = b > a
    c["rmask"] = mk_.reshape(128, 192)
    return c


IN_SPECS = [("ln_in_g", [1, D]), ("ln_in_b", [1, D]), ("rel_bias", [32, 12]), ("w_in", [DEPTH * D, 8192]),
            ("shift_mu", [DEPTH * 2, 1792]), ("rwkv_w0", [DEPTH * 2, 512]), ("rwkv_w_up", [DEPTH * 2 * 64, 512]),
            ("rwkv_a0", [DEPTH * 2, 512]), ("rwkv_a_up", [DEPTH * 2 * 64, 512]), ("rwkv_k_k", [DEPTH, 512]),
            ("rwkv_k_a", [DEPTH, 512]), ("rwkv_r_k", [DEPTH, 512]), ("rwkv_ln_g", [DEPTH, 512]),
            ("rwkv_ln_b", [DEPTH, 512]), ("c_qnorm_g", [DEPTH, 64]), ("c_knorm_g", [DEPTH, 64]),
            ("d_lambda", [DEPTH, 256]), ("d_subln_g", [DEPTH, 128]), ("w_mem_kv", [DEPTH * D, 512]),
            ("w_branch", [DEPTH * 2304, D]), ("w_gate", [DEPTH * D, 5120]), ("b_gate", [DEPTH, 5120]),
            ("w_out", [DEPTH * D, D]), ("ln_g", [DEPTH, D]), ("ln_b", [DEPTH, D]),
            ("ident", [128, 128]), ("onehot", [32, 4096]), ("multab", [12, 4096]), ("ropec", [S, 64]),
            ("ropes", [S, 64]), ("tri", [768, 128]), ("rmask", [128, 192])]


class KB:
    def __init__(self, debug=False, mixers="MCADB", layers=DEPTH):
        self.debug = debug
        self.mixers = mixers
        nc = bass.Bass("TRN2", target_bir_lowering=False)
        self.nc = nc
        P = Prog(nc)
        self.P = P
        I = {}
        I["x"] = nc.dram_tensor("x", [S, D], F32, kind="ExternalInput").ap()
        I["mem"] = nc.dram_tensor("mem", [256, D], F32, kind="ExternalInput").ap()
        for nm, shp in IN_SPECS:
            I[nm] = nc.dram_tensor(nm, list(shp), F32, kind="ExternalInput").ap()
        self.I = I
        self.out = nc.dram_tensor("out", [S, D], F32, kind="ExternalOutput").ap()
        self.hres = [nc.dram_tensor("hres%d" % i, [S, D], F32, kind="ExternalOutput" if debug else "Internal").ap() for i in range(2)]
        self.dbg_outs = []
        self.o_scr = nc.dram_tensor("o_scr", [S, 2304], F32, kind="ExternalOutput" if debug else "Internal").ap()
        self.xtab = nc.dram_tensor("xtab", [12, 4096], F32).ap()
        self.rk_scr = nc.dram_tensor("rk_scr", [1024, S], F32).ap()
        self.wa_scr = nc.dram_tensor("wa_scr", [256, S], F32).ap()
        self.v_scr = nc.dram_tensor("v_scr", [S, 512], F32).ap()
        self.y_scr = nc.dram_tensor("y_scr", [2 * S, 520], F32, kind="ExternalOutput" if debug else "Internal").ap()
        self.sg_scr = nc.dram_tensor("sg_scr", [S, 512], F32).ap()
        self.ident = P.sb("ident", [128, 128], F32)
        self.hT = P.sb("hT", [128, 8, S + 2], BF16)
        self.BIG = [P.sb("BIG%d" % i, [128, 9216], BF16) for i in range(4)]
        self.G = P.sb("G", [128, GW], F32)
        self.wst = [P.sb("wst%d" % i, [128, 8, WC], F32) for i in range(2)]
        self.wbf = [P.sb("wbf%d" % i, [128, 8, WC], BF16) for i in range(4)]
        self.ropec = P.sb("ropec", [128, NT, 64], F32)
        self.ropes = P.sb("ropes", [128, NT, 64], F32)
        self.lnx = [P.sb("lnx%d" % i, [128, D], F32) for i in range(2)]
        self.junk = P.sb("junk", [128, D], F32)
        self.lng = P.sb("lng", [128, D], F32)
        self.lnb = P.sb("lnb", [128, D], F32)
        self.pt = [P.sb("pt%d" % i, [128, 512], BF16) for i in range(4)]
        self.pe_ = [P.sb("pe%d" % i, [128, 512], BF16) for i in range(2)]
        self.ost = [P.sb("ost%d" % i, [128, 128], F32) for i in range(4)]
        self.sm = P.sb("sm", [128, 64], F32)
        self.tmp = [P.sb("tmp%d" % i, [128, 512], F32) for i in range(3)]
        self.onesf = P.sb("onesf", [128, 128], F32)
        self.brow = [P.sb("brow%d" % i, [1, WC], F32) for i in range(2)]
        self.gq = P.sb("gq", [128, 128], F32)
        self.subg = P.sb("subg", [128, 128], F32)
        self.lamt = P.sb("lamt", [128, 264], F32)
        self.pbar = P.sb("pbar", [1, 8], F32)
        self.memT = P.sb("memT", [128, 8, 256], BF16)
        self.bank = [P.ps("bank%d" % i, [128, 512], F32) for i in range(8)]
        self.cnt = {}
        self.pbi = 0
        B0_, B1_, B2_, B3_ = [b[:] for b in self.BIG]
        self.qT = B0_[:, 0:8192].rearrange("p (c t) -> p c t", t=S)
        self.kT = B1_[:, 0:8192].rearrange("p (c t) -> p c t", t=S)
        self.sg = B3_[:, 0:8192].rearrange("p (t c) -> p t c", c=512)
        self.prelude()
        for l in range(layers):
            self.layer(l, last=(l == layers - 1))
        P.emit()
        P.close()

    def nxt(self, name, n):
        v = self.cnt.get(name, 0)
        self.cnt[name] = (v + 1) % n
        return v

    def bk(self, i):
        return "bank%d" % i

    def pbank(self):
        self.pbi ^= 1
        return 2 + self.pbi

    def barrier(self):
        pbar = self.pbar
        self.P.barrier(lambda e: e.memset(pbar[:], 0.0))

    def mm(self, out, lhsT, rhs, start, stop, reads, writes):
        self.P.op("pe", lambda e: e.matmul(out, lhsT=lhsT, rhs=rhs, start=start, stop=stop), reads=reads, writes=writes)

    def tr(self, out, in_, reads, writes, np_=128):
        ident = self.ident
        self.P.op("pe", lambda e: e.transpose(out, in_, ident[0:np_, 0:np_]), reads=list(reads) + ["ident"], writes=writes)

    def cp(self, eng, out, in_, reads, writes):
        if eng == "act":
            self.P.op("act", lambda e: e.copy(out=out, in_=in_), reads=reads, writes=writes)
        else:
            self.P.op(eng, lambda e: e.tensor_copy(out=out, in_=in_), reads=reads, writes=writes)

    def act(self, out, in_, func, reads, writes, **kw):
        self.P.op("act", lambda e: e.activation(out=out, in_=in_, func=func, **kw), reads=reads, writes=writes)

    def tt(self, eng, out, in0, in1, op, reads, writes):
        self.P.op(eng, lambda e: e.tensor_tensor(out=out, in0=in0, in1=in1, op=op), reads=reads, writes=writes)

    def ts(self, eng, out, in0, s1, s2, op0, op1, reads, writes):
        if s2 is None:
            self.P.op(eng, lambda e: e.tensor_scalar(out=out, in0=in0, scalar1=s1, scalar2=None, op0=op0), reads=reads, writes=writes)
        else:
            self.P.op(eng, lambda e: e.tensor_scalar(out=out, in0=in0, scalar1=s1, scalar2=s2, op0=op0, op1=op1), reads=reads, writes=writes)

    def stt(self, eng, out, in0, scalar, in1, op0, op1, reads, writes):
        self.P.op(eng, lambda e: e.scalar_tensor_tensor(out=out, in0=in0, scalar=scalar, in1=in1, op0=op0, op1=op1), reads=reads, writes=writes)

    def memset(self, eng, ap, val, writes):
        self.P.op(eng, lambda e: e.memset(ap, val), writes=writes)

    def rsqrt_cols(self, src, dst, scale, eps, key="sm"):
        self.ts("dve", dst, src, scale, eps, ALU.mult, ALU.add, [key], [key])
        self.P.op("act", lambda e: e.sqrt(out=dst, in_=dst), reads=[key], writes=[key])
        self.P.op("dve", lambda e: e.reciprocal(out=dst, in_=dst), reads=[key], writes=[key])

    def wload(self, src2d, n, kc=8, variants=None):
        P = self.P
        i = self.nxt("w", 2)
        wst = self.wst[i]
        P.dma("sp", wst[:, 0:kc, 0:n], src2d.rearrange("(c p) n -> p c n", p=128), writes=["wst%d" % i])
        res = []
        if variants is None:
            j = self.nxt("wb", 4)
            self.cp("pool", self.wbf[j][:, 0:kc, 0:n], wst[:, 0:kc, 0:n], ["wst%d" % i], ["wbf%d" % j])
            return [(self.wbf[j], "wbf%d" % j)]
        for (vap, vkey) in variants:
            j = self.nxt("wb", 4)
            self.tt("pool", self.wbf[j][:, 0:kc, 0:n], wst[:, 0:kc, 0:n], vap.unsqueeze(1).broadcast_to([128, kc, n]), ALU.mult,
                    ["wst%d" % i, vkey], ["wbf%d" % j])
            res.append((self.wbf[j], "wbf%d" % j))
        return res

    def proj_T(self, wl, n, consume, shifts=(0,), rhs_fn=None, rkey="hT", ntb=4, tbw=512):
        hT = self.hT
        for ct in range(n // 128):
            for tb in range(ntb):
                b = self.pbank()
                nmm = 8 * len(shifts)
                m = 0
                for (wap, wkey), s in zip(wl, shifts):
                    for c in range(8):
                        if rhs_fn is None:
                            lo = 1 + tb * 512 + s
                            rhs = hT[:, c, lo:lo + 512]
                        else:
                            rhs = rhs_fn(c, tb)
                        self.mm(self.bank[b][:, 0:tbw], wap[:, c, ct * 128:(ct + 1) * 128], rhs, m == 0, m == nmm - 1,
                                [wkey, rkey], [self.bk(b)])
                        m += 1
                consume(b, ct, tb)

    def proj_N(self, wl, n, consume, shifts=(0,), lhs_fn=None, lkey="hT", ntt=NT, kc=8):
        hT = self.hT
        for tt in range(ntt):
            b = self.pbank()
            nmm = kc * len(shifts)
            m = 0
            for (wap, wkey), s in zip(wl, shifts):
                for c in range(kc):
                    if lhs_fn is None:
                        lo = 1 + tt * 128 + s
                        lh = hT[:, c, lo:lo + 128]
                    else:
                        lh = lhs_fn(c, tt)
                    self.mm(self.bank[b][:, 0:n], lh, wap[:, c, 0:n], m == 0, m == nmm - 1, [wkey, lkey], [self.bk(b)])
                    m += 1
            consume(b, tt)

    def ln_inplace(self, xt, xkey, eps=1e-5):
        sm, junk = self.sm, self.junk
        P = self.P
        P.op("dve", lambda e: e.reduce_sum(out=sm[:, 0:1], in_=xt, axis=AX.X), reads=[xkey], writes=["sm"])
        self.ts("dve", sm[:, 1:2], sm[:, 0:1], -1.0 / D, None, ALU.mult, None, ["sm"], ["sm"])
        self.ts("dve", xt, xt, sm[:, 1:2], None, ALU.add, None, [xkey, "sm"], [xkey])
        self.memset("dve", sm[:, 2:3], 0.0, ["sm"])
        self.act(junk[:], xt, AF.Square, [xkey, "sm"], ["junk", "sm"], accum_out=sm[:, 2:3])
        self.rsqrt_cols(sm[:, 2:3], sm[:, 3:4], 1.0 / D, eps)
        self.stt("dve", xt, xt, sm[:, 3:4], self.lng[:], ALU.mult, ALU.mult, [xkey, "sm", "lng"], [xkey])
        self.tt("dve", xt, xt, self.lnb[:], ALU.add, [xkey, "lnb"], [xkey])

    def to_hT(self, src, skey, tt):
        hT = self.hT
        for half in range(2):
            b = self.pbank()
            for c4 in range(4):
                c = half * 4 + c4
                self.tr(self.bank[b][:, c4 * 128:(c4 + 1) * 128], src[:, c * 128:(c + 1) * 128], [skey], [self.bk(b)])
            self.cp("act" if half else "dve", hT[:, half * 4:half * 4 + 4, 1 + tt * 128:1 + (tt + 1) * 128],
                    self.bank[b][:, :].rearrange("p (c t) -> p c t", t=128), [self.bk(b)], ["hT"])

    def prelude(self):
        P, I = self.P, self.I
        P.dma("sp", self.ident[:], I["ident"], writes=["ident"])
        P.dma("sp", self.ropec[:], I["ropec"].rearrange("(t p) c -> p t c", p=128), writes=["ropec"])
        P.dma("sp", self.ropes[:], I["ropes"].rearrange("(t p) c -> p t c", p=128), writes=["ropes"])
        self.memset("pool", self.onesf[:], 1.0, ["onesf"])
        self.memset("pool", self.hT[:, :, 0:1], 0.0, ["hT"])
        self.memset("pool", self.hT[:, :, S + 1:S + 2], 0.0, ["hT"])
        tmpA = self.tmp[0]
        rb = tmpA[0:32, 0:12]
        P.dma("sp", rb, I["rel_bias"], writes=["tmp0"])
        ohs = self.BIG[0][:].bitcast(F32)
        P.dma("sp", ohs[0:32, 0:4096], I["onehot"], writes=["BIG0"])
        mts = self.BIG[1][:].bitcast(F32)
        P.dma("sp", mts[0:12, 0:4096], I["multab"], writes=["BIG1"])
        xts = self.BIG[2][:].bitcast(F32)
        for j in range(8):
            b = self.pbank()
            self.mm(self.bank[b][0:12, :], rb, ohs[0:32, j * 512:(j + 1) * 512], True, True, ["tmp0", "BIG0"], [self.bk(b)])
            self.act(xts[0:12, j * 512:(j + 1) * 512], self.bank[b][0:12, :], AF.Exp, [self.bk(b)], ["BIG2"])
        self.tt("dve", xts[0:12, 0:4096], xts[0:12, 0:4096], mts[0:12, 0:4096], ALU.mult, ["BIG2", "BIG1"], ["BIG2"])
        P.dma("sp", self.xtab, xts[0:12, 0:4096], reads=["BIG2"], writes=["xtab"])
        self.barrier()
        for mt_ in range(2):
            i = self.nxt("ln", 2)
            P.dma("sp", self.lnx[i][:], I["mem"][mt_ * 128:(mt_ + 1) * 128, :], writes=["lnx%d" % i])
            for half in range(2):
                b = self.pbank()
                for c4 in range(4):
                    c = half * 4 + c4
                    self.tr(self.bank[b][:, c4 * 128:(c4 + 1) * 128], self.lnx[i][:, c * 128:(c + 1) * 128], ["lnx%d" % i], [self.bk(b)])
                self.cp("dve", self.memT[:, half * 4:half * 4 + 4, mt_ * 128:(mt_ + 1) * 128],
                        self.bank[b][:, :].rearrange("p (c t) -> p c t", t=128), [self.bk(b)], ["memT"])
        P.dma("sp", self.lng[:], I["ln_in_g"].partition_broadcast(128), writes=["lng"])
        P.dma("sp", self.lnb[:], I["ln_in_b"].partition_broadcast(128), writes=["lnb"])
        for tt in range(NT):
            i = self.nxt("ln", 2)
            P.dma("sp", self.lnx[i][:], I["x"][tt * 128:(tt + 1) * 128, :], writes=["lnx%d" % i])
            self.ln_inplace(self.lnx[i][:], "lnx%d" % i)
            P.dma("sp", self.hres[0][tt * 128:(tt + 1) * 128, :], self.lnx[i][:], reads=["lnx%d" % i], writes=["hres0"])
        self.barrier()

    def layer(self, l, last):
        P, I = self.P, self.I
        hin = self.hres[l % 2]
        for tt in range(NT):
            i = self.nxt("ln", 2)
            P.dma("sp", self.lnx[i][:], hin[tt * 128:(tt + 1) * 128, :], reads=["hres%d" % (l % 2)], writes=["lnx%d" % i])
            self.to_hT(self.lnx[i], "lnx%d" % i, tt)
        self.win = I["w_in"][l * D:(l + 1) * D, :]
        for mx in "MCADB":
            if mx in self.mixers:
                getattr(self, "mixer_" + mx)(l)
            else:
                self.zero_o(mx)
            self.barrier()
        self.merge(l, last)
        self.barrier()

    def zero_o(self, mx):
        c0, w = {"M": (OM, 256), "C": (OC, 512), "A": (OA, 512), "D": (OD, 512), "B": (OB, 512)}[mx]
        t = self.tmp[2]
        self.memset("pool", t[:, :], 0.0, ["tmp2"])
        for tt in range(NT):
            self.P.dma("sp", self.o_scr[tt * 128:(tt + 1) * 128, c0:c0 + w], t[:, 0:w], reads=["tmp2"], writes=["o_scr"])

    def attn_head(self, maps, vfn, vkey, nkt, dv1, table, band, post):
        nm = len(maps)
        G = self.G

        nqt = 4 if nm == 1 else 2
        QB = nqt * 128

        def accap(m, qt):
            bi = 4 + m * nqt + qt
            return self.bank[bi][:, 0:dv1], bi

        for qb in range(S // QB):
            q0 = qb * QB
            kts = []
            for kt in range(nkt):
                dk = kt * 128 - q0
                if band and (dk - (QB - 1) > 1024 or dk + 127 < -1024):
                    continue
                kts.append(kt)
            steps = [(idx, kt, m) for idx, kt in enumerate(kts) for m in range(nm)]

            def stageA(si):
                idx, kt, m = steps[si]
                q_ap, kfn = maps[m]
                sb_ = si % 2
                self.mm(self.bank[sb_][:, 0:QB], kfn(kt), q_ap[:, q0:q0 + QB], True, True, ["BIG0", "BIG1"], [self.bk(sb_)])

            def stageBC(si):
                idx, kt, m = steps[si]
                sb_ = si % 2
                pti = self.nxt("pt", 4)
                ptile = self.pt[pti]
                if table:
                    pei = self.nxt("pe", 2)
                    self.act(self.pe_[pei][:, 0:QB], self.bank[sb_][:, 0:QB], AF.Exp, [self.bk(sb_)], ["pe%d" % pei], scale=0.125)
                    j0 = kt * 128 - q0 + XC
                    gs = G[:, j0 - (QB - 1):j0 + 1][:, ::-1]
                    self.tt("dve", ptile[:, 0:QB], self.pe_[pei][:, 0:QB], gs, ALU.mult, ["pe%d" % pei, "G"], ["pt%d" % pti])
                else:
                    self.act(ptile[:, 0:QB], self.bank[sb_][:, 0:QB], AF.Exp, [self.bk(sb_)], ["pt%d" % pti], scale=0.125)
                for qt in range(nqt):
                    acc, bi = accap(m, qt)
                    self.mm(acc, ptile[:, qt * 128:(qt + 1) * 128], vfn(kt), idx == 0, idx == len(kts) - 1,
                            ["pt%d" % pti, vkey], [self.bk(bi)])

            stageA(0)
            for si in range(len(steps)):
                if si + 1 < len(steps):
                    stageA(si + 1)
                stageBC(si)
            for qt in range(nqt):
                accs = [accap(m, qt) for m in range(nm)]
                post(qb * nqt + qt, [a for a, _ in accs], [self.bk(bi) for _, bi in accs])

    def post_simple(self, h, hd, ocol):
        def post(tt, accs, keys):
            acc = accs[0]
            sm = self.sm
            i = self.nxt("ost", 4)
            self.P.op("dve", lambda e: e.reciprocal(out=sm[:, 8:9], in_=acc[:, hd:hd + 1]), reads=keys, writes=["sm"])
            self.stt("dve", self.ost[i][:, 0:hd], acc[:, 0:hd], sm[:, 8:9], self.sg[:, tt, h * hd:(h + 1) * hd], ALU.mult, ALU.mult,
                     keys + ["sm", "BIG3"], ["ost%d" % i])
            self.P.dma("sp", self.o_scr[tt * 128:(tt + 1) * 128, ocol + h * hd:ocol + (h + 1) * hd], self.ost[i][:, 0:hd],
                       reads=["ost%d" % i], writes=["o_scr"])
        return post

    def cons_T(self, dst, dkey):
        def consume(b, ct, tb):
            self.cp("dve" if (ct + tb) % 2 else "act", dst[:, ct, tb * 512:(tb + 1) * 512], self.bank[b][:, :], [self.bk(b)], [dkey])
        return consume

    def cons_sg(self, c0, n):
        def consume(b, tt):
            self.act(self.sg[:, tt, c0:c0 + n], self.bank[b][:, 0:n], AF.Silu, [self.bk(b)], ["BIG3"])
        return consume

    def cons_v(self, V1, h0, nh, hd):
        def consume(b, tt):
            self.cp("dve", V1[:, tt, h0:h0 + nh, 0:hd], self.bank[b][:, 0:nh * hd].rearrange("p (h d) -> p h d", d=hd), [self.bk(b)], ["BIG2"])
        return consume

    def mixer_M(self, l):
        P, I = self.P, self.I
        wkv = I["w_mem_kv"][l * D:(l + 1) * D, :]
        V1 = self.BIG[2][:, 0:520].rearrange("p (k h d) -> p k h d", k=2, h=4, d=65)
        self.memset("pool", self.BIG[2][:, 0:520], 1.0, ["BIG2"])
        memT = self.memT
        wl = self.wload(wkv[:, 0:256], 256)
        self.proj_T(wl, 256, lambda b, ct, tb: self.cp("dve", self.kT[:, ct, 0:256], self.bank[b][:, 0:256], [self.bk(b)], ["BIG1"]),
                    rhs_fn=lambda c, tb: memT[:, c, 0:256], rkey="memT", ntb=1, tbw=256)
        wl = self.wload(wkv[:, 256:512], 256)
        self.proj_N(wl, 256, self.cons_v(V1, 0, 4, 64), lhs_fn=lambda c, tt: memT[:, c, tt * 128:(tt + 1) * 128], lkey="memT", ntt=2)
        wl = self.wload(self.win[:, M0:M0 + 256], 256)
        self.proj_T(wl, 256, self.cons_T(self.qT, "BIG0"))
        wl = self.wload(self.win[:, M0 + 256:M0 + 512], 256)
        self.proj_N(wl, 256, self.cons_sg(0, 256))
        if l == 0 and "m" in self.mixers:
            self.dbg("qT", self.BIG[0][:, 0:8192], ["BIG0"], BF16)
            self.dbg("kT", self.BIG[1][:, 0:8192], ["BIG1"], BF16)
            self.dbg("V1", self.BIG[2][:, 0:520], ["BIG2"], BF16)
            self.dbg("sg", self.BIG[3][:, 0:8192], ["BIG3"], BF16)
            self.dbg("hT", self.hT[:, :, :].rearrange("p c t -> p (c t)"), ["hT"], BF16)
        for h in range(4):
            ct, pb = h // 2, (h % 2) * 64
            q_ap = self.qT[pb:pb + 64, ct, :]
            self.attn_head([(q_ap, lambda kt, ct=ct, pb=pb: self.kT[pb:pb + 64, ct, kt * 128:(kt + 1) * 128])],
                           lambda kt, h=h: V1[:, kt, h, :], "BIG2", 2, 65, False, False, self.post_simple(h, 64, OM))

    def mixer_A(self, l):
        P = self.P
        V1 = self.BIG[2][:, 0:8320].rearrange("p (k h d) -> p k h d", k=16, h=8, d=65)
        self.memset("pool", self.BIG[2][:, 0:8320], 1.0, ["BIG2"])
        for j in range(2):
            wl = self.wload(self.win[:, A0 + j * 256:A0 + (j + 1) * 256], 256)
            self.proj_T(wl, 256, lambda b, ct, tb, j=j: self.cons_T(self.qT, "BIG0")(b, ct + 2 * j, tb))
        for j in range(2):
            wl = self.wload(self.win[:, A0 + 512 + j * 256:A0 + 512 + (j + 1) * 256], 256)
            self.proj_T(wl, 256, lambda b, ct, tb, j=j: self.cons_T(self.kT, "BIG1")(b, ct + 2 * j, tb))
        for j in range(2):
            wl = self.wload(self.win[:, A0 + 1024 + j * 256:A0 + 1024 + (j + 1) * 256], 256)
            self.proj_N(wl, 256, self.cons_v(V1, 4 * j, 4, 64))
        for j in range(2):
            wl = self.wload(self.win[:, A0 + 1536 + j * 256:A0 + 1536 + (j + 1) * 256], 256)
            self.proj_N(wl, 256, self.cons_sg(j * 256, 256))
        for h in range(8):
            ct, pb = h // 2, (h % 2) * 64
            P.dma("sp", self.G[:, 0:3968], AP(self.xtab.tensor, h * 4096, [[1, 128], [1, 3968]]), reads=["xtab"], writes=["G"])
            q_ap = self.qT[pb:pb + 64, ct, :]
            self.attn_head([(q_ap, lambda kt, ct=ct, pb=pb: self.kT[pb:pb + 64, ct, kt * 128:(kt + 1) * 128])],
                           lambda kt, h=h: V1[:, kt, h, :], "BIG2", 16, 65, True, True, self.post_simple(h, 64, OA))

    def normrope(self, b, tt, nh, gcol):
        n = nh * 64
        sm = self.sm
        t0, t1, t2 = self.tmp
        ps = self.bank[b][:, 0:n]
        self.act(t0[:, 0:n], ps, AF.Square, [self.bk(b)], ["tmp0"])
        self.P.op("dve", lambda e: e.reduce_sum(out=sm[:, 16:16 + nh], in_=t0[:, 0:n].rearrange("p (h d) -> p h d", d=64), axis=AX.X),
                  reads=["tmp0"], writes=["sm"])
        self.rsqrt_cols(sm[:, 16:16 + nh], sm[:, 24:24 + nh], 1.0 / 64, 1e-6)
        v3 = lambda ap: ap.rearrange("p (h d) -> p h d", d=64)
        self.tt("dve", v3(t0[:, 0:n]), v3(ps), sm[:, 24:24 + nh].unsqueeze(2).broadcast_to([128, nh, 64]), ALU.mult,
                [self.bk(b), "sm"], ["tmp0"])
        self.tt("dve", v3(t0[:, 0:n]), v3(t0[:, 0:n]), self.gq[:, gcol:gcol + 64].unsqueeze(1).broadcast_to([128, nh, 64]), ALU.mult,
                ["tmp0", "gq"], ["tmp0"])
        self.tt("pool", v3(t1[:, 0:n]), v3(t0[:, 0:n]), self.ropec[:, tt, :].unsqueeze(1).broadcast_to([128, nh, 64]), ALU.mult,
                ["tmp0", "ropec"], ["tmp1"])
        v5 = lambda ap: ap.rearrange("p (h a b c) -> p h a b c", a=2, b=2, c=16)
        rs = self.ropes[:, tt, :].rearrange("p (a b c) -> p a b c", a=2, b=2, c=16)
        for bb in range(2):
            self.tt("dve", v5(t2[:, 0:n])[:, :, :, bb, :], v5(t0[:, 0:n])[:, :, :, 1 - bb, :],
                    rs[:, :, bb, :].unsqueeze(1).broadcast_to([128, nh, 2, 16]), ALU.mult, ["tmp0", "ropes"], ["tmp2"])
        self.tt("dve", t1[:, 0:n], t1[:, 0:n], t2[:, 0:n], ALU.add, ["tmp1", "tmp2"], ["tmp1"])

    def mixer_C(self, l):
        P, I = self.P, self.I
        V1 = self.BIG[2][:, 0:2080].rearrange("p (k h d) -> p k h d", k=16, h=2, d=65)
        self.memset("pool", self.BIG[2][:, 0:2080], 1.0, ["BIG2"])
        P.dma("sp", self.gq[:, 0:64], I["c_qnorm_g"][l:l + 1, :].partition_broadcast(128), writes=["gq"])
        P.dma("sp", self.gq[:, 64:128], I["c_knorm_g"][l:l + 1, :].partition_broadcast(128), writes=["gq"])
        t1 = self.tmp[1]
        for j in range(2):
            wl = self.wload(self.win[:, C0 + j * 256:C0 + (j + 1) * 256], 256)

            def cons_q(b, tt, j=j):
                self.normrope(b, tt, 4, 0)
                b2 = self.pbank()
                for u in range(2):
                    self.tr(self.bank[b2][:, u * 128:(u + 1) * 128], t1[:, u * 128:(u + 1) * 128], ["tmp1"], [self.bk(b2)])
                self.cp("act", self.qT[:, 2 * j:2 * j + 2, tt * 128:(tt + 1) * 128],
                        self.bank[b2][:, 0:256].rearrange("p (c t) -> p c t", t=128), [self.bk(b2)], ["BIG0"])
            self.proj_N(wl, 256, cons_q)
        wl = self.wload(self.win[:, C0 + 512:C0 + 768], 256)

        def cons_kv(b, tt):
            self.cp("act", V1[:, tt, :, 0:64], self.bank[b][:, 128:256].rearrange("p (h d) -> p h d", d=64), [self.bk(b)], ["BIG2"])
            self.normrope(b, tt, 2, 64)
            t2 = self.tmp[2]
            self.cp("dve", t2[:, 0:256].rearrange("p (g r d) -> p g r d", g=2, r=2, d=64),
                    t1[:, 0:128].rearrange("p (g d) -> p g d", d=64).unsqueeze(2).broadcast_to([128, 2, 2, 64]), ["tmp1"], ["tmp2"])
            b2 = self.pbank()
            for u in range(2):
                self.tr(self.bank[b2][:, u * 128:(u + 1) * 128], t2[:, u * 128:(u + 1) * 128], ["tmp2"], [self.bk(b2)])
            self.cp("act", self.kT[:, 0:2, tt * 128:(tt + 1) * 128],
                    self.bank[b2][:, 0:256].rearrange("p (c t) -> p c t", t=128), [self.bk(b2)], ["BIG1"])
        self.proj_N(wl, 256, cons_kv)
        for j in range(2):
            wl = self.wload(self.win[:, C0 + 768 + j * 256:C0 + 768 + (j + 1) * 256], 256)
            self.proj_N(wl, 256, self.cons_sg(j * 256, 256))
        for h in range(8):
            ct, pb = h // 2, (h % 2) * 64
            g = h // 4
            q_ap = self.qT[pb:pb + 64, ct, :]
            self.attn_head([(q_ap, lambda kt, g=g, pb=pb: self.kT[pb:pb + 64, g, kt * 128:(kt + 1) * 128])],
                           lambda kt, g=g: V1[:, kt, g, :], "BIG2", 16, 65, False, False, self.post_simple(h, 64, OC))

    def mixer_D(self, l):
        P, I = self.P, self.I
        V1 = self.BIG[2][:, 0:8256].rearrange("p (k h d) -> p k h d", k=16, h=4, d=129)
        self.memset("pool", self.BIG[2][:, 0:8256], 1.0, ["BIG2"])
        lam_init = 0.8 - 0.6 * math.exp(-0.3 * l)
        lamt, sm = self.lamt, self.sm
        P.dma("sp", lamt[:, 0:256], I["d_lambda"][l:l + 1, :].partition_broadcast(128), writes=["lamt"])
        P.dma("sp", self.subg[:], I["d_subln_g"][l:l + 1, :].partition_broadcast(128), writes=["subg"])
        self.ts("pool", self.subg[:], self.subg[:], 1.0 - lam_init, None, ALU.mult, None, ["subg"], ["subg"])
        lv = lamt[:, 0:256].rearrange("p (a b c) -> p a b c", a=2, b=2, c=64)
        lp = self.tmp[2][:, 0:128].rearrange("p (a c) -> p a c", c=64)
        self.tt("dve", lp, lv[:, :, 0, :], lv[:, :, 1, :], ALU.mult, ["lamt"], ["tmp2"])
        P.op("dve", lambda e: e.reduce_sum(out=lamt[:, 256:258], in_=lp, axis=AX.X), reads=["tmp2"], writes=["lamt"])
        self.act(lamt[:, 258:260], lamt[:, 256:258], AF.Exp, ["lamt"], ["lamt"])
        self.tt("dve", lamt[:, 260:261], lamt[:, 259:260], lamt[:, 258:259], ALU.subtract, ["lamt"], ["lamt"])
        self.ts("dve", lamt[:, 260:261], lamt[:, 260:261], -lam_init, None, ALU.add, None, ["lamt"], ["lamt"])
        for j in range(2):
            wl = self.wload(self.win[:, D0 + j * 256:D0 + (j + 1) * 256], 256)
            self.proj_T(wl, 256, lambda b, ct, tb, j=j: self.cons_T(self.qT, "BIG0")(b, ct + 2 * j, tb))
        for j in range(2):
            wl = self.wload(self.win[:, D0 + 512 + j * 256:D0 + 512 + (j + 1) * 256], 256)
            self.proj_T(wl, 256, lambda b, ct, tb, j=j: self.cons_T(self.kT, "BIG1")(b, ct + 2 * j, tb))
        for j in range(2):
            wl = self.wload(self.win[:, D0 + 1024 + j * 256:D0 + 1024 + (j + 1) * 256], 256)
            self.proj_N(wl, 256, self.cons_v(V1, 2 * j, 2, 128))
        for j in range(2):
            wl = self.wload(self.win[:, D0 + 1536 + j * 256:D0 + 1536 + (j + 1) * 256], 256)
            self.proj_N(wl, 256, self.cons_sg(j * 256, 256))
        t0 = self.tmp[0]
        for h in range(4):
            P.dma("sp", self.G[:, 0:3968], AP(self.xtab.tensor, (8 + h) * 4096, [[1, 128], [1, 3968]]), reads=["xtab"], writes=["G"])

            def post(tt, accs, keys, h=h):
                a1, a2 = accs
                i = self.nxt("ost", 4)
                P.op("dve", lambda e: e.reciprocal(out=sm[:, 8:9], in_=a1[:, 128:129]), reads=keys, writes=["sm"])
                P.op("dve", lambda e: e.reciprocal(out=sm[:, 9:10], in_=a2[:, 128:129]), reads=keys, writes=["sm"])
                self.tt("dve", sm[:, 9:10], sm[:, 9:10], lamt[:, 260:261], ALU.mult, ["sm", "lamt"], ["sm"])
                self.ts("dve", t0[:, 0:128], a1[:, 0:128], sm[:, 8:9], None, ALU.mult, None, keys + ["sm"], ["tmp0"])
                self.stt("dve", t0[:, 128:256], a2[:, 0:128], sm[:, 9:10], t0[:, 0:128], ALU.mult, ALU.add, keys + ["sm", "tmp0"], ["tmp0"])
                self.memset("dve", sm[:, 10:11], 0.0, ["sm"])
                self.act(t0[:, 256:384], t0[:, 128:256], AF.Square, ["tmp0", "sm"], ["tmp0", "sm"], accum_out=sm[:, 10:11])
                self.rsqrt_cols(sm[:, 10:11], sm[:, 11:12], 1.0 / 128, 1e-5)
                self.stt("dve", t0[:, 128:256], t0[:, 128:256], sm[:, 11:12], self.subg[:], ALU.mult, ALU.mult, ["tmp0", "sm", "subg"], ["tmp0"])
                self.tt("dve", self.ost[i][:], t0[:, 128:256], self.sg[:, tt, h * 128:(h + 1) * 128], ALU.mult, ["tmp0", "BIG3"], ["ost%d" % i])
                P.dma("sp", self.o_scr[tt * 128:(tt + 1) * 128, OD + h * 128:OD + (h + 1) * 128], self.ost[i][:],
                      reads=["ost%d" % i], writes=["o_scr"])
            maps = [(self.qT[c * 64:(c + 1) * 64, h, :], (lambda kt, c=c, h=h: self.kT[c * 64:(c + 1) * 64, h, kt * 128:(kt + 1) * 128]))
                    for c in range(2)]
            self.attn_head(maps, lambda kt, h=h: V1[:, kt, h, :], "BIG2", 16, 129, True, False, post)

    def dbg(self, name, ap, reads, dt=F32):
        if not self.debug:
            return
        t = self.nc.dram_tensor("dbg_" + name, list(ap.shape), dt, kind="ExternalOutput").ap()
        self.P.dma("sp", t, ap, reads=reads, is_output=True)
        self.dbg_outs.append("dbg_" + name)

    def mixer_B(self, l):
        P, I = self.P, self.I
        CW = 0.6065306597126334
        t_ring = self.tmp
        mub = self.lnx[0][:, 0:768].rearrange("p (v n) -> p v n", n=256)

        def load_mu(c0):
            for v in range(2):
                P.dma("sp", mub[:, 1 + v, :], I["shift_mu"][l * 2 + v:l * 2 + v + 1, c0:c0 + 256].partition_broadcast(128), writes=["lnx0"])
            self.tt("dve", mub[:, 0, :], mub[:, 1, :], mub[:, 2, :], ALU.add, ["lnx0"], ["lnx0"])
            self.ts("dve", mub[:, 0, :], mub[:, 0, :], -1.0, 1.0, ALU.mult, ALU.add, ["lnx0"], ["lnx0"])
            return [(mub[:, 0, :], "lnx0"), (mub[:, 1, :], "lnx0"), (mub[:, 2, :], "lnx0")]

        def stage_out(dst_ap, dkey, func=None):
            def f(b, n_part=128, ncol=512):
                i = self.nxt("tmp", 3)
                if func is None:
                    self.cp("dve", t_ring[i][0:n_part, 0:ncol], self.bank[b][0:n_part, 0:ncol], [self.bk(b)], ["tmp%d" % i])
                else:
                    self.act(t_ring[i][0:n_part, 0:ncol], self.bank[b][0:n_part, 0:ncol], func, [self.bk(b)], ["tmp%d" % i])
                P.dma("sp", dst_ap, t_ring[i][0:n_part, 0:ncol], reads=["tmp%d" % i], writes=[dkey])
            return f

        for j in range(4):
            c0 = j * 256
            wl = self.wload(self.win[:, B0 + c0:B0 + c0 + 256], 256, variants=load_mu(c0))
            self.proj_T(wl, 256, lambda b, ct, tb, c0=c0: stage_out(self.rk_scr[c0 + ct * 128:c0 + (ct + 1) * 128, tb * 512:(tb + 1) * 512], "rk_scr")(b),
                        shifts=(0, -1, 1))
        for j in range(2):
            c0 = 1024 + j * 256
            wl = self.wload(self.win[:, B0 + c0:B0 + c0 + 256], 256, variants=load_mu(c0))
            self.proj_N(wl, 256, lambda b, tt, j=j: stage_out(self.v_scr[tt * 128:(tt + 1) * 128, j * 256:(j + 1) * 256], "v_scr")(b, 128, 256),
                        shifts=(0, -1, 1))
        wl = self.wload(self.win[:, B0 + 1536:B0 + 1792], 256, variants=load_mu(1536))
        self.proj_T(wl, 256, lambda b, ct, tb: stage_out(self.wa_scr[ct * 128:(ct + 1) * 128, tb * 512:(tb + 1) * 512], "wa_scr",
                                                         AF.Tanh if ct == 0 else AF.Copy)(b), shifts=(0, -1, 1))
        for j in range(2):
            wl = self.wload(self.win[:, B0 + 1792 + j * 256:B0 + 1792 + (j + 1) * 256], 256)
            self.proj_N(wl, 256, lambda b, tt, j=j: stage_out(self.sg_scr[tt * 128:(tt + 1) * 128, j * 256:(j + 1) * 256], "sg_scr", AF.Silu)(b, 128, 256))
        self.barrier()
        slots = []
        for bi in range(4):
            a = self.BIG[bi][:].bitcast(F32)
            for q in range(4):
                slots.append(a[:, q * 1024:(q + 1) * 1024])
        for q in range(4):
            slots.append(self.G[:, q * 1024:(q + 1) * 1024])
        for wi in range(2):
            a = self.wst[wi][:, :, :].rearrange("p c n -> p (c n)")
            for q in range(2):
                slots.append(a[:, q * 1024:(q + 1) * 1024])
        si = [0]

        def slot(full=True):
            if full:
                if si[0] % 2:
                    si[0] += 1
                a = slots[si[0] // 2]
                si[0] += 2
                return a
            a = slots[si[0] // 2][:, (si[0] % 2) * 512:(si[0] % 2) * 512 + 512]
            si[0] += 1
            return a

        def v3(ap, w):
            return ap[0:64, 0:8 * w].rearrange("p (h t) -> p h t", t=w)

        w_upS = slot()[0:64, :].rearrange("p (e c) -> p e c", c=512)
        a_upS = slot()[0:64, :].rearrange("p (e c) -> p e c", c=512)
        w0B = slot()[0:64, :].rearrange("p (e c) -> p e c", c=512)
        rkT = slot()[0:64, :].rearrange("p (g t) -> p g t", t=64)
        AR = slot()[0:64, :].rearrange("p (h t) -> p h t", t=128)
        NP = [slot()[0:64, :].rearrange("p (h t) -> p h t", t=128) for _ in range(2)]
        ysb = slot()[0:64, 0:520]
        rmaskS = slot(False)[0:64, 0:384].rearrange("p (e n) -> p e n", n=192)
        waT = slot(False)[0:64, 0:256].rearrange("p (g t) -> p g t", t=64)
        vtok = slot(False)[0:64, :]
        sgw = slot(False)[0:64, :]
        asT, kkn, ke, be, tE0, tE1, bch, kch, z = [v3(slot(False), 64) for _ in range(9)]
        eLs, Bt, Kt = [slot(False)[0:64, :] for _ in range(3)]
        Mm = [v3(slot(False), 64) for _ in range(2)]
        Mrb, Mak, Mrk, Xs, Us, tmpS = [v3(slot(False), 64) for _ in range(6)]
        Sst = [v3(slot(False), 64) for _ in range(2)]
        assert si[0] <= 2 * len(slots), si[0]
        rwp = self.gq[0:64, 0:40]
        omka = self.gq[0:64, 40:48]
        ident64 = self.ident[0:64, 0:64]
        ones64 = self.onesf[0:64, 0:64]
        P.dma("sp", w_upS, I["rwkv_w_up"][l * 128:(l + 1) * 128, :].rearrange("(e r) c -> r e c", r=64), writes=["w_upS"])
        P.dma("sp", a_upS, I["rwkv_a_up"][l * 128:(l + 1) * 128, :].rearrange("(e r) c -> r e c", r=64), writes=["a_upS"])
        for e in range(2):
            P.dma("sp", w0B[:, e, :], I["rwkv_w0"][l * 2 + e:l * 2 + e + 1, :].partition_broadcast(64), writes=["w0B"])
        P.dma("sp", rmaskS, I["rmask"].rearrange("(e p) n -> p e n", p=64), writes=["rmaskS"])
        pm = self.tmp[0]
        P.dma("sp", pm[0:16, 0:64], I["rwkv_a0"][l * 2:(l + 1) * 2, :].rearrange("e (h c) -> (e h) c", c=64), writes=["tmp0"])
        P.dma("sp", pm[16:24, 0:64], I["rwkv_k_k"][l:l + 1, :].rearrange("e (h c) -> (e h) c", c=64), writes=["tmp0"])
        P.dma("sp", pm[24:32, 0:64], I["rwkv_k_a"][l:l + 1, :].rearrange("e (h c) -> (e h) c", c=64), writes=["tmp0"])
        P.dma("sp", pm[32:40, 0:64], I["rwkv_r_k"][l:l + 1, :].rearrange("e (h c) -> (e h) c", c=64), writes=["tmp0"])
        b = self.pbank()
        self.P.op("pe", lambda e_: e_.transpose(self.bank[b][0:64, 0:40], pm[0:40, 0:64], self.ident[0:40, 0:40]), reads=["tmp0", "ident"], writes=[self.bk(b)])
        self.cp("dve", rwp, self.bank[b][0:64, 0:40], [self.bk(b)], ["gq"])
        self.ts("dve", omka, rwp[:, 24:32], -1.0, 1.0, ALU.mult, ALU.add, ["gq"], ["gq"])
        bc3 = lambda ap: ap.unsqueeze(2).broadcast_to([64, 8, 64])
        hb = lambda b_, h, w=64: self.bank[b_][0:64, h * w:(h + 1) * w]
        b3 = lambda b_, w=64: self.bank[b_][0:64, 0:8 * w].rearrange("p (h t) -> p h t", t=w)

        for e in range(2):
            Scur = 0
            self.memset("dve", Sst[0], 0.0, ["S0"])
            order = range(32) if e == 0 else range(31, -1, -1)
            tl = 63 if e == 0 else 0
            mS, mI, mT = rmaskS[:, e, 0:64], rmaskS[:, e, 64:128], rmaskS[:, e, 128:192]
            for ch in order:
                t0 = ch * 64
                P.dma("sp", rkT, self.rk_scr.rearrange("(g p) t -> p g t", p=64)[:, :, t0:t0 + 64], reads=["rk_scr"], writes=["rkT"])
                P.dma("sp", waT, self.wa_scr.rearrange("(g p) t -> p g t", p=64)[:, :, t0:t0 + 64], reads=["wa_scr"], writes=["waT"])
                P.dma("sp", vtok, self.v_scr[t0:t0 + 64, :], reads=["v_scr"], writes=["vtok"])
                rT, kT_ = rkT[:, 0:8, :], rkT[:, 8:16, :]
                b = self.pbank()
                self.mm(self.bank[b][0:64, :], waT[:, e, :], w_upS[:, e, :], True, True, ["waT", "w_upS"], [self.bk(b)])
                self.tt("dve", sgw, self.bank[b][0:64, :], w0B[:, e, :], ALU.add, [self.bk(b), "w0B"], ["sgw"])
                self.act(sgw, sgw, AF.Sigmoid, ["sgw"], ["sgw"])
                b = self.pbank()
                for h in range(8):
                    self.mm(hb(b, h), a_upS[:, e, h * 64:(h + 1) * 64], waT[:, 2 + e, :], True, True, ["waT", "a_upS"], [self.bk(b)])
                self.tt("dve", asT, b3(b), bc3(rwp[:, e * 8:(e + 1) * 8]), ALU.add, [self.bk(b), "gq"], ["asT"])
                self.act(asT, asT, AF.Sigmoid, ["asT"], ["asT"])
                self.tt("dve", kkn, kT_, bc3(rwp[:, 16:24]), ALU.mult, ["rkT", "gq"], ["kkn"])
                self.act(tE0, kkn, AF.Square, ["kkn"], ["tE0"])
                b = self.pbank()
                self.mm(self.bank[b][0:64, :], ones64, tE0.rearrange("p h t -> p (h t)"), True, True, ["tE0", "onesf"], [self.bk(b)])
                self.act(tE0, b3(b), AF.Sqrt, [self.bk(b)], ["tE0"])
                self.ts("dve", tE0, tE0, 1e-12, None, ALU.max, None, ["tE0"], ["tE0"])
                self.P.op("dve", lambda e_: e_.reciprocal(out=tE0, in_=tE0), reads=["tE0"], writes=["tE0"])
                self.tt("dve", kkn, kkn, tE0, ALU.mult, ["kkn", "tE0"], ["kkn"])
                self.tt("pool", ke, asT, bc3(rwp[:, 24:32]), ALU.mult, ["asT", "gq"], ["ke"])
                self.tt("pool", ke, ke, bc3(omka), ALU.add, ["ke", "gq"], ["ke"])
                self.tt("pool", ke, ke, kT_, ALU.mult, ["ke", "rkT"], ["ke"])
                self.tt("pool", be, kkn, asT, ALU.mult, ["kkn", "asT"], ["be"])
                self.tt("pool", z, rT, ke, ALU.mult, ["rkT", "ke"], ["z"])
                bLi = self.pbank()
                for h in range(8):
                    self.mm(hb(bLi, h), sgw[:, h * 64:(h + 1) * 64], mI, True, True, ["sgw", "rmaskS"], [self.bk(bLi)])
                self.act(tE0, b3(bLi), AF.Exp, [self.bk(bLi)], ["tE0"], scale=-CW)
                self.act(tE1, b3(bLi), AF.Exp, [self.bk(bLi)], ["tE1"], scale=CW)
                self.tt("dve", AR[:, :, 64:128], rT, tE0, ALU.mult, ["rkT", "tE0"], ["AR"])
                self.cp("dve", self.sm[0:64, 32:40], tE0[:, :, tl], ["tE0"], ["sm"])
                self.tt("dve", bch, be, tE1, ALU.mult, ["be", "tE1"], ["bch"])
                self.tt("pool", kch, ke, tE1, ALU.mult, ["ke", "tE1"], ["kch"])
                bLe = self.pbank()
                for h in range(8):
                    self.mm(hb(bLe, h), sgw[:, h * 64:(h + 1) * 64], mS, True, True, ["sgw", "rmaskS"], [self.bk(bLe)])
                self.act(tE0, b3(bLe), AF.Exp, [self.bk(bLe)], ["tE0"], scale=-CW)
                self.stt("dve", AR[:, :, 0:64], kkn, -1.0, tE0, ALU.mult, ALU.mult, ["kkn", "tE0"], ["AR"])
                b = self.pbank()
                self.mm(self.bank[b][0:64, :], mT, sgw, True, True, ["sgw", "rmaskS"], [self.bk(b)])
                self.act(eLs, self.bank[b][0:64, :], AF.Exp, [self.bk(b)], ["eLs"], scale=-CW)
                for src, skey, dst, dkey in ((be, "be", Bt, "Bt"), (ke, "ke", Kt, "Kt")):
                    b = self.pbank()
                    for h in range(8):
                        self.P.op("pe", lambda e_, b=b, h=h, src=src: e_.transpose(hb(b, h), src[:, h, :], ident64), reads=[skey, "ident"], writes=[self.bk(b)])
                    self.tt("dve", dst, self.bank[b][0:64, :], eLs, ALU.mult, [self.bk(b), "eLs"], [dkey])
                b = self.pbank()
                for h in range(8):
                    self.mm(self.bank[b][0:64, h:h + 1], z[:, h, :], rwp[:, 32 + h:33 + h], True, True, ["z", "gq"], [self.bk(b)])
                self.cp("act", ysb[:, 512:520], self.bank[b][0:64, 0:8], [self.bk(b)], ["ysb"])
                for h in range(8):
                    self.mm(self.bank[h // 4][0:64, (h % 4) * 128:(h % 4 + 1) * 128], bch[:, h, :], AR[:, h, :], True, True, ["bch", "AR"], [self.bk(h // 4)])
                for h in range(8):
                    self.mm(self.bank[4 + h // 4][0:64, (h % 4) * 128:(h % 4 + 1) * 128], kch[:, h, :], AR[:, h, :], True, True, ["kch", "AR"], [self.bk(4 + h // 4)])
                for h in range(8):
                    self.mm(hb(6, h), AR[:, h, 0:64], bch[:, h, :], True, True, ["bch", "AR"], [self.bk(6)])
                m4 = lambda m_: m_.unsqueeze(1).broadcast_to([64, 4, 64])
                for g in range(2):
                    bb = self.bank[g][0:64, :].rearrange("p (h t) -> p h t", t=128)
                    kb = self.bank[4 + g][0:64, :].rearrange("p (h t) -> p h t", t=128)
                    self.tt("dve", NP[0][:, 4 * g:4 * g + 4, 0:64], bb[:, :, 0:64], m4(mS), ALU.mult, [self.bk(g), "rmaskS"], ["NP0"])
                    self.tt("dve", Mrb[:, 4 * g:4 * g + 4, :], bb[:, :, 64:128], m4(mI), ALU.mult, [self.bk(g), "rmaskS"], ["Mrb"])
                    self.tt("dve", Mak[:, 4 * g:4 * g + 4, :], kb[:, :, 0:64], m4(mS), ALU.mult, [self.bk(4 + g), "rmaskS"], ["Mak"])
                    self.tt("dve", Mrk[:, 4 * g:4 * g + 4, :], kb[:, :, 64:128], m4(mI), ALU.mult, [self.bk(4 + g), "rmaskS"], ["Mrk"])
                self.tt("dve", Mm[0], b3(6), mT.unsqueeze(1).broadcast_to([64, 8, 64]), ALU.mult, [self.bk(6), "rmaskS"], ["Mm0"])
                self.tt("pool", NP[0][:, :, 64:128], NP[0][:, :, 0:64], ident64.unsqueeze(1).broadcast_to([64, 8, 64]), ALU.add, ["NP0", "ident"], ["NP0"])
                cur = 0
                for step in range(6):
                    nx = 1 - cur
                    pbk = (0, 1) if step % 2 == 0 else (4, 5)
                    mbk = 6 if step % 2 else 7
                    ncur, nnx, mcur, mnx = "NP%d" % cur, "NP%d" % nx, "Mm%d" % cur, "Mm%d" % nx
                    if step == 0:
                        for h in range(8):
                            self.mm(self.bank[pbk[h // 4]][0:64, (h % 4) * 128:(h % 4) * 128 + 64], Mm[cur][:, h, :], NP[cur][:, h, 0:64], True, True,
                                    [mcur, ncur], [self.bk(pbk[h // 4])])
                    elif step < 5:
                        for h in range(8):
                            self.mm(self.bank[pbk[h // 4]][0:64, (h % 4) * 128:(h % 4 + 1) * 128], Mm[cur][:, h, :], NP[cur][:, h, :], True, True,
                                    [mcur, ncur], [self.bk(pbk[h // 4])])
                    else:
                        for h in range(8):
                            self.mm(self.bank[pbk[h // 4]][0:64, (h % 4) * 128 + 64:(h % 4 + 1) * 128], Mm[cur][:, h, :], NP[cur][:, h, 64:128], True, True,
                                    [mcur, ncur], [self.bk(pbk[h // 4])])
                    if step < 5:
                        for h in range(8):
                            self.mm(hb(mbk, h), NP[cur][:, h, 0:64], Mm[cur][:, h, :], True, True, [mcur, ncur], [self.bk(mbk)])
                    for g in range(2):
                        pv = self.bank[pbk[g]][0:64, :].rearrange("p (h t) -> p h t", t=128)
                        if step < 5:
                            self.cp("act", NP[nx][:, 4 * g:4 * g + 4, 0:64], pv[:, :, 0:64], [self.bk(pbk[g])], [nnx])
                        if step == 0:
                            self.cp("dve", NP[nx][:, 4 * g:4 * g + 4, 64:128], NP[cur][:, 4 * g:4 * g + 4, 64:128], [ncur], [nnx])
                        else:
                            self.tt("dve", NP[nx][:, 4 * g:4 * g + 4, 64:128], pv[:, :, 64:128], NP[cur][:, 4 * g:4 * g + 4, 64:128], ALU.add,
                                    [self.bk(pbk[g]), ncur], [nnx])
                    if step < 5:
                        self.cp("act", Mm[nx], b3(mbk), [self.bk(mbk)], [mnx])
                    cur = nx
                TT, tkey = NP[cur], "NP%d" % cur
                S0, skey = Sst[Scur], "S%d" % Scur
                S1, s1key = Sst[1 - Scur], "S%d" % (1 - Scur)
                bX = self.pbank()
                for h in range(8):
                    self.mm(hb(bX, h), AR[:, h, 0:64], S0[:, h, :], True, False, ["AR", skey], [self.bk(bX)])
                    self.mm(hb(bX, h), Mak[:, h, :], vtok[:, h * 64:(h + 1) * 64], False, True, ["Mak", "vtok"], [self.bk(bX)])
                self.cp("dve", Xs, b3(bX), [self.bk(bX)], ["Xs"])
                bU = self.pbank()
                for h in range(8):
                    self.mm(hb(bU, h), TT[:, h, 64:128], Xs[:, h, :], True, True, [tkey, "Xs"], [self.bk(bU)])
                self.cp("act", Us, b3(bU), [self.bk(bU)], ["Us"])
                bY = self.pbank()
                for h in range(8):
                    self.mm(hb(bY, h), AR[:, h, 64:128], S0[:, h, :], True, False, ["AR", skey], [self.bk(bY)])
                    self.mm(hb(bY, h), Mrb[:, h, :], Us[:, h, :], False, False, ["Mrb", "Us"], [self.bk(bY)])
                    self.mm(hb(bY, h), Mrk[:, h, :], vtok[:, h * 64:(h + 1) * 64], False, True, ["Mrk", "vtok"], [self.bk(bY)])
                self.cp("act", ysb[:, 0:512], self.bank[bY][0:64, :], [self.bk(bY)], ["ysb"])
                P.dma("sp", self.y_scr[e * S + t0:e * S + t0 + 64, :], ysb, reads=["ysb"], writes=["y_scr"])
                bS = self.pbank()
                for h in range(8):
                    self.mm(hb(bS, h), Bt[:, h * 64:(h + 1) * 64], Us[:, h, :], True, False, ["Bt", "Us"], [self.bk(bS)])
                    self.mm(hb(bS, h), Kt[:, h * 64:(h + 1) * 64], vtok[:, h * 64:(h + 1) * 64], False, True, ["Kt", "vtok"], [self.bk(bS)])
                self.tt("pool", tmpS, S0, bc3(self.sm[0:64, 32:40]), ALU.mult, [skey, "sm"], ["tmpS"])
                self.tt("dve", S1, tmpS, b3(bS), ALU.add, ["tmpS", self.bk(bS)], [s1key])
                Scur = 1 - Scur
        self.barrier()
        P.dma("sp", self.lng[:, 0:512], I["rwkv_ln_g"][l:l + 1, :].partition_broadcast(128), writes=["lng"])
        P.dma("sp", self.lnb[:, 0:512], I["rwkv_ln_b"][l:l + 1, :].partition_broadcast(128), writes=["lnb"])
        yf, yb, vt = self.lnx[0], self.lnx[1], self.junk
        sm = self.sm
        t0_, t1_, t2_ = self.tmp
        for tt in range(NT):
            P.dma("sp", yf[:, 0:520], self.y_scr[tt * 128:(tt + 1) * 128, :], reads=["y_scr"], writes=["lnx0"])
            P.dma("sp", yb[:, 0:520], self.y_scr[S + tt * 128:S + (tt + 1) * 128, :], reads=["y_scr"], writes=["lnx1"])
            P.dma("sp", vt[:, 0:512], self.v_scr[tt * 128:(tt + 1) * 128, :], reads=["v_scr"], writes=["junk"])
            P.dma("sp", vt[:, 512:1024], self.sg_scr[tt * 128:(tt + 1) * 128, :], reads=["sg_scr"], writes=["junk"])
            self.tt("dve", yf[:, 0:520], yf[:, 0:520], yb[:, 0:520], ALU.add, ["lnx0", "lnx1"], ["lnx0"])
            y3 = yf[:, 0:512].rearrange("p (h d) -> p h d", d=64)
            P.op("dve", lambda e_, y3=y3: e_.reduce_sum(out=sm[:, 40:48], in_=y3, axis=AX.X), reads=["lnx0"], writes=["sm"])
            self.ts("dve", sm[:, 40:48], sm[:, 40:48], -1.0 / 64, None, ALU.mult, None, ["sm"], ["sm"])
            self.tt("dve", y3, y3, sm[:, 40:48].unsqueeze(2).broadcast_to([128, 8, 64]), ALU.add, ["lnx0", "sm"], ["lnx0"])
            self.act(t0_[:, 0:512], yf[:, 0:512], AF.Square, ["lnx0"], ["tmp0"])
            P.op("dve", lambda e_: e_.reduce_sum(out=sm[:, 48:56], in_=t0_[:, 0:512].rearrange("p (h d) -> p h d", d=64), axis=AX.X), reads=["tmp0"], writes=["sm"])
            self.rsqrt_cols(sm[:, 48:56], sm[:, 56:64], 1.0 / 64, 64e-5)
            self.tt("dve", y3, y3, sm[:, 56:64].unsqueeze(2).broadcast_to([128, 8, 64]), ALU.mult, ["lnx0", "sm"], ["lnx0"])
            self.tt("dve", yf[:, 0:512], yf[:, 0:512], self.lng[:, 0:512], ALU.mult, ["lnx0", "lng"], ["lnx0"])
            self.tt("pool", yf[:, 0:512], yf[:, 0:512], self.lnb[:, 0:512], ALU.add, ["lnx0", "lnb"], ["lnx0"])
            self.tt("pool", t1_[:, 0:512].rearrange("p (h d) -> p h d", d=64), vt[:, 0:512].rearrange("p (h d) -> p h d", d=64),
                    yf[:, 512:520].unsqueeze(2).broadcast_to([128, 8, 64]), ALU.mult, ["junk", "lnx0"], ["tmp1"])
            self.tt("dve", t1_[:, 0:512], t1_[:, 0:512], yf[:, 0:512], ALU.add, ["tmp1", "lnx0"], ["tmp1"])
            self.tt("dve", t2_[:, 0:512], t1_[:, 0:512], vt[:, 512:1024], ALU.mult, ["tmp1", "junk"], ["tmp2"])
            P.dma("sp", self.o_scr[tt * 128:(tt + 1) * 128, OB:OB + 512], t2_[:, 0:512], reads=["tmp2"], writes=["o_scr"])

    def merge(self, l, last):
        P, I = self.P, self.I
        wg_all = I["w_gate"][l * D:(l + 1) * D, :]
        wb_all = I["w_branch"][l * 2304:(l + 1) * 2304, :]
        wo_all = I["w_out"][l * D:(l + 1) * D, :]
        P.dma("sp", self.lng[:], I["ln_g"][l:l + 1, :].partition_broadcast(128), writes=["lng"])
        P.dma("sp", self.lnb[:], I["ln_b"][l:l + 1, :].partition_broadcast(128), writes=["lnb"])
        oT = self.BIG[0][:, 0:9216].rearrange("p (j t) -> p j t", t=512)
        ygrp = self.BIG[1][:].bitcast(F32)[:, 0:4096].rearrange("p (q c) -> p q c", c=1024)
        otile = self.BIG[2][:].bitcast(F32)[:, 0:2304]
        yT = self.BIG[3][:, 0:4096].rearrange("p (c t) -> p c t", t=512)
        hgrp = self.G[:, 0:4096].rearrange("p (q c) -> p q c", c=1024)
        hin = self.hres[l % 2]
        hout = self.out if last else self.hres[(l + 1) % 2]
        t0, t1 = self.tmp[0], self.tmp[1]
        for grp in range(4):
            for tq in range(4):
                tt = grp * 4 + tq
                P.dma("sp", otile, self.o_scr[tt * 128:(tt + 1) * 128, :], reads=["o_scr"], writes=["BIG2"])
                P.dma("sp", hgrp[:, tq, :], hin[tt * 128:(tt + 1) * 128, :], reads=["hres%d" % (l % 2)], writes=["G"])
                for j4 in range(5):
                    nj = min(4, 18 - j4 * 4)
                    b = self.pbank()
                    for u in range(nj):
                        j = j4 * 4 + u
                        self.tr(self.bank[b][:, u * 128:(u + 1) * 128], otile[:, j * 128:(j + 1) * 128], ["BIG2"], [self.bk(b)])
                    self.cp("act" if j4 % 2 else "dve", oT[:, j4 * 4:j4 * 4 + nj, tq * 128:(tq + 1) * 128],
                            self.bank[b][:, 0:nj * 128].rearrange("p (c t) -> p c t", t=128), [self.bk(b)], ["BIG0"])
            for i, (r0, rw) in enumerate(BROWS):
                kci = rw // 128
                for cc in range(4):
                    wg, wgk = self.wload(wg_all[:, i * 1024 + cc * 256:i * 1024 + (cc + 1) * 256], 256)[0]
                    wb, wbk = self.wload(wb_all[r0:r0 + rw, cc * 256:(cc + 1) * 256], 256, kc=kci)[0]
                    bi = self.nxt("brow", 2)
                    P.dma("sp", self.brow[bi][0:1, :], I["b_gate"][l:l + 1, i * 1024 + cc * 256:i * 1024 + (cc + 1) * 256], writes=["brow%d" % bi])
                    for tq in range(4):
                        tt = grp * 4 + tq
                        b1 = self.pbank()
                        self.mm(self.bank[b1][:, 0:256], self.onesf[0:1, 0:128], self.brow[bi][0:1, :], True, False,
                                ["onesf", "brow%d" % bi], [self.bk(b1)])
                        for c in range(8):
                            self.mm(self.bank[b1][:, 0:256], self.hT[:, c, 1 + tt * 128:1 + (tt + 1) * 128], wg[:, c, :], False, c == 7,
                                    [wgk, "hT"], [self.bk(b1)])
                        self.act(t0[:, 0:256], self.bank[b1][:, 0:256], AF.Sigmoid, [self.bk(b1)], ["tmp0"])
                        b2 = self.pbank()
                        for c in range(kci):
                            self.mm(self.bank[b2][:, 0:256], oT[:, r0 // 128 + c, tq * 128:(tq + 1) * 128], wb[:, c, :], c == 0, c == kci - 1,
                                    [wbk, "BIG0"], [self.bk(b2)])
                        ysl = ygrp[:, tq, cc * 256:(cc + 1) * 256]
                        if i == 0:
                            self.tt("dve", ysl, self.bank[b2][:, 0:256], t0[:, 0:256], ALU.mult, [self.bk(b2), "tmp0"], ["BIG1"])
                        else:
                            self.tt("dve", t1[:, 0:256], self.bank[b2][:, 0:256], t0[:, 0:256], ALU.mult, [self.bk(b2), "tmp0"], ["tmp1"])
                            self.tt("pool", ysl, ysl, t1[:, 0:256], ALU.add, ["BIG1", "tmp1"], ["BIG1"])
            for tq in range(4):
                for half in range(2):
                    b = self.pbank()
                    for c4 in range(4):
                        c = half * 4 + c4
                        self.tr(self.bank[b][:, c4 * 128:(c4 + 1) * 128], ygrp[:, tq, c * 128:(c + 1) * 128], ["BIG1"], [self.bk(b)])
                    self.cp("act" if half else "dve", yT[:, half * 4:half * 4 + 4, tq * 128:(tq + 1) * 128],
                            self.bank[b][:, :].rearrange("p (c t) -> p c t", t=128), [self.bk(b)], ["BIG3"])
            for cc in range(4):
                wo, wok = self.wload(wo_all[:, cc * 256:(cc + 1) * 256], 256)[0]
                for tq in range(4):
                    b = self.pbank()
                    for c in range(8):
                        self.mm(self.bank[b][:, 0:256], yT[:, c, tq * 128:(tq + 1) * 128], wo[:, c, :], c == 0, c == 7, [wok, "BIG3"], [self.bk(b)])
                    hs = hgrp[:, tq, cc * 256:(cc + 1) * 256]
                    self.stt("dve", hs, hs, ALPHA, self.bank[b][:, 0:256], ALU.mult, ALU.add, ["G", self.bk(b)], ["G"])
            for tq in range(4):
                tt = grp * 4 + tq
                self.ln_inplace(hgrp[:, tq, :], "G")
                P.dma("sp", hout[tt * 128:(tt + 1) * 128, :], hgrp[:, tq, :], reads=["G"],
                      writes=["out" if last else "hres%d" % ((l + 1) % 2)], is_output=last)


def make_in_map(inputs, b, consts):
    m = {"x": np.ascontiguousarray(inputs["x"][b]), "mem": np.ascontiguousarray(inputs["mem"][b])}
    for nm, shp in IN_SPECS:
        if nm in consts:
            m[nm] = consts[nm]
        else:
            m[nm] = np.ascontiguousarray(np.asarray(inputs[nm], dtype=np.float32).reshape(shp))
    return m


def kernel(**inputs):
    consts = host_consts()
    kb = KB(debug=False)
    nb = inputs["x"].shape[0]
    in_maps = [make_in_map(inputs, b, consts) for b in range(nb)]
    res = run_bass_kernel_spmd(kb.nc, in_maps, core_ids=list(range(nb)))
    out = np.stack([np.asarray(r["out"], dtype=np.float32).reshape(S, D) for r in res.results], axis=0)
    return out
```

```python
import math
from concourse.ap import AP
import contextlib
import numpy as np
import concourse.bass as bass
import concourse.mybir as mybir
from concourse.bass_utils import run_bass_kernel_spmd

F32 = mybir.dt.float32
BF16 = mybir.dt.bfloat16
I32 = mybir.dt.int32
AF = mybir.ActivationFunctionType
ALU = mybir.AluOpType
AX = mybir.AxisListType

ENGS = ("pe", "act", "dve", "pool", "sp")
DMA_SEMS = 8


class Op:
    __slots__ = ("eng", "fn", "waits", "is_dma", "idx", "marked", "dma_slot", "dma_val", "prewait")

    def __init__(self, eng, fn, is_dma):
        self.eng = eng
        self.fn = fn
        self.is_dma = is_dma
        self.waits = []
        self.marked = False
        self.idx = None
        self.dma_slot = None
        self.dma_val = None
        self.prewait = None


class Prog:
    def __init__(self, nc, same_engine_sync=True):
        self.nc = nc
        self.ops = {e: [] for e in ENGS}
        self.last_write = {}
        self.readers = {}
        self.same_engine_sync = same_engine_sync
        self.dma_count = {e: 0 for e in ENGS}
        self.dma_hist = {e: [] for e in ENGS}
        self.all_dma_out = []
        self.stack = contextlib.ExitStack()
        self.n_ops = 0

    def sb(self, name, shape, dt):
        return self.stack.enter_context(self.nc.sbuf_tensor("s_" + name, list(shape), dt))

    def ps(self, name, shape, dt):
        return self.stack.enter_context(self.nc.psum_tensor("p_" + name, list(shape), dt))

    def _deps(self, op, reads, writes):
        deps = []
        for k in reads:
            w = self.last_write.get(k)
            if w is not None:
                deps.append(w)
        for k in writes:
            w = self.last_write.get(k)
            if w is not None:
                deps.append(w)
            for r in self.readers.get(k, ()):
                deps.append(r)
        best = {}
        for d in deps:
            if d is op:
                continue
            key = (d.eng, d.is_dma, d.dma_slot if d.is_dma else None)
            cur = best.get(key)
            if cur is None or d.idx > cur.idx:
                best[key] = d
        for d in best.values():
            if (not d.is_dma) and d.eng == op.eng and not op.is_dma:
                if op.eng == "pe" or not self.same_engine_sync:
                    continue
            op.waits.append(d)
            d.marked = True
        for k in reads:
            self.readers.setdefault(k, []).append(op)
        for k in writes:
            self.last_write[k] = op
            self.readers[k] = []

    def barrier(self, fn):
        o = Op("pool", fn, False)
        o.idx = len(self.ops["pool"])
        self.ops["pool"].append(o)
        self._deps(o, [], ["__phase__"])
        return o

    def op(self, eng, fn, reads=(), writes=()):
        reads = list(reads) + ["__phase__"]
        o = Op(eng, fn, False)
        o.idx = len(self.ops[eng])
        self.ops[eng].append(o)
        self._deps(o, reads, writes)
        self.n_ops += 1
        return o

    def dma(self, eng, out, in_, reads=(), writes=(), is_output=False, **kw):
        def fn(e, out=out, in_=in_, kw=kw):
            return e.dma_start(out=out, in_=in_, **kw)
        reads = list(reads) + ["__phase__"]
        o = Op(eng, fn, True)
        o.idx = len(self.ops[eng])
        n = self.dma_count[eng]
        self.dma_count[eng] += 1
        o.dma_slot = n % DMA_SEMS
        o.dma_val = 16 * (n // DMA_SEMS + 1)
        if n >= DMA_SEMS:
            o.prewait = self.dma_hist[eng][n - DMA_SEMS]
        self.dma_hist[eng].append(o)
        self.ops[eng].append(o)
        self._deps(o, reads, writes)
        if is_output:
            self.all_dma_out.append(o)
        self.n_ops += 1
        return o

    def emit(self):
        nc = self.nc
        st = self.stack
        fin = Op("sp", None, False)
        fin.idx = len(self.ops["sp"])
        for o in self.all_dma_out:
            fin.waits.append(o)
        self.ops["sp"].append(fin)
        csem = {e: st.enter_context(nc.semaphore("c_" + e)) for e in ENGS}
        dsem = {e: [st.enter_context(nc.semaphore("d_%s_%d" % (e, i))) for i in range(DMA_SEMS)]
                for e in ENGS if self.dma_count[e] > 0}
        for e in ENGS:
            c = 0
            for o in self.ops[e]:
                if o.is_dma:
                    continue
                if o.marked:
                    c += 1
                    o.dma_val = c
        block = st.enter_context(nc.Block())
        prog = self

        def run(e, eng):
            seen = {}
            for o in prog.ops[e]:
                ws = list(o.waits)
                if o.prewait is not None:
                    ws.append(o.prewait)
                for d in ws:
                    if d.is_dma:
                        sem, val = dsem[d.eng][d.dma_slot], d.dma_val
                    else:
                        sem, val = csem[d.eng], d.dma_val
                    k = id(sem)
                    if seen.get(k, 0) >= val:
                        continue
                    seen[k] = val
                    eng.wait_ge(sem, val)
                if o.fn is None:
                    continue
                ins = o.fn(eng)
                if o.is_dma:
                    ins.then_inc(dsem[e][o.dma_slot], 16)
                elif o.marked:
                    ins.then_inc(csem[e], 1)

        @block.tensor
        def _(eng):
            run("pe", eng)

        @block.scalar
        def _(eng):
            run("act", eng)

        @block.vector
        def _(eng):
            run("dve", eng)

        @block.gpsimd
        def _(eng):
            run("pool", eng)

        @block.sync
        def _(eng):
            run("sp", eng)

    def close(self):
        self.stack.close()


F32R = mybir.dt.float32r

S = 2048
D = 1024
NT = 16
DEPTH = 2
WC = 256
XC = 2047
GW = 4096
ALPHA = (2 * DEPTH) ** 0.25
A0, B0, C0, D0, M0 = 0, 2048, 4352, 5632, 7680
OA, OB, OC, OD, OM = 0, 512, 1024, 1536, 2048
BROWS = [(0, 512), (512, 512), (1024, 512), (1536, 512), (2048, 256)]


def rel_bucket_np(rel):
    nb = 16
    max_exact = 8
    n = np.abs(rel)
    nf = np.maximum(n, 1).astype(np.float32)
    large = max_exact + (np.log(nf / max_exact) / np.float32(math.log(1024 / max_exact)) * (nb - max_exact)).astype(np.int32)
    large = np.minimum(large, nb - 1)
    return np.where(rel > 0, nb, 0) + np.where(n < max_exact, n, large)


def host_consts():
    c = {}
    c["ident"] = np.eye(128, dtype=np.float32)
    rel = np.arange(4096) - XC
    bkt = rel_bucket_np(rel)
    oh = np.zeros((32, 4096), np.float32)
    oh[bkt, np.arange(4096)] = 1.0
    c["onehot"] = oh
    n = np.abs(rel)
    mA = (n <= 64).astype(np.float32) + ((rel % 4 == 0) & (n <= 256)) + ((rel % 16 == 0) & (n <= 1024))
    mt = np.ones((12, 4096), np.float32)
    mt[:8] = mA[None, :]
    c["multab"] = mt
    t = np.arange(S)
    row = (t // 64).astype(np.float32)
    col = (t % 64).astype(np.float32)
    freqs = (10000.0 ** (-(np.arange(16, dtype=np.float32) / 16))).astype(np.float32)
    ar = row[:, None] * freqs[None, :]
    ac = col[:, None] * freqs[None, :]
    c["ropec"] = np.concatenate([np.cos(ar), np.cos(ar), np.cos(ac), np.cos(ac)], 1).astype(np.float32)
    c["ropes"] = np.concatenate([-np.sin(ar), np.sin(ar), -np.sin(ac), np.sin(ac)], 1).astype(np.float32)
    tri = np.zeros((2, 3, 128, 128), np.float32)
    sg = np.arange(128)[:, None]
    tt = np.arange(128)[None, :]
    same = (sg // 64) == (tt // 64)
    tri[0, 0] = same & (sg <= tt)
    tri[0, 1] = same & (sg < tt)
    tri[0, 2] = same & (sg > tt)
    tri[1, 0] = same & (sg >= tt)
    tri[1, 1] = same & (sg > tt)
    tri[1, 2] = same & (sg < tt)
    c["tri"] = tri.reshape(6 * 128, 128)
    mk_ = np.zeros((2, 64, 192), np.float32)
    a = np.arange(64)[:, None]
    b = np.arange(64)[None, :]
    mk_[0, :, 0:64] = a < b
    mk_[0, :, 64:128] = a <= b
    mk_[0, :, 128:192] = b < a
    mk_[1, :, 0:64] = a > b
    mk_[1, :, 64:128] = a >= b
    mk_[1, :, 128:192] = b > a
    c["rmask"] = mk_.reshape(128, 192)
    return c


IN_SPECS = [("ln_in_g", [1, D]), ("ln_in_b", [1, D]), ("rel_bias", [32, 12]), ("w_in", [DEPTH * D, 8192]),
            ("shift_mu", [DEPTH * 2, 1792]), ("rwkv_w0", [DEPTH * 2, 512]), ("rwkv_w_up", [DEPTH * 2 * 64, 512]),
            ("rwkv_a0", [DEPTH * 2, 512]), ("rwkv_a_up", [DEPTH * 2 * 64, 512]), ("rwkv_k_k", [DEPTH, 512]),
            ("rwkv_k_a", [DEPTH, 512]), ("rwkv_r_k", [DEPTH, 512]), ("rwkv_ln_g", [DEPTH, 512]),
            ("rwkv_ln_b", [DEPTH, 512]), ("c_qnorm_g", [DEPTH, 64]), ("c_knorm_g", [DEPTH, 64]),
            ("d_lambda", [DEPTH, 256]), ("d_subln_g", [DEPTH, 128]), ("w_mem_kv", [DEPTH * D, 512]),
            ("w_branch", [DEPTH * 2304, D]), ("w_gate", [DEPTH * D, 5120]), ("b_gate", [DEPTH, 5120]),
            ("w_out", [DEPTH * D, D]), ("ln_g", [DEPTH, D]), ("ln_b", [DEPTH, D]),
            ("ident", [128, 128]), ("onehot", [32, 4096]), ("multab", [12, 4096]), ("ropec", [S, 64]),
            ("ropes", [S, 64]), ("tri", [768, 128]), ("rmask", [128, 192])]


class KB:
    def __init__(self, debug=False, mixers="MCADB", layers=DEPTH):
        self.debug = debug
        self.mixers = mixers
        nc = bass.Bass("TRN2", target_bir_lowering=False)
        self.nc = nc
        P = Prog(nc)
        self.P = P
        I = {}
        I["x"] = nc.dram_tensor("x", [S, D], F32, kind="ExternalInput").ap()
        I["mem"] = nc.dram_tensor("mem", [256, D], F32, kind="ExternalInput").ap()
        for nm, shp in IN_SPECS:
            I[nm] = nc.dram_tensor(nm, list(shp), F32, kind="ExternalInput").ap()
        self.I = I
        self.out = nc.dram_tensor("out", [S, D], F32, kind="ExternalOutput").ap()
        self.hres = [nc.dram_tensor("hres%d" % i, [S, D], F32, kind="ExternalOutput" if debug else "Internal").ap() for i in range(2)]
        self.dbg_outs = []
        self.o_scr = nc.dram_tensor("o_scr", [S, 2304], F32, kind="ExternalOutput" if debug else "Internal").ap()
        self.xtab = nc.dram_tensor("xtab", [12, 4096], F32).ap()
        self.rk_scr = nc.dram_tensor("rk_scr", [1024, S], F32).ap()
        self.wa_scr = nc.dram_tensor("wa_scr", [256, S], F32).ap()
        self.v_scr = nc.dram_tensor("v_scr", [S, 512], F32).ap()
        self.y_scr = nc.dram_tensor("y_scr", [2 * S, 520], F32, kind="ExternalOutput" if debug else "Internal").ap()
        self.sg_scr = nc.dram_tensor("sg_scr", [S, 512], F32).ap()
        self.ident = P.sb("ident", [128, 128], F32)
        self.hT = P.sb("hT", [128, 8, S + 2], BF16)
        self.BIG = [P.sb("BIG%d" % i, [128, 9216], BF16) for i in range(4)]
        self.G = P.sb("G", [128, GW], F32)
        self.wst = [P.sb("wst%d" % i, [128, 8, WC], F32) for i in range(1)]
        self.RX = P.sb("RX", [64, 3072], F32)
        self.wbf = [P.sb("wbf%d" % i, [128, 8, WC], BF16) for i in range(4)]
        self.ropec = P.sb("ropec", [128, NT, 64], F32)
        self.ropes = P.sb("ropes", [128, NT, 64], F32)
        self.lnx = [P.sb("lnx%d" % i, [128, D], F32) for i in range(2)]
        self.junk = P.sb("junk", [128, D], F32)
        self.lng = P.sb("lng", [128, D], F32)
        self.lnb = P.sb("lnb", [128, D], F32)
        self.pt = [P.sb("pt%d" % i, [128, 512], BF16) for i in range(4)]
        self.pe_ = [P.sb("pe%d" % i, [128, 512], BF16) for i in range(2)]
        self.ost = [P.sb("ost%d" % i, [128, 128], F32) for i in range(4)]
        self.sm = P.sb("sm", [128, 64], F32)
        self.tmp = [P.sb("tmp%d" % i, [128, 512], F32) for i in range(3)]
        self.onesf = P.sb("onesf", [128, 128], F32)
        self.brow = [P.sb("brow%d" % i, [1, WC], F32) for i in range(2)]
        self.gq = P.sb("gq", [128, 128], F32)
        self.subg = P.sb("subg", [128, 128], F32)
        self.lamt = P.sb("lamt", [128, 264], F32)
        self.pbar = P.sb("pbar", [1, 8], F32)
        self.memT = P.sb("memT", [128, 8, 256], BF16)
        self.bank = [P.ps("bank%d" % i, [128, 512], F32) for i in range(8)]
        self.cnt = {}
        self.pbi = 0
        B0_, B1_, B2_, B3_ = [b[:] for b in self.BIG]
        self.qT = B0_[:, 0:8192].rearrange("p (c t) -> p c t", t=S)
        self.kT = B1_[:, 0:8192].rearrange("p (c t) -> p c t", t=S)
        self.sg = B3_[:, 0:8192].rearrange("p (t c) -> p t c", c=512)
        self.prelude()
        for l in range(layers):
            self.layer(l, last=(l == layers - 1))
        P.emit()
        P.close()

    def nxt(self, name, n):
        v = self.cnt.get(name, 0)
        self.cnt[name] = (v + 1) % n
        return v

    def bk(self, i):
        return "bank%d" % i

    def pbank(self):
        self.pbi ^= 1
        return 2 + self.pbi

    def barrier(self):
        pbar = self.pbar
        self.P.barrier(lambda e: e.memset(pbar[:], 0.0))

    def R(self, ap):
        if ap.dtype == F32 and ap.name == "s_RX":
            return ap.bitcast(F32R)
        return ap

    def mm(self, out, lhsT, rhs, start, stop, reads, writes):
        if lhsT.name == "s_RX" and rhs.name == "s_RX":
            lhsT, rhs = self.R(lhsT), self.R(rhs)
        self.P.op("pe", lambda e: e.matmul(out, lhsT=lhsT, rhs=rhs, start=start, stop=stop), reads=reads, writes=writes)

    def tr(self, out, in_, reads, writes, np_=128):
        ident = self.ident
        self.P.op("pe", lambda e: e.transpose(out, in_, ident[0:np_, 0:np_]), reads=list(reads) + ["ident"], writes=writes)

    def cp(self, eng, out, in_, reads, writes):
        out = self.R(out)
        if eng == "act":
            self.P.op("act", lambda e: e.copy(out=out, in_=in_), reads=reads, writes=writes)
        else:
            self.P.op(eng, lambda e: e.tensor_copy(out=out, in_=in_), reads=reads, writes=writes)

    def act(self, out, in_, func, reads, writes, **kw):
        out = self.R(out)
        self.P.op("act", lambda e: e.activation(out=out, in_=in_, func=func, **kw), reads=reads, writes=writes)

    def tt(self, eng, out, in0, in1, op, reads, writes):
        out = self.R(out)
        self.P.op(eng, lambda e: e.tensor_tensor(out=out, in0=in0, in1=in1, op=op), reads=reads, writes=writes)

    def ts(self, eng, out, in0, s1, s2, op0, op1, reads, writes):
        out = self.R(out)
        if s2 is None:
            self.P.op(eng, lambda e: e.tensor_scalar(out=out, in0=in0, scalar1=s1, scalar2=None, op0=op0), reads=reads, writes=writes)
        else:
            self.P.op(eng, lambda e: e.tensor_scalar(out=out, in0=in0, scalar1=s1, scalar2=s2, op0=op0, op1=op1), reads=reads, writes=writes)

    def stt(self, eng, out, in0, scalar, in1, op0, op1, reads, writes):
        out = self.R(out)
        self.P.op(eng, lambda e: e.scalar_tensor_tensor(out=out, in0=in0, scalar=scalar, in1=in1, op0=op0, op1=op1), reads=reads, writes=writes)

    def memset(self, eng, ap, val, writes):
        ap = self.R(ap)
        self.P.op(eng, lambda e: e.memset(ap, val), writes=writes)

    def rsqrt_cols(self, src, dst, scale, eps, key="sm"):
        self.ts("dve", dst, src, scale, eps, ALU.mult, ALU.add, [key], [key])
        self.P.op("act", lambda e: e.sqrt(out=dst, in_=dst), reads=[key], writes=[key])
        self.P.op("dve", lambda e: e.reciprocal(out=dst, in_=dst), reads=[key], writes=[key])

    def wload(self, src2d, n, kc=8, variants=None):
        P = self.P
        i = 0
        wst = self.wst[i]
        P.dma("sp", wst[:, 0:kc, 0:n], src2d.rearrange("(c p) n -> p c n", p=128), writes=["wst%d" % i])
        res = []
        if variants is None:
            j = self.nxt("wb", 4)
            self.cp("pool", self.wbf[j][:, 0:kc, 0:n], wst[:, 0:kc, 0:n], ["wst%d" % i], ["wbf%d" % j])
            return [(self.wbf[j], "wbf%d" % j)]
        for (vap, vkey) in variants:
            j = self.nxt("wb", 4)
            self.tt("pool", self.wbf[j][:, 0:kc, 0:n], wst[:, 0:kc, 0:n], vap.unsqueeze(1).broadcast_to([128, kc, n]), ALU.mult,
                    ["wst%d" % i, vkey], ["wbf%d" % j])
            res.append((self.wbf[j], "wbf%d" % j))
        return res

    def proj_T(self, wl, n, consume, shifts=(0,), rhs_fn=None, rkey="hT", ntb=4, tbw=512):
        hT = self.hT
        for ct in range(n // 128):
            for tb in range(ntb):
                b = self.pbank()
                nmm = 8 * len(shifts)
                m = 0
                for (wap, wkey), s in zip(wl, shifts):
                    for c in range(8):
                        if rhs_fn is None:
                            lo = 1 + tb * 512 + s
                            rhs = hT[:, c, lo:lo + 512]
                        else:
                            rhs = rhs_fn(c, tb)
                        self.mm(self.bank[b][:, 0:tbw], wap[:, c, ct * 128:(ct + 1) * 128], rhs, m == 0, m == nmm - 1,
                                [wkey, rkey], [self.bk(b)])
                        m += 1
                consume(b, ct, tb)

    def proj_N(self, wl, n, consume, shifts=(0,), lhs_fn=None, lkey="hT", ntt=NT, kc=8):
        hT = self.hT
        for tt in range(ntt):
            b = self.pbank()
            nmm = kc * len(shifts)
            m = 0
            for (wap, wkey), s in zip(wl, shifts):
                for c in range(kc):
                    if lhs_fn is None:
                        lo = 1 + tt * 128 + s
                        lh = hT[:, c, lo:lo + 128]
                    else:
                        lh = lhs_fn(c, tt)
                    self.mm(self.bank[b][:, 0:n], lh, wap[:, c, 0:n], m == 0, m == nmm - 1, [wkey, lkey], [self.bk(b)])
                    m += 1
            consume(b, tt)

    def ln_inplace(self, xt, xkey, eps=1e-5):
        sm, junk = self.sm, self.junk
        P = self.P
        P.op("dve", lambda e: e.reduce_sum(out=sm[:, 0:1], in_=xt, axis=AX.X), reads=[xkey], writes=["sm"])
        self.ts("dve", sm[:, 1:2], sm[:, 0:1], -1.0 / D, None, ALU.mult, None, ["sm"], ["sm"])
        self.ts("dve", xt, xt, sm[:, 1:2], None, ALU.add, None, [xkey, "sm"], [xkey])
        self.memset("dve", sm[:, 2:3], 0.0, ["sm"])
        self.act(junk[:], xt, AF.Square, [xkey, "sm"], ["junk", "sm"], accum_out=sm[:, 2:3])
        self.rsqrt_cols(sm[:, 2:3], sm[:, 3:4], 1.0 / D, eps)
        self.stt("dve", xt, xt, sm[:, 3:4], self.lng[:], ALU.mult, ALU.mult, [xkey, "sm", "lng"], [xkey])
        self.tt("dve", xt, xt, self.lnb[:], ALU.add, [xkey, "lnb"], [xkey])

    def to_hT(self, src, skey, tt):
        hT = self.hT
        for half in range(2):
            b = self.pbank()
            for c4 in range(4):
                c = half * 4 + c4
                self.tr(self.bank[b][:, c4 * 128:(c4 + 1) * 128], src[:, c * 128:(c + 1) * 128], [skey], [self.bk(b)])
            self.cp("act" if half else "dve", hT[:, half * 4:half * 4 + 4, 1 + tt * 128:1 + (tt + 1) * 128],
                    self.bank[b][:, :].rearrange("p (c t) -> p c t", t=128), [self.bk(b)], ["hT"])

    def prelude(self):
        P, I = self.P, self.I
        P.dma("sp", self.ident[:], I["ident"], writes=["ident"])
        P.dma("sp", self.ropec[:], I["ropec"].rearrange("(t p) c -> p t c", p=128), writes=["ropec"])
        P.dma("sp", self.ropes[:], I["ropes"].rearrange("(t p) c -> p t c", p=128), writes=["ropes"])
        self.memset("pool", self.onesf[:], 1.0, ["onesf"])
        self.memset("pool", self.hT[:, :, 0:1], 0.0, ["hT"])
        self.memset("pool", self.hT[:, :, S + 1:S + 2], 0.0, ["hT"])
        tmpA = self.tmp[0]
        rb = tmpA[0:32, 0:12]
        P.dma("sp", rb, I["rel_bias"], writes=["tmp0"])
        ohs = self.BIG[0][:].bitcast(F32)
        P.dma("sp", ohs[0:32, 0:4096], I["onehot"], writes=["BIG0"])
        mts = self.BIG[1][:].bitcast(F32)
        P.dma("sp", mts[0:12, 0:4096], I["multab"], writes=["BIG1"])
        xts = self.BIG[2][:].bitcast(F32)
        for j in range(8):
            b = self.pbank()
            self.mm(self.bank[b][0:12, :], rb, ohs[0:32, j * 512:(j + 1) * 512], True, True, ["tmp0", "BIG0"], [self.bk(b)])
            self.act(xts[0:12, j * 512:(j + 1) * 512], self.bank[b][0:12, :], AF.Exp, [self.bk(b)], ["BIG2"])
        self.tt("dve", xts[0:12, 0:4096], xts[0:12, 0:4096], mts[0:12, 0:4096], ALU.mult, ["BIG2", "BIG1"], ["BIG2"])
        P.dma("sp", self.xtab, xts[0:12, 0:4096], reads=["BIG2"], writes=["xtab"])
        self.barrier()
        for mt_ in range(2):
            i = self.nxt("ln", 2)
            P.dma("sp", self.lnx[i][:], I["mem"][mt_ * 128:(mt_ + 1) * 128, :], writes=["lnx%d" % i])
            for half in range(2):
                b = self.pbank()
                for c4 in range(4):
                    c = half * 4 + c4
                    self.tr(self.bank[b][:, c4 * 128:(c4 + 1) * 128], self.lnx[i][:, c * 128:(c + 1) * 128], ["lnx%d" % i], [self.bk(b)])
                self.cp("dve", self.memT[:, half * 4:half * 4 + 4, mt_ * 128:(mt_ + 1) * 128],
                        self.bank[b][:, :].rearrange("p (c t) -> p c t", t=128), [self.bk(b)], ["memT"])
        P.dma("sp", self.lng[:], I["ln_in_g"].partition_broadcast(128), writes=["lng"])
        P.dma("sp", self.lnb[:], I["ln_in_b"].partition_broadcast(128), writes=["lnb"])
        for tt in range(NT):
            i = self.nxt("ln", 2)
            P.dma("sp", self.lnx[i][:], I["x"][tt * 128:(tt + 1) * 128, :], writes=["lnx%d" % i])
            self.ln_inplace(self.lnx[i][:], "lnx%d" % i)
            P.dma("sp", self.hres[0][tt * 128:(tt + 1) * 128, :], self.lnx[i][:], reads=["lnx%d" % i], writes=["hres0"])
        self.barrier()

    def layer(self, l, last):
        P, I = self.P, self.I
        hin = self.hres[l % 2]
        for tt in range(NT):
            i = self.nxt("ln", 2)
            P.dma("sp", self.lnx[i][:], hin[tt * 128:(tt + 1) * 128, :], reads=["hres%d" % (l % 2)], writes=["lnx%d" % i])
            self.to_hT(self.lnx[i], "lnx%d" % i, tt)
        self.win = I["w_in"][l * D:(l + 1) * D, :]
        for mx in "MCADB":
            if mx in self.mixers:
                getattr(self, "mixer_" + mx)(l)
            else:
                self.zero_o(mx)
            self.barrier()
        self.merge(l, last)
        self.barrier()

    def zero_o(self, mx):
        c0, w = {"M": (OM, 256), "C": (OC, 512), "A": (OA, 512), "D": (OD, 512), "B": (OB, 512)}[mx]
        t = self.tmp[2]
        self.memset("pool", t[:, :], 0.0, ["tmp2"])
        for tt in range(NT):
            self.P.dma("sp", self.o_scr[tt * 128:(tt + 1) * 128, c0:c0 + w], t[:, 0:w], reads=["tmp2"], writes=["o_scr"])

    def attn_head(self, maps, vfn, vkey, nkt, dv1, table, band, post):
        nm = len(maps)
        G = self.G

        nqt = 4 if nm == 1 else 2
        QB = nqt * 128

        def accap(m, qt):
            bi = 4 + m * nqt + qt
            return self.bank[bi][:, 0:dv1], bi

        for qb in range(S // QB):
            q0 = qb * QB
            kts = []
            for kt in range(nkt):
                dk = kt * 128 - q0
                if band and (dk - (QB - 1) > 1024 or dk + 127 < -1024):
                    continue
                kts.append(kt)
            steps = [(idx, kt, m) for idx, kt in enumerate(kts) for m in range(nm)]

            def stageA(si):
                idx, kt, m = steps[si]
                q_ap, kfn = maps[m]
                sb_ = si % 2
                self.mm(self.bank[sb_][:, 0:QB], kfn(kt), q_ap[:, q0:q0 + QB], True, True, ["BIG0", "BIG1"], [self.bk(sb_)])

            def stageBC(si):
                idx, kt, m = steps[si]
                sb_ = si % 2
                pti = self.nxt("pt", 4)
                ptile = self.pt[pti]
                if table:
                    pei = self.nxt("pe", 2)
                    self.act(self.pe_[pei][:, 0:QB], self.bank[sb_][:, 0:QB], AF.Exp, [self.bk(sb_)], ["pe%d" % pei], scale=0.125)
                    j0 = kt * 128 - q0 + XC
                    gs = G[:, j0 - (QB - 1):j0 + 1][:, ::-1]
                    self.tt("dve", ptile[:, 0:QB], self.pe_[pei][:, 0:QB], gs, ALU.mult, ["pe%d" % pei, "G"], ["pt%d" % pti])
                else:
                    self.act(ptile[:, 0:QB], self.bank[sb_][:, 0:QB], AF.Exp, [self.bk(sb_)], ["pt%d" % pti], scale=0.125)
                for qt in range(nqt):
                    acc, bi = accap(m, qt)
                    self.mm(acc, ptile[:, qt * 128:(qt + 1) * 128], vfn(kt), idx == 0, idx == len(kts) - 1,
                            ["pt%d" % pti, vkey], [self.bk(bi)])

            stageA(0)
            for si in range(len(steps)):
                if si + 1 < len(steps):
                    stageA(si + 1)
                stageBC(si)
            for qt in range(nqt):
                accs = [accap(m, qt) for m in range(nm)]
                post(qb * nqt + qt, [a for a, _ in accs], [self.bk(bi) for _, bi in accs])

    def post_simple(self, h, hd, ocol):
        def post(tt, accs, keys):
            acc = accs[0]
            sm = self.sm
            i = self.nxt("ost", 4)
            self.P.op("dve", lambda e: e.reciprocal(out=sm[:, 8:9], in_=acc[:, hd:hd + 1]), reads=keys, writes=["sm"])
            self.stt("dve", self.ost[i][:, 0:hd], acc[:, 0:hd], sm[:, 8:9], self.sg[:, tt, h * hd:(h + 1) * hd], ALU.mult, ALU.mult,
                     keys + ["sm", "BIG3"], ["ost%d" % i])
            self.P.dma("sp", self.o_scr[tt * 128:(tt + 1) * 128, ocol + h * hd:ocol + (h + 1) * hd], self.ost[i][:, 0:hd],
                       reads=["ost%d" % i], writes=["o_scr"])
        return post

    def cons_T(self, dst, dkey):
        def consume(b, ct, tb):
            self.cp("dve" if (ct + tb) % 2 else "act", dst[:, ct, tb * 512:(tb + 1) * 512], self.bank[b][:, :], [self.bk(b)], [dkey])
        return consume

    def cons_sg(self, c0, n):
        def consume(b, tt):
            self.act(self.sg[:, tt, c0:c0 + n], self.bank[b][:, 0:n], AF.Silu, [self.bk(b)], ["BIG3"])
        return consume

    def cons_v(self, V1, h0, nh, hd):
        def consume(b, tt):
            self.cp("dve", V1[:, tt, h0:h0 + nh, 0:hd], self.bank[b][:, 0:nh * hd].rearrange("p (h d) -> p h d", d=hd), [self.bk(b)], ["BIG2"])
        return consume

    def mixer_M(self, l):
        P, I = self.P, self.I
        wkv = I["w_mem_kv"][l * D:(l + 1) * D, :]
        V1 = self.BIG[2][:, 0:520].rearrange("p (k h d) -> p k h d", k=2, h=4, d=65)
        self.memset("pool", self.BIG[2][:, 0:520], 1.0, ["BIG2"])
        memT = self.memT
        wl = self.wload(wkv[:, 0:256], 256)
        self.proj_T(wl, 256, lambda b, ct, tb: self.cp("dve", self.kT[:, ct, 0:256], self.bank[b][:, 0:256], [self.bk(b)], ["BIG1"]),
                    rhs_fn=lambda c, tb: memT[:, c, 0:256], rkey="memT", ntb=1, tbw=256)
        wl = self.wload(wkv[:, 256:512], 256)
        self.proj_N(wl, 256, self.cons_v(V1, 0, 4, 64), lhs_fn=lambda c, tt: memT[:, c, tt * 128:(tt + 1) * 128], lkey="memT", ntt=2)
        wl = self.wload(self.win[:, M0:M0 + 256], 256)
        self.proj_T(wl, 256, self.cons_T(self.qT, "BIG0"))
        wl = self.wload(self.win[:, M0 + 256:M0 + 512], 256)
        self.proj_N(wl, 256, self.cons_sg(0, 256))
        if l == 0 and "m" in self.mixers:
            self.dbg("qT", self.BIG[0][:, 0:8192], ["BIG0"], BF16)
            self.dbg("kT", self.BIG[1][:, 0:8192], ["BIG1"], BF16)
            self.dbg("V1", self.BIG[2][:, 0:520], ["BIG2"], BF16)
            self.dbg("sg", self.BIG[3][:, 0:8192], ["BIG3"], BF16)
            self.dbg("hT", self.hT[:, :, :].rearrange("p c t -> p (c t)"), ["hT"], BF16)
        for h in range(4):
            ct, pb = h // 2, (h % 2) * 64
            q_ap = self.qT[pb:pb + 64, ct, :]
            self.attn_head([(q_ap, lambda kt, ct=ct, pb=pb: self.kT[pb:pb + 64, ct, kt * 128:(kt + 1) * 128])],
                           lambda kt, h=h: V1[:, kt, h, :], "BIG2", 2, 65, False, False, self.post_simple(h, 64, OM))

    def mixer_A(self, l):
        P = self.P
        V1 = self.BIG[2][:, 0:8320].rearrange("p (k h d) -> p k h d", k=16, h=8, d=65)
        self.memset("pool", self.BIG[2][:, 0:8320], 1.0, ["BIG2"])
        for j in range(2):
            wl = self.wload(self.win[:, A0 + j * 256:A0 + (j + 1) * 256], 256)
            self.proj_T(wl, 256, lambda b, ct, tb, j=j: self.cons_T(self.qT, "BIG0")(b, ct + 2 * j, tb))
        for j in range(2):
            wl = self.wload(self.win[:, A0 + 512 + j * 256:A0 + 512 + (j + 1) * 256], 256)
            self.proj_T(wl, 256, lambda b, ct, tb, j=j: self.cons_T(self.kT, "BIG1")(b, ct + 2 * j, tb))
        for j in range(2):
            wl = self.wload(self.win[:, A0 + 1024 + j * 256:A0 + 1024 + (j + 1) * 256], 256)
            self.proj_N(wl, 256, self.cons_v(V1, 4 * j, 4, 64))
        for j in range(2):
            wl = self.wload(self.win[:, A0 + 1536 + j * 256:A0 + 1536 + (j + 1) * 256], 256)
            self.proj_N(wl, 256, self.cons_sg(j * 256, 256))
        for h in range(8):
            ct, pb = h // 2, (h % 2) * 64
            P.dma("sp", self.G[:, 0:3968], AP(self.xtab.tensor, h * 4096, [[1, 128], [1, 3968]]), reads=["xtab"], writes=["G"])
            q_ap = self.qT[pb:pb + 64, ct, :]
            self.attn_head([(q_ap, lambda kt, ct=ct, pb=pb: self.kT[pb:pb + 64, ct, kt * 128:(kt + 1) * 128])],
                           lambda kt, h=h: V1[:, kt, h, :], "BIG2", 16, 65, True, True, self.post_simple(h, 64, OA))

    def normrope(self, b, tt, nh, gcol):
        n = nh * 64
        sm = self.sm
        t0, t1, t2 = self.tmp
        ps = self.bank[b][:, 0:n]
        self.act(t0[:, 0:n], ps, AF.Square, [self.bk(b)], ["tmp0"])
        self.P.op("dve", lambda e: e.reduce_sum(out=sm[:, 16:16 + nh], in_=t0[:, 0:n].rearrange("p (h d) -> p h d", d=64), axis=AX.X),
                  reads=["tmp0"], writes=["sm"])
        self.rsqrt_cols(sm[:, 16:16 + nh], sm[:, 24:24 + nh], 1.0 / 64, 1e-6)
        v3 = lambda ap: ap.rearrange("p (h d) -> p h d", d=64)
        self.tt("dve", v3(t0[:, 0:n]), v3(ps), sm[:, 24:24 + nh].unsqueeze(2).broadcast_to([128, nh, 64]), ALU.mult,
                [self.bk(b), "sm"], ["tmp0"])
        self.tt("dve", v3(t0[:, 0:n]), v3(t0[:, 0:n]), self.gq[:, gcol:gcol + 64].unsqueeze(1).broadcast_to([128, nh, 64]), ALU.mult,
                ["tmp0", "gq"], ["tmp0"])
        self.tt("pool", v3(t1[:, 0:n]), v3(t0[:, 0:n]), self.ropec[:, tt, :].unsqueeze(1).broadcast_to([128, nh, 64]), ALU.mult,
                ["tmp0", "ropec"], ["tmp1"])
        v5 = lambda ap: ap.rearrange("p (h a b c) -> p h a b c", a=2, b=2, c=16)
        rs = self.ropes[:, tt, :].rearrange("p (a b c) -> p a b c", a=2, b=2, c=16)
        for bb in range(2):
            self.tt("dve", v5(t2[:, 0:n])[:, :, :, bb, :], v5(t0[:, 0:n])[:, :, :, 1 - bb, :],
                    rs[:, :, bb, :].unsqueeze(1).broadcast_to([128, nh, 2, 16]), ALU.mult, ["tmp0", "ropes"], ["tmp2"])
        self.tt("dve", t1[:, 0:n], t1[:, 0:n], t2[:, 0:n], ALU.add, ["tmp1", "tmp2"], ["tmp1"])

    def mixer_C(self, l):
        P, I = self.P, self.I
        V1 = self.BIG[2][:, 0:2080].rearrange("p (k h d) -> p k h d", k=16, h=2, d=65)
        self.memset("pool", self.BIG[2][:, 0:2080], 1.0, ["BIG2"])
        P.dma("sp", self.gq[:, 0:64], I["c_qnorm_g"][l:l + 1, :].partition_broadcast(128), writes=["gq"])
        P.dma("sp", self.gq[:, 64:128], I["c_knorm_g"][l:l + 1, :].partition_broadcast(128), writes=["gq"])
        t1 = self.tmp[1]
        for j in range(2):
            wl = self.wload(self.win[:, C0 + j * 256:C0 + (j + 1) * 256], 256)

            def cons_q(b, tt, j=j):
                self.normrope(b, tt, 4, 0)
                b2 = self.pbank()
                for u in range(2):
                    self.tr(self.bank[b2][:, u * 128:(u + 1) * 128], t1[:, u * 128:(u + 1) * 128], ["tmp1"], [self.bk(b2)])
                self.cp("act", self.qT[:, 2 * j:2 * j + 2, tt * 128:(tt + 1) * 128],
                        self.bank[b2][:, 0:256].rearrange("p (c t) -> p c t", t=128), [self.bk(b2)], ["BIG0"])
            self.proj_N(wl, 256, cons_q)
        wl = self.wload(self.win[:, C0 + 512:C0 + 768], 256)

        def cons_kv(b, tt):
            self.cp("act", V1[:, tt, :, 0:64], self.bank[b][:, 128:256].rearrange("p (h d) -> p h d", d=64), [self.bk(b)], ["BIG2"])
            self.normrope(b, tt, 2, 64)
            t2 = self.tmp[2]
            self.cp("dve", t2[:, 0:256].rearrange("p (g r d) -> p g r d", g=2, r=2, d=64),
                    t1[:, 0:128].rearrange("p (g d) -> p g d", d=64).unsqueeze(2).broadcast_to([128, 2, 2, 64]), ["tmp1"], ["tmp2"])
            b2 = self.pbank()
            for u in range(2):
                self.tr(self.bank[b2][:, u * 128:(u + 1) * 128], t2[:, u * 128:(u + 1) * 128], ["tmp2"], [self.bk(b2)])
            self.cp("act", self.kT[:, 0:2, tt * 128:(tt + 1) * 128],
                    self.bank[b2][:, 0:256].rearrange("p (c t) -> p c t", t=128), [self.bk(b2)], ["BIG1"])
        self.proj_N(wl, 256, cons_kv)
        for j in range(2):
            wl = self.wload(self.win[:, C0 + 768 + j * 256:C0 + 768 + (j + 1) * 256], 256)
            self.proj_N(wl, 256, self.cons_sg(j * 256, 256))
        for h in range(8):
            ct, pb = h // 2, (h % 2) * 64
            g = h // 4
            q_ap = self.qT[pb:pb + 64, ct, :]
            self.attn_head([(q_ap, lambda kt, g=g, pb=pb: self.kT[pb:pb + 64, g, kt * 128:(kt + 1) * 128])],
                           lambda kt, g=g: V1[:, kt, g, :], "BIG2", 16, 65, False, False, self.post_simple(h, 64, OC))

    def mixer_D(self, l):
        P, I = self.P, self.I
        V1 = self.BIG[2][:, 0:8256].rearrange("p (k h d) -> p k h d", k=16, h=4, d=129)
        self.memset("pool", self.BIG[2][:, 0:8256], 1.0, ["BIG2"])
        lam_init = 0.8 - 0.6 * math.exp(-0.3 * l)
        lamt, sm = self.lamt, self.sm
        P.dma("sp", lamt[:, 0:256], I["d_lambda"][l:l + 1, :].partition_broadcast(128), writes=["lamt"])
        P.dma("sp", self.subg[:], I["d_subln_g"][l:l + 1, :].partition_broadcast(128), writes=["subg"])
        self.ts("pool", self.subg[:], self.subg[:], 1.0 - lam_init, None, ALU.mult, None, ["subg"], ["subg"])
        lv = lamt[:, 0:256].rearrange("p (a b c) -> p a b c", a=2, b=2, c=64)
        lp = self.tmp[2][:, 0:128].rearrange("p (a c) -> p a c", c=64)
        self.tt("dve", lp, lv[:, :, 0, :], lv[:, :, 1, :], ALU.mult, ["lamt"], ["tmp2"])
        P.op("dve", lambda e: e.reduce_sum(out=lamt[:, 256:258], in_=lp, axis=AX.X), reads=["tmp2"], writes=["lamt"])
        self.act(lamt[:, 258:260], lamt[:, 256:258], AF.Exp, ["lamt"], ["lamt"])
        self.tt("dve", lamt[:, 260:261], lamt[:, 259:260], lamt[:, 258:259], ALU.subtract, ["lamt"], ["lamt"])
        self.ts("dve", lamt[:, 260:261], lamt[:, 260:261], -lam_init, None, ALU.add, None, ["lamt"], ["lamt"])
        for j in range(2):
            wl = self.wload(self.win[:, D0 + j * 256:D0 + (j + 1) * 256], 256)
            self.proj_T(wl, 256, lambda b, ct, tb, j=j: self.cons_T(self.qT, "BIG0")(b, ct + 2 * j, tb))
        for j in range(2):
            wl = self.wload(self.win[:, D0 + 512 + j * 256:D0 + 512 + (j + 1) * 256], 256)
            self.proj_T(wl, 256, lambda b, ct, tb, j=j: self.cons_T(self.kT, "BIG1")(b, ct + 2 * j, tb))
        for j in range(2):
            wl = self.wload(self.win[:, D0 + 1024 + j * 256:D0 + 1024 + (j + 1) * 256], 256)
            self.proj_N(wl, 256, self.cons_v(V1, 2 * j, 2, 128))
        for j in range(2):
            wl = self.wload(self.win[:, D0 + 1536 + j * 256:D0 + 1536 + (j + 1) * 256], 256)
            self.proj_N(wl, 256, self.cons_sg(j * 256, 256))
        t0 = self.tmp[0]
        for h in range(4):
            P.dma("sp", self.G[:, 0:3968], AP(self.xtab.tensor, (8 + h) * 4096, [[1, 128], [1, 3968]]), reads=["xtab"], writes=["G"])

            def post(tt, accs, keys, h=h):
                a1, a2 = accs
                i = self.nxt("ost", 4)
                P.op("dve", lambda e: e.reciprocal(out=sm[:, 8:9], in_=a1[:, 128:129]), reads=keys, writes=["sm"])
                P.op("dve", lambda e: e.reciprocal(out=sm[:, 9:10], in_=a2[:, 128:129]), reads=keys, writes=["sm"])
                self.tt("dve", sm[:, 9:10], sm[:, 9:10], lamt[:, 260:261], ALU.mult, ["sm", "lamt"], ["sm"])
                self.ts("dve", t0[:, 0:128], a1[:, 0:128], sm[:, 8:9], None, ALU.mult, None, keys + ["sm"], ["tmp0"])
                self.stt("dve", t0[:, 128:256], a2[:, 0:128], sm[:, 9:10], t0[:, 0:128], ALU.mult, ALU.add, keys + ["sm", "tmp0"], ["tmp0"])
                self.memset("dve", sm[:, 10:11], 0.0, ["sm"])
                self.act(t0[:, 256:384], t0[:, 128:256], AF.Square, ["tmp0", "sm"], ["tmp0", "sm"], accum_out=sm[:, 10:11])
                self.rsqrt_cols(sm[:, 10:11], sm[:, 11:12], 1.0 / 128, 1e-5)
                self.stt("dve", t0[:, 128:256], t0[:, 128:256], sm[:, 11:12], self.subg[:], ALU.mult, ALU.mult, ["tmp0", "sm", "subg"], ["tmp0"])
                self.tt("dve", self.ost[i][:], t0[:, 128:256], self.sg[:, tt, h * 128:(h + 1) * 128], ALU.mult, ["tmp0", "BIG3"], ["ost%d" % i])
                P.dma("sp", self.o_scr[tt * 128:(tt + 1) * 128, OD + h * 128:OD + (h + 1) * 128], self.ost[i][:],
                      reads=["ost%d" % i], writes=["o_scr"])
            maps = [(self.qT[c * 64:(c + 1) * 64, h, :], (lambda kt, c=c, h=h: self.kT[c * 64:(c + 1) * 64, h, kt * 128:(kt + 1) * 128]))
                    for c in range(2)]
            self.attn_head(maps, lambda kt, h=h: V1[:, kt, h, :], "BIG2", 16, 129, True, False, post)

    def dbg(self, name, ap, reads, dt=F32):
        if not self.debug:
            return
        t = self.nc.dram_tensor("dbg_" + name, list(ap.shape), dt, kind="ExternalOutput").ap()
        self.P.dma("sp", t, ap, reads=reads, is_output=True)
        self.dbg_outs.append("dbg_" + name)

    def mixer_B(self, l):
        P, I = self.P, self.I
        CW = 0.6065306597126334
        t_ring = self.tmp
        mub = self.lnx[0][:, 0:768].rearrange("p (v n) -> p v n", n=256)

        def load_mu(c0):
            for v in range(2):
                P.dma("sp", mub[:, 1 + v, :], I["shift_mu"][l * 2 + v:l * 2 + v + 1, c0:c0 + 256].partition_broadcast(128), writes=["lnx0"])
            self.tt("dve", mub[:, 0, :], mub[:, 1, :], mub[:, 2, :], ALU.add, ["lnx0"], ["lnx0"])
            self.ts("dve", mub[:, 0, :], mub[:, 0, :], -1.0, 1.0, ALU.mult, ALU.add, ["lnx0"], ["lnx0"])
            return [(mub[:, 0, :], "lnx0"), (mub[:, 1, :], "lnx0"), (mub[:, 2, :], "lnx0")]

        def stage_out(dst_ap, dkey, func=None):
            def f(b, n_part=128, ncol=512):
                i = self.nxt("tmp", 3)
                if func is None:
                    self.cp("dve", t_ring[i][0:n_part, 0:ncol], self.bank[b][0:n_part, 0:ncol], [self.bk(b)], ["tmp%d" % i])
                else:
                    self.act(t_ring[i][0:n_part, 0:ncol], self.bank[b][0:n_part, 0:ncol], func, [self.bk(b)], ["tmp%d" % i])
                P.dma("sp", dst_ap, t_ring[i][0:n_part, 0:ncol], reads=["tmp%d" % i], writes=[dkey])
            return f

        for j in range(4):
            c0 = j * 256
            wl = self.wload(self.win[:, B0 + c0:B0 + c0 + 256], 256, variants=load_mu(c0))
            self.proj_T(wl, 256, lambda b, ct, tb, c0=c0: stage_out(self.rk_scr[c0 + ct * 128:c0 + (ct + 1) * 128, tb * 512:(tb + 1) * 512], "rk_scr")(b),
                        shifts=(0, -1, 1))
        for j in range(2):
            c0 = 1024 + j * 256
            wl = self.wload(self.win[:, B0 + c0:B0 + c0 + 256], 256, variants=load_mu(c0))
            self.proj_N(wl, 256, lambda b, tt, j=j: stage_out(self.v_scr[tt * 128:(tt + 1) * 128, j * 256:(j + 1) * 256], "v_scr")(b, 128, 256),
                        shifts=(0, -1, 1))
        wl = self.wload(self.win[:, B0 + 1536:B0 + 1792], 256, variants=load_mu(1536))
        self.proj_T(wl, 256, lambda b, ct, tb: stage_out(self.wa_scr[ct * 128:(ct + 1) * 128, tb * 512:(tb + 1) * 512], "wa_scr",
                                                         AF.Tanh if ct == 0 else AF.Copy)(b), shifts=(0, -1, 1))
        for j in range(2):
            wl = self.wload(self.win[:, B0 + 1792 + j * 256:B0 + 1792 + (j + 1) * 256], 256)
            self.proj_N(wl, 256, lambda b, tt, j=j: stage_out(self.sg_scr[tt * 128:(tt + 1) * 128, j * 256:(j + 1) * 256], "sg_scr", AF.Silu)(b, 128, 256))
        self.barrier()
        slots = []
        for bi in range(4):
            a = self.BIG[bi][:].bitcast(F32)
            for q in range(4):
                slots.append(a[:, q * 1024:(q + 1) * 1024])
        for q in range(4):
            slots.append(self.G[:, q * 1024:(q + 1) * 1024])
        for wi in range(1):
            a = self.wst[wi][:, :, :].rearrange("p c n -> p (c n)")
            for q in range(2):
                slots.append(a[:, q * 1024:(q + 1) * 1024])
        si = [0]

        def slot(full=True):
            if full:
                if si[0] % 2:
                    si[0] += 1
                a = slots[si[0] // 2]
                si[0] += 2
                return a
            a = slots[si[0] // 2][:, (si[0] % 2) * 512:(si[0] % 2) * 512 + 512]
            si[0] += 1
            return a

        def v3(ap, w):
            return ap[0:64, 0:8 * w].rearrange("p (h t) -> p h t", t=w)

        w_upS = slot()[0:64, :].rearrange("p (e c) -> p e c", c=512)
        a_upS = slot()[0:64, :].rearrange("p (e c) -> p e c", c=512)
        w0B = slot()[0:64, :].rearrange("p (e c) -> p e c", c=512)
        rkT = slot()[0:64, :].rearrange("p (g t) -> p g t", t=64)
        AR = slot()[0:64, :].rearrange("p (h t) -> p h t", t=128)
        NP = [self.RX[:, q * 1024:(q + 1) * 1024].rearrange("p (h t) -> p h t", t=128) for q in range(2)]
        ysb = slot()[0:64, 0:520]
        rmaskS = slot(False)[0:64, 0:384].rearrange("p (e n) -> p e n", n=192)
        waT = slot(False)[0:64, 0:256].rearrange("p (g t) -> p g t", t=64)
        vtok = slot(False)[0:64, :]
        sgw = slot(False)[0:64, :]
        asT, kkn, ke, be, tE0, tE1, bch, kch, z = [v3(slot(False), 64) for _ in range(9)]
        eLs, Bt, Kt = [slot(False)[0:64, :] for _ in range(3)]
        Mm = [self.RX[:, 2048 + q * 512:2048 + (q + 1) * 512].rearrange("p (h t) -> p h t", t=64) for q in range(2)]
        Mrb, Mak, Mrk, Xs, Us, tmpS = [v3(slot(False), 64) for _ in range(6)]
        Sst = [v3(slot(False), 64) for _ in range(2)]
        assert si[0] <= 2 * len(slots), si[0]
        rwp = self.gq[0:64, 0:40]
        omka = self.gq[0:64, 40:48]
        ident64 = self.ident[0:64, 0:64]
        ones64 = self.onesf[0:64, 0:64]
        self.r32 = True
        P.dma("sp", w_upS, I["rwkv_w_up"][l * 128:(l + 1) * 128, :].rearrange("(e r) c -> r e c", r=64), writes=["w_upS"])
        P.dma("sp", a_upS, I["rwkv_a_up"][l * 128:(l + 1) * 128, :].rearrange("(e r) c -> r e c", r=64), writes=["a_upS"])
        for e in range(2):
            P.dma("sp", w0B[:, e, :], I["rwkv_w0"][l * 2 + e:l * 2 + e + 1, :].partition_broadcast(64), writes=["w0B"])
        P.dma("sp", rmaskS, I["rmask"].rearrange("(e p) n -> p e n", p=64), writes=["rmaskS"])

        pm = self.tmp[0]
        P.dma("sp", pm[0:16, 0:64], I["rwkv_a0"][l * 2:(l + 1) * 2, :].rearrange("e (h c) -> (e h) c", c=64), writes=["tmp0"])
        P.dma("sp", pm[16:24, 0:64], I["rwkv_k_k"][l:l + 1, :].rearrange("e (h c) -> (e h) c", c=64), writes=["tmp0"])
        P.dma("sp", pm[24:32, 0:64], I["rwkv_k_a"][l:l + 1, :].rearrange("e (h c) -> (e h) c", c=64), writes=["tmp0"])
        P.dma("sp", pm[32:40, 0:64], I["rwkv_r_k"][l:l + 1, :].rearrange("e (h c) -> (e h) c", c=64), writes=["tmp0"])
        b = self.pbank()
        self.P.op("pe", lambda e_: e_.transpose(self.bank[b][0:64, 0:40], pm[0:40, 0:64], self.ident[0:40, 0:40]), reads=["tmp0", "ident"], writes=[self.bk(b)])
        self.cp("dve", rwp, self.bank[b][0:64, 0:40], [self.bk(b)], ["gq"])
        self.ts("dve", omka, rwp[:, 24:32], -1.0, 1.0, ALU.mult, ALU.add, ["gq"], ["gq"])
        bc3 = lambda ap: ap.unsqueeze(2).broadcast_to([64, 8, 64])
        hb = lambda b_, h, w=64: self.bank[b_][0:64, h * w:(h + 1) * w]
        b3 = lambda b_, w=64: self.bank[b_][0:64, 0:8 * w].rearrange("p (h t) -> p h t", t=w)

        for e in range(2):
            Scur = 0
            self.memset("dve", Sst[0], 0.0, ["S0"])
            order = range(32) if e == 0 else range(31, -1, -1)
            tl = 63 if e == 0 else 0
            mS, mI, mT = rmaskS[:, e, 0:64], rmaskS[:, e, 64:128], rmaskS[:, e, 128:192]
            for ch in order:
                t0 = ch * 64
                P.dma("sp", rkT, self.rk_scr.rearrange("(g p) t -> p g t", p=64)[:, :, t0:t0 + 64], reads=["rk_scr"], writes=["rkT"])
                P.dma("sp", waT, self.wa_scr.rearrange("(g p) t -> p g t", p=64)[:, :, t0:t0 + 64], reads=["wa_scr"], writes=["waT"])
                P.dma("sp", vtok, self.v_scr[t0:t0 + 64, :], reads=["v_scr"], writes=["vtok"])
                rT, kT_ = rkT[:, 0:8, :], rkT[:, 8:16, :]
                b = self.pbank()
                self.mm(self.bank[b][0:64, :], waT[:, e, :], w_upS[:, e, :], True, True, ["waT", "w_upS"], [self.bk(b)])
                self.tt("dve", sgw, self.bank[b][0:64, :], w0B[:, e, :], ALU.add, [self.bk(b), "w0B"], ["sgw"])
                self.act(sgw, sgw, AF.Sigmoid, ["sgw"], ["sgw"])
                b = self.pbank()
                for h in range(8):
                    self.mm(hb(b, h), a_upS[:, e, h * 64:(h + 1) * 64], waT[:, 2 + e, :], True, True, ["waT", "a_upS"], [self.bk(b)])
                self.tt("dve", asT, b3(b), bc3(rwp[:, e * 8:(e + 1) * 8]), ALU.add, [self.bk(b), "gq"], ["asT"])
                self.act(asT, asT, AF.Sigmoid, ["asT"], ["asT"])
                self.tt("dve", kkn, kT_, bc3(rwp[:, 16:24]), ALU.mult, ["rkT", "gq"], ["kkn"])
                self.act(tE0, kkn, AF.Square, ["kkn"], ["tE0"])
                b = self.pbank()
                self.mm(self.bank[b][0:64, :], ones64, tE0.rearrange("p h t -> p (h t)"), True, True, ["tE0", "onesf"], [self.bk(b)])
                self.act(tE0, b3(b), AF.Sqrt, [self.bk(b)], ["tE0"])
                self.ts("dve", tE0, tE0, 1e-12, None, ALU.max, None, ["tE0"], ["tE0"])
                self.P.op("dve", lambda e_: e_.reciprocal(out=tE0, in_=tE0), reads=["tE0"], writes=["tE0"])
                self.tt("dve", kkn, kkn, tE0, ALU.mult, ["kkn", "tE0"], ["kkn"])
                self.tt("pool", ke, asT, bc3(rwp[:, 24:32]), ALU.mult, ["asT", "gq"], ["ke"])
                self.tt("pool", ke, ke, bc3(omka), ALU.add, ["ke", "gq"], ["ke"])
                self.tt("pool", ke, ke, kT_, ALU.mult, ["ke", "rkT"], ["ke"])
                self.tt("pool", be, kkn, asT, ALU.mult, ["kkn", "asT"], ["be"])
                self.tt("pool", z, rT, ke, ALU.mult, ["rkT", "ke"], ["z"])
                bLi = self.pbank()
                for h in range(8):
                    self.mm(hb(bLi, h), sgw[:, h * 64:(h + 1) * 64], mI, True, True, ["sgw", "rmaskS"], [self.bk(bLi)])
                self.act(tE0, b3(bLi), AF.Exp, [self.bk(bLi)], ["tE0"], scale=-CW)
                self.act(tE1, b3(bLi), AF.Exp, [self.bk(bLi)], ["tE1"], scale=CW)
                self.tt("dve", AR[:, :, 64:128], rT, tE0, ALU.mult, ["rkT", "tE0"], ["AR"])
                self.cp("dve", self.sm[0:64, 32:40], tE0[:, :, tl], ["tE0"], ["sm"])
                self.tt("dve", bch, be, tE1, ALU.mult, ["be", "tE1"], ["bch"])
                self.tt("pool", kch, ke, tE1, ALU.mult, ["ke", "tE1"], ["kch"])
                bLe = self.pbank()
                for h in range(8):
                    self.mm(hb(bLe, h), sgw[:, h * 64:(h + 1) * 64], mS, True, True, ["sgw", "rmaskS"], [self.bk(bLe)])
                self.act(tE0, b3(bLe), AF.Exp, [self.bk(bLe)], ["tE0"], scale=-CW)
                self.stt("dve", AR[:, :, 0:64], kkn, -1.0, tE0, ALU.mult, ALU.mult, ["kkn", "tE0"], ["AR"])
                b = self.pbank()
                self.mm(self.bank[b][0:64, :], mT, sgw, True, True, ["sgw", "rmaskS"], [self.bk(b)])
                self.act(eLs, self.bank[b][0:64, :], AF.Exp, [self.bk(b)], ["eLs"], scale=-CW)
                for src, skey, dst, dkey in ((be, "be", Bt, "Bt"), (ke, "ke", Kt, "Kt")):
                    b = self.pbank()
                    for h in range(8):
                        self.P.op("pe", lambda e_, b=b, h=h, src=src: e_.transpose(hb(b, h), src[:, h, :], ident64), reads=[skey, "ident"], writes=[self.bk(b)])
                    self.tt("dve", dst, self.bank[b][0:64, :], eLs, ALU.mult, [self.bk(b), "eLs"], [dkey])
                b = self.pbank()
                for h in range(8):
                    self.mm(self.bank[b][0:64, h:h + 1], z[:, h, :], rwp[:, 32 + h:33 + h], True, True, ["z", "gq"], [self.bk(b)])
                self.cp("act", ysb[:, 512:520], self.bank[b][0:64, 0:8], [self.bk(b)], ["ysb"])
                for h in range(8):
                    self.mm(self.bank[h // 4][0:64, (h % 4) * 128:(h % 4 + 1) * 128], bch[:, h, :], AR[:, h, :], True, True, ["bch", "AR"], [self.bk(h // 4)])
                for h in range(8):
                    self.mm(self.bank[4 + h // 4][0:64, (h % 4) * 128:(h % 4 + 1) * 128], kch[:, h, :], AR[:, h, :], True, True, ["kch", "AR"], [self.bk(4 + h // 4)])
                for h in range(8):
                    self.mm(hb(6, h), AR[:, h, 0:64], bch[:, h, :], True, True, ["bch", "AR"], [self.bk(6)])
                m4 = lambda m_: m_.unsqueeze(1).broadcast_to([64, 4, 64])
                for g in range(2):
                    bb = self.bank[g][0:64, :].rearrange("p (h t) -> p h t", t=128)
                    kb = self.bank[4 + g][0:64, :].rearrange("p (h t) -> p h t", t=128)
                    self.tt("dve", NP[0][:, 4 * g:4 * g + 4, 0:64], bb[:, :, 0:64], m4(mS), ALU.mult, [self.bk(g), "rmaskS"], ["NP0"])
                    self.tt("dve", Mrb[:, 4 * g:4 * g + 4, :], bb[:, :, 64:128], m4(mI), ALU.mult, [self.bk(g), "rmaskS"], ["Mrb"])
                    self.tt("dve", Mak[:, 4 * g:4 * g + 4, :], kb[:, :, 0:64], m4(mS), ALU.mult, [self.bk(4 + g), "rmaskS"], ["Mak"])
                    self.tt("dve", Mrk[:, 4 * g:4 * g + 4, :], kb[:, :, 64:128], m4(mI), ALU.mult, [self.bk(4 + g), "rmaskS"], ["Mrk"])
                self.tt("dve", Mm[0], b3(6), mT.unsqueeze(1).broadcast_to([64, 8, 64]), ALU.mult, [self.bk(6), "rmaskS"], ["Mm0"])
                self.tt("pool", NP[0][:, :, 64:128], NP[0][:, :, 0:64], ident64.unsqueeze(1).broadcast_to([64, 8, 64]), ALU.add, ["NP0", "ident"], ["NP0"])
                cur = 0
                for step in range(6):
                    nx = 1 - cur
                    pbk = (0, 1) if step % 2 == 0 else (4, 5)
                    mbk = 6 if step % 2 else 7
                    ncur, nnx, mcur, mnx = "NP%d" % cur, "NP%d" % nx, "Mm%d" % cur, "Mm%d" % nx
                    if step == 0:
                        for h in range(8):
                            self.mm(self.bank[pbk[h // 4]][0:64, (h % 4) * 128:(h % 4) * 128 + 64], Mm[cur][:, h, :], NP[cur][:, h, 0:64], True, True,
                                    [mcur, ncur], [self.bk(pbk[h // 4])])
                    elif step < 5:
                        for h in range(8):
                            self.mm(self.bank[pbk[h // 4]][0:64, (h % 4) * 128:(h % 4 + 1) * 128], Mm[cur][:, h, :], NP[cur][:, h, :], True, True,
                                    [mcur, ncur], [self.bk(pbk[h // 4])])
                    else:
                        for h in range(8):
                            self.mm(self.bank[pbk[h // 4]][0:64, (h % 4) * 128 + 64:(h % 4 + 1) * 128], Mm[cur][:, h, :], NP[cur][:, h, 64:128], True, True,
                                    [mcur, ncur], [self.bk(pbk[h // 4])])
                    if step < 5:
                        for h in range(8):
                            self.mm(hb(mbk, h), NP[cur][:, h, 0:64], Mm[cur][:, h, :], True, True, [mcur, ncur], [self.bk(mbk)])
                    for g in range(2):
                        pv = self.bank[pbk[g]][0:64, :].rearrange("p (h t) -> p h t", t=128)
                        if step < 5:
                            self.cp("act", NP[nx][:, 4 * g:4 * g + 4, 0:64], pv[:, :, 0:64], [self.bk(pbk[g])], [nnx])
                        if step == 0:
                            self.cp("dve", NP[nx][:, 4 * g:4 * g + 4, 64:128], NP[cur][:, 4 * g:4 * g + 4, 64:128], [ncur], [nnx])
                        else:
                            self.tt("dve", NP[nx][:, 4 * g:4 * g + 4, 64:128], pv[:, :, 64:128], NP[cur][:, 4 * g:4 * g + 4, 64:128], ALU.add,
                                    [self.bk(pbk[g]), ncur], [nnx])
                    if step < 5:
                        self.cp("act", Mm[nx], b3(mbk), [self.bk(mbk)], [mnx])
                    cur = nx
                TT, tkey = NP[cur], "NP%d" % cur
                S0, skey = Sst[Scur], "S%d" % Scur
                S1, s1key = Sst[1 - Scur], "S%d" % (1 - Scur)
                bX = self.pbank()
                for h in range(8):
                    self.mm(hb(bX, h), AR[:, h, 0:64], S0[:, h, :], True, False, ["AR", skey], [self.bk(bX)])
                    self.mm(hb(bX, h), Mak[:, h, :], vtok[:, h * 64:(h + 1) * 64], False, True, ["Mak", "vtok"], [self.bk(bX)])
                self.cp("dve", Xs, b3(bX), [self.bk(bX)], ["Xs"])
                bU = self.pbank()
                for h in range(8):
                    self.mm(hb(bU, h), TT[:, h, 64:128], Xs[:, h, :], True, True, [tkey, "Xs"], [self.bk(bU)])
                self.cp("act", Us, b3(bU), [self.bk(bU)], ["Us"])
                bY = self.pbank()
                for h in range(8):
                    self.mm(hb(bY, h), AR[:, h, 64:128], S0[:, h, :], True, False, ["AR", skey], [self.bk(bY)])
                    self.mm(hb(bY, h), Mrb[:, h, :], Us[:, h, :], False, False, ["Mrb", "Us"], [self.bk(bY)])
                    self.mm(hb(bY, h), Mrk[:, h, :], vtok[:, h * 64:(h + 1) * 64], False, True, ["Mrk", "vtok"], [self.bk(bY)])
                self.cp("act", ysb[:, 0:512], self.bank[bY][0:64, :], [self.bk(bY)], ["ysb"])
                P.dma("sp", self.y_scr[e * S + t0:e * S + t0 + 64, :], ysb, reads=["ysb"], writes=["y_scr"])
                bS = self.pbank()
                for h in range(8):
                    self.mm(hb(bS, h), Bt[:, h * 64:(h + 1) * 64], Us[:, h, :], True, False, ["Bt", "Us"], [self.bk(bS)])
                    self.mm(hb(bS, h), Kt[:, h * 64:(h + 1) * 64], vtok[:, h * 64:(h + 1) * 64], False, True, ["Kt", "vtok"], [self.bk(bS)])
                self.tt("pool", tmpS, S0, bc3(self.sm[0:64, 32:40]), ALU.mult, [skey, "sm"], ["tmpS"])
                self.tt("dve", S1, tmpS, b3(bS), ALU.add, ["tmpS", self.bk(bS)], [s1key])
                Scur = 1 - Scur
        self.r32 = False
        self.barrier()
        P.dma("sp", self.lng[:, 0:512], I["rwkv_ln_g"][l:l + 1, :].partition_broadcast(128), writes=["lng"])
        P.dma("sp", self.lnb[:, 0:512], I["rwkv_ln_b"][l:l + 1, :].partition_broadcast(128), writes=["lnb"])
        yf, yb, vt = self.lnx[0], self.lnx[1], self.junk
        sm = self.sm
        t0_, t1_, t2_ = self.tmp
        for tt in range(NT):
            P.dma("sp", yf[:, 0:520], self.y_scr[tt * 128:(tt + 1) * 128, :], reads=["y_scr"], writes=["lnx0"])
            P.dma("sp", yb[:, 0:520], self.y_scr[S + tt * 128:S + (tt + 1) * 128, :], reads=["y_scr"], writes=["lnx1"])
            P.dma("sp", vt[:, 0:512], self.v_scr[tt * 128:(tt + 1) * 128, :], reads=["v_scr"], writes=["junk"])
            P.dma("sp", vt[:, 512:1024], self.sg_scr[tt * 128:(tt + 1) * 128, :], reads=["sg_scr"], writes=["junk"])
            self.tt("dve", yf[:, 0:520], yf[:, 0:520], yb[:, 0:520], ALU.add, ["lnx0", "lnx1"], ["lnx0"])
            y3 = yf[:, 0:512].rearrange("p (h d) -> p h d", d=64)
            P.op("dve", lambda e_, y3=y3: e_.reduce_sum(out=sm[:, 40:48], in_=y3, axis=AX.X), reads=["lnx0"], writes=["sm"])
            self.ts("dve", sm[:, 40:48], sm[:, 40:48], -1.0 / 64, None, ALU.mult, None, ["sm"], ["sm"])
            self.tt("dve", y3, y3, sm[:, 40:48].unsqueeze(2).broadcast_to([128, 8, 64]), ALU.add, ["lnx0", "sm"], ["lnx0"])
            self.act(t0_[:, 0:512], yf[:, 0:512], AF.Square, ["lnx0"], ["tmp0"])
            P.op("dve", lambda e_: e_.reduce_sum(out=sm[:, 48:56], in_=t0_[:, 0:512].rearrange("p (h d) -> p h d", d=64), axis=AX.X), reads=["tmp0"], writes=["sm"])
            self.rsqrt_cols(sm[:, 48:56], sm[:, 56:64], 1.0 / 64, 64e-5)
            self.tt("dve", y3, y3, sm[:, 56:64].unsqueeze(2).broadcast_to([128, 8, 64]), ALU.mult, ["lnx0", "sm"], ["lnx0"])
            self.tt("dve", yf[:, 0:512], yf[:, 0:512], self.lng[:, 0:512], ALU.mult, ["lnx0", "lng"], ["lnx0"])
            self.tt("pool", yf[:, 0:512], yf[:, 0:512], self.lnb[:, 0:512], ALU.add, ["lnx0", "lnb"], ["lnx0"])
            self.tt("pool", t1_[:, 0:512].rearrange("p (h d) -> p h d", d=64), vt[:, 0:512].rearrange("p (h d) -> p h d", d=64),
                    yf[:, 512:520].unsqueeze(2).broadcast_to([128, 8, 64]), ALU.mult, ["junk", "lnx0"], ["tmp1"])
            self.tt("dve", t1_[:, 0:512], t1_[:, 0:512], yf[:, 0:512], ALU.add, ["tmp1", "lnx0"], ["tmp1"])
            self.tt("dve", t2_[:, 0:512], t1_[:, 0:512], vt[:, 512:1024], ALU.mult, ["tmp1", "junk"], ["tmp2"])
            P.dma("sp", self.o_scr[tt * 128:(tt + 1) * 128, OB:OB + 512], t2_[:, 0:512], reads=["tmp2"], writes=["o_scr"])

    def merge(self, l, last):
        P, I = self.P, self.I
        wg_all = I["w_gate"][l * D:(l + 1) * D, :]
        wb_all = I["w_branch"][l * 2304:(l + 1) * 2304, :]
        wo_all = I["w_out"][l * D:(l + 1) * D, :]
        P.dma("sp", self.lng[:], I["ln_g"][l:l + 1, :].partition_broadcast(128), writes=["lng"])
        P.dma("sp", self.lnb[:], I["ln_b"][l:l + 1, :].partition_broadcast(128), writes=["lnb"])
        oT = self.BIG[0][:, 0:9216].rearrange("p (j t) -> p j t", t=512)
        ygrp = self.BIG[1][:].bitcast(F32)[:, 0:4096].rearrange("p (q c) -> p q c", c=1024)
        otile = self.BIG[2][:].bitcast(F32)[:, 0:2304]
        yT = self.BIG[3][:, 0:4096].rearrange("p (c t) -> p c t", t=512)
        hgrp = self.G[:, 0:4096].rearrange("p (q c) -> p q c", c=1024)
        hin = self.hres[l % 2]
        hout = self.out if last else self.hres[(l + 1) % 2]
        t0, t1 = self.tmp[0], self.tmp[1]
        for grp in range(4):
            for tq in range(4):
                tt = grp * 4 + tq
                P.dma("sp", otile, self.o_scr[tt * 128:(tt + 1) * 128, :], reads=["o_scr"], writes=["BIG2"])
                P.dma("sp", hgrp[:, tq, :], hin[tt * 128:(tt + 1) * 128, :], reads=["hres%d" % (l % 2)], writes=["G"])
                for j4 in range(5):
                    nj = min(4, 18 - j4 * 4)
                    b = self.pbank()
                    for u in range(nj):
                        j = j4 * 4 + u
                        self.tr(self.bank[b][:, u * 128:(u + 1) * 128], otile[:, j * 128:(j + 1) * 128], ["BIG2"], [self.bk(b)])
                    self.cp("act" if j4 % 2 else "dve", oT[:, j4 * 4:j4 * 4 + nj, tq * 128:(tq + 1) * 128],
                            self.bank[b][:, 0:nj * 128].rearrange("p (c t) -> p c t", t=128), [self.bk(b)], ["BIG0"])
            for i, (r0, rw) in enumerate(BROWS):
                kci = rw // 128
                for cc in range(4):
                    wg, wgk = self.wload(wg_all[:, i * 1024 + cc * 256:i * 1024 + (cc + 1) * 256], 256)[0]
                    wb, wbk = self.wload(wb_all[r0:r0 + rw, cc * 256:(cc + 1) * 256], 256, kc=kci)[0]
                    bi = self.nxt("brow", 2)
                    P.dma("sp", self.brow[bi][0:1, :], I["b_gate"][l:l + 1, i * 1024 + cc * 256:i * 1024 + (cc + 1) * 256], writes=["brow%d" % bi])
                    for tq in range(4):
                        tt = grp * 4 + tq
                        b1 = self.pbank()
                        self.mm(self.bank[b1][:, 0:256], self.onesf[0:1, 0:128], self.brow[bi][0:1, :], True, False,
                                ["onesf", "brow%d" % bi], [self.bk(b1)])
                        for c in range(8):
                            self.mm(self.bank[b1][:, 0:256], self.hT[:, c, 1 + tt * 128:1 + (tt + 1) * 128], wg[:, c, :], False, c == 7,
                                    [wgk, "hT"], [self.bk(b1)])
                        self.act(t0[:, 0:256], self.bank[b1][:, 0:256], AF.Sigmoid, [self.bk(b1)], ["tmp0"])
                        b2 = self.pbank()
                        for c in range(kci):
                            self.mm(self.bank[b2][:, 0:256], oT[:, r0 // 128 + c, tq * 128:(tq + 1) * 128], wb[:, c, :], c == 0, c == kci - 1,
                                    [wbk, "BIG0"], [self.bk(b2)])
                        ysl = ygrp[:, tq, cc * 256:(cc + 1) * 256]
                        if i == 0:
                            self.tt("dve", ysl, self.bank[b2][:, 0:256], t0[:, 0:256], ALU.mult, [self.bk(b2), "tmp0"], ["BIG1"])
                        else:
                            self.tt("dve", t1[:, 0:256], self.bank[b2][:, 0:256], t0[:, 0:256], ALU.mult, [self.bk(b2), "tmp0"], ["tmp1"])
                            self.tt("pool", ysl, ysl, t1[:, 0:256], ALU.add, ["BIG1", "tmp1"], ["BIG1"])
            for tq in range(4):
                for half in range(2):
                    b = self.pbank()
                    for c4 in range(4):
                        c = half * 4 + c4
                        self.tr(self.bank[b][:, c4 * 128:(c4 + 1) * 128], ygrp[:, tq, c * 128:(c + 1) * 128], ["BIG1"], [self.bk(b)])
                    self.cp("act" if half else "dve", yT[:, half * 4:half * 4 + 4, tq * 128:(tq + 1) * 128],
                            self.bank[b][:, :].rearrange("p (c t) -> p c t", t=128), [self.bk(b)], ["BIG3"])
            for cc in range(4):
                wo, wok = self.wload(wo_all[:, cc * 256:(cc + 1) * 256], 256)[0]
                for tq in range(4):
                    b = self.pbank()
                    for c in range(8):
                        self.mm(self.bank[b][:, 0:256], yT[:, c, tq * 128:(tq + 1) * 128], wo[:, c, :], c == 0, c == 7, [wok, "BIG3"], [self.bk(b)])
                    hs = hgrp[:, tq, cc * 256:(cc + 1) * 256]
                    self.stt("dve", hs, hs, ALPHA, self.bank[b][:, 0:256], ALU.mult, ALU.add, ["G", self.bk(b)], ["G"])
            for tq in range(4):
                tt = grp * 4 + tq
                self.ln_inplace(hgrp[:, tq, :], "G")
                P.dma("sp", hout[tt * 128:(tt + 1) * 128, :], hgrp[:, tq, :], reads=["G"],
                      writes=["out" if last else "hres%d" % ((l + 1) % 2)], is_output=last)


def make_in_map(inputs, b, consts):
    m = {"x": np.ascontiguousarray(inputs["x"][b]), "mem": np.ascontiguousarray(inputs["mem"][b])}
    for nm, shp in IN_SPECS:
        if nm in consts:
            m[nm] = consts[nm]
        else:
            m[nm] = np.ascontiguousarray(np.asarray(inputs[nm], dtype=np.float32).reshape(shp))
    return m


def kernel(**inputs):
    consts = host_consts()
    kb = KB(debug=False)
    nb = inputs["x"].shape[0]
    in_maps = [make_in_map(inputs, b, consts) for b in range(nb)]
    res = run_bass_kernel_spmd(kb.nc, in_maps, core_ids=list(range(nb)))
    out = np.stack([np.asarray(r["out"], dtype=np.float32).reshape(S, D) for r in res.results], axis=0)
    return out
```

```python
import math
from concourse.ap import AP
import contextlib
import numpy as np
import concourse.bass as bass
import concourse.mybir as mybir
from concourse.bass_utils import run_bass_kernel_spmd

F32 = mybir.dt.float32
BF16 = mybir.dt.bfloat16
I32 = mybir.dt.int32
AF = mybir.ActivationFunctionType
ALU = mybir.AluOpType
AX = mybir.AxisListType

ENGS = ("pe", "act", "dve", "pool", "sp")
DMA_SEMS = 8


class Op:
    __slots__ = ("eng", "fn", "waits", "is_dma", "idx", "marked", "dma_slot", "dma_val", "prewait")

    def __init__(self, eng, fn, is_dma):
        self.eng = eng
        self.fn = fn
        self.is_dma = is_dma
        self.waits = []
        self.marked = False
        self.idx = None
        self.dma_slot = None
        self.dma_val = None
        self.prewait = None


class Prog:
    def __init__(self, nc, same_engine_sync=True):
        self.nc = nc
        self.ops = {e: [] for e in ENGS}
        self.last_write = {}
        self.readers = {}
        self.same_engine_sync = same_engine_sync
        self.dma_count = {e: 0 for e in ENGS}
        self.dma_hist = {e: [] for e in ENGS}
        self.all_dma_out = []
        self.stack = contextlib.ExitStack()
        self.n_ops = 0

    def sb(self, name, shape, dt):
        return self.stack.enter_context(self.nc.sbuf_tensor("s_" + name, list(shape), dt))

    def ps(self, name, shape, dt):
        return self.stack.enter_context(self.nc.psum_tensor("p_" + name, list(shape), dt))

    def _deps(self, op, reads, writes):
        deps = []
        for k in reads:
            w = self.last_write.get(k)
            if w is not None:
                deps.append(w)
        for k in writes:
            w = self.last_write.get(k)
            if w is not None:
                deps.append(w)
            for r in self.readers.get(k, ()):
                deps.append(r)
        best = {}
        for d in deps:
            if d is op:
                continue
            key = (d.eng, d.is_dma, d.dma_slot if d.is_dma else None)
            cur = best.get(key)
            if cur is None or d.idx > cur.idx:
                best[key] = d
        for d in best.values():
            if (not d.is_dma) and d.eng == op.eng and not op.is_dma:
                if op.eng == "pe" or not self.same_engine_sync:
                    continue
            op.waits.append(d)
            d.marked = True
        for k in reads:
            self.readers.setdefault(k, []).append(op)
        for k in writes:
            self.last_write[k] = op
            self.readers[k] = []

    def barrier(self, fn):
        o = Op("pool", fn, False)
        o.idx = len(self.ops["pool"])
        self.ops["pool"].append(o)
        self._deps(o, [], ["__phase__"])
        return o

    def op(self, eng, fn, reads=(), writes=()):
        reads = list(reads) + ["__phase__"]
        o = Op(eng, fn, False)
        o.idx = len(self.ops[eng])
        self.ops[eng].append(o)
        self._deps(o, reads, writes)
        self.n_ops += 1
        return o

    def dma(self, eng, out, in_, reads=(), writes=(), is_output=False, **kw):
        def fn(e, out=out, in_=in_, kw=kw):
            return e.dma_start(out=out, in_=in_, **kw)
        reads = list(reads) + ["__phase__"]
        o = Op(eng, fn, True)
        o.idx = len(self.ops[eng])
        n = self.dma_count[eng]
        self.dma_count[eng] += 1
        o.dma_slot = n % DMA_SEMS
        o.dma_val = 16 * (n // DMA_SEMS + 1)
        if n >= DMA_SEMS:
            o.prewait = self.dma_hist[eng][n - DMA_SEMS]
        self.dma_hist[eng].append(o)
        self.ops[eng].append(o)
        self._deps(o, reads, writes)
        if is_output:
            self.all_dma_out.append(o)
        self.n_ops += 1
        return o

    def emit(self):
        nc = self.nc
        st = self.stack
        fin = Op("sp", None, False)
        fin.idx = len(self.ops["sp"])
        for o in self.all_dma_out:
            fin.waits.append(o)
        self.ops["sp"].append(fin)
        csem = {e: st.enter_context(nc.semaphore("c_" + e)) for e in ENGS}
        dsem = {e: [st.enter_context(nc.semaphore("d_%s_%d" % (e, i))) for i in range(DMA_SEMS)]
                for e in ENGS if self.dma_count[e] > 0}
        for e in ENGS:
            c = 0
            for o in self.ops[e]:
                if o.is_dma:
                    continue
                if o.marked:
                    c += 1
                    o.dma_val = c
        block = st.enter_context(nc.Block())
        prog = self

        def run(e, eng):
            seen = {}
            for o in prog.ops[e]:
                ws = list(o.waits)
                if o.prewait is not None:
                    ws.append(o.prewait)
                for d in ws:
                    if d.is_dma:
                        sem, val = dsem[d.eng][d.dma_slot], d.dma_val
                    else:
                        sem, val = csem[d.eng], d.dma_val
                    k = id(sem)
                    if seen.get(k, 0) >= val:
                        continue
                    seen[k] = val
                    eng.wait_ge(sem, val)
                if o.fn is None:
                    continue
                ins = o.fn(eng)
                if o.is_dma:
                    ins.then_inc(dsem[e][o.dma_slot], 16)
                elif o.marked:
                    ins.then_inc(csem[e], 1)

        @block.tensor
        def _(eng):
            run("pe", eng)

        @block.scalar
        def _(eng):
            run("act", eng)

        @block.vector
        def _(eng):
            run("dve", eng)

        @block.gpsimd
        def _(eng):
            run("pool", eng)

        @block.sync
        def _(eng):
            run("sp", eng)

    def close(self):
        self.stack.close()


F32R = mybir.dt.float32r

S = 2048
D = 1024
NT = 16
DEPTH = 2
WC = 256
XC = 2047
GW = 4096
ALPHA = (2 * DEPTH) ** 0.25
A0, B0, C0, D0, M0 = 0, 2048, 4352, 5632, 7680
OA, OB, OC, OD, OM = 0, 512, 1024, 1536, 2048
BROWS = [(0, 512), (512, 512), (1024, 512), (1536, 512), (2048, 256)]


def rel_bucket_np(rel):
    nb = 16
    max_exact = 8
    n = np.abs(rel)
    nf = np.maximum(n, 1).astype(np.float32)
    large = max_exact + (np.log(nf / max_exact) / np.float32(math.log(1024 / max_exact)) * (nb - max_exact)).astype(np.int32)
    large = np.minimum(large, nb - 1)
    return np.where(rel > 0, nb, 0) + np.where(n < max_exact, n, large)


def host_consts():
    c = {}
    c["ident"] = np.eye(128, dtype=np.float32)
    rel = np.arange(4096) - XC
    bkt = rel_bucket_np(rel)
    oh = np.zeros((32, 4096), np.float32)
    oh[bkt, np.arange(4096)] = 1.0
    c["onehot"] = oh
    n = np.abs(rel)
    mA = (n <= 64).astype(np.float32) + ((rel % 4 == 0) & (n <= 256)) + ((rel % 16 == 0) & (n <= 1024))
    mt = np.ones((12, 4096), np.float32)
    mt[:8] = mA[None, :]
    c["multab"] = mt
    t = np.arange(S)
    row = (t // 64).astype(np.float32)
    col = (t % 64).astype(np.float32)
    freqs = (10000.0 ** (-(np.arange(16, dtype=np.float32) / 16))).astype(np.float32)
    ar = row[:, None] * freqs[None, :]
    ac = col[:, None] * freqs[None, :]
    c["ropec"] = np.concatenate([np.cos(ar), np.cos(ar), np.cos(ac), np.cos(ac)], 1).astype(np.float32)
    c["ropes"] = np.concatenate([-np.sin(ar), np.sin(ar), -np.sin(ac), np.sin(ac)], 1).astype(np.float32)
    tri = np.zeros((2, 3, 128, 128), np.float32)
    sg = np.arange(128)[:, None]
    tt = np.arange(128)[None, :]
    same = (sg // 64) == (tt // 64)
    tri[0, 0] = same & (sg <= tt)
    tri[0, 1] = same & (sg < tt)
    tri[0, 2] = same & (sg > tt)
    tri[1, 0] = same & (sg >= tt)
    tri[1, 1] = same & (sg > tt)
    tri[1, 2] = same & (sg < tt)
    c["tri"] = tri.reshape(6 * 128, 128)
    mk_ = np.zeros((2, 64, 192), np.float32)
    a = np.arange(64)[:, None]
    b = np.arange(64)[None, :]
    mk_[0, :, 0:64] = a < b
    mk_[0, :, 64:128] = a <= b
    mk_[0, :, 128:192] = b < a
    mk_[1, :, 0:64] = a > b
    mk_[1, :, 64:128] = a >= b
    mk_[1, :, 128:192] = b > a
    c["rmask"] = mk_.reshape(128, 192)
    return c


IN_SPECS = [("ln_in_g", [1, D]), ("ln_in_b", [1, D]), ("rel_bias", [32, 12]), ("w_in", [DEPTH * D, 8192]),
            ("shift_mu", [DEPTH * 2, 1792]), ("rwkv_w0", [DEPTH * 2, 512]), ("rwkv_w_up", [DEPTH * 2 * 64, 512]),
            ("rwkv_a0", [DEPTH * 2, 512]), ("rwkv_a_up", [DEPTH * 2 * 64, 512]), ("rwkv_k_k", [DEPTH, 512]),
            ("rwkv_k_a", [DEPTH, 512]), ("rwkv_r_k", [DEPTH, 512]), ("rwkv_ln_g", [DEPTH, 512]),
            ("rwkv_ln_b", [DEPTH, 512]), ("c_qnorm_g", [DEPTH, 64]), ("c_knorm_g", [DEPTH, 64]),
            ("d_lambda", [DEPTH, 256]), ("d_subln_g", [DEPTH, 128]), ("w_mem_kv", [DEPTH * D, 512]),
            ("w_branch", [DEPTH * 2304, D]), ("w_gate", [DEPTH * D, 5120]), ("b_gate", [DEPTH, 5120]),
            ("w_out", [DEPTH * D, D]), ("ln_g", [DEPTH, D]), ("ln_b", [DEPTH, D]),
            ("ident", [128, 128]), ("onehot", [32, 4096]), ("multab", [12, 4096]), ("ropec", [S, 64]),
            ("ropes", [S, 64]), ("tri", [768, 128]), ("rmask", [128, 192])]


class KB:
    def __init__(self, debug=False, mixers="MCADB", layers=DEPTH):
        self.debug = debug
        self.mixers = mixers
        nc = bass.Bass("TRN2", target_bir_lowering=False)
        self.nc = nc
        P = Prog(nc)
        self.P = P
        I = {}
        I["x"] = nc.dram_tensor("x", [S, D], F32, kind="ExternalInput").ap()
        I["mem"] = nc.dram_tensor("mem", [256, D], F32, kind="ExternalInput").ap()
        for nm, shp in IN_SPECS:
            I[nm] = nc.dram_tensor(nm, list(shp), F32, kind="ExternalInput").ap()
        self.I = I
        self.out = nc.dram_tensor("out", [S, D], F32, kind="ExternalOutput").ap()
        self.hres = [nc.dram_tensor("hres%d" % i, [S, D], F32, kind="ExternalOutput" if debug else "Internal").ap() for i in range(2)]
        self.dbg_outs = []
        self.o_scr = nc.dram_tensor("o_scr", [S, 2304], F32, kind="ExternalOutput" if debug else "Internal").ap()
        self.xtab = nc.dram_tensor("xtab", [12, 4096], F32).ap()
        self.rk_scr = nc.dram_tensor("rk_scr", [1024, S], F32).ap()
        self.wa_scr = nc.dram_tensor("wa_scr", [256, S], F32).ap()
        self.v_scr = nc.dram_tensor("v_scr", [S, 512], F32).ap()
        self.y_scr = nc.dram_tensor("y_scr", [2 * S, 520], F32, kind="ExternalOutput" if debug else "Internal").ap()
        self.sg_scr = nc.dram_tensor("sg_scr", [S, 512], F32).ap()
        self.ident = P.sb("ident", [128, 128], F32)
        self.hT = P.sb("hT", [128, 8, S + 2], BF16)
        self.BIG = [P.sb("BIG%d" % i, [128, 9216], BF16) for i in range(4)]
        self.G = P.sb("G", [128, GW], F32)
        self.wst = [P.sb("wst%d" % i, [128, 8, WC], F32) for i in range(1)]
        self.RX = P.sb("RX", [64, 3072], F32)
        self.wbf = [P.sb("wbf%d" % i, [128, 8, WC], BF16) for i in range(4)]
        self.ropec = P.sb("ropec", [128, NT, 64], F32)
        self.ropes = P.sb("ropes", [128, NT, 64], F32)
        self.lnx = [P.sb("lnx%d" % i, [128, D], F32) for i in range(2)]
        self.junk = P.sb("junk", [128, D], F32)
        self.lng = P.sb("lng", [128, D], F32)
        self.lnb = P.sb("lnb", [128, D], F32)
        self.pt = [P.sb("pt%d" % i, [128, 512], BF16) for i in range(4)]
        self.pe_ = [P.sb("pe%d" % i, [128, 512], BF16) for i in range(2)]
        self.ost = [P.sb("ost%d" % i, [128, 128], F32) for i in range(4)]
        self.sm = P.sb("sm", [128, 64], F32)
        self.tmp = [P.sb("tmp%d" % i, [128, 512], F32) for i in range(3)]
        self.onesf = P.sb("onesf", [128, 128], F32)
        self.brow = [P.sb("brow%d" % i, [1, WC], F32) for i in range(2)]
        self.gq = P.sb("gq", [128, 128], F32)
        self.subg = P.sb("subg", [128, 128], F32)
        self.lamt = P.sb("lamt", [128, 264], F32)
        self.pbar = P.sb("pbar", [1, 8], F32)
        self.memT = P.sb("memT", [128, 8, 256], BF16)
        self.bank = [P.ps("bank%d" % i, [128, 512], F32) for i in range(8)]
        self.cnt = {}
        self.pbi = 0
        B0_, B1_, B2_, B3_ = [b[:] for b in self.BIG]
        self.qT = B0_[:, 0:8192].rearrange("p (c t) -> p c t", t=S)
        self.kT = B1_[:, 0:8192].rearrange("p (c t) -> p c t", t=S)
        self.sg = B3_[:, 0:8192].rearrange("p (t c) -> p t c", c=512)
        self.prelude()
        for l in range(layers):
            self.layer(l, last=(l == layers - 1))
        P.emit()
        P.close()

    def nxt(self, name, n):
        v = self.cnt.get(name, 0)
        self.cnt[name] = (v + 1) % n
        return v

    def bk(self, i):
        return "bank%d" % i

    def pbank(self):
        self.pbi ^= 1
        return 2 + self.pbi

    def barrier(self):
        pbar = self.pbar
        self.P.barrier(lambda e: e.memset(pbar[:], 0.0))

    def R(self, ap):
        if ap.dtype == F32 and ap.name == "s_RX":
            return ap.bitcast(F32R)
        return ap

    def mm(self, out, lhsT, rhs, start, stop, reads, writes):
        if lhsT.name == "s_RX" and rhs.name == "s_RX":
            lhsT, rhs = self.R(lhsT), self.R(rhs)
        self.P.op("pe", lambda e: e.matmul(out, lhsT=lhsT, rhs=rhs, start=start, stop=stop), reads=reads, writes=writes)

    def tr(self, out, in_, reads, writes, np_=128):
        ident = self.ident
        self.P.op("pe", lambda e: e.transpose(out, in_, ident[0:np_, 0:np_]), reads=list(reads) + ["ident"], writes=writes)

    def cp(self, eng, out, in_, reads, writes):
        out = self.R(out)
        if eng == "act":
            self.P.op("act", lambda e: e.copy(out=out, in_=in_), reads=reads, writes=writes)
        else:
            self.P.op(eng, lambda e: e.tensor_copy(out=out, in_=in_), reads=reads, writes=writes)

    def act(self, out, in_, func, reads, writes, **kw):
        out = self.R(out)
        self.P.op("act", lambda e: e.activation(out=out, in_=in_, func=func, **kw), reads=reads, writes=writes)

    def tt(self, eng, out, in0, in1, op, reads, writes):
        out = self.R(out)
        self.P.op(eng, lambda e: e.tensor_tensor(out=out, in0=in0, in1=in1, op=op), reads=reads, writes=writes)

    def ts(self, eng, out, in0, s1, s2, op0, op1, reads, writes):
        out = self.R(out)
        if s2 is None:
            self.P.op(eng, lambda e: e.tensor_scalar(out=out, in0=in0, scalar1=s1, scalar2=None, op0=op0), reads=reads, writes=writes)
        else:
            self.P.op(eng, lambda e: e.tensor_scalar(out=out, in0=in0, scalar1=s1, scalar2=s2, op0=op0, op1=op1), reads=reads, writes=writes)

    def stt(self, eng, out, in0, scalar, in1, op0, op1, reads, writes):
        out = self.R(out)
        self.P.op(eng, lambda e: e.scalar_tensor_tensor(out=out, in0=in0, scalar=scalar, in1=in1, op0=op0, op1=op1), reads=reads, writes=writes)

    def memset(self, eng, ap, val, writes):
        ap = self.R(ap)
        self.P.op(eng, lambda e: e.memset(ap, val), writes=writes)

    def rsqrt_cols(self, src, dst, scale, eps, key="sm"):
        self.ts("dve", dst, src, scale, eps, ALU.mult, ALU.add, [key], [key])
        self.P.op("act", lambda e: e.sqrt(out=dst, in_=dst), reads=[key], writes=[key])
        self.P.op("dve", lambda e: e.reciprocal(out=dst, in_=dst), reads=[key], writes=[key])

    def wload(self, src2d, n, kc=8, variants=None):
        P = self.P
        src = src2d.rearrange("(c p) n -> p c n", p=128)
        if variants is None:
            j = self.nxt("wb", 4)
            P.dma("pool", self.wbf[j][:, 0:kc, 0:n], src, writes=["wbf%d" % j])
            return [(self.wbf[j], "wbf%d" % j)]
        wst = self.wst[0]
        P.dma("sp", wst[:, 0:kc, 0:n], src, writes=["wst0"])
        res = []
        for vi, (vap, vkey) in enumerate(variants):
            j = self.nxt("wb", 4)
            self.tt("dve" if vi != 1 else "pool", self.wbf[j][:, 0:kc, 0:n], wst[:, 0:kc, 0:n], vap.unsqueeze(1).broadcast_to([128, kc, n]), ALU.mult,
                    ["wst0", vkey], ["wbf%d" % j])
            res.append((self.wbf[j], "wbf%d" % j))
        return res

    def proj_T(self, wl, n, consume, shifts=(0,), rhs_fn=None, rkey="hT", ntb=4, tbw=512):
        hT = self.hT
        for ct in range(n // 128):
            for tb in range(ntb):
                b = self.pbank()
                nmm = 8 * len(shifts)
                m = 0
                for (wap, wkey), s in zip(wl, shifts):
                    for c in range(8):
                        if rhs_fn is None:
                            lo = 1 + tb * 512 + s
                            rhs = hT[:, c, lo:lo + 512]
                        else:
                            rhs = rhs_fn(c, tb)
                        self.mm(self.bank[b][:, 0:tbw], wap[:, c, ct * 128:(ct + 1) * 128], rhs, m == 0, m == nmm - 1,
                                [wkey, rkey], [self.bk(b)])
                        m += 1
                consume(b, ct, tb)

    def proj_N(self, wl, n, consume, shifts=(0,), lhs_fn=None, lkey="hT", ntt=NT, kc=8):
        hT = self.hT
        for tt in range(ntt):
            b = self.pbank()
            nmm = kc * len(shifts)
            m = 0
            for (wap, wkey), s in zip(wl, shifts):
                for c in range(kc):
                    if lhs_fn is None:
                        lo = 1 + tt * 128 + s
                        lh = hT[:, c, lo:lo + 128]
                    else:
                        lh = lhs_fn(c, tt)
                    self.mm(self.bank[b][:, 0:n], lh, wap[:, c, 0:n], m == 0, m == nmm - 1, [wkey, lkey], [self.bk(b)])
                    m += 1
            consume(b, tt)

    def ln_inplace(self, xt, xkey, eps=1e-5):
        sm, junk = self.sm, self.junk
        P = self.P
        P.op("dve", lambda e: e.reduce_sum(out=sm[:, 0:1], in_=xt, axis=AX.X), reads=[xkey], writes=["sm"])
        self.ts("dve", sm[:, 1:2], sm[:, 0:1], -1.0 / D, None, ALU.mult, None, ["sm"], ["sm"])
        self.ts("dve", xt, xt, sm[:, 1:2], None, ALU.add, None, [xkey, "sm"], [xkey])
        self.memset("dve", sm[:, 2:3], 0.0, ["sm"])
        self.act(junk[:], xt, AF.Square, [xkey, "sm"], ["junk", "sm"], accum_out=sm[:, 2:3])
        self.rsqrt_cols(sm[:, 2:3], sm[:, 3:4], 1.0 / D, eps)
        self.stt("dve", xt, xt, sm[:, 3:4], self.lng[:], ALU.mult, ALU.mult, [xkey, "sm", "lng"], [xkey])
        self.tt("dve", xt, xt, self.lnb[:], ALU.add, [xkey, "lnb"], [xkey])

    def to_hT(self, src, skey, tt):
        hT = self.hT
        for half in range(2):
            b = self.pbank()
            for c4 in range(4):
                c = half * 4 + c4
                self.tr(self.bank[b][:, c4 * 128:(c4 + 1) * 128], src[:, c * 128:(c + 1) * 128], [skey], [self.bk(b)])
            self.cp("act" if half else "dve", hT[:, half * 4:half * 4 + 4, 1 + tt * 128:1 + (tt + 1) * 128],
                    self.bank[b][:, :].rearrange("p (c t) -> p c t", t=128), [self.bk(b)], ["hT"])

    def prelude(self):
        P, I = self.P, self.I
        P.dma("sp", self.ident[:], I["ident"], writes=["ident"])
        P.dma("sp", self.ropec[:], I["ropec"].rearrange("(t p) c -> p t c", p=128), writes=["ropec"])
        P.dma("sp", self.ropes[:], I["ropes"].rearrange("(t p) c -> p t c", p=128), writes=["ropes"])
        self.memset("pool", self.onesf[:], 1.0, ["onesf"])
        self.memset("pool", self.hT[:, :, 0:1], 0.0, ["hT"])
        self.memset("pool", self.hT[:, :, S + 1:S + 2], 0.0, ["hT"])
        tmpA = self.tmp[0]
        rb = tmpA[0:32, 0:12]
        P.dma("sp", rb, I["rel_bias"], writes=["tmp0"])
        ohs = self.BIG[0][:].bitcast(F32)
        P.dma("sp", ohs[0:32, 0:4096], I["onehot"], writes=["BIG0"])
        mts = self.BIG[1][:].bitcast(F32)
        P.dma("sp", mts[0:12, 0:4096], I["multab"], writes=["BIG1"])
        xts = self.BIG[2][:].bitcast(F32)
        for j in range(8):
            b = self.pbank()
            self.mm(self.bank[b][0:12, :], rb, ohs[0:32, j * 512:(j + 1) * 512], True, True, ["tmp0", "BIG0"], [self.bk(b)])
            self.act(xts[0:12, j * 512:(j + 1) * 512], self.bank[b][0:12, :], AF.Exp, [self.bk(b)], ["BIG2"])
        self.tt("dve", xts[0:12, 0:4096], xts[0:12, 0:4096], mts[0:12, 0:4096], ALU.mult, ["BIG2", "BIG1"], ["BIG2"])
        P.dma("sp", self.xtab, xts[0:12, 0:4096], reads=["BIG2"], writes=["xtab"])
        self.barrier()
        for mt_ in range(2):
            i = self.nxt("ln", 2)
            P.dma("sp", self.lnx[i][:], I["mem"][mt_ * 128:(mt_ + 1) * 128, :], writes=["lnx%d" % i])
            for half in range(2):
                b = self.pbank()
                for c4 in range(4):
                    c = half * 4 + c4
                    self.tr(self.bank[b][:, c4 * 128:(c4 + 1) * 128], self.lnx[i][:, c * 128:(c + 1) * 128], ["lnx%d" % i], [self.bk(b)])
                self.cp("dve", self.memT[:, half * 4:half * 4 + 4, mt_ * 128:(mt_ + 1) * 128],
                        self.bank[b][:, :].rearrange("p (c t) -> p c t", t=128), [self.bk(b)], ["memT"])
        P.dma("sp", self.lng[:], I["ln_in_g"].partition_broadcast(128), writes=["lng"])
        P.dma("sp", self.lnb[:], I["ln_in_b"].partition_broadcast(128), writes=["lnb"])
        for tt in range(NT):
            i = self.nxt("ln", 2)
            P.dma("sp", self.lnx[i][:], I["x"][tt * 128:(tt + 1) * 128, :], writes=["lnx%d" % i])
            self.ln_inplace(self.lnx[i][:], "lnx%d" % i)
            P.dma("sp", self.hres[0][tt * 128:(tt + 1) * 128, :], self.lnx[i][:], reads=["lnx%d" % i], writes=["hres0"])
        self.barrier()

    def layer(self, l, last):
        P, I = self.P, self.I
        hin = self.hres[l % 2]
        for tt in range(NT):
            i = self.nxt("ln", 2)
            P.dma("sp", self.lnx[i][:], hin[tt * 128:(tt + 1) * 128, :], reads=["hres%d" % (l % 2)], writes=["lnx%d" % i])
            self.to_hT(self.lnx[i], "lnx%d" % i, tt)
        self.win = I["w_in"][l * D:(l + 1) * D, :]
        for mx in "MCADB":
            if mx in self.mixers:
                getattr(self, "mixer_" + mx)(l)
            else:
                self.zero_o(mx)
            self.barrier()
        self.merge(l, last)
        self.barrier()

    def zero_o(self, mx):
        c0, w = {"M": (OM, 256), "C": (OC, 512), "A": (OA, 512), "D": (OD, 512), "B": (OB, 512)}[mx]
        t = self.tmp[2]
        self.memset("pool", t[:, :], 0.0, ["tmp2"])
        for tt in range(NT):
            self.P.dma("sp", self.o_scr[tt * 128:(tt + 1) * 128, c0:c0 + w], t[:, 0:w], reads=["tmp2"], writes=["o_scr"])

    def attn_head(self, maps, vfn, vkey, nkt, dv1, table, band, post):
        nm = len(maps)
        G = self.G

        nqt = 4 if nm == 1 else 2
        QB = nqt * 128

        def accap(m, qt):
            bi = 4 + m * nqt + qt
            return self.bank[bi][:, 0:dv1], bi

        for qb in range(S // QB):
            q0 = qb * QB
            kts = []
            for kt in range(nkt):
                dk = kt * 128 - q0
                if band and (dk - (QB - 1) > 1024 or dk + 127 < -1024):
                    continue
                kts.append(kt)
            steps = [(idx, kt, m) for idx, kt in enumerate(kts) for m in range(nm)]

            def stageA(si):
                idx, kt, m = steps[si]
                q_ap, kfn = maps[m]
                sb_ = si % 2
                self.mm(self.bank[sb_][:, 0:QB], kfn(kt), q_ap[:, q0:q0 + QB], True, True, ["BIG0", "BIG1"], [self.bk(sb_)])

            def stageBC(si):
                idx, kt, m = steps[si]
                sb_ = si % 2
                pti = self.nxt("pt", 4)
                ptile = self.pt[pti]
                if table:
                    pei = self.nxt("pe", 2)
                    self.act(self.pe_[pei][:, 0:QB], self.bank[sb_][:, 0:QB], AF.Exp, [self.bk(sb_)], ["pe%d" % pei], scale=0.125)
                    j0 = kt * 128 - q0 + XC
                    gs = G[:, j0 - (QB - 1):j0 + 1][:, ::-1]
                    self.tt("dve", ptile[:, 0:QB], self.pe_[pei][:, 0:QB], gs, ALU.mult, ["pe%d" % pei, "G"], ["pt%d" % pti])
                else:
                    self.act(ptile[:, 0:QB], self.bank[sb_][:, 0:QB], AF.Exp, [self.bk(sb_)], ["pt%d" % pti], scale=0.125)
                for qt in range(nqt):
                    acc, bi = accap(m, qt)
                    self.mm(acc, ptile[:, qt * 128:(qt + 1) * 128], vfn(kt), idx == 0, idx == len(kts) - 1,
                            ["pt%d" % pti, vkey], [self.bk(bi)])

            stageA(0)
            for si in range(len(steps)):
                if si + 1 < len(steps):
                    stageA(si + 1)
                stageBC(si)
            for qt in range(nqt):
                accs = [accap(m, qt) for m in range(nm)]
                post(qb * nqt + qt, [a for a, _ in accs], [self.bk(bi) for _, bi in accs])

    def post_simple(self, h, hd, ocol):
        def post(tt, accs, keys):
            acc = accs[0]
            sm = self.sm
            i = self.nxt("ost", 4)
            self.P.op("dve", lambda e: e.reciprocal(out=sm[:, 8:9], in_=acc[:, hd:hd + 1]), reads=keys, writes=["sm"])
            self.stt("dve", self.ost[i][:, 0:hd], acc[:, 0:hd], sm[:, 8:9], self.sg[:, tt, h * hd:(h + 1) * hd], ALU.mult, ALU.mult,
                     keys + ["sm", "BIG3"], ["ost%d" % i])
            self.P.dma("sp", self.o_scr[tt * 128:(tt + 1) * 128, ocol + h * hd:ocol + (h + 1) * hd], self.ost[i][:, 0:hd],
                       reads=["ost%d" % i], writes=["o_scr"])
        return post

    def cons_T(self, dst, dkey):
        def consume(b, ct, tb):
            self.cp("dve" if (ct + tb) % 2 else "act", dst[:, ct, tb * 512:(tb + 1) * 512], self.bank[b][:, :], [self.bk(b)], [dkey])
        return consume

    def cons_sg(self, c0, n):
        def consume(b, tt):
            self.act(self.sg[:, tt, c0:c0 + n], self.bank[b][:, 0:n], AF.Silu, [self.bk(b)], ["BIG3"])
        return consume

    def cons_v(self, V1, h0, nh, hd):
        def consume(b, tt):
            self.cp("dve", V1[:, tt, h0:h0 + nh, 0:hd], self.bank[b][:, 0:nh * hd].rearrange("p (h d) -> p h d", d=hd), [self.bk(b)], ["BIG2"])
        return consume

    def mixer_M(self, l):
        P, I = self.P, self.I
        wkv = I["w_mem_kv"][l * D:(l + 1) * D, :]
        V1 = self.BIG[2][:, 0:520].rearrange("p (k h d) -> p k h d", k=2, h=4, d=65)
        self.memset("pool", self.BIG[2][:, 0:520], 1.0, ["BIG2"])
        memT = self.memT
        wl = self.wload(wkv[:, 0:256], 256)
        self.proj_T(wl, 256, lambda b, ct, tb: self.cp("dve", self.kT[:, ct, 0:256], self.bank[b][:, 0:256], [self.bk(b)], ["BIG1"]),
                    rhs_fn=lambda c, tb: memT[:, c, 0:256], rkey="memT", ntb=1, tbw=256)
        wl = self.wload(wkv[:, 256:512], 256)
        self.proj_N(wl, 256, self.cons_v(V1, 0, 4, 64), lhs_fn=lambda c, tt: memT[:, c, tt * 128:(tt + 1) * 128], lkey="memT", ntt=2)
        wl = self.wload(self.win[:, M0:M0 + 256], 256)
        self.proj_T(wl, 256, self.cons_T(self.qT, "BIG0"))
        wl = self.wload(self.win[:, M0 + 256:M0 + 512], 256)
        self.proj_N(wl, 256, self.cons_sg(0, 256))
        if l == 0 and "m" in self.mixers:
            self.dbg("qT", self.BIG[0][:, 0:8192], ["BIG0"], BF16)
            self.dbg("kT", self.BIG[1][:, 0:8192], ["BIG1"], BF16)
            self.dbg("V1", self.BIG[2][:, 0:520], ["BIG2"], BF16)
            self.dbg("sg", self.BIG[3][:, 0:8192], ["BIG3"], BF16)
            self.dbg("hT", self.hT[:, :, :].rearrange("p c t -> p (c t)"), ["hT"], BF16)
        for h in range(4):
            ct, pb = h // 2, (h % 2) * 64
            q_ap = self.qT[pb:pb + 64, ct, :]
            self.attn_head([(q_ap, lambda kt, ct=ct, pb=pb: self.kT[pb:pb + 64, ct, kt * 128:(kt + 1) * 128])],
                           lambda kt, h=h: V1[:, kt, h, :], "BIG2", 2, 65, False, False, self.post_simple(h, 64, OM))

    def mixer_A(self, l):
        P = self.P
        V1 = self.BIG[2][:, 0:8320].rearrange("p (k h d) -> p k h d", k=16, h=8, d=65)
        self.memset("pool", self.BIG[2][:, 0:8320], 1.0, ["BIG2"])
        for j in range(2):
            wl = self.wload(self.win[:, A0 + j * 256:A0 + (j + 1) * 256], 256)
            self.proj_T(wl, 256, lambda b, ct, tb, j=j: self.cons_T(self.qT, "BIG0")(b, ct + 2 * j, tb))
        for j in range(2):
            wl = self.wload(self.win[:, A0 + 512 + j * 256:A0 + 512 + (j + 1) * 256], 256)
            self.proj_T(wl, 256, lambda b, ct, tb, j=j: self.cons_T(self.kT, "BIG1")(b, ct + 2 * j, tb))
        for j in range(2):
            wl = self.wload(self.win[:, A0 + 1024 + j * 256:A0 + 1024 + (j + 1) * 256], 256)
            self.proj_N(wl, 256, self.cons_v(V1, 4 * j, 4, 64))
        for j in range(2):
            wl = self.wload(self.win[:, A0 + 1536 + j * 256:A0 + 1536 + (j + 1) * 256], 256)
            self.proj_N(wl, 256, self.cons_sg(j * 256, 256))
        for h in range(8):
            ct, pb = h // 2, (h % 2) * 64
            P.dma("sp", self.G[:, 0:3968], AP(self.xtab.tensor, h * 4096, [[1, 128], [1, 3968]]), reads=["xtab"], writes=["G"])
            q_ap = self.qT[pb:pb + 64, ct, :]
            self.attn_head([(q_ap, lambda kt, ct=ct, pb=pb: self.kT[pb:pb + 64, ct, kt * 128:(kt + 1) * 128])],
                           lambda kt, h=h: V1[:, kt, h, :], "BIG2", 16, 65, True, True, self.post_simple(h, 64, OA))

    def normrope(self, b, tt, nh, gcol):
        n = nh * 64
        sm = self.sm
        t0, t1, t2 = self.tmp
        ps = self.bank[b][:, 0:n]
        self.act(t0[:, 0:n], ps, AF.Square, [self.bk(b)], ["tmp0"])
        self.P.op("dve", lambda e: e.reduce_sum(out=sm[:, 16:16 + nh], in_=t0[:, 0:n].rearrange("p (h d) -> p h d", d=64), axis=AX.X),
                  reads=["tmp0"], writes=["sm"])
        self.rsqrt_cols(sm[:, 16:16 + nh], sm[:, 24:24 + nh], 1.0 / 64, 1e-6)
        v3 = lambda ap: ap.rearrange("p (h d) -> p h d", d=64)
        self.tt("dve", v3(t0[:, 0:n]), v3(ps), sm[:, 24:24 + nh].unsqueeze(2).broadcast_to([128, nh, 64]), ALU.mult,
                [self.bk(b), "sm"], ["tmp0"])
        self.tt("dve", v3(t0[:, 0:n]), v3(t0[:, 0:n]), self.gq[:, gcol:gcol + 64].unsqueeze(1).broadcast_to([128, nh, 64]), ALU.mult,
                ["tmp0", "gq"], ["tmp0"])
        self.tt("pool", v3(t1[:, 0:n]), v3(t0[:, 0:n]), self.ropec[:, tt, :].unsqueeze(1).broadcast_to([128, nh, 64]), ALU.mult,
                ["tmp0", "ropec"], ["tmp1"])
        v5 = lambda ap: ap.rearrange("p (h a b c) -> p h a b c", a=2, b=2, c=16)
        rs = self.ropes[:, tt, :].rearrange("p (a b c) -> p a b c", a=2, b=2, c=16)
        for bb in range(2):
            self.tt("dve", v5(t2[:, 0:n])[:, :, :, bb, :], v5(t0[:, 0:n])[:, :, :, 1 - bb, :],
                    rs[:, :, bb, :].unsqueeze(1).broadcast_to([128, nh, 2, 16]), ALU.mult, ["tmp0", "ropes"], ["tmp2"])
        self.tt("dve", t1[:, 0:n], t1[:, 0:n], t2[:, 0:n], ALU.add, ["tmp1", "tmp2"], ["tmp1"])

    def mixer_C(self, l):
        P, I = self.P, self.I
        V1 = self.BIG[2][:, 0:2080].rearrange("p (k h d) -> p k h d", k=16, h=2, d=65)
        self.memset("pool", self.BIG[2][:, 0:2080], 1.0, ["BIG2"])
        P.dma("sp", self.gq[:, 0:64], I["c_qnorm_g"][l:l + 1, :].partition_broadcast(128), writes=["gq"])
        P.dma("sp", self.gq[:, 64:128], I["c_knorm_g"][l:l + 1, :].partition_broadcast(128), writes=["gq"])
        t1 = self.tmp[1]
        for j in range(2):
            wl = self.wload(self.win[:, C0 + j * 256:C0 + (j + 1) * 256], 256)

            def cons_q(b, tt, j=j):
                self.normrope(b, tt, 4, 0)
                b2 = self.pbank()
                for u in range(2):
                    self.tr(self.bank[b2][:, u * 128:(u + 1) * 128], t1[:, u * 128:(u + 1) * 128], ["tmp1"], [self.bk(b2)])
                self.cp("act", self.qT[:, 2 * j:2 * j + 2, tt * 128:(tt + 1) * 128],
                        self.bank[b2][:, 0:256].rearrange("p (c t) -> p c t", t=128), [self.bk(b2)], ["BIG0"])
            self.proj_N(wl, 256, cons_q)
        wl = self.wload(self.win[:, C0 + 512:C0 + 768], 256)

        def cons_kv(b, tt):
            self.cp("act", V1[:, tt, :, 0:64], self.bank[b][:, 128:256].rearrange("p (h d) -> p h d", d=64), [self.bk(b)], ["BIG2"])
            self.normrope(b, tt, 2, 64)
            t2 = self.tmp[2]
            self.cp("dve", t2[:, 0:256].rearrange("p (g r d) -> p g r d", g=2, r=2, d=64),
                    t1[:, 0:128].rearrange("p (g d) -> p g d", d=64).unsqueeze(2).broadcast_to([128, 2, 2, 64]), ["tmp1"], ["tmp2"])
            b2 = self.pbank()
            for u in range(2):
                self.tr(self.bank[b2][:, u * 128:(u + 1) * 128], t2[:, u * 128:(u + 1) * 128], ["tmp2"], [self.bk(b2)])
            self.cp("act", self.kT[:, 0:2, tt * 128:(tt + 1) * 128],
                    self.bank[b2][:, 0:256].rearrange("p (c t) -> p c t", t=128), [self.bk(b2)], ["BIG1"])
        self.proj_N(wl, 256, cons_kv)
        for j in range(2):
            wl = self.wload(self.win[:, C0 + 768 + j * 256:C0 + 768 + (j + 1) * 256], 256)
            self.proj_N(wl, 256, self.cons_sg(j * 256, 256))
        for h in range(8):
            ct, pb = h // 2, (h % 2) * 64
            g = h // 4
            q_ap = self.qT[pb:pb + 64, ct, :]
            self.attn_head([(q_ap, lambda kt, g=g, pb=pb: self.kT[pb:pb + 64, g, kt * 128:(kt + 1) * 128])],
                           lambda kt, g=g: V1[:, kt, g, :], "BIG2", 16, 65, False, False, self.post_simple(h, 64, OC))

    def mixer_D(self, l):
        P, I = self.P, self.I
        V1 = self.BIG[2][:, 0:8256].rearrange("p (k h d) -> p k h d", k=16, h=4, d=129)
        self.memset("pool", self.BIG[2][:, 0:8256], 1.0, ["BIG2"])
        lam_init = 0.8 - 0.6 * math.exp(-0.3 * l)
        lamt, sm = self.lamt, self.sm
        P.dma("sp", lamt[:, 0:256], I["d_lambda"][l:l + 1, :].partition_broadcast(128), writes=["lamt"])
        P.dma("sp", self.subg[:], I["d_subln_g"][l:l + 1, :].partition_broadcast(128), writes=["subg"])
        self.ts("pool", self.subg[:], self.subg[:], 1.0 - lam_init, None, ALU.mult, None, ["subg"], ["subg"])
        lv = lamt[:, 0:256].rearrange("p (a b c) -> p a b c", a=2, b=2, c=64)
        lp = self.tmp[2][:, 0:128].rearrange("p (a c) -> p a c", c=64)
        self.tt("dve", lp, lv[:, :, 0, :], lv[:, :, 1, :], ALU.mult, ["lamt"], ["tmp2"])
        P.op("dve", lambda e: e.reduce_sum(out=lamt[:, 256:258], in_=lp, axis=AX.X), reads=["tmp2"], writes=["lamt"])
        self.act(lamt[:, 258:260], lamt[:, 256:258], AF.Exp, ["lamt"], ["lamt"])
        self.tt("dve", lamt[:, 260:261], lamt[:, 259:260], lamt[:, 258:259], ALU.subtract, ["lamt"], ["lamt"])
        self.ts("dve", lamt[:, 260:261], lamt[:, 260:261], -lam_init, None, ALU.add, None, ["lamt"], ["lamt"])
        for j in range(2):
            wl = self.wload(self.win[:, D0 + j * 256:D0 + (j + 1) * 256], 256)
            self.proj_T(wl, 256, lambda b, ct, tb, j=j: self.cons_T(self.qT, "BIG0")(b, ct + 2 * j, tb))
        for j in range(2):
            wl = self.wload(self.win[:, D0 + 512 + j * 256:D0 + 512 + (j + 1) * 256], 256)
            self.proj_T(wl, 256, lambda b, ct, tb, j=j: self.cons_T(self.kT, "BIG1")(b, ct + 2 * j, tb))
        for j in range(2):
            wl = self.wload(self.win[:, D0 + 1024 + j * 256:D0 + 1024 + (j + 1) * 256], 256)
            self.proj_N(wl, 256, self.cons_v(V1, 2 * j, 2, 128))
        for j in range(2):
            wl = self.wload(self.win[:, D0 + 1536 + j * 256:D0 + 1536 + (j + 1) * 256], 256)
            self.proj_N(wl, 256, self.cons_sg(j * 256, 256))
        t0 = self.tmp[0]
        for h in range(4):
            P.dma("sp", self.G[:, 0:3968], AP(self.xtab.tensor, (8 + h) * 4096, [[1, 128], [1, 3968]]), reads=["xtab"], writes=["G"])

            def post(tt, accs, keys, h=h):
                a1, a2 = accs
                i = self.nxt("ost", 4)
                P.op("dve", lambda e: e.reciprocal(out=sm[:, 8:9], in_=a1[:, 128:129]), reads=keys, writes=["sm"])
                P.op("dve", lambda e: e.reciprocal(out=sm[:, 9:10], in_=a2[:, 128:129]), reads=keys, writes=["sm"])
                self.tt("dve", sm[:, 9:10], sm[:, 9:10], lamt[:, 260:261], ALU.mult, ["sm", "lamt"], ["sm"])
                self.ts("dve", t0[:, 0:128], a1[:, 0:128], sm[:, 8:9], None, ALU.mult, None, keys + ["sm"], ["tmp0"])
                self.stt("dve", t0[:, 128:256], a2[:, 0:128], sm[:, 9:10], t0[:, 0:128], ALU.mult, ALU.add, keys + ["sm", "tmp0"], ["tmp0"])
                self.memset("dve", sm[:, 10:11], 0.0, ["sm"])
                self.act(t0[:, 256:384], t0[:, 128:256], AF.Square, ["tmp0", "sm"], ["tmp0", "sm"], accum_out=sm[:, 10:11])
                self.rsqrt_cols(sm[:, 10:11], sm[:, 11:12], 1.0 / 128, 1e-5)
                self.stt("dve", t0[:, 128:256], t0[:, 128:256], sm[:, 11:12], self.subg[:], ALU.mult, ALU.mult, ["tmp0", "sm", "subg"], ["tmp0"])
                self.tt("dve", self.ost[i][:], t0[:, 128:256], self.sg[:, tt, h * 128:(h + 1) * 128], ALU.mult, ["tmp0", "BIG3"], ["ost%d" % i])
                P.dma("sp", self.o_scr[tt * 128:(tt + 1) * 128, OD + h * 128:OD + (h + 1) * 128], self.ost[i][:],
                      reads=["ost%d" % i], writes=["o_scr"])
            maps = [(self.qT[c * 64:(c + 1) * 64, h, :], (lambda kt, c=c, h=h: self.kT[c * 64:(c + 1) * 64, h, kt * 128:(kt + 1) * 128]))
                    for c in range(2)]
            self.attn_head(maps, lambda kt, h=h: V1[:, kt, h, :], "BIG2", 16, 129, True, False, post)

    def dbg(self, name, ap, reads, dt=F32):
        if not self.debug:
            return
        t = self.nc.dram_tensor("dbg_" + name, list(ap.shape), dt, kind="ExternalOutput").ap()
        self.P.dma("sp", t, ap, reads=reads, is_output=True)
        self.dbg_outs.append("dbg_" + name)

    def mixer_B(self, l):
        P, I = self.P, self.I
        CW = 0.6065306597126334
        t_ring = self.tmp
        mub = self.lnx[0][:, 0:768].rearrange("p (v n) -> p v n", n=256)

        def load_mu(c0):
            for v in range(2):
                P.dma("sp", mub[:, 1 + v, :], I["shift_mu"][l * 2 + v:l * 2 + v + 1, c0:c0 + 256].partition_broadcast(128), writes=["lnx0"])
            self.tt("dve", mub[:, 0, :], mub[:, 1, :], mub[:, 2, :], ALU.add, ["lnx0"], ["lnx0"])
            self.ts("dve", mub[:, 0, :], mub[:, 0, :], -1.0, 1.0, ALU.mult, ALU.add, ["lnx0"], ["lnx0"])
            return [(mub[:, 0, :], "lnx0"), (mub[:, 1, :], "lnx0"), (mub[:, 2, :], "lnx0")]

        def stage_out(dst_ap, dkey, func=None):
            def f(b, n_part=128, ncol=512):
                i = self.nxt("tmp", 3)
                if func is None:
                    self.cp("dve", t_ring[i][0:n_part, 0:ncol], self.bank[b][0:n_part, 0:ncol], [self.bk(b)], ["tmp%d" % i])
                else:
                    self.act(t_ring[i][0:n_part, 0:ncol], self.bank[b][0:n_part, 0:ncol], func, [self.bk(b)], ["tmp%d" % i])
                P.dma("sp", dst_ap, t_ring[i][0:n_part, 0:ncol], reads=["tmp%d" % i], writes=[dkey])
            return f

        for j in range(4):
            c0 = j * 256
            wl = self.wload(self.win[:, B0 + c0:B0 + c0 + 256], 256, variants=load_mu(c0))
            self.proj_T(wl, 256, lambda b, ct, tb, c0=c0: stage_out(self.rk_scr[c0 + ct * 128:c0 + (ct + 1) * 128, tb * 512:(tb + 1) * 512], "rk_scr")(b),
                        shifts=(0, -1, 1))
        for j in range(2):
            c0 = 1024 + j * 256
            wl = self.wload(self.win[:, B0 + c0:B0 + c0 + 256], 256, variants=load_mu(c0))
            self.proj_N(wl, 256, lambda b, tt, j=j: stage_out(self.v_scr[tt * 128:(tt + 1) * 128, j * 256:(j + 1) * 256], "v_scr")(b, 128, 256),
                        shifts=(0, -1, 1))
        wl = self.wload(self.win[:, B0 + 1536:B0 + 1792], 256, variants=load_mu(1536))
        self.proj_T(wl, 256, lambda b, ct, tb: stage_out(self.wa_scr[ct * 128:(ct + 1) * 128, tb * 512:(tb + 1) * 512], "wa_scr",
                                                         AF.Tanh if ct == 0 else AF.Copy)(b), shifts=(0, -1, 1))
        for j in range(2):
            wl = self.wload(self.win[:, B0 + 1792 + j * 256:B0 + 1792 + (j + 1) * 256], 256)
            self.proj_N(wl, 256, lambda b, tt, j=j: stage_out(self.sg_scr[tt * 128:(tt + 1) * 128, j * 256:(j + 1) * 256], "sg_scr", AF.Silu)(b, 128, 256))
        self.barrier()
        slots = []
        for bi in range(4):
            a = self.BIG[bi][:].bitcast(F32)
            for q in range(4):
                slots.append(a[:, q * 1024:(q + 1) * 1024])
        for q in range(4):
            slots.append(self.G[:, q * 1024:(q + 1) * 1024])
        for wi in range(1):
            a = self.wst[wi][:, :, :].rearrange("p c n -> p (c n)")
            for q in range(2):
                slots.append(a[:, q * 1024:(q + 1) * 1024])
        si = [0]

        def slot(full=True):
            if full:
                if si[0] % 2:
                    si[0] += 1
                a = slots[si[0] // 2]
                si[0] += 2
                return a
            a = slots[si[0] // 2][:, (si[0] % 2) * 512:(si[0] % 2) * 512 + 512]
            si[0] += 1
            return a

        def v3(ap, w):
            return ap[0:64, 0:8 * w].rearrange("p (h t) -> p h t", t=w)

        w_upS = slot()[0:64, :].rearrange("p (e c) -> p e c", c=512)
        a_upS = slot()[0:64, :].rearrange("p (e c) -> p e c", c=512)
        w0B = slot()[0:64, :].rearrange("p (e c) -> p e c", c=512)
        rkT = slot()[0:64, :].rearrange("p (g t) -> p g t", t=64)
        AR = slot()[0:64, :].rearrange("p (h t) -> p h t", t=128)
        NP = [self.RX[:, q * 1024:(q + 1) * 1024].rearrange("p (h t) -> p h t", t=128) for q in range(2)]
        ysb = slot()[0:64, 0:520]
        rmaskS = slot(False)[0:64, 0:384].rearrange("p (e n) -> p e n", n=192)
        waT = slot(False)[0:64, 0:256].rearrange("p (g t) -> p g t", t=64)
        vtok = slot(False)[0:64, :]
        sgw = slot(False)[0:64, :]
        asT, kkn, ke, be, tE0, tE1, bch, kch, z = [v3(slot(False), 64) for _ in range(9)]
        eLs, Bt, Kt = [slot(False)[0:64, :] for _ in range(3)]
        Mm = [self.RX[:, 2048 + q * 512:2048 + (q + 1) * 512].rearrange("p (h t) -> p h t", t=64) for q in range(2)]
        Mrb, Mak, Mrk, Xs, Us, tmpS = [v3(slot(False), 64) for _ in range(6)]
        Sst = [v3(slot(False), 64) for _ in range(2)]
        assert si[0] <= 2 * len(slots), si[0]
        rwp = self.gq[0:64, 0:40]
        omka = self.gq[0:64, 40:48]
        ident64 = self.ident[0:64, 0:64]
        ones64 = self.onesf[0:64, 0:64]
        self.r32 = True
        P.dma("sp", w_upS, I["rwkv_w_up"][l * 128:(l + 1) * 128, :].rearrange("(e r) c -> r e c", r=64), writes=["w_upS"])
        P.dma("sp", a_upS, I["rwkv_a_up"][l * 128:(l + 1) * 128, :].rearrange("(e r) c -> r e c", r=64), writes=["a_upS"])
        for e in range(2):
            P.dma("sp", w0B[:, e, :], I["rwkv_w0"][l * 2 + e:l * 2 + e + 1, :].partition_broadcast(64), writes=["w0B"])
        P.dma("sp", rmaskS, I["rmask"].rearrange("(e p) n -> p e n", p=64), writes=["rmaskS"])

        pm = self.tmp[0]
        P.dma("sp", pm[0:16, 0:64], I["rwkv_a0"][l * 2:(l + 1) * 2, :].rearrange("e (h c) -> (e h) c", c=64), writes=["tmp0"])
        P.dma("sp", pm[16:24, 0:64], I["rwkv_k_k"][l:l + 1, :].rearrange("e (h c) -> (e h) c", c=64), writes=["tmp0"])
        P.dma("sp", pm[24:32, 0:64], I["rwkv_k_a"][l:l + 1, :].rearrange("e (h c) -> (e h) c", c=64), writes=["tmp0"])
        P.dma("sp", pm[32:40, 0:64], I["rwkv_r_k"][l:l + 1, :].rearrange("e (h c) -> (e h) c", c=64), writes=["tmp0"])
        b = self.pbank()
        self.P.op("pe", lambda e_: e_.transpose(self.bank[b][0:64, 0:40], pm[0:40, 0:64], self.ident[0:40, 0:40]), reads=["tmp0", "ident"], writes=[self.bk(b)])
        self.cp("dve", rwp, self.bank[b][0:64, 0:40], [self.bk(b)], ["gq"])
        self.ts("dve", omka, rwp[:, 24:32], -1.0, 1.0, ALU.mult, ALU.add, ["gq"], ["gq"])
        bc3 = lambda ap: ap.unsqueeze(2).broadcast_to([64, 8, 64])
        hb = lambda b_, h, w=64: self.bank[b_][0:64, h * w:(h + 1) * w]
        b3 = lambda b_, w=64: self.bank[b_][0:64, 0:8 * w].rearrange("p (h t) -> p h t", t=w)

        for e in range(2):
            Scur = 0
            self.memset("dve", Sst[0], 0.0, ["S0"])
            order = range(32) if e == 0 else range(31, -1, -1)
            tl = 63 if e == 0 else 0
            mS, mI, mT = rmaskS[:, e, 0:64], rmaskS[:, e, 64:128], rmaskS[:, e, 128:192]
            for ch in order:
                t0 = ch * 64
                P.dma("sp", rkT, self.rk_scr.rearrange("(g p) t -> p g t", p=64)[:, :, t0:t0 + 64], reads=["rk_scr"], writes=["rkT"])
                P.dma("sp", waT, self.wa_scr.rearrange("(g p) t -> p g t", p=64)[:, :, t0:t0 + 64], reads=["wa_scr"], writes=["waT"])
                P.dma("sp", vtok, self.v_scr[t0:t0 + 64, :], reads=["v_scr"], writes=["vtok"])
                rT, kT_ = rkT[:, 0:8, :], rkT[:, 8:16, :]
                b = self.pbank()
                self.mm(self.bank[b][0:64, :], waT[:, e, :], w_upS[:, e, :], True, True, ["waT", "w_upS"], [self.bk(b)])
                self.tt("dve", sgw, self.bank[b][0:64, :], w0B[:, e, :], ALU.add, [self.bk(b), "w0B"], ["sgw"])
                self.act(sgw, sgw, AF.Sigmoid, ["sgw"], ["sgw"])
                b = self.pbank()
                for h in range(8):
                    self.mm(hb(b, h), a_upS[:, e, h * 64:(h + 1) * 64], waT[:, 2 + e, :], True, True, ["waT", "a_upS"], [self.bk(b)])
                self.tt("dve", asT, b3(b), bc3(rwp[:, e * 8:(e + 1) * 8]), ALU.add, [self.bk(b), "gq"], ["asT"])
                self.act(asT, asT, AF.Sigmoid, ["asT"], ["asT"])
                self.tt("dve", kkn, kT_, bc3(rwp[:, 16:24]), ALU.mult, ["rkT", "gq"], ["kkn"])
                self.act(tE0, kkn, AF.Square, ["kkn"], ["tE0"])
                b = self.pbank()
                self.mm(self.bank[b][0:64, :], ones64, tE0.rearrange("p h t -> p (h t)"), True, True, ["tE0", "onesf"], [self.bk(b)])
                self.act(tE0, b3(b), AF.Sqrt, [self.bk(b)], ["tE0"])
                self.ts("dve", tE0, tE0, 1e-12, None, ALU.max, None, ["tE0"], ["tE0"])
                self.P.op("dve", lambda e_: e_.reciprocal(out=tE0, in_=tE0), reads=["tE0"], writes=["tE0"])
                self.tt("dve", kkn, kkn, tE0, ALU.mult, ["kkn", "tE0"], ["kkn"])
                self.tt("pool", ke, asT, bc3(rwp[:, 24:32]), ALU.mult, ["asT", "gq"], ["ke"])
                self.tt("pool", ke, ke, bc3(omka), ALU.add, ["ke", "gq"], ["ke"])
                self.tt("pool", ke, ke, kT_, ALU.mult, ["ke", "rkT"], ["ke"])
                self.tt("pool", be, kkn, asT, ALU.mult, ["kkn", "asT"], ["be"])
                self.tt("pool", z, rT, ke, ALU.mult, ["rkT", "ke"], ["z"])
                bLi = self.pbank()
                for h in range(8):
                    self.mm(hb(bLi, h), sgw[:, h * 64:(h + 1) * 64], mI, True, True, ["sgw", "rmaskS"], [self.bk(bLi)])
                self.act(tE0, b3(bLi), AF.Exp, [self.bk(bLi)], ["tE0"], scale=-CW)
                self.act(tE1, b3(bLi), AF.Exp, [self.bk(bLi)], ["tE1"], scale=CW)
                self.tt("dve", AR[:, :, 64:128], rT, tE0, ALU.mult, ["rkT", "tE0"], ["AR"])
                self.cp("dve", self.sm[0:64, 32:40], tE0[:, :, tl], ["tE0"], ["sm"])
                self.tt("dve", bch, be, tE1, ALU.mult, ["be", "tE1"], ["bch"])
                self.tt("pool", kch, ke, tE1, ALU.mult, ["ke", "tE1"], ["kch"])
                bLe = self.pbank()
                for h in range(8):
                    self.mm(hb(bLe, h), sgw[:, h * 64:(h + 1) * 64], mS, True, True, ["sgw", "rmaskS"], [self.bk(bLe)])
                self.act(tE0, b3(bLe), AF.Exp, [self.bk(bLe)], ["tE0"], scale=-CW)
                self.stt("dve", AR[:, :, 0:64], kkn, -1.0, tE0, ALU.mult, ALU.mult, ["kkn", "tE0"], ["AR"])
                b = self.pbank()
                self.mm(self.bank[b][0:64, :], mT, sgw, True, True, ["sgw", "rmaskS"], [self.bk(b)])
                self.act(eLs, self.bank[b][0:64, :], AF.Exp, [self.bk(b)], ["eLs"], scale=-CW)
                for src, skey, dst, dkey in ((be, "be", Bt, "Bt"), (ke, "ke", Kt, "Kt")):
                    b = self.pbank()
                    for h in range(8):
                        self.P.op("pe", lambda e_, b=b, h=h, src=src: e_.transpose(hb(b, h), src[:, h, :], ident64), reads=[skey, "ident"], writes=[self.bk(b)])
                    self.tt("dve", dst, self.bank[b][0:64, :], eLs, ALU.mult, [self.bk(b), "eLs"], [dkey])
                b = self.pbank()
                for h in range(8):
                    self.mm(self.bank[b][0:64, h:h + 1], z[:, h, :], rwp[:, 32 + h:33 + h], True, True, ["z", "gq"], [self.bk(b)])
                self.cp("act", ysb[:, 512:520], self.bank[b][0:64, 0:8], [self.bk(b)], ["ysb"])
                for h in range(8):
                    self.mm(self.bank[h // 4][0:64, (h % 4) * 128:(h % 4 + 1) * 128], bch[:, h, :], AR[:, h, :], True, True, ["bch", "AR"], [self.bk(h // 4)])
                for h in range(8):
                    self.mm(self.bank[4 + h // 4][0:64, (h % 4) * 128:(h % 4 + 1) * 128], kch[:, h, :], AR[:, h, :], True, True, ["kch", "AR"], [self.bk(4 + h // 4)])
                for h in range(8):
                    self.mm(hb(6, h), AR[:, h, 0:64], bch[:, h, :], True, True, ["bch", "AR"], [self.bk(6)])
                m4 = lambda m_: m_.unsqueeze(1).broadcast_to([64, 4, 64])
                for g in range(2):
                    bb = self.bank[g][0:64, :].rearrange("p (h t) -> p h t", t=128)
                    kb = self.bank[4 + g][0:64, :].rearrange("p (h t) -> p h t", t=128)
                    self.tt("dve", NP[0][:, 4 * g:4 * g + 4, 0:64], bb[:, :, 0:64], m4(mS), ALU.mult, [self.bk(g), "rmaskS"], ["NP0"])
                    self.tt("dve", Mrb[:, 4 * g:4 * g + 4, :], bb[:, :, 64:128], m4(mI), ALU.mult, [self.bk(g), "rmaskS"], ["Mrb"])
                    self.tt("dve", Mak[:, 4 * g:4 * g + 4, :], kb[:, :, 0:64], m4(mS), ALU.mult, [self.bk(4 + g), "rmaskS"], ["Mak"])
                    self.tt("dve", Mrk[:, 4 * g:4 * g + 4, :], kb[:, :, 64:128], m4(mI), ALU.mult, [self.bk(4 + g), "rmaskS"], ["Mrk"])
                self.tt("dve", Mm[0], b3(6), mT.unsqueeze(1).broadcast_to([64, 8, 64]), ALU.mult, [self.bk(6), "rmaskS"], ["Mm0"])
                self.tt("pool", NP[0][:, :, 64:128], NP[0][:, :, 0:64], ident64.unsqueeze(1).broadcast_to([64, 8, 64]), ALU.add, ["NP0", "ident"], ["NP0"])
                cur = 0
                for step in range(6):
                    nx = 1 - cur
                    pbk = (0, 1) if step % 2 == 0 else (4, 5)
                    mbk = 6 if step % 2 else 7
                    ncur, nnx, mcur, mnx = "NP%d" % cur, "NP%d" % nx, "Mm%d" % cur, "Mm%d" % nx
                    if step == 0:
                        for h in range(8):
                            self.mm(self.bank[pbk[h // 4]][0:64, (h % 4) * 128:(h % 4) * 128 + 64], Mm[cur][:, h, :], NP[cur][:, h, 0:64], True, True,
                                    [mcur, ncur], [self.bk(pbk[h // 4])])
                    elif step < 5:
                        for h in range(8):
                            self.mm(self.bank[pbk[h // 4]][0:64, (h % 4) * 128:(h % 4 + 1) * 128], Mm[cur][:, h, :], NP[cur][:, h, :], True, True,
                                    [mcur, ncur], [self.bk(pbk[h // 4])])
                    else:
                        for h in range(8):
                            self.mm(self.bank[pbk[h // 4]][0:64, (h % 4) * 128 + 64:(h % 4 + 1) * 128], Mm[cur][:, h, :], NP[cur][:, h, 64:128], True, True,
                                    [mcur, ncur], [self.bk(pbk[h // 4])])
                    if step < 5:
                        for h in range(8):
                            self.mm(hb(mbk, h), NP[cur][:, h, 0:64], Mm[cur][:, h, :], True, True, [mcur, ncur], [self.bk(mbk)])
                    for g in range(2):
                        pv = self.bank[pbk[g]][0:64, :].rearrange("p (h t) -> p h t", t=128)
                        if step < 5:
                            self.cp("act", NP[nx][:, 4 * g:4 * g + 4, 0:64], pv[:, :, 0:64], [self.bk(pbk[g])], [nnx])
                        if step == 0:
                            self.cp("dve", NP[nx][:, 4 * g:4 * g + 4, 64:128], NP[cur][:, 4 * g:4 * g + 4, 64:128], [ncur], [nnx])
                        else:
                            self.tt("dve", NP[nx][:, 4 * g:4 * g + 4, 64:128], pv[:, :, 64:128], NP[cur][:, 4 * g:4 * g + 4, 64:128], ALU.add,
                                    [self.bk(pbk[g]), ncur], [nnx])
                    if step < 5:
                        self.cp("act", Mm[nx], b3(mbk), [self.bk(mbk)], [mnx])
                    cur = nx
                TT, tkey = NP[cur], "NP%d" % cur
                S0, skey = Sst[Scur], "S%d" % Scur
                S1, s1key = Sst[1 - Scur], "S%d" % (1 - Scur)
                bX = self.pbank()
                for h in range(8):
                    self.mm(hb(bX, h), AR[:, h, 0:64], S0[:, h, :], True, False, ["AR", skey], [self.bk(bX)])
                    self.mm(hb(bX, h), Mak[:, h, :], vtok[:, h * 64:(h + 1) * 64], False, True, ["Mak", "vtok"], [self.bk(bX)])
                self.cp("dve", Xs, b3(bX), [self.bk(bX)], ["Xs"])
                bU = self.pbank()
                for h in range(8):
                    self.mm(hb(bU, h), TT[:, h, 64:128], Xs[:, h, :], True, True, [tkey, "Xs"], [self.bk(bU)])
                self.cp("act", Us, b3(bU), [self.bk(bU)], ["Us"])
                bY = self.pbank()
                for h in range(8):
                    self.mm(hb(bY, h), AR[:, h, 64:128], S0[:, h, :], True, False, ["AR", skey], [self.bk(bY)])
                    self.mm(hb(bY, h), Mrb[:, h, :], Us[:, h, :], False, False, ["Mrb", "Us"], [self.bk(bY)])
                    self.mm(hb(bY, h), Mrk[:, h, :], vtok[:, h * 64:(h + 1) * 64], False, True, ["Mrk", "vtok"], [self.bk(bY)])
                self.cp("act", ysb[:, 0:512], self.bank[bY][0:64, :], [self.bk(bY)], ["ysb"])
                P.dma("sp", self.y_scr[e * S + t0:e * S + t0 + 64, :], ysb, reads=["ysb"], writes=["y_scr"])
                bS = self.pbank()
                for h in range(8):
                    self.mm(hb(bS, h), Bt[:, h * 64:(h + 1) * 64], Us[:, h, :], True, False, ["Bt", "Us"], [self.bk(bS)])
                    self.mm(hb(bS, h), Kt[:, h * 64:(h + 1) * 64], vtok[:, h * 64:(h + 1) * 64], False, True, ["Kt", "vtok"], [self.bk(bS)])
                self.tt("pool", tmpS, S0, bc3(self.sm[0:64, 32:40]), ALU.mult, [skey, "sm"], ["tmpS"])
                self.tt("dve", S1, tmpS, b3(bS), ALU.add, ["tmpS", self.bk(bS)], [s1key])
                Scur = 1 - Scur
        self.r32 = False
        self.barrier()
        P.dma("sp", self.lng[:, 0:512], I["rwkv_ln_g"][l:l + 1, :].partition_broadcast(128), writes=["lng"])
        P.dma("sp", self.lnb[:, 0:512], I["rwkv_ln_b"][l:l + 1, :].partition_broadcast(128), writes=["lnb"])
        yf, yb, vt = self.lnx[0], self.lnx[1], self.junk
        sm = self.sm
        t0_, t1_, t2_ = self.tmp
        for tt in range(NT):
            P.dma("sp", yf[:, 0:520], self.y_scr[tt * 128:(tt + 1) * 128, :], reads=["y_scr"], writes=["lnx0"])
            P.dma("sp", yb[:, 0:520], self.y_scr[S + tt * 128:S + (tt + 1) * 128, :], reads=["y_scr"], writes=["lnx1"])
            P.dma("sp", vt[:, 0:512], self.v_scr[tt * 128:(tt + 1) * 128, :], reads=["v_scr"], writes=["junk"])
            P.dma("sp", vt[:, 512:1024], self.sg_scr[tt * 128:(tt + 1) * 128, :], reads=["sg_scr"], writes=["junk"])
            self.tt("dve", yf[:, 0:520], yf[:, 0:520], yb[:, 0:520], ALU.add, ["lnx0", "lnx1"], ["lnx0"])
            y3 = yf[:, 0:512].rearrange("p (h d) -> p h d", d=64)
            P.op("dve", lambda e_, y3=y3: e_.reduce_sum(out=sm[:, 40:48], in_=y3, axis=AX.X), reads=["lnx0"], writes=["sm"])
            self.ts("dve", sm[:, 40:48], sm[:, 40:48], -1.0 / 64, None, ALU.mult, None, ["sm"], ["sm"])
            self.tt("dve", y3, y3, sm[:, 40:48].unsqueeze(2).broadcast_to([128, 8, 64]), ALU.add, ["lnx0", "sm"], ["lnx0"])
            self.act(t0_[:, 0:512], yf[:, 0:512], AF.Square, ["lnx0"], ["tmp0"])
            P.op("dve", lambda e_: e_.reduce_sum(out=sm[:, 48:56], in_=t0_[:, 0:512].rearrange("p (h d) -> p h d", d=64), axis=AX.X), reads=["tmp0"], writes=["sm"])
            self.rsqrt_cols(sm[:, 48:56], sm[:, 56:64], 1.0 / 64, 64e-5)
            self.tt("dve", y3, y3, sm[:, 56:64].unsqueeze(2).broadcast_to([128, 8, 64]), ALU.mult, ["lnx0", "sm"], ["lnx0"])
            self.tt("dve", yf[:, 0:512], yf[:, 0:512], self.lng[:, 0:512], ALU.mult, ["lnx0", "lng"], ["lnx0"])
            self.tt("pool", yf[:, 0:512], yf[:, 0:512], self.lnb[:, 0:512], ALU.add, ["lnx0", "lnb"], ["lnx0"])
            self.tt("pool", t1_[:, 0:512].rearrange("p (h d) -> p h d", d=64), vt[:, 0:512].rearrange("p (h d) -> p h d", d=64),
                    yf[:, 512:520].unsqueeze(2).broadcast_to([128, 8, 64]), ALU.mult, ["junk", "lnx0"], ["tmp1"])
            self.tt("dve", t1_[:, 0:512], t1_[:, 0:512], yf[:, 0:512], ALU.add, ["tmp1", "lnx0"], ["tmp1"])
            self.tt("dve", t2_[:, 0:512], t1_[:, 0:512], vt[:, 512:1024], ALU.mult, ["tmp1", "junk"], ["tmp2"])
            P.dma("sp", self.o_scr[tt * 128:(tt + 1) * 128, OB:OB + 512], t2_[:, 0:512], reads=["tmp2"], writes=["o_scr"])

    def merge(self, l, last):
        P, I = self.P, self.I
        wg_all = I["w_gate"][l * D:(l + 1) * D, :]
        wb_all = I["w_branch"][l * 2304:(l + 1) * 2304, :]
        wo_all = I["w_out"][l * D:(l + 1) * D, :]
        P.dma("sp", self.lng[:], I["ln_g"][l:l + 1, :].partition_broadcast(128), writes=["lng"])
        P.dma("sp", self.lnb[:], I["ln_b"][l:l + 1, :].partition_broadcast(128), writes=["lnb"])
        bgT = self.lamt[:, 0:40]
        for i5 in range(5):
            P.dma("sp", bgT[:, i5 * 8:(i5 + 1) * 8], I["b_gate"][l:l + 1, i5 * 1024:(i5 + 1) * 1024].rearrange("e (g c) -> c (e g)", c=128),
                  writes=["lamt"], allow_slow_non_contiguous=True)
        oT = self.BIG[0][:, 0:9216].rearrange("p (j t) -> p j t", t=512)
        yTf = self.BIG[1][:].bitcast(F32)[:, 0:4096].rearrange("p (c t) -> p c t", t=512)
        otile = self.BIG[2][:].bitcast(F32)[:, 0:2304]
        yTb = self.BIG[3][:, 0:4096].rearrange("p (c t) -> p c t", t=512)
        hgrp = self.G[:, 0:4096].rearrange("p (q c) -> p q c", c=1024)
        hin = self.hres[l % 2]
        hout = self.out if last else self.hres[(l + 1) % 2]
        t0, t1 = self.tmp[0], self.tmp[1]
        mb = [0]

        def mbank():
            mb[0] = (mb[0] + 1) % 8
            return mb[0]

        for grp in range(4):
            for tq in range(4):
                tt = grp * 4 + tq
                P.dma("sp", otile, self.o_scr[tt * 128:(tt + 1) * 128, :], reads=["o_scr"], writes=["BIG2"])
                P.dma("sp", hgrp[:, tq, :], hin[tt * 128:(tt + 1) * 128, :], reads=["hres%d" % (l % 2)], writes=["G"])
                for j4 in range(5):
                    nj = min(4, 18 - j4 * 4)
                    b = mbank()
                    for u in range(nj):
                        j = j4 * 4 + u
                        self.tr(self.bank[b][:, u * 128:(u + 1) * 128], otile[:, j * 128:(j + 1) * 128], ["BIG2"], [self.bk(b)])
                    self.cp("act" if j4 % 2 else "dve", oT[:, j4 * 4:j4 * 4 + nj, tq * 128:(tq + 1) * 128],
                            self.bank[b][:, 0:nj * 128].rearrange("p (c t) -> p c t", t=128), [self.bk(b)], ["BIG0"])
            hsl = lambda c: self.hT[:, c, 1 + grp * 512:1 + (grp + 1) * 512]
            for i, (r0, rw) in enumerate(BROWS):
                kci = rw // 128
                for cc in range(4):
                    wg, wgk = self.wload(wg_all[:, i * 1024 + cc * 256:i * 1024 + (cc + 1) * 256], 256)[0]
                    wb, wbk = self.wload(wb_all[r0:r0 + rw, cc * 256:(cc + 1) * 256], 256, kc=kci)[0]
                    for u in range(2):
                        ct = cc * 2 + u
                        b1 = mbank()
                        for c in range(8):
                            self.mm(self.bank[b1][:, :], wg[:, c, u * 128:(u + 1) * 128], hsl(c), c == 0, c == 7, [wgk, "hT"], [self.bk(b1)])
                        ti = self.nxt("mt", 2)
                        tg = self.tmp[ti]
                        self.act(tg[:, :], self.bank[b1][:, :], AF.Sigmoid, [self.bk(b1), "lamt"], ["tmp%d" % ti], bias=bgT[:, i * 8 + ct:i * 8 + ct + 1])
                        b2 = mbank()
                        for c in range(kci):
                            self.mm(self.bank[b2][:, :], wb[:, c, u * 128:(u + 1) * 128], oT[:, r0 // 128 + c, :], c == 0, c == kci - 1,
                                    [wbk, "BIG0"], [self.bk(b2)])
                        ysl = yTf[:, ct, :]
                        if i == 0:
                            self.tt("dve", ysl, self.bank[b2][:, :], tg[:, :], ALU.mult, [self.bk(b2), "tmp%d" % ti], ["BIG1"])
                        else:
                            self.tt("dve", tg[:, :], self.bank[b2][:, :], tg[:, :], ALU.mult, [self.bk(b2), "tmp%d" % ti], ["tmp%d" % ti])
                            self.tt("pool", ysl, ysl, tg[:, :], ALU.add, ["BIG1", "tmp%d" % ti], ["BIG1"])
            if l == 0 and grp == 0:
                self.dbg("yTf", self.BIG[1][:].bitcast(F32)[:, 0:4096], ["BIG1"])
                self.dbg("oT", self.BIG[0][:, 0:9216], ["BIG0"], BF16)
                self.dbg("bgT", self.lamt[:, 0:40], ["lamt"])
            for half in range(2):
                self.cp("act" if half else "dve", yTb[:, half * 4:half * 4 + 4, :], yTf[:, half * 4:half * 4 + 4, :], ["BIG1"], ["BIG3"])
            for cc in range(4):
                wo, wok = self.wload(wo_all[:, cc * 256:(cc + 1) * 256], 256)[0]
                for tq in range(4):
                    b = mbank()
                    for c in range(8):
                        self.mm(self.bank[b][:, 0:256], yTb[:, c, tq * 128:(tq + 1) * 128], wo[:, c, :], c == 0, c == 7, [wok, "BIG3"], [self.bk(b)])
                    hs = hgrp[:, tq, cc * 256:(cc + 1) * 256]
                    self.stt("dve", hs, hs, ALPHA, self.bank[b][:, 0:256], ALU.mult, ALU.add, ["G", self.bk(b)], ["G"])
            for tq in range(4):
                tt = grp * 4 + tq
                self.ln_inplace(hgrp[:, tq, :], "G")
                P.dma("sp", hout[tt * 128:(tt + 1) * 128, :], hgrp[:, tq, :], reads=["G"],
                      writes=["out" if last else "hres%d" % ((l + 1) % 2)], is_output=last)


def make_in_map(inputs, b, consts):
    m = {"x": np.ascontiguousarray(inputs["x"][b]), "mem": np.ascontiguousarray(inputs["mem"][b])}
    for nm, shp in IN_SPECS:
        if nm in consts:
            m[nm] = consts[nm]
        else:
            m[nm] = np.ascontiguousarray(np.asarray(inputs[nm], dtype=np.float32).reshape(shp))
    return m


def kernel(**inputs):
    consts = host_consts()
    kb = KB(debug=False)
    nb = inputs["x"].shape[0]
    in_maps = [make_in_map(inputs, b, consts) for b in range(nb)]
    res = run_bass_kernel_spmd(kb.nc, in_maps, core_ids=list(range(nb)))
    out = np.stack([np.asarray(r["out"], dtype=np.float32).reshape(S, D) for r in res.results], axis=0)
    return out
```

```python
import math
from concourse.ap import AP
import contextlib
import numpy as np
import concourse.bass as bass
import concourse.mybir as mybir
from concourse.bass_utils import run_bass_kernel_spmd

F32 = mybir.dt.float32
BF16 = mybir.dt.bfloat16
I32 = mybir.dt.int32
AF = mybir.ActivationFunctionType
ALU = mybir.AluOpType
AX = mybir.AxisListType

ENGS = ("pe", "act", "dve", "pool", "sp")
DMA_SEMS = 8


class Op:
    __slots__ = ("eng", "fn", "waits", "is_dma", "idx", "marked", "dma_slot", "dma_val", "prewait")

    def __init__(self, eng, fn, is_dma):
        self.eng = eng
        self.fn = fn
        self.is_dma = is_dma
        self.waits = []
        self.marked = False
        self.idx = None
        self.dma_slot = None
        self.dma_val = None
        self.prewait = None


class Prog:
    def __init__(self, nc, same_engine_sync=True):
        self.nc = nc
        self.ops = {e: [] for e in ENGS}
        self.last_write = {}
        self.readers = {}
        self.same_engine_sync = same_engine_sync
        self.dma_count = {e: 0 for e in ENGS}
        self.dma_hist = {e: [] for e in ENGS}
        self.all_dma_out = []
        self.stack = contextlib.ExitStack()
        self.n_ops = 0

    def sb(self, name, shape, dt):
        return self.stack.enter_context(self.nc.sbuf_tensor("s_" + name, list(shape), dt))

    def ps(self, name, shape, dt):
        return self.stack.enter_context(self.nc.psum_tensor("p_" + name, list(shape), dt))

    def _deps(self, op, reads, writes):
        deps = []
        for k in reads:
            w = self.last_write.get(k)
            if w is not None:
                deps.append(w)
        for k in writes:
            w = self.last_write.get(k)
            if w is not None:
                deps.append(w)
            for r in self.readers.get(k, ()):
                deps.append(r)
        best = {}
        for d in deps:
            if d is op:
                continue
            key = (d.eng, d.is_dma, d.dma_slot if d.is_dma else None)
            cur = best.get(key)
            if cur is None or d.idx > cur.idx:
                best[key] = d
        for d in best.values():
            if (not d.is_dma) and d.eng == op.eng and not op.is_dma:
                if op.eng == "pe" or not self.same_engine_sync:
                    continue
            op.waits.append(d)
            d.marked = True
        for k in reads:
            self.readers.setdefault(k, []).append(op)
        for k in writes:
            self.last_write[k] = op
            self.readers[k] = []

    def barrier(self, fn):
        o = Op("pool", fn, False)
        o.idx = len(self.ops["pool"])
        self.ops["pool"].append(o)
        self._deps(o, [], ["__phase__"])
        return o

    def op(self, eng, fn, reads=(), writes=()):
        reads = list(reads) + ["__phase__"]
        o = Op(eng, fn, False)
        o.idx = len(self.ops[eng])
        self.ops[eng].append(o)
        self._deps(o, reads, writes)
        self.n_ops += 1
        return o

    def dma(self, eng, out, in_, reads=(), writes=(), is_output=False, **kw):
        def fn(e, out=out, in_=in_, kw=kw):
            return e.dma_start(out=out, in_=in_, **kw)
        reads = list(reads) + ["__phase__"]
        o = Op(eng, fn, True)
        o.idx = len(self.ops[eng])
        n = self.dma_count[eng]
        self.dma_count[eng] += 1
        o.dma_slot = n % DMA_SEMS
        o.dma_val = 16 * (n // DMA_SEMS + 1)
        if n >= DMA_SEMS:
            o.prewait = self.dma_hist[eng][n - DMA_SEMS]
        self.dma_hist[eng].append(o)
        self.ops[eng].append(o)
        self._deps(o, reads, writes)
        if is_output:
            self.all_dma_out.append(o)
        self.n_ops += 1
        return o

    def emit(self):
        nc = self.nc
        st = self.stack
        fin = Op("sp", None, False)
        fin.idx = len(self.ops["sp"])
        for o in self.all_dma_out:
            fin.waits.append(o)
        self.ops["sp"].append(fin)
        csem = {e: st.enter_context(nc.semaphore("c_" + e)) for e in ENGS}
        dsem = {e: [st.enter_context(nc.semaphore("d_%s_%d" % (e, i))) for i in range(DMA_SEMS)]
                for e in ENGS if self.dma_count[e] > 0}
        for e in ENGS:
            c = 0
            for o in self.ops[e]:
                if o.is_dma:
                    continue
                if o.marked:
                    c += 1
                    o.dma_val = c
        block = st.enter_context(nc.Block())
        prog = self

        def run(e, eng):
            seen = {}
            for o in prog.ops[e]:
                ws = list(o.waits)
                if o.prewait is not None:
                    ws.append(o.prewait)
                for d in ws:
                    if d.is_dma:
                        sem, val = dsem[d.eng][d.dma_slot], d.dma_val
                    else:
                        sem, val = csem[d.eng], d.dma_val
                    k = id(sem)
                    if seen.get(k, 0) >= val:
                        continue
                    seen[k] = val
                    eng.wait_ge(sem, val)
                if o.fn is None:
                    continue
                ins = o.fn(eng)
                if o.is_dma:
                    ins.then_inc(dsem[e][o.dma_slot], 16)
                elif o.marked:
                    ins.then_inc(csem[e], 1)

        @block.tensor
        def _(eng):
            run("pe", eng)

        @block.scalar
        def _(eng):
            run("act", eng)

        @block.vector
        def _(eng):
            run("dve", eng)

        @block.gpsimd
        def _(eng):
            run("pool", eng)

        @block.sync
        def _(eng):
            run("sp", eng)

    def close(self):
        self.stack.close()


F32R = mybir.dt.float32r

S = 2048
D = 1024
NT = 16
DEPTH = 2
WC = 256
XC = 2047
GW = 4096
ALPHA = (2 * DEPTH) ** 0.25
A0, B0, C0, D0, M0 = 0, 2048, 4352, 5632, 7680
OA, OB, OC, OD, OM = 0, 512, 1024, 1536, 2048
BROWS = [(0, 512), (512, 512), (1024, 512), (1536, 512), (2048, 256)]


def rel_bucket_np(rel):
    nb = 16
    max_exact = 8
    n = np.abs(rel)
    nf = np.maximum(n, 1).astype(np.float32)
    large = max_exact + (np.log(nf / max_exact) / np.float32(math.log(1024 / max_exact)) * (nb - max_exact)).astype(np.int32)
    large = np.minimum(large, nb - 1)
    return np.where(rel > 0, nb, 0) + np.where(n < max_exact, n, large)


def host_consts():
    c = {}
    c["ident"] = np.eye(128, dtype=np.float32)
    rel = np.arange(4096) - XC
    bkt = rel_bucket_np(rel)
    oh = np.zeros((32, 4096), np.float32)
    oh[bkt, np.arange(4096)] = 1.0
    c["onehot"] = oh
    n = np.abs(rel)
    mA = (n <= 64).astype(np.float32) + ((rel % 4 == 0) & (n <= 256)) + ((rel % 16 == 0) & (n <= 1024))
    mt = np.ones((12, 4096), np.float32)
    mt[:8] = mA[None, :]
    c["multab"] = mt
    t = np.arange(S)
    row = (t // 64).astype(np.float32)
    col = (t % 64).astype(np.float32)
    freqs = (10000.0 ** (-(np.arange(16, dtype=np.float32) / 16))).astype(np.float32)
    ar = row[:, None] * freqs[None, :]
    ac = col[:, None] * freqs[None, :]
    c["ropec"] = np.concatenate([np.cos(ar), np.cos(ar), np.cos(ac), np.cos(ac)], 1).astype(np.float32)
    c["ropes"] = np.concatenate([-np.sin(ar), np.sin(ar), -np.sin(ac), np.sin(ac)], 1).astype(np.float32)
    tri = np.zeros((2, 3, 128, 128), np.float32)
    sg = np.arange(128)[:, None]
    tt = np.arange(128)[None, :]
    same = (sg // 64) == (tt // 64)
    tri[0, 0] = same & (sg <= tt)
    tri[0, 1] = same & (sg < tt)
    tri[0, 2] = same & (sg > tt)
    tri[1, 0] = same & (sg >= tt)
    tri[1, 1] = same & (sg > tt)
    tri[1, 2] = same & (sg < tt)
    c["tri"] = tri.reshape(6 * 128, 128)
    mk_ = np.zeros((2, 64, 192), np.float32)
    a = np.arange(64)[:, None]
    b = np.arange(64)[None, :]
    mk_[0, :, 0:64] = a < b
    mk_[0, :, 64:128] = a <= b
    mk_[0, :, 128:192] = b < a
    mk_[1, :, 0:64] = a > b
    mk_[1, :, 64:128] = a >= b
    mk_[1, :, 128:192] = b > a
    c["rmask"] = mk_.reshape(128, 192)
    return c


IN_SPECS = [("ln_in_g", [1, D]), ("ln_in_b", [1, D]), ("rel_bias", [32, 12]), ("w_in", [DEPTH * D, 8192]),
            ("shift_mu", [DEPTH * 2, 1792]), ("rwkv_w0", [DEPTH * 2, 512]), ("rwkv_w_up", [DEPTH * 2 * 64, 512]),
            ("rwkv_a0", [DEPTH * 2, 512]), ("rwkv_a_up", [DEPTH * 2 * 64, 512]), ("rwkv_k_k", [DEPTH, 512]),
            ("rwkv_k_a", [DEPTH, 512]), ("rwkv_r_k", [DEPTH, 512]), ("rwkv_ln_g", [DEPTH, 512]),
            ("rwkv_ln_b", [DEPTH, 512]), ("c_qnorm_g", [DEPTH, 64]), ("c_knorm_g", [DEPTH, 64]),
            ("d_lambda", [DEPTH, 256]), ("d_subln_g", [DEPTH, 128]), ("w_mem_kv", [DEPTH * D, 512]),
            ("w_branch", [DEPTH * 2304, D]), ("w_gate", [DEPTH * D, 5120]), ("b_gate", [DEPTH, 5120]),
            ("w_out", [DEPTH * D, D]), ("ln_g", [DEPTH, D]), ("ln_b", [DEPTH, D]),
            ("ident", [128, 128]), ("onehot", [32, 4096]), ("multab", [12, 4096]), ("ropec", [S, 64]),
            ("ropes", [S, 64]), ("tri", [768, 128]), ("rmask", [128, 192])]


class KB:
    def __init__(self, debug=False, mixers="MCADB", layers=DEPTH):
        self.debug = debug
        self.mixers = mixers
        nc = bass.Bass("TRN2", target_bir_lowering=False)
        self.nc = nc
        P = Prog(nc)
        self.P = P
        I = {}
        I["x"] = nc.dram_tensor("x", [S, D], F32, kind="ExternalInput").ap()
        I["mem"] = nc.dram_tensor("mem", [256, D], F32, kind="ExternalInput").ap()
        for nm, shp in IN_SPECS:
            I[nm] = nc.dram_tensor(nm, list(shp), F32, kind="ExternalInput").ap()
        self.I = I
        self.out = nc.dram_tensor("out", [S, D], F32, kind="ExternalOutput").ap()
        self.hres = [nc.dram_tensor("hres%d" % i, [S, D], F32, kind="ExternalOutput" if debug else "Internal").ap() for i in range(2)]
        self.dbg_outs = []
        self.o_scr = nc.dram_tensor("o_scr", [S, 2304], F32, kind="ExternalOutput" if debug else "Internal").ap()
        self.xtab = nc.dram_tensor("xtab", [12, 4096], F32).ap()
        self.rk_scr = nc.dram_tensor("rk_scr", [1024, S], F32).ap()
        self.wa_scr = nc.dram_tensor("wa_scr", [256, S], F32).ap()
        self.v_scr = nc.dram_tensor("v_scr", [S, 512], F32).ap()
        self.y_scr = nc.dram_tensor("y_scr", [2 * S, 520], F32, kind="ExternalOutput" if debug else "Internal").ap()
        self.sg_scr = nc.dram_tensor("sg_scr", [S, 512], F32).ap()
        self.ident = P.sb("ident", [128, 128], F32)
        self.hT = P.sb("hT", [128, 8, S + 2], BF16)
        self.BIG = [P.sb("BIG%d" % i, [128, 9216], BF16) for i in range(4)]
        self.G = P.sb("G", [128, GW], F32)
        self.wst = [P.sb("wst%d" % i, [128, 8, WC], F32) for i in range(1)]
        self.RX = P.sb("RX", [64, 3072], F32)
        self.wbf = [P.sb("wbf%d" % i, [128, 8, WC], BF16) for i in range(4)]
        self.ropec = P.sb("ropec", [128, NT, 64], F32)
        self.ropes = P.sb("ropes", [128, NT, 64], F32)
        self.lnx = [P.sb("lnx%d" % i, [128, D], F32) for i in range(2)]
        self.junk = P.sb("junk", [128, D], F32)
        self.lng = P.sb("lng", [128, D], F32)
        self.lnb = P.sb("lnb", [128, D], F32)
        self.pt = [P.sb("pt%d" % i, [128, 512], BF16) for i in range(4)]
        self.pe_ = [P.sb("pe%d" % i, [128, 512], BF16) for i in range(4)]
        self.ost = [P.sb("ost%d" % i, [128, 128], F32) for i in range(4)]
        self.sm = P.sb("sm", [128, 64], F32)
        self.tmp = [P.sb("tmp%d" % i, [128, 512], F32) for i in range(3)]
        self.onesf = P.sb("onesf", [128, 128], F32)
        self.gq = P.sb("gq", [128, 128], F32)
        self.subg = P.sb("subg", [128, 128], F32)
        self.lamt = P.sb("lamt", [128, 264], F32)
        self.pbar = P.sb("pbar", [1, 8], F32)
        self.memT = P.sb("memT", [128, 8, 256], BF16)
        self.bank = [P.ps("bank%d" % i, [128, 512], F32) for i in range(8)]
        self.cnt = {}
        self.pbi = 0
        B0_, B1_, B2_, B3_ = [b[:] for b in self.BIG]
        self.qT = B0_[:, 0:8192].rearrange("p (c t) -> p c t", t=S)
        self.kT = B1_[:, 0:8192].rearrange("p (c t) -> p c t", t=S)
        self.sg = B3_[:, 0:8192].rearrange("p (t c) -> p t c", c=512)
        self.prelude()
        for l in range(layers):
            self.layer(l, last=(l == layers - 1))
        P.emit()
        P.close()

    def nxt(self, name, n):
        v = self.cnt.get(name, 0)
        self.cnt[name] = (v + 1) % n
        return v

    def bk(self, i):
        return "bank%d" % i

    def pbank(self):
        self.pbi ^= 1
        return 2 + self.pbi

    def barrier(self):
        pbar = self.pbar
        self.P.barrier(lambda e: e.memset(pbar[:], 0.0))

    def R(self, ap):
        if ap.dtype == F32 and ap.name == "s_RX":
            return ap.bitcast(F32R)
        return ap

    def mm(self, out, lhsT, rhs, start, stop, reads, writes):
        if lhsT.name == "s_RX" and rhs.name == "s_RX":
            lhsT, rhs = self.R(lhsT), self.R(rhs)
        self.P.op("pe", lambda e: e.matmul(out, lhsT=lhsT, rhs=rhs, start=start, stop=stop), reads=reads, writes=writes)

    def tr(self, out, in_, reads, writes, np_=128):
        ident = self.ident
        self.P.op("pe", lambda e: e.transpose(out, in_, ident[0:np_, 0:np_]), reads=list(reads) + ["ident"], writes=writes)

    def cp(self, eng, out, in_, reads, writes):
        out = self.R(out)
        if eng == "act":
            self.P.op("act", lambda e: e.copy(out=out, in_=in_), reads=reads, writes=writes)
        else:
            self.P.op(eng, lambda e: e.tensor_copy(out=out, in_=in_), reads=reads, writes=writes)

    def act(self, out, in_, func, reads, writes, **kw):
        out = self.R(out)
        self.P.op("act", lambda e: e.activation(out=out, in_=in_, func=func, **kw), reads=reads, writes=writes)

    def tt(self, eng, out, in0, in1, op, reads, writes):
        out = self.R(out)
        self.P.op(eng, lambda e: e.tensor_tensor(out=out, in0=in0, in1=in1, op=op), reads=reads, writes=writes)

    def ts(self, eng, out, in0, s1, s2, op0, op1, reads, writes):
        out = self.R(out)
        if s2 is None:
            self.P.op(eng, lambda e: e.tensor_scalar(out=out, in0=in0, scalar1=s1, scalar2=None, op0=op0), reads=reads, writes=writes)
        else:
            self.P.op(eng, lambda e: e.tensor_scalar(out=out, in0=in0, scalar1=s1, scalar2=s2, op0=op0, op1=op1), reads=reads, writes=writes)

    def stt(self, eng, out, in0, scalar, in1, op0, op1, reads, writes):
        out = self.R(out)
        self.P.op(eng, lambda e: e.scalar_tensor_tensor(out=out, in0=in0, scalar=scalar, in1=in1, op0=op0, op1=op1), reads=reads, writes=writes)

    def memset(self, eng, ap, val, writes):
        ap = self.R(ap)
        self.P.op(eng, lambda e: e.memset(ap, val), writes=writes)

    def rsqrt_cols(self, src, dst, scale, eps, key="sm"):
        self.ts("dve", dst, src, scale, eps, ALU.mult, ALU.add, [key], [key])
        self.P.op("act", lambda e: e.sqrt(out=dst, in_=dst), reads=[key], writes=[key])
        self.P.op("dve", lambda e: e.reciprocal(out=dst, in_=dst), reads=[key], writes=[key])

    def wload(self, src2d, n, kc=8, variants=None):
        P = self.P
        src = src2d.rearrange("(c p) n -> p c n", p=128)
        if variants is None:
            j = self.nxt("wb", 4)
            P.dma("pool", self.wbf[j][:, 0:kc, 0:n], src, writes=["wbf%d" % j])
            return [(self.wbf[j], "wbf%d" % j)]
        wst = self.wst[0]
        P.dma("sp", wst[:, 0:kc, 0:n], src, writes=["wst0"])
        res = []
        for vi, (vap, vkey) in enumerate(variants):
            j = self.nxt("wb", 4)
            self.tt("dve" if vi != 1 else "pool", self.wbf[j][:, 0:kc, 0:n], wst[:, 0:kc, 0:n], vap.unsqueeze(1).broadcast_to([128, kc, n]), ALU.mult,
                    ["wst0", vkey], ["wbf%d" % j])
            res.append((self.wbf[j], "wbf%d" % j))
        return res

    def proj_T(self, wl, n, consume, shifts=(0,), rhs_fn=None, rkey="hT", ntb=4, tbw=512):
        hT = self.hT
        for ct in range(n // 128):
            for tb in range(ntb):
                b = self.pbank()
                nmm = 8 * len(shifts)
                m = 0
                for (wap, wkey), s in zip(wl, shifts):
                    for c in range(8):
                        if rhs_fn is None:
                            lo = 1 + tb * 512 + s
                            rhs = hT[:, c, lo:lo + 512]
                        else:
                            rhs = rhs_fn(c, tb)
                        self.mm(self.bank[b][:, 0:tbw], wap[:, c, ct * 128:(ct + 1) * 128], rhs, m == 0, m == nmm - 1,
                                [wkey, rkey], [self.bk(b)])
                        m += 1
                consume(b, ct, tb)

    def proj_N(self, wl, n, consume, shifts=(0,), lhs_fn=None, lkey="hT", ntt=NT, kc=8):
        hT = self.hT
        for tt in range(ntt):
            b = self.pbank()
            nmm = kc * len(shifts)
            m = 0
            for (wap, wkey), s in zip(wl, shifts):
                for c in range(kc):
                    if lhs_fn is None:
                        lo = 1 + tt * 128 + s
                        lh = hT[:, c, lo:lo + 128]
                    else:
                        lh = lhs_fn(c, tt)
                    self.mm(self.bank[b][:, 0:n], lh, wap[:, c, 0:n], m == 0, m == nmm - 1, [wkey, lkey], [self.bk(b)])
                    m += 1
            consume(b, tt)

    def ln_inplace(self, xt, xkey, eps=1e-5):
        sm, junk = self.sm, self.junk
        P = self.P
        P.op("dve", lambda e: e.reduce_sum(out=sm[:, 0:1], in_=xt, axis=AX.X), reads=[xkey], writes=["sm"])
        self.ts("dve", sm[:, 1:2], sm[:, 0:1], -1.0 / D, None, ALU.mult, None, ["sm"], ["sm"])
        self.ts("dve", xt, xt, sm[:, 1:2], None, ALU.add, None, [xkey, "sm"], [xkey])
        self.memset("dve", sm[:, 2:3], 0.0, ["sm"])
        self.act(junk[:], xt, AF.Square, [xkey, "sm"], ["junk", "sm"], accum_out=sm[:, 2:3])
        self.rsqrt_cols(sm[:, 2:3], sm[:, 3:4], 1.0 / D, eps)
        self.stt("dve", xt, xt, sm[:, 3:4], self.lng[:], ALU.mult, ALU.mult, [xkey, "sm", "lng"], [xkey])
        self.tt("dve", xt, xt, self.lnb[:], ALU.add, [xkey, "lnb"], [xkey])

    def to_hT(self, src, skey, tt):
        hT = self.hT
        for half in range(2):
            b = self.pbank()
            for c4 in range(4):
                c = half * 4 + c4
                self.tr(self.bank[b][:, c4 * 128:(c4 + 1) * 128], src[:, c * 128:(c + 1) * 128], [skey], [self.bk(b)])
            self.cp("act" if half else "dve", hT[:, half * 4:half * 4 + 4, 1 + tt * 128:1 + (tt + 1) * 128],
                    self.bank[b][:, :].rearrange("p (c t) -> p c t", t=128), [self.bk(b)], ["hT"])

    def prelude(self):
        P, I = self.P, self.I
        P.dma("sp", self.ident[:], I["ident"], writes=["ident"])
        P.dma("sp", self.ropec[:], I["ropec"].rearrange("(t p) c -> p t c", p=128), writes=["ropec"])
        P.dma("sp", self.ropes[:], I["ropes"].rearrange("(t p) c -> p t c", p=128), writes=["ropes"])
        self.memset("pool", self.onesf[:], 1.0, ["onesf"])
        self.memset("pool", self.hT[:, :, 0:1], 0.0, ["hT"])
        self.memset("pool", self.hT[:, :, S + 1:S + 2], 0.0, ["hT"])
        tmpA = self.tmp[0]
        rb = tmpA[0:32, 0:12]
        P.dma("sp", rb, I["rel_bias"], writes=["tmp0"])
        ohs = self.BIG[0][:].bitcast(F32)
        P.dma("sp", ohs[0:32, 0:4096], I["onehot"], writes=["BIG0"])
        mts = self.BIG[1][:].bitcast(F32)
        P.dma("sp", mts[0:12, 0:4096], I["multab"], writes=["BIG1"])
        xts = self.BIG[2][:].bitcast(F32)
        for j in range(8):
            b = self.pbank()
            self.mm(self.bank[b][0:12, :], rb, ohs[0:32, j * 512:(j + 1) * 512], True, True, ["tmp0", "BIG0"], [self.bk(b)])
            self.act(xts[0:12, j * 512:(j + 1) * 512], self.bank[b][0:12, :], AF.Exp, [self.bk(b)], ["BIG2"])
        self.tt("dve", xts[0:12, 0:4096], xts[0:12, 0:4096], mts[0:12, 0:4096], ALU.mult, ["BIG2", "BIG1"], ["BIG2"])
        P.dma("sp", self.xtab, xts[0:12, 0:4096], reads=["BIG2"], writes=["xtab"])
        self.barrier()
        for mt_ in range(2):
            i = self.nxt("ln", 2)
            P.dma("sp", self.lnx[i][:], I["mem"][mt_ * 128:(mt_ + 1) * 128, :], writes=["lnx%d" % i])
            for half in range(2):
                b = self.pbank()
                for c4 in range(4):
                    c = half * 4 + c4
                    self.tr(self.bank[b][:, c4 * 128:(c4 + 1) * 128], self.lnx[i][:, c * 128:(c + 1) * 128], ["lnx%d" % i], [self.bk(b)])
                self.cp("dve", self.memT[:, half * 4:half * 4 + 4, mt_ * 128:(mt_ + 1) * 128],
                        self.bank[b][:, :].rearrange("p (c t) -> p c t", t=128), [self.bk(b)], ["memT"])
        P.dma("sp", self.lng[:], I["ln_in_g"].partition_broadcast(128), writes=["lng"])
        P.dma("sp", self.lnb[:], I["ln_in_b"].partition_broadcast(128), writes=["lnb"])
        for tt in range(NT):
            i = self.nxt("ln", 2)
            P.dma("sp", self.lnx[i][:], I["x"][tt * 128:(tt + 1) * 128, :], writes=["lnx%d" % i])
            self.ln_inplace(self.lnx[i][:], "lnx%d" % i)
            P.dma("sp", self.hres[0][tt * 128:(tt + 1) * 128, :], self.lnx[i][:], reads=["lnx%d" % i], writes=["hres0"])
        self.barrier()

    def layer(self, l, last):
        P, I = self.P, self.I
        hin = self.hres[l % 2]
        for tt in range(NT):
            i = self.nxt("ln", 2)
            P.dma("sp", self.lnx[i][:], hin[tt * 128:(tt + 1) * 128, :], reads=["hres%d" % (l % 2)], writes=["lnx%d" % i])
            self.to_hT(self.lnx[i], "lnx%d" % i, tt)
        self.win = I["w_in"][l * D:(l + 1) * D, :]
        for mx in "MCADB":
            if mx in self.mixers:
                getattr(self, "mixer_" + mx)(l)
            else:
                self.zero_o(mx)
            self.barrier()
        self.merge(l, last)
        self.barrier()

    def zero_o(self, mx):
        c0, w = {"M": (OM, 256), "C": (OC, 512), "A": (OA, 512), "D": (OD, 512), "B": (OB, 512)}[mx]
        t = self.tmp[2]
        self.memset("pool", t[:, :], 0.0, ["tmp2"])
        for tt in range(NT):
            self.P.dma("sp", self.o_scr[tt * 128:(tt + 1) * 128, c0:c0 + w], t[:, 0:w], reads=["tmp2"], writes=["o_scr"])

    def attn_head(self, maps, vfn, vkey, nkt, dv1, table, band, post):
        nm = len(maps)
        G = self.G

        nqt = 4 if nm == 1 else 2
        QB = nqt * 128

        def accap(m, qt):
            bi = 4 + m * nqt + qt
            return self.bank[bi][:, 0:dv1], bi

        for qb in range(S // QB):
            q0 = qb * QB
            kts = []
            for kt in range(nkt):
                dk = kt * 128 - q0
                if band and (dk - (QB - 1) > 1024 or dk + 127 < -1024):
                    continue
                kts.append(kt)
            steps = [(idx, kt, m) for idx, kt in enumerate(kts) for m in range(nm)]

            def stageA(si):
                idx, kt, m = steps[si]
                q_ap, kfn = maps[m]
                sb_ = si % 4
                self.mm(self.bank[sb_][:, 0:QB], kfn(kt), q_ap[:, q0:q0 + QB], True, True, ["BIG0", "BIG1"], [self.bk(sb_)])

            def stageBC(si):
                idx, kt, m = steps[si]
                sb_ = si % 4
                pti = self.nxt("pt", 4)
                ptile = self.pt[pti]
                if table:
                    pei = self.nxt("pe", 4)
                    self.act(self.pe_[pei][:, 0:QB], self.bank[sb_][:, 0:QB], AF.Exp, [self.bk(sb_)], ["pe%d" % pei], scale=0.125)
                    j0 = kt * 128 - q0 + XC
                    gs = G[:, j0 - (QB - 1):j0 + 1][:, ::-1]
                    self.tt("dve", ptile[:, 0:QB], self.pe_[pei][:, 0:QB], gs, ALU.mult, ["pe%d" % pei, "G"], ["pt%d" % pti])
                else:
                    self.act(ptile[:, 0:QB], self.bank[sb_][:, 0:QB], AF.Exp, [self.bk(sb_)], ["pt%d" % pti], scale=0.125)
                for qt in range(nqt):
                    acc, bi = accap(m, qt)
                    self.mm(acc, ptile[:, qt * 128:(qt + 1) * 128], vfn(kt), idx == 0, idx == len(kts) - 1,
                            ["pt%d" % pti, vkey], [self.bk(bi)])

            PF = 3
            for si in range(min(PF, len(steps))):
                stageA(si)
            for si in range(len(steps)):
                if si + PF < len(steps):
                    stageA(si + PF)
                stageBC(si)
            for qt in range(nqt):
                accs = [accap(m, qt) for m in range(nm)]
                post(qb * nqt + qt, [a for a, _ in accs], [self.bk(bi) for _, bi in accs])

    def post_simple(self, h, hd, ocol):
        def post(tt, accs, keys):
            acc = accs[0]
            sm = self.sm
            i = self.nxt("ost", 4)
            self.P.op("dve", lambda e: e.reciprocal(out=sm[:, 8:9], in_=acc[:, hd:hd + 1]), reads=keys, writes=["sm"])
            self.stt("dve", self.ost[i][:, 0:hd], acc[:, 0:hd], sm[:, 8:9], self.sg[:, tt, h * hd:(h + 1) * hd], ALU.mult, ALU.mult,
                     keys + ["sm", "BIG3"], ["ost%d" % i])
            self.P.dma("sp", self.o_scr[tt * 128:(tt + 1) * 128, ocol + h * hd:ocol + (h + 1) * hd], self.ost[i][:, 0:hd],
                       reads=["ost%d" % i], writes=["o_scr"])
        return post

    def cons_T(self, dst, dkey):
        def consume(b, ct, tb):
            self.cp("dve" if (ct + tb) % 2 else "act", dst[:, ct, tb * 512:(tb + 1) * 512], self.bank[b][:, :], [self.bk(b)], [dkey])
        return consume

    def cons_sg(self, c0, n):
        def consume(b, tt):
            self.act(self.sg[:, tt, c0:c0 + n], self.bank[b][:, 0:n], AF.Silu, [self.bk(b)], ["BIG3"])
        return consume

    def cons_v(self, V1, h0, nh, hd):
        def consume(b, tt):
            self.cp("dve", V1[:, tt, h0:h0 + nh, 0:hd], self.bank[b][:, 0:nh * hd].rearrange("p (h d) -> p h d", d=hd), [self.bk(b)], ["BIG2"])
        return consume

    def mixer_M(self, l):
        P, I = self.P, self.I
        wkv = I["w_mem_kv"][l * D:(l + 1) * D, :]
        V1 = self.BIG[2][:, 0:520].rearrange("p (k h d) -> p k h d", k=2, h=4, d=65)
        self.memset("pool", self.BIG[2][:, 0:520], 1.0, ["BIG2"])
        memT = self.memT
        wl = self.wload(wkv[:, 0:256], 256)
        self.proj_T(wl, 256, lambda b, ct, tb: self.cp("dve", self.kT[:, ct, 0:256], self.bank[b][:, 0:256], [self.bk(b)], ["BIG1"]),
                    rhs_fn=lambda c, tb: memT[:, c, 0:256], rkey="memT", ntb=1, tbw=256)
        wl = self.wload(wkv[:, 256:512], 256)
        self.proj_N(wl, 256, self.cons_v(V1, 0, 4, 64), lhs_fn=lambda c, tt: memT[:, c, tt * 128:(tt + 1) * 128], lkey="memT", ntt=2)
        wl = self.wload(self.win[:, M0:M0 + 256], 256)
        self.proj_T(wl, 256, self.cons_T(self.qT, "BIG0"))
        wl = self.wload(self.win[:, M0 + 256:M0 + 512], 256)
        self.proj_N(wl, 256, self.cons_sg(0, 256))
        if l == 0 and "m" in self.mixers:
            self.dbg("qT", self.BIG[0][:, 0:8192], ["BIG0"], BF16)
            self.dbg("kT", self.BIG[1][:, 0:8192], ["BIG1"], BF16)
            self.dbg("V1", self.BIG[2][:, 0:520], ["BIG2"], BF16)
            self.dbg("sg", self.BIG[3][:, 0:8192], ["BIG3"], BF16)
            self.dbg("hT", self.hT[:, :, :].rearrange("p c t -> p (c t)"), ["hT"], BF16)
        for h in range(4):
            ct, pb = h // 2, (h % 2) * 64
            q_ap = self.qT[pb:pb + 64, ct, :]
            self.attn_head([(q_ap, lambda kt, ct=ct, pb=pb: self.kT[pb:pb + 64, ct, kt * 128:(kt + 1) * 128])],
                           lambda kt, h=h: V1[:, kt, h, :], "BIG2", 2, 65, False, False, self.post_simple(h, 64, OM))

    def mixer_A(self, l):
        P = self.P
        V1 = self.BIG[2][:, 0:8320].rearrange("p (k h d) -> p k h d", k=16, h=8, d=65)
        self.memset("pool", self.BIG[2][:, 0:8320], 1.0, ["BIG2"])
        for j in range(2):
            wl = self.wload(self.win[:, A0 + j * 256:A0 + (j + 1) * 256], 256)
            self.proj_T(wl, 256, lambda b, ct, tb, j=j: self.cons_T(self.qT, "BIG0")(b, ct + 2 * j, tb))
        for j in range(2):
            wl = self.wload(self.win[:, A0 + 512 + j * 256:A0 + 512 + (j + 1) * 256], 256)
            self.proj_T(wl, 256, lambda b, ct, tb, j=j: self.cons_T(self.kT, "BIG1")(b, ct + 2 * j, tb))
        for j in range(2):
            wl = self.wload(self.win[:, A0 + 1024 + j * 256:A0 + 1024 + (j + 1) * 256], 256)
            self.proj_N(wl, 256, self.cons_v(V1, 4 * j, 4, 64))
        for j in range(2):
            wl = self.wload(self.win[:, A0 + 1536 + j * 256:A0 + 1536 + (j + 1) * 256], 256)
            self.proj_N(wl, 256, self.cons_sg(j * 256, 256))
        for h in range(8):
            ct, pb = h // 2, (h % 2) * 64
            P.dma("sp", self.G[:, 0:3968], AP(self.xtab.tensor, h * 4096, [[1, 128], [1, 3968]]), reads=["xtab"], writes=["G"])
            q_ap = self.qT[pb:pb + 64, ct, :]
            self.attn_head([(q_ap, lambda kt, ct=ct, pb=pb: self.kT[pb:pb + 64, ct, kt * 128:(kt + 1) * 128])],
                           lambda kt, h=h: V1[:, kt, h, :], "BIG2", 16, 65, True, True, self.post_simple(h, 64, OA))

    def normrope(self, b, tt, nh, gcol):
        n = nh * 64
        sm = self.sm
        t0, t1, t2 = self.tmp
        ps = self.bank[b][:, 0:n]
        self.act(t0[:, 0:n], ps, AF.Square, [self.bk(b)], ["tmp0"])
        self.P.op("dve", lambda e: e.reduce_sum(out=sm[:, 16:16 + nh], in_=t0[:, 0:n].rearrange("p (h d) -> p h d", d=64), axis=AX.X),
                  reads=["tmp0"], writes=["sm"])
        self.rsqrt_cols(sm[:, 16:16 + nh], sm[:, 24:24 + nh], 1.0 / 64, 1e-6)
        v3 = lambda ap: ap.rearrange("p (h d) -> p h d", d=64)
        self.tt("dve", v3(t0[:, 0:n]), v3(ps), sm[:, 24:24 + nh].unsqueeze(2).broadcast_to([128, nh, 64]), ALU.mult,
                [self.bk(b), "sm"], ["tmp0"])
        self.tt("dve", v3(t0[:, 0:n]), v3(t0[:, 0:n]), self.gq[:, gcol:gcol + 64].unsqueeze(1).broadcast_to([128, nh, 64]), ALU.mult,
                ["tmp0", "gq"], ["tmp0"])
        self.tt("pool", v3(t1[:, 0:n]), v3(t0[:, 0:n]), self.ropec[:, tt, :].unsqueeze(1).broadcast_to([128, nh, 64]), ALU.mult,
                ["tmp0", "ropec"], ["tmp1"])
        v5 = lambda ap: ap.rearrange("p (h a b c) -> p h a b c", a=2, b=2, c=16)
        rs = self.ropes[:, tt, :].rearrange("p (a b c) -> p a b c", a=2, b=2, c=16)
        for bb in range(2):
            self.tt("dve", v5(t2[:, 0:n])[:, :, :, bb, :], v5(t0[:, 0:n])[:, :, :, 1 - bb, :],
                    rs[:, :, bb, :].unsqueeze(1).broadcast_to([128, nh, 2, 16]), ALU.mult, ["tmp0", "ropes"], ["tmp2"])
        self.tt("dve", t1[:, 0:n], t1[:, 0:n], t2[:, 0:n], ALU.add, ["tmp1", "tmp2"], ["tmp1"])

    def mixer_C(self, l):
        P, I = self.P, self.I
        V1 = self.BIG[2][:, 0:2080].rearrange("p (k h d) -> p k h d", k=16, h=2, d=65)
        self.memset("pool", self.BIG[2][:, 0:2080], 1.0, ["BIG2"])
        P.dma("sp", self.gq[:, 0:64], I["c_qnorm_g"][l:l + 1, :].partition_broadcast(128), writes=["gq"])
        P.dma("sp", self.gq[:, 64:128], I["c_knorm_g"][l:l + 1, :].partition_broadcast(128), writes=["gq"])
        t1 = self.tmp[1]
        for j in range(2):
            wl = self.wload(self.win[:, C0 + j * 256:C0 + (j + 1) * 256], 256)

            def cons_q(b, tt, j=j):
                self.normrope(b, tt, 4, 0)
                b2 = self.pbank()
                for u in range(2):
                    self.tr(self.bank[b2][:, u * 128:(u + 1) * 128], t1[:, u * 128:(u + 1) * 128], ["tmp1"], [self.bk(b2)])
                self.cp("act", self.qT[:, 2 * j:2 * j + 2, tt * 128:(tt + 1) * 128],
                        self.bank[b2][:, 0:256].rearrange("p (c t) -> p c t", t=128), [self.bk(b2)], ["BIG0"])
            self.proj_N(wl, 256, cons_q)
        wl = self.wload(self.win[:, C0 + 512:C0 + 768], 256)

        def cons_kv(b, tt):
            self.cp("act", V1[:, tt, :, 0:64], self.bank[b][:, 128:256].rearrange("p (h d) -> p h d", d=64), [self.bk(b)], ["BIG2"])
            self.normrope(b, tt, 2, 64)
            t2 = self.tmp[2]
            self.cp("dve", t2[:, 0:256].rearrange("p (g r d) -> p g r d", g=2, r=2, d=64),
                    t1[:, 0:128].rearrange("p (g d) -> p g d", d=64).unsqueeze(2).broadcast_to([128, 2, 2, 64]), ["tmp1"], ["tmp2"])
            b2 = self.pbank()
            for u in range(2):
                self.tr(self.bank[b2][:, u * 128:(u + 1) * 128], t2[:, u * 128:(u + 1) * 128], ["tmp2"], [self.bk(b2)])
            self.cp("act", self.kT[:, 0:2, tt * 128:(tt + 1) * 128],
                    self.bank[b2][:, 0:256].rearrange("p (c t) -> p c t", t=128), [self.bk(b2)], ["BIG1"])
        self.proj_N(wl, 256, cons_kv)
        for j in range(2):
            wl = self.wload(self.win[:, C0 + 768 + j * 256:C0 + 768 + (j + 1) * 256], 256)
            self.proj_N(wl, 256, self.cons_sg(j * 256, 256))
        for h in range(8):
            ct, pb = h // 2, (h % 2) * 64
            g = h // 4
            q_ap = self.qT[pb:pb + 64, ct, :]
            self.attn_head([(q_ap, lambda kt, g=g, pb=pb: self.kT[pb:pb + 64, g, kt * 128:(kt + 1) * 128])],
                           lambda kt, g=g: V1[:, kt, g, :], "BIG2", 16, 65, False, False, self.post_simple(h, 64, OC))

    def mixer_D(self, l):
        P, I = self.P, self.I
        V1 = self.BIG[2][:, 0:8256].rearrange("p (k h d) -> p k h d", k=16, h=4, d=129)
        self.memset("pool", self.BIG[2][:, 0:8256], 1.0, ["BIG2"])
        lam_init = 0.8 - 0.6 * math.exp(-0.3 * l)
        lamt, sm = self.lamt, self.sm
        P.dma("sp", lamt[:, 0:256], I["d_lambda"][l:l + 1, :].partition_broadcast(128), writes=["lamt"])
        P.dma("sp", self.subg[:], I["d_subln_g"][l:l + 1, :].partition_broadcast(128), writes=["subg"])
        self.ts("pool", self.subg[:], self.subg[:], 1.0 - lam_init, None, ALU.mult, None, ["subg"], ["subg"])
        lv = lamt[:, 0:256].rearrange("p (a b c) -> p a b c", a=2, b=2, c=64)
        lp = self.tmp[2][:, 0:128].rearrange("p (a c) -> p a c", c=64)
        self.tt("dve", lp, lv[:, :, 0, :], lv[:, :, 1, :], ALU.mult, ["lamt"], ["tmp2"])
        P.op("dve", lambda e: e.reduce_sum(out=lamt[:, 256:258], in_=lp, axis=AX.X), reads=["tmp2"], writes=["lamt"])
        self.act(lamt[:, 258:260], lamt[:, 256:258], AF.Exp, ["lamt"], ["lamt"])
        self.tt("dve", lamt[:, 260:261], lamt[:, 259:260], lamt[:, 258:259], ALU.subtract, ["lamt"], ["lamt"])
        self.ts("dve", lamt[:, 260:261], lamt[:, 260:261], -lam_init, None, ALU.add, None, ["lamt"], ["lamt"])
        for j in range(2):
            wl = self.wload(self.win[:, D0 + j * 256:D0 + (j + 1) * 256], 256)
            self.proj_T(wl, 256, lambda b, ct, tb, j=j: self.cons_T(self.qT, "BIG0")(b, ct + 2 * j, tb))
        for j in range(2):
            wl = self.wload(self.win[:, D0 + 512 + j * 256:D0 + 512 + (j + 1) * 256], 256)
            self.proj_T(wl, 256, lambda b, ct, tb, j=j: self.cons_T(self.kT, "BIG1")(b, ct + 2 * j, tb))
        for j in range(2):
            wl = self.wload(self.win[:, D0 + 1024 + j * 256:D0 + 1024 + (j + 1) * 256], 256)
            self.proj_N(wl, 256, self.cons_v(V1, 2 * j, 2, 128))
        for j in range(2):
            wl = self.wload(self.win[:, D0 + 1536 + j * 256:D0 + 1536 + (j + 1) * 256], 256)
            self.proj_N(wl, 256, self.cons_sg(j * 256, 256))
        t0 = self.tmp[0]
        for h in range(4):
            P.dma("sp", self.G[:, 0:3968], AP(self.xtab.tensor, (8 + h) * 4096, [[1, 128], [1, 3968]]), reads=["xtab"], writes=["G"])

            def post(tt, accs, keys, h=h):
                a1, a2 = accs
                i = self.nxt("ost", 4)
                P.op("dve", lambda e: e.reciprocal(out=sm[:, 8:9], in_=a1[:, 128:129]), reads=keys, writes=["sm"])
                P.op("dve", lambda e: e.reciprocal(out=sm[:, 9:10], in_=a2[:, 128:129]), reads=keys, writes=["sm"])
                self.tt("dve", sm[:, 9:10], sm[:, 9:10], lamt[:, 260:261], ALU.mult, ["sm", "lamt"], ["sm"])
                self.ts("dve", t0[:, 0:128], a1[:, 0:128], sm[:, 8:9], None, ALU.mult, None, keys + ["sm"], ["tmp0"])
                self.stt("dve", t0[:, 128:256], a2[:, 0:128], sm[:, 9:10], t0[:, 0:128], ALU.mult, ALU.add, keys + ["sm", "tmp0"], ["tmp0"])
                self.memset("dve", sm[:, 10:11], 0.0, ["sm"])
                self.act(t0[:, 256:384], t0[:, 128:256], AF.Square, ["tmp0", "sm"], ["tmp0", "sm"], accum_out=sm[:, 10:11])
                self.rsqrt_cols(sm[:, 10:11], sm[:, 11:12], 1.0 / 128, 1e-5)
                self.stt("dve", t0[:, 128:256], t0[:, 128:256], sm[:, 11:12], self.subg[:], ALU.mult, ALU.mult, ["tmp0", "sm", "subg"], ["tmp0"])
                self.tt("dve", self.ost[i][:], t0[:, 128:256], self.sg[:, tt, h * 128:(h + 1) * 128], ALU.mult, ["tmp0", "BIG3"], ["ost%d" % i])
                P.dma("sp", self.o_scr[tt * 128:(tt + 1) * 128, OD + h * 128:OD + (h + 1) * 128], self.ost[i][:],
                      reads=["ost%d" % i], writes=["o_scr"])
            maps = [(self.qT[c * 64:(c + 1) * 64, h, :], (lambda kt, c=c, h=h: self.kT[c * 64:(c + 1) * 64, h, kt * 128:(kt + 1) * 128]))
                    for c in range(2)]
            self.attn_head(maps, lambda kt, h=h: V1[:, kt, h, :], "BIG2", 16, 129, True, False, post)

    def dbg(self, name, ap, reads, dt=F32):
        if not self.debug:
            return
        t = self.nc.dram_tensor("dbg_" + name, list(ap.shape), dt, kind="ExternalOutput").ap()
        self.P.dma("sp", t, ap, reads=reads, is_output=True)
        self.dbg_outs.append("dbg_" + name)

    def mixer_B(self, l):
        P, I = self.P, self.I
        CW = 0.6065306597126334
        t_ring = self.tmp
        mub = self.lnx[0][:, 0:768].rearrange("p (v n) -> p v n", n=256)

        def load_mu(c0):
            for v in range(2):
                P.dma("sp", mub[:, 1 + v, :], I["shift_mu"][l * 2 + v:l * 2 + v + 1, c0:c0 + 256].partition_broadcast(128), writes=["lnx0"])
            self.tt("dve", mub[:, 0, :], mub[:, 1, :], mub[:, 2, :], ALU.add, ["lnx0"], ["lnx0"])
            self.ts("dve", mub[:, 0, :], mub[:, 0, :], -1.0, 1.0, ALU.mult, ALU.add, ["lnx0"], ["lnx0"])
            return [(mub[:, 0, :], "lnx0"), (mub[:, 1, :], "lnx0"), (mub[:, 2, :], "lnx0")]

        def stage_out(dst_ap, dkey, func=None):
            def f(b, n_part=128, ncol=512):
                i = self.nxt("tmp", 3)
                if func is None:
                    self.cp("dve", t_ring[i][0:n_part, 0:ncol], self.bank[b][0:n_part, 0:ncol], [self.bk(b)], ["tmp%d" % i])
                else:
                    self.act(t_ring[i][0:n_part, 0:ncol], self.bank[b][0:n_part, 0:ncol], func, [self.bk(b)], ["tmp%d" % i])
                P.dma("sp", dst_ap, t_ring[i][0:n_part, 0:ncol], reads=["tmp%d" % i], writes=[dkey])
            return f

        for j in range(4):
            c0 = j * 256
            wl = self.wload(self.win[:, B0 + c0:B0 + c0 + 256], 256, variants=load_mu(c0))
            self.proj_T(wl, 256, lambda b, ct, tb, c0=c0: stage_out(self.rk_scr[c0 + ct * 128:c0 + (ct + 1) * 128, tb * 512:(tb + 1) * 512], "rk_scr")(b),
                        shifts=(0, -1, 1))
        for j in range(2):
            c0 = 1024 + j * 256
            wl = self.wload(self.win[:, B0 + c0:B0 + c0 + 256], 256, variants=load_mu(c0))
            self.proj_N(wl, 256, lambda b, tt, j=j: stage_out(self.v_scr[tt * 128:(tt + 1) * 128, j * 256:(j + 1) * 256], "v_scr")(b, 128, 256),
                        shifts=(0, -1, 1))
        wl = self.wload(self.win[:, B0 + 1536:B0 + 1792], 256, variants=load_mu(1536))
        self.proj_T(wl, 256, lambda b, ct, tb: stage_out(self.wa_scr[ct * 128:(ct + 1) * 128, tb * 512:(tb + 1) * 512], "wa_scr",
                                                         AF.Tanh if ct == 0 else AF.Copy)(b), shifts=(0, -1, 1))
        for j in range(2):
            wl = self.wload(self.win[:, B0 + 1792 + j * 256:B0 + 1792 + (j + 1) * 256], 256)
            self.proj_N(wl, 256, lambda b, tt, j=j: stage_out(self.sg_scr[tt * 128:(tt + 1) * 128, j * 256:(j + 1) * 256], "sg_scr", AF.Silu)(b, 128, 256))
        self.barrier()
        slots = []
        for bi in range(4):
            a = self.BIG[bi][:].bitcast(F32)
            for q in range(4):
                slots.append(a[:, q * 1024:(q + 1) * 1024])
        for q in range(4):
            slots.append(self.G[:, q * 1024:(q + 1) * 1024])
        for wi in range(1):
            a = self.wst[wi][:, :, :].rearrange("p c n -> p (c n)")
            for q in range(2):
                slots.append(a[:, q * 1024:(q + 1) * 1024])
        si = [0]

        def slot(full=True):
            if full:
                if si[0] % 2:
                    si[0] += 1
                a = slots[si[0] // 2]
                si[0] += 2
                return a
            a = slots[si[0] // 2][:, (si[0] % 2) * 512:(si[0] % 2) * 512 + 512]
            si[0] += 1
            return a

        def v3(ap, w):
            return ap[0:64, 0:8 * w].rearrange("p (h t) -> p h t", t=w)

        w_upS = slot()[0:64, :].rearrange("p (e c) -> p e c", c=512)
        a_upS = slot()[0:64, :].rearrange("p (e c) -> p e c", c=512)
        w0B = slot()[0:64, :].rearrange("p (e c) -> p e c", c=512)
        rkT = slot()[0:64, :].rearrange("p (g t) -> p g t", t=64)
        AR = slot()[0:64, :].rearrange("p (h t) -> p h t", t=128)
        NP = [self.RX[:, q * 1024:(q + 1) * 1024].rearrange("p (h t) -> p h t", t=128) for q in range(2)]
        ysb = slot()[0:64, 0:520]
        rmaskS = slot(False)[0:64, 0:384].rearrange("p (e n) -> p e n", n=192)
        waT = slot(False)[0:64, 0:256].rearrange("p (g t) -> p g t", t=64)
        vtok = slot(False)[0:64, :]
        sgw = slot(False)[0:64, :]
        asT, kkn, ke, be, tE0, tE1, bch, kch, z = [v3(slot(False), 64) for _ in range(9)]
        eLs, Bt, Kt = [slot(False)[0:64, :] for _ in range(3)]
        Mm = [self.RX[:, 2048 + q * 512:2048 + (q + 1) * 512].rearrange("p (h t) -> p h t", t=64) for q in range(2)]
        Mrb, Mak, Mrk, Xs, Us, tmpS = [v3(slot(False), 64) for _ in range(6)]
        Sst = [v3(slot(False), 64) for _ in range(2)]
        assert si[0] <= 2 * len(slots), si[0]
        rwp = self.gq[0:64, 0:40]
        omka = self.gq[0:64, 40:48]
        ident64 = self.ident[0:64, 0:64]
        ones64 = self.onesf[0:64, 0:64]
        self.r32 = True
        P.dma("sp", w_upS, I["rwkv_w_up"][l * 128:(l + 1) * 128, :].rearrange("(e r) c -> r e c", r=64), writes=["w_upS"])
        P.dma("sp", a_upS, I["rwkv_a_up"][l * 128:(l + 1) * 128, :].rearrange("(e r) c -> r e c", r=64), writes=["a_upS"])
        for e in range(2):
            P.dma("sp", w0B[:, e, :], I["rwkv_w0"][l * 2 + e:l * 2 + e + 1, :].partition_broadcast(64), writes=["w0B"])
        P.dma("sp", rmaskS, I["rmask"].rearrange("(e p) n -> p e n", p=64), writes=["rmaskS"])

        pm = self.tmp[0]
        P.dma("sp", pm[0:16, 0:64], I["rwkv_a0"][l * 2:(l + 1) * 2, :].rearrange("e (h c) -> (e h) c", c=64), writes=["tmp0"])
        P.dma("sp", pm[16:24, 0:64], I["rwkv_k_k"][l:l + 1, :].rearrange("e (h c) -> (e h) c", c=64), writes=["tmp0"])
        P.dma("sp", pm[24:32, 0:64], I["rwkv_k_a"][l:l + 1, :].rearrange("e (h c) -> (e h) c", c=64), writes=["tmp0"])
        P.dma("sp", pm[32:40, 0:64], I["rwkv_r_k"][l:l + 1, :].rearrange("e (h c) -> (e h) c", c=64), writes=["tmp0"])
        b = self.pbank()
        self.P.op("pe", lambda e_: e_.transpose(self.bank[b][0:64, 0:40], pm[0:40, 0:64], self.ident[0:40, 0:40]), reads=["tmp0", "ident"], writes=[self.bk(b)])
        self.cp("dve", rwp, self.bank[b][0:64, 0:40], [self.bk(b)], ["gq"])
        self.ts("dve", omka, rwp[:, 24:32], -1.0, 1.0, ALU.mult, ALU.add, ["gq"], ["gq"])
        bc3 = lambda ap: ap.unsqueeze(2).broadcast_to([64, 8, 64])
        hb = lambda b_, h, w=64: self.bank[b_][0:64, h * w:(h + 1) * w]
        b3 = lambda b_, w=64: self.bank[b_][0:64, 0:8 * w].rearrange("p (h t) -> p h t", t=w)

        for e in range(2):
            Scur = 0
            self.memset("dve", Sst[0], 0.0, ["S0"])
            order = range(32) if e == 0 else range(31, -1, -1)
            tl = 63 if e == 0 else 0
            mS, mI, mT = rmaskS[:, e, 0:64], rmaskS[:, e, 64:128], rmaskS[:, e, 128:192]
            for ch in order:
                t0 = ch * 64
                P.dma("sp", rkT, self.rk_scr.rearrange("(g p) t -> p g t", p=64)[:, :, t0:t0 + 64], reads=["rk_scr"], writes=["rkT"])
                P.dma("sp", waT, self.wa_scr.rearrange("(g p) t -> p g t", p=64)[:, :, t0:t0 + 64], reads=["wa_scr"], writes=["waT"])
                P.dma("sp", vtok, self.v_scr[t0:t0 + 64, :], reads=["v_scr"], writes=["vtok"])
                rT, kT_ = rkT[:, 0:8, :], rkT[:, 8:16, :]
                b = self.pbank()
                self.mm(self.bank[b][0:64, :], waT[:, e, :], w_upS[:, e, :], True, True, ["waT", "w_upS"], [self.bk(b)])
                self.tt("dve", sgw, self.bank[b][0:64, :], w0B[:, e, :], ALU.add, [self.bk(b), "w0B"], ["sgw"])
                self.act(sgw, sgw, AF.Sigmoid, ["sgw"], ["sgw"])
                b = self.pbank()
                for h in range(8):
                    self.mm(hb(b, h), a_upS[:, e, h * 64:(h + 1) * 64], waT[:, 2 + e, :], True, True, ["waT", "a_upS"], [self.bk(b)])
                self.tt("dve", asT, b3(b), bc3(rwp[:, e * 8:(e + 1) * 8]), ALU.add, [self.bk(b), "gq"], ["asT"])
                self.act(asT, asT, AF.Sigmoid, ["asT"], ["asT"])
                self.tt("dve", kkn, kT_, bc3(rwp[:, 16:24]), ALU.mult, ["rkT", "gq"], ["kkn"])
                self.act(tE0, kkn, AF.Square, ["kkn"], ["tE0"])
                b = self.pbank()
                self.mm(self.bank[b][0:64, :], ones64, tE0.rearrange("p h t -> p (h t)"), True, True, ["tE0", "onesf"], [self.bk(b)])
                self.act(tE0, b3(b), AF.Sqrt, [self.bk(b)], ["tE0"])
                self.ts("dve", tE0, tE0, 1e-12, None, ALU.max, None, ["tE0"], ["tE0"])
                self.P.op("dve", lambda e_: e_.reciprocal(out=tE0, in_=tE0), reads=["tE0"], writes=["tE0"])
                self.tt("dve", kkn, kkn, tE0, ALU.mult, ["kkn", "tE0"], ["kkn"])
                self.tt("pool", ke, asT, bc3(rwp[:, 24:32]), ALU.mult, ["asT", "gq"], ["ke"])
                self.tt("pool", ke, ke, bc3(omka), ALU.add, ["ke", "gq"], ["ke"])
                self.tt("pool", ke, ke, kT_, ALU.mult, ["ke", "rkT"], ["ke"])
                self.tt("pool", be, kkn, asT, ALU.mult, ["kkn", "asT"], ["be"])
                self.tt("pool", z, rT, ke, ALU.mult, ["rkT", "ke"], ["z"])
                bLi = self.pbank()
                for h in range(8):
                    self.mm(hb(bLi, h), sgw[:, h * 64:(h + 1) * 64], mI, True, True, ["sgw", "rmaskS"], [self.bk(bLi)])
                self.act(tE0, b3(bLi), AF.Exp, [self.bk(bLi)], ["tE0"], scale=-CW)
                self.act(tE1, b3(bLi), AF.Exp, [self.bk(bLi)], ["tE1"], scale=CW)
                self.tt("dve", AR[:, :, 64:128], rT, tE0, ALU.mult, ["rkT", "tE0"], ["AR"])
                self.cp("dve", self.sm[0:64, 32:40], tE0[:, :, tl], ["tE0"], ["sm"])
                self.tt("dve", bch, be, tE1, ALU.mult, ["be", "tE1"], ["bch"])
                self.tt("pool", kch, ke, tE1, ALU.mult, ["ke", "tE1"], ["kch"])
                bLe = self.pbank()
                for h in range(8):
                    self.mm(hb(bLe, h), sgw[:, h * 64:(h + 1) * 64], mS, True, True, ["sgw", "rmaskS"], [self.bk(bLe)])
                self.act(tE0, b3(bLe), AF.Exp, [self.bk(bLe)], ["tE0"], scale=-CW)
                self.stt("dve", AR[:, :, 0:64], kkn, -1.0, tE0, ALU.mult, ALU.mult, ["kkn", "tE0"], ["AR"])
                b = self.pbank()
                self.mm(self.bank[b][0:64, :], mT, sgw, True, True, ["sgw", "rmaskS"], [self.bk(b)])
                self.act(eLs, self.bank[b][0:64, :], AF.Exp, [self.bk(b)], ["eLs"], scale=-CW)
                for src, skey, dst, dkey in ((be, "be", Bt, "Bt"), (ke, "ke", Kt, "Kt")):
                    b = self.pbank()
                    for h in range(8):
                        self.P.op("pe", lambda e_, b=b, h=h, src=src: e_.transpose(hb(b, h), src[:, h, :], ident64), reads=[skey, "ident"], writes=[self.bk(b)])
                    self.tt("dve", dst, self.bank[b][0:64, :], eLs, ALU.mult, [self.bk(b), "eLs"], [dkey])
                b = self.pbank()
                for h in range(8):
                    self.mm(self.bank[b][0:64, h:h + 1], z[:, h, :], rwp[:, 32 + h:33 + h], True, True, ["z", "gq"], [self.bk(b)])
                self.cp("act", ysb[:, 512:520], self.bank[b][0:64, 0:8], [self.bk(b)], ["ysb"])
                for h in range(8):
                    self.mm(self.bank[h // 4][0:64, (h % 4) * 128:(h % 4 + 1) * 128], bch[:, h, :], AR[:, h, :], True, True, ["bch", "AR"], [self.bk(h // 4)])
                for h in range(8):
                    self.mm(self.bank[4 + h // 4][0:64, (h % 4) * 128:(h % 4 + 1) * 128], kch[:, h, :], AR[:, h, :], True, True, ["kch", "AR"], [self.bk(4 + h // 4)])
                for h in range(8):
                    self.mm(hb(6, h), AR[:, h, 0:64], bch[:, h, :], True, True, ["bch", "AR"], [self.bk(6)])
                m4 = lambda m_: m_.unsqueeze(1).broadcast_to([64, 4, 64])
                for g in range(2):
                    bb = self.bank[g][0:64, :].rearrange("p (h t) -> p h t", t=128)
                    kb = self.bank[4 + g][0:64, :].rearrange("p (h t) -> p h t", t=128)
                    self.tt("dve", NP[0][:, 4 * g:4 * g + 4, 0:64], bb[:, :, 0:64], m4(mS), ALU.mult, [self.bk(g), "rmaskS"], ["NP0"])
                    self.tt("dve", Mrb[:, 4 * g:4 * g + 4, :], bb[:, :, 64:128], m4(mI), ALU.mult, [self.bk(g), "rmaskS"], ["Mrb"])
                    self.tt("dve", Mak[:, 4 * g:4 * g + 4, :], kb[:, :, 0:64], m4(mS), ALU.mult, [self.bk(4 + g), "rmaskS"], ["Mak"])
                    self.tt("dve", Mrk[:, 4 * g:4 * g + 4, :], kb[:, :, 64:128], m4(mI), ALU.mult, [self.bk(4 + g), "rmaskS"], ["Mrk"])
                self.tt("dve", Mm[0], b3(6), mT.unsqueeze(1).broadcast_to([64, 8, 64]), ALU.mult, [self.bk(6), "rmaskS"], ["Mm0"])
                self.tt("pool", NP[0][:, :, 64:128], NP[0][:, :, 0:64], ident64.unsqueeze(1).broadcast_to([64, 8, 64]), ALU.add, ["NP0", "ident"], ["NP0"])
                cur = 0
                for step in range(6):
                    nx = 1 - cur
                    pbk = (0, 1) if step % 2 == 0 else (4, 5)
                    mbk = 6 if step % 2 else 7
                    ncur, nnx, mcur, mnx = "NP%d" % cur, "NP%d" % nx, "Mm%d" % cur, "Mm%d" % nx
                    if step == 0:
                        for h in range(8):
                            self.mm(self.bank[pbk[h // 4]][0:64, (h % 4) * 128:(h % 4) * 128 + 64], Mm[cur][:, h, :], NP[cur][:, h, 0:64], True, True,
                                    [mcur, ncur], [self.bk(pbk[h // 4])])
                    elif step < 5:
                        for h in range(8):
                            self.mm(self.bank[pbk[h // 4]][0:64, (h % 4) * 128:(h % 4 + 1) * 128], Mm[cur][:, h, :], NP[cur][:, h, :], True, True,
                                    [mcur, ncur], [self.bk(pbk[h // 4])])
                    else:
                        for h in range(8):
                            self.mm(self.bank[pbk[h // 4]][0:64, (h % 4) * 128 + 64:(h % 4 + 1) * 128], Mm[cur][:, h, :], NP[cur][:, h, 64:128], True, True,
                                    [mcur, ncur], [self.bk(pbk[h // 4])])
                    if step < 5:
                        for h in range(8):
                            self.mm(hb(mbk, h), NP[cur][:, h, 0:64], Mm[cur][:, h, :], True, True, [mcur, ncur], [self.bk(mbk)])
                    for g in range(2):
                        pv = self.bank[pbk[g]][0:64, :].rearrange("p (h t) -> p h t", t=128)
                        if step < 5:
                            self.cp("act", NP[nx][:, 4 * g:4 * g + 4, 0:64], pv[:, :, 0:64], [self.bk(pbk[g])], [nnx])
                        if step == 0:
                            self.cp("dve", NP[nx][:, 4 * g:4 * g + 4, 64:128], NP[cur][:, 4 * g:4 * g + 4, 64:128], [ncur], [nnx])
                        else:
                            self.tt("dve", NP[nx][:, 4 * g:4 * g + 4, 64:128], pv[:, :, 64:128], NP[cur][:, 4 * g:4 * g + 4, 64:128], ALU.add,
                                    [self.bk(pbk[g]), ncur], [nnx])
                    if step < 5:
                        self.cp("act", Mm[nx], b3(mbk), [self.bk(mbk)], [mnx])
                    cur = nx
                TT, tkey = NP[cur], "NP%d" % cur
                S0, skey = Sst[Scur], "S%d" % Scur
                S1, s1key = Sst[1 - Scur], "S%d" % (1 - Scur)
                bX = self.pbank()
                for h in range(8):
                    self.mm(hb(bX, h), AR[:, h, 0:64], S0[:, h, :], True, False, ["AR", skey], [self.bk(bX)])
                    self.mm(hb(bX, h), Mak[:, h, :], vtok[:, h * 64:(h + 1) * 64], False, True, ["Mak", "vtok"], [self.bk(bX)])
                self.cp("dve", Xs, b3(bX), [self.bk(bX)], ["Xs"])
                bU = self.pbank()
                for h in range(8):
                    self.mm(hb(bU, h), TT[:, h, 64:128], Xs[:, h, :], True, True, [tkey, "Xs"], [self.bk(bU)])
                self.cp("act", Us, b3(bU), [self.bk(bU)], ["Us"])
                bY = self.pbank()
                for h in range(8):
                    self.mm(hb(bY, h), AR[:, h, 64:128], S0[:, h, :], True, False, ["AR", skey], [self.bk(bY)])
                    self.mm(hb(bY, h), Mrb[:, h, :], Us[:, h, :], False, False, ["Mrb", "Us"], [self.bk(bY)])
                    self.mm(hb(bY, h), Mrk[:, h, :], vtok[:, h * 64:(h + 1) * 64], False, True, ["Mrk", "vtok"], [self.bk(bY)])
                self.cp("act", ysb[:, 0:512], self.bank[bY][0:64, :], [self.bk(bY)], ["ysb"])
                P.dma("sp", self.y_scr[e * S + t0:e * S + t0 + 64, :], ysb, reads=["ysb"], writes=["y_scr"])
                bS = self.pbank()
                for h in range(8):
                    self.mm(hb(bS, h), Bt[:, h * 64:(h + 1) * 64], Us[:, h, :], True, False, ["Bt", "Us"], [self.bk(bS)])
                    self.mm(hb(bS, h), Kt[:, h * 64:(h + 1) * 64], vtok[:, h * 64:(h + 1) * 64], False, True, ["Kt", "vtok"], [self.bk(bS)])
                self.tt("pool", tmpS, S0, bc3(self.sm[0:64, 32:40]), ALU.mult, [skey, "sm"], ["tmpS"])
                self.tt("dve", S1, tmpS, b3(bS), ALU.add, ["tmpS", self.bk(bS)], [s1key])
                Scur = 1 - Scur
        self.r32 = False
        self.barrier()
        P.dma("sp", self.lng[:, 0:512], I["rwkv_ln_g"][l:l + 1, :].partition_broadcast(128), writes=["lng"])
        P.dma("sp", self.lnb[:, 0:512], I["rwkv_ln_b"][l:l + 1, :].partition_broadcast(128), writes=["lnb"])
        yf, yb, vt = self.lnx[0], self.lnx[1], self.junk
        sm = self.sm
        t0_, t1_, t2_ = self.tmp
        for tt in range(NT):
            P.dma("sp", yf[:, 0:520], self.y_scr[tt * 128:(tt + 1) * 128, :], reads=["y_scr"], writes=["lnx0"])
            P.dma("sp", yb[:, 0:520], self.y_scr[S + tt * 128:S + (tt + 1) * 128, :], reads=["y_scr"], writes=["lnx1"])
            P.dma("sp", vt[:, 0:512], self.v_scr[tt * 128:(tt + 1) * 128, :], reads=["v_scr"], writes=["junk"])
            P.dma("sp", vt[:, 512:1024], self.sg_scr[tt * 128:(tt + 1) * 128, :], reads=["sg_scr"], writes=["junk"])
            self.tt("dve", yf[:, 0:520], yf[:, 0:520], yb[:, 0:520], ALU.add, ["lnx0", "lnx1"], ["lnx0"])
            y3 = yf[:, 0:512].rearrange("p (h d) -> p h d", d=64)
            P.op("dve", lambda e_, y3=y3: e_.reduce_sum(out=sm[:, 40:48], in_=y3, axis=AX.X), reads=["lnx0"], writes=["sm"])
            self.ts("dve", sm[:, 40:48], sm[:, 40:48], -1.0 / 64, None, ALU.mult, None, ["sm"], ["sm"])
            self.tt("dve", y3, y3, sm[:, 40:48].unsqueeze(2).broadcast_to([128, 8, 64]), ALU.add, ["lnx0", "sm"], ["lnx0"])
            self.act(t0_[:, 0:512], yf[:, 0:512], AF.Square, ["lnx0"], ["tmp0"])
            P.op("dve", lambda e_: e_.reduce_sum(out=sm[:, 48:56], in_=t0_[:, 0:512].rearrange("p (h d) -> p h d", d=64), axis=AX.X), reads=["tmp0"], writes=["sm"])
            self.rsqrt_cols(sm[:, 48:56], sm[:, 56:64], 1.0 / 64, 64e-5)
            self.tt("dve", y3, y3, sm[:, 56:64].unsqueeze(2).broadcast_to([128, 8, 64]), ALU.mult, ["lnx0", "sm"], ["lnx0"])
            self.tt("dve", yf[:, 0:512], yf[:, 0:512], self.lng[:, 0:512], ALU.mult, ["lnx0", "lng"], ["lnx0"])
            self.tt("pool", yf[:, 0:512], yf[:, 0:512], self.lnb[:, 0:512], ALU.add, ["lnx0", "lnb"], ["lnx0"])
            self.tt("pool", t1_[:, 0:512].rearrange("p (h d) -> p h d", d=64), vt[:, 0:512].rearrange("p (h d) -> p h d", d=64),
                    yf[:, 512:520].unsqueeze(2).broadcast_to([128, 8, 64]), ALU.mult, ["junk", "lnx0"], ["tmp1"])
            self.tt("dve", t1_[:, 0:512], t1_[:, 0:512], yf[:, 0:512], ALU.add, ["tmp1", "lnx0"], ["tmp1"])
            self.tt("dve", t2_[:, 0:512], t1_[:, 0:512], vt[:, 512:1024], ALU.mult, ["tmp1", "junk"], ["tmp2"])
            P.dma("sp", self.o_scr[tt * 128:(tt + 1) * 128, OB:OB + 512], t2_[:, 0:512], reads=["tmp2"], writes=["o_scr"])

    def merge(self, l, last):
        P, I = self.P, self.I
        wg_all = I["w_gate"][l * D:(l + 1) * D, :]
        wb_all = I["w_branch"][l * 2304:(l + 1) * 2304, :]
        wo_all = I["w_out"][l * D:(l + 1) * D, :]
        P.dma("sp", self.lng[:], I["ln_g"][l:l + 1, :].partition_broadcast(128), writes=["lng"])
        P.dma("sp", self.lnb[:], I["ln_b"][l:l + 1, :].partition_broadcast(128), writes=["lnb"])
        bgT = self.lamt[:, 0:40]
        for i5 in range(5):
            P.dma("sp", bgT[:, i5 * 8:(i5 + 1) * 8], I["b_gate"][l:l + 1, i5 * 1024:(i5 + 1) * 1024].rearrange("e (g c) -> c (e g)", c=128),
                  writes=["lamt"], allow_slow_non_contiguous=True)
        oT = self.BIG[0][:, 0:9216].rearrange("p (j t) -> p j t", t=512)
        yTf = self.BIG[1][:].bitcast(F32)[:, 0:4096].rearrange("p (c t) -> p c t", t=512)
        otile = self.BIG[2][:].bitcast(F32)[:, 0:2304]
        yTb = self.BIG[3][:, 0:4096].rearrange("p (c t) -> p c t", t=512)
        hgrp = self.G[:, 0:4096].rearrange("p (q c) -> p q c", c=1024)
        hin = self.hres[l % 2]
        hout = self.out if last else self.hres[(l + 1) % 2]
        t0, t1 = self.tmp[0], self.tmp[1]
        mb = [0]

        def mbank():
            mb[0] = (mb[0] + 1) % 8
            return mb[0]

        for grp in range(4):
            for tq in range(4):
                tt = grp * 4 + tq
                P.dma("sp", otile, self.o_scr[tt * 128:(tt + 1) * 128, :], reads=["o_scr"], writes=["BIG2"])
                P.dma("sp", hgrp[:, tq, :], hin[tt * 128:(tt + 1) * 128, :], reads=["hres%d" % (l % 2)], writes=["G"])
                for j4 in range(5):
                    nj = min(4, 18 - j4 * 4)
                    b = mbank()
                    for u in range(nj):
                        j = j4 * 4 + u
                        self.tr(self.bank[b][:, u * 128:(u + 1) * 128], otile[:, j * 128:(j + 1) * 128], ["BIG2"], [self.bk(b)])
                    self.cp("act" if j4 % 2 else "dve", oT[:, j4 * 4:j4 * 4 + nj, tq * 128:(tq + 1) * 128],
                            self.bank[b][:, 0:nj * 128].rearrange("p (c t) -> p c t", t=128), [self.bk(b)], ["BIG0"])
            hsl = lambda c: self.hT[:, c, 1 + grp * 512:1 + (grp + 1) * 512]
            for i, (r0, rw) in enumerate(BROWS):
                kci = rw // 128
                for cc in range(4):
                    wg, wgk = self.wload(wg_all[:, i * 1024 + cc * 256:i * 1024 + (cc + 1) * 256], 256)[0]
                    wb, wbk = self.wload(wb_all[r0:r0 + rw, cc * 256:(cc + 1) * 256], 256, kc=kci)[0]
                    for u in range(2):
                        ct = cc * 2 + u
                        b1 = mbank()
                        for c in range(8):
                            self.mm(self.bank[b1][:, :], wg[:, c, u * 128:(u + 1) * 128], hsl(c), c == 0, c == 7, [wgk, "hT"], [self.bk(b1)])
                        ti = self.nxt("mt", 2)
                        tg = self.tmp[ti]
                        self.act(tg[:, :], self.bank[b1][:, :], AF.Sigmoid, [self.bk(b1), "lamt"], ["tmp%d" % ti], bias=bgT[:, i * 8 + ct:i * 8 + ct + 1])
                        b2 = mbank()
                        for c in range(kci):
                            self.mm(self.bank[b2][:, :], wb[:, c, u * 128:(u + 1) * 128], oT[:, r0 // 128 + c, :], c == 0, c == kci - 1,
                                    [wbk, "BIG0"], [self.bk(b2)])
                        ysl = yTf[:, ct, :]
                        if i == 0:
                            self.tt("dve", ysl, self.bank[b2][:, :], tg[:, :], ALU.mult, [self.bk(b2), "tmp%d" % ti], ["BIG1"])
                        else:
                            self.tt("dve", tg[:, :], self.bank[b2][:, :], tg[:, :], ALU.mult, [self.bk(b2), "tmp%d" % ti], ["tmp%d" % ti])
                            self.tt("pool", ysl, ysl, tg[:, :], ALU.add, ["BIG1", "tmp%d" % ti], ["BIG1"])
            if l == 0 and grp == 0:
                self.dbg("yTf", self.BIG[1][:].bitcast(F32)[:, 0:4096], ["BIG1"])
                self.dbg("oT", self.BIG[0][:, 0:9216], ["BIG0"], BF16)
                self.dbg("bgT", self.lamt[:, 0:40], ["lamt"])
            for half in range(2):
                self.cp("act" if half else "dve", yTb[:, half * 4:half * 4 + 4, :], yTf[:, half * 4:half * 4 + 4, :], ["BIG1"], ["BIG3"])
            for cc in range(4):
                wo, wok = self.wload(wo_all[:, cc * 256:(cc + 1) * 256], 256)[0]
                for tq in range(4):
                    b = mbank()
                    for c in range(8):
                        self.mm(self.bank[b][:, 0:256], yTb[:, c, tq * 128:(tq + 1) * 128], wo[:, c, :], c == 0, c == 7, [wok, "BIG3"], [self.bk(b)])
                    hs = hgrp[:, tq, cc * 256:(cc + 1) * 256]
                    self.stt("dve", hs, hs, ALPHA, self.bank[b][:, 0:256], ALU.mult, ALU.add, ["G", self.bk(b)], ["G"])
            for tq in range(4):
                tt = grp * 4 + tq
                self.ln_inplace(hgrp[:, tq, :], "G")
                P.dma("sp", hout[tt * 128:(tt + 1) * 128, :], hgrp[:, tq, :], reads=["G"],
                      writes=["out" if last else "hres%d" % ((l + 1) % 2)], is_output=last)


def make_in_map(inputs, b, consts):
    m = {"x": np.ascontiguousarray(inputs["x"][b]), "mem": np.ascontiguousarray(inputs["mem"][b])}
    for nm, shp in IN_SPECS:
        if nm in consts:
            m[nm] = consts[nm]
        else:
            m[nm] = np.ascontiguousarray(np.asarray(inputs[nm], dtype=np.float32).reshape(shp))
    return m


def kernel(**inputs):
    consts = host_consts()
    kb = KB(debug=False)
    nb = inputs["x"].shape[0]
    in_maps = [make_in_map(inputs, b, consts) for b in range(nb)]
    res = run_bass_kernel_spmd(kb.nc, in_maps, core_ids=list(range(nb)))
    out = np.stack([np.asarray(r["out"], dtype=np.float32).reshape(S, D) for r in res.results], axis=0)
    return out
```

```python
import math
from concourse.ap import AP
import contextlib
import numpy as np
import concourse.bass as bass
import concourse.mybir as mybir
from concourse.bass_utils import run_bass_kernel_spmd

F32 = mybir.dt.float32
BF16 = mybir.dt.bfloat16
I32 = mybir.dt.int32
AF = mybir.ActivationFunctionType
ALU = mybir.AluOpType
AX = mybir.AxisListType

ENGS = ("pe", "act", "dve", "pool", "sp")
DMA_SEMS = 8


class Op:
    __slots__ = ("eng", "fn", "waits", "is_dma", "idx", "marked", "dma_slot", "dma_val", "prewait")

    def __init__(self, eng, fn, is_dma):
        self.eng = eng
        self.fn = fn
        self.is_dma = is_dma
        self.waits = []
        self.marked = False
        self.idx = None
        self.dma_slot = None
        self.dma_val = None
        self.prewait = None


class Prog:
    def __init__(self, nc, same_engine_sync=True):
        self.nc = nc
        self.ops = {e: [] for e in ENGS}
        self.last_write = {}
        self.readers = {}
        self.same_engine_sync = same_engine_sync
        self.dma_count = {e: 0 for e in ENGS}
        self.dma_hist = {e: [] for e in ENGS}
        self.all_dma_out = []
        self.stack = contextlib.ExitStack()
        self.n_ops = 0

    def sb(self, name, shape, dt):
        return self.stack.enter_context(self.nc.sbuf_tensor("s_" + name, list(shape), dt))

    def ps(self, name, shape, dt):
        return self.stack.enter_context(self.nc.psum_tensor("p_" + name, list(shape), dt))

    def _deps(self, op, reads, writes):
        deps = []
        for k in reads:
            w = self.last_write.get(k)
            if w is not None:
                deps.append(w)
        for k in writes:
            w = self.last_write.get(k)
            if w is not None:
                deps.append(w)
            for r in self.readers.get(k, ()):
                deps.append(r)
        best = {}
        for d in deps:
            if d is op:
                continue
            key = (d.eng, d.is_dma, d.dma_slot if d.is_dma else None)
            cur = best.get(key)
            if cur is None or d.idx > cur.idx:
                best[key] = d
        for d in best.values():
            if (not d.is_dma) and d.eng == op.eng and not op.is_dma:
                if op.eng == "pe" or not self.same_engine_sync:
                    continue
            op.waits.append(d)
            d.marked = True
        for k in reads:
            self.readers.setdefault(k, []).append(op)
        for k in writes:
            self.last_write[k] = op
            self.readers[k] = []

    def barrier(self, fn):
        o = Op("pool", fn, False)
        o.idx = len(self.ops["pool"])
        self.ops["pool"].append(o)
        self._deps(o, [], ["__phase__"])
        return o

    def op(self, eng, fn, reads=(), writes=()):
        reads = list(reads) + ["__phase__"]
        o = Op(eng, fn, False)
        o.idx = len(self.ops[eng])
        self.ops[eng].append(o)
        self._deps(o, reads, writes)
        self.n_ops += 1
        return o

    def dma(self, eng, out, in_, reads=(), writes=(), is_output=False, **kw):
        def fn(e, out=out, in_=in_, kw=kw):
            return e.dma_start(out=out, in_=in_, **kw)
        reads = list(reads) + ["__phase__"]
        o = Op(eng, fn, True)
        o.idx = len(self.ops[eng])
        n = self.dma_count[eng]
        self.dma_count[eng] += 1
        o.dma_slot = n % DMA_SEMS
        o.dma_val = 16 * (n // DMA_SEMS + 1)
        if n >= DMA_SEMS:
            o.prewait = self.dma_hist[eng][n - DMA_SEMS]
        self.dma_hist[eng].append(o)
        self.ops[eng].append(o)
        self._deps(o, reads, writes)
        if is_output:
            self.all_dma_out.append(o)
        self.n_ops += 1
        return o

    def emit(self):
        nc = self.nc
        st = self.stack
        fin = Op("sp", None, False)
        fin.idx = len(self.ops["sp"])
        for o in self.all_dma_out:
            fin.waits.append(o)
        self.ops["sp"].append(fin)
        csem = {e: st.enter_context(nc.semaphore("c_" + e)) for e in ENGS}
        dsem = {e: [st.enter_context(nc.semaphore("d_%s_%d" % (e, i))) for i in range(DMA_SEMS)]
                for e in ENGS if self.dma_count[e] > 0}
        for e in ENGS:
            c = 0
            for o in self.ops[e]:
                if o.is_dma:
                    continue
                if o.marked:
                    c += 1
                    o.dma_val = c
        block = st.enter_context(nc.Block())
        prog = self

        def run(e, eng):
            seen = {}
            for o in prog.ops[e]:
                ws = list(o.waits)
                if o.prewait is not None:
                    ws.append(o.prewait)
                for d in ws:
                    if d.is_dma:
                        sem, val = dsem[d.eng][d.dma_slot], d.dma_val
                    else:
                        sem, val = csem[d.eng], d.dma_val
                    k = id(sem)
                    if seen.get(k, 0) >= val:
                        continue
                    seen[k] = val
                    eng.wait_ge(sem, val)
                if o.fn is None:
                    continue
                ins = o.fn(eng)
                if o.is_dma:
                    ins.then_inc(dsem[e][o.dma_slot], 16)
                elif o.marked:
                    ins.then_inc(csem[e], 1)

        @block.tensor
        def _(eng):
            run("pe", eng)

        @block.scalar
        def _(eng):
            run("act", eng)

        @block.vector
        def _(eng):
            run("dve", eng)

        @block.gpsimd
        def _(eng):
            run("pool", eng)

        @block.sync
        def _(eng):
            run("sp", eng)

    def close(self):
        self.stack.close()


F32R = mybir.dt.float32r

S = 2048
D = 1024
NT = 16
DEPTH = 2
WC = 256
XC = 2047
GW = 4096
ALPHA = (2 * DEPTH) ** 0.25
A0, B0, C0, D0, M0 = 0, 2048, 4352, 5632, 7680
OA, OB, OC, OD, OM = 0, 512, 1024, 1536, 2048
BROWS = [(0, 512), (512, 512), (1024, 512), (1536, 512), (2048, 256)]


def rel_bucket_np(rel):
    nb = 16
    max_exact = 8
    n = np.abs(rel)
    nf = np.maximum(n, 1).astype(np.float32)
    large = max_exact + (np.log(nf / max_exact) / np.float32(math.log(1024 / max_exact)) * (nb - max_exact)).astype(np.int32)
    large = np.minimum(large, nb - 1)
    return np.where(rel > 0, nb, 0) + np.where(n < max_exact, n, large)


def host_consts():
    c = {}
    c["ident"] = np.eye(128, dtype=np.float32)
    rel = np.arange(4096) - XC
    bkt = rel_bucket_np(rel)
    oh = np.zeros((32, 4096), np.float32)
    oh[bkt, np.arange(4096)] = 1.0
    c["onehot"] = oh
    n = np.abs(rel)
    mA = (n <= 64).astype(np.float32) + ((rel % 4 == 0) & (n <= 256)) + ((rel % 16 == 0) & (n <= 1024))
    mt = np.ones((12, 4096), np.float32)
    mt[:8] = mA[None, :]
    c["multab"] = mt
    t = np.arange(S)
    row = (t // 64).astype(np.float32)
    col = (t % 64).astype(np.float32)
    freqs = (10000.0 ** (-(np.arange(16, dtype=np.float32) / 16))).astype(np.float32)
    ar = row[:, None] * freqs[None, :]
    ac = col[:, None] * freqs[None, :]
    c["ropec"] = np.concatenate([np.cos(ar), np.cos(ar), np.cos(ac), np.cos(ac)], 1).astype(np.float32)
    c["ropes"] = np.concatenate([-np.sin(ar), np.sin(ar), -np.sin(ac), np.sin(ac)], 1).astype(np.float32)
    tri = np.zeros((2, 3, 128, 128), np.float32)
    sg = np.arange(128)[:, None]
    tt = np.arange(128)[None, :]
    same = (sg // 64) == (tt // 64)
    tri[0, 0] = same & (sg <= tt)
    tri[0, 1] = same & (sg < tt)
    tri[0, 2] = same & (sg > tt)
    tri[1, 0] = same & (sg >= tt)
    tri[1, 1] = same & (sg > tt)
    tri[1, 2] = same & (sg < tt)
    c["tri"] = tri.reshape(6 * 128, 128)
    mk_ = np.zeros((2, 64, 192), np.float32)
    a = np.arange(64)[:, None]
    b = np.arange(64)[None, :]
    mk_[0, :, 0:64] = a < b
    mk_[0, :, 64:128] = a <= b
    mk_[0, :, 128:192] = b < a
    mk_[1, :, 0:64] = a > b
    mk_[1, :, 64:128] = a >= b
    mk_[1, :, 128:192] = b > a
    c["rmask"] = mk_.reshape(128, 192)
    return c


IN_SPECS = [("ln_in_g", [1, D]), ("ln_in_b", [1, D]), ("rel_bias", [32, 12]), ("w_in", [DEPTH * D, 8192]),
            ("shift_mu", [DEPTH * 2, 1792]), ("rwkv_w0", [DEPTH * 2, 512]), ("rwkv_w_up", [DEPTH * 2 * 64, 512]),
            ("rwkv_a0", [DEPTH * 2, 512]), ("rwkv_a_up", [DEPTH * 2 * 64, 512]), ("rwkv_k_k", [DEPTH, 512]),
            ("rwkv_k_a", [DEPTH, 512]), ("rwkv_r_k", [DEPTH, 512]), ("rwkv_ln_g", [DEPTH, 512]),
            ("rwkv_ln_b", [DEPTH, 512]), ("c_qnorm_g", [DEPTH, 64]), ("c_knorm_g", [DEPTH, 64]),
            ("d_lambda", [DEPTH, 256]), ("d_subln_g", [DEPTH, 128]), ("w_mem_kv", [DEPTH * D, 512]),
            ("w_branch", [DEPTH * 2304, D]), ("w_gate", [DEPTH * D, 5120]), ("b_gate", [DEPTH, 5120]),
            ("w_out", [DEPTH * D, D]), ("ln_g", [DEPTH, D]), ("ln_b", [DEPTH, D]),
            ("ident", [128, 128]), ("onehot", [32, 4096]), ("multab", [12, 4096]), ("ropec", [S, 64]),
            ("ropes", [S, 64]), ("tri", [768, 128]), ("rmask", [128, 192])]


class KB:
    def __init__(self, debug=False, mixers="MCADB", layers=DEPTH):
        self.debug = debug
        self.mixers = mixers
        nc = bass.Bass("TRN2", target_bir_lowering=False)
        self.nc = nc
        P = Prog(nc)
        self.P = P
        I = {}
        I["x"] = nc.dram_tensor("x", [S, D], F32, kind="ExternalInput").ap()
        I["mem"] = nc.dram_tensor("mem", [256, D], F32, kind="ExternalInput").ap()
        for nm, shp in IN_SPECS:
            I[nm] = nc.dram_tensor(nm, list(shp), F32, kind="ExternalInput").ap()
        self.I = I
        self.out = nc.dram_tensor("out", [S, D], F32, kind="ExternalOutput").ap()
        self.hres = [nc.dram_tensor("hres%d" % i, [S, D], F32, kind="ExternalOutput" if debug else "Internal").ap() for i in range(2)]
        self.dbg_outs = []
        self.o_scr = nc.dram_tensor("o_scr", [S, 2304], F32, kind="ExternalOutput" if debug else "Internal").ap()
        self.xtab = nc.dram_tensor("xtab", [12, 4096], F32).ap()
        self.rk_scr = nc.dram_tensor("rk_scr", [1024, S], F32).ap()
        self.wa_scr = nc.dram_tensor("wa_scr", [256, S], F32).ap()
        self.v_scr = nc.dram_tensor("v_scr", [S, 512], F32).ap()
        self.y_scr = nc.dram_tensor("y_scr", [2 * S, 520], F32, kind="ExternalOutput" if debug else "Internal").ap()
        self.sg_scr = nc.dram_tensor("sg_scr", [S, 512], F32).ap()
        self.ident = P.sb("ident", [128, 128], F32)
        self.hT = P.sb("hT", [128, 8, S + 2], BF16)
        self.BIG = [P.sb("BIG%d" % i, [128, 9216], BF16) for i in range(4)]
        self.G = P.sb("G", [128, GW], F32)
        self.wst = [P.sb("wst%d" % i, [128, 8, WC], F32) for i in range(1)]
        self.RX = P.sb("RX", [64, 3072], F32)
        self.wbf = [P.sb("wbf%d" % i, [128, 8, WC], BF16) for i in range(4)]
        self.ropec = P.sb("ropec", [128, NT, 64], F32)
        self.ropes = P.sb("ropes", [128, NT, 64], F32)
        self.lnx = [P.sb("lnx%d" % i, [128, D], F32) for i in range(2)]
        self.junk = P.sb("junk", [128, D], F32)
        self.lng = P.sb("lng", [128, D], F32)
        self.lnb = P.sb("lnb", [128, D], F32)
        self.pt = [P.sb("pt%d" % i, [128, 512], BF16) for i in range(4)]
        self.pe_ = [P.sb("pe%d" % i, [128, 512], BF16) for i in range(4)]
        self.ost = [P.sb("ost%d" % i, [128, 128], F32) for i in range(4)]
        self.sm = P.sb("sm", [128, 64], F32)
        self.tmp = [P.sb("tmp%d" % i, [128, 512], F32) for i in range(3)]
        self.onesf = P.sb("onesf", [128, 128], F32)
        self.gq = P.sb("gq", [128, 128], F32)
        self.subg = P.sb("subg", [128, 128], F32)
        self.lamt = P.sb("lamt", [128, 264], F32)
        self.pbar = P.sb("pbar", [1, 8], F32)
        self.memT = P.sb("memT", [128, 8, 256], BF16)
        self.bank = [P.ps("bank%d" % i, [128, 512], F32) for i in range(8)]
        self.cnt = {}
        self.pbi = 0
        B0_, B1_, B2_, B3_ = [b[:] for b in self.BIG]
        self.qT = B0_[:, 0:8192].rearrange("p (c t) -> p c t", t=S)
        self.kT = B1_[:, 0:8192].rearrange("p (c t) -> p c t", t=S)
        self.sg = B3_[:, 0:8192].rearrange("p (t c) -> p t c", c=512)
        self.prelude()
        for l in range(layers):
            self.layer(l, last=(l == layers - 1))
        P.emit()
        P.close()

    def nxt(self, name, n):
        v = self.cnt.get(name, 0)
        self.cnt[name] = (v + 1) % n
        return v

    def bk(self, i):
        return "bank%d" % i

    def pbank(self):
        self.pbi ^= 1
        return 2 + self.pbi

    def barrier(self):
        pbar = self.pbar
        self.P.barrier(lambda e: e.memset(pbar[:], 0.0))

    def R(self, ap):
        if ap.dtype == F32 and ap.name == "s_RX":
            return ap.bitcast(F32R)
        return ap

    def mm(self, out, lhsT, rhs, start, stop, reads, writes):
        if lhsT.name == "s_RX" and rhs.name == "s_RX":
            lhsT, rhs = self.R(lhsT), self.R(rhs)
        self.P.op("pe", lambda e: e.matmul(out, lhsT=lhsT, rhs=rhs, start=start, stop=stop), reads=reads, writes=writes)

    def tr(self, out, in_, reads, writes, np_=128):
        ident = self.ident
        self.P.op("pe", lambda e: e.transpose(out, in_, ident[0:np_, 0:np_]), reads=list(reads) + ["ident"], writes=writes)

    def cp(self, eng, out, in_, reads, writes):
        out = self.R(out)
        if eng == "act":
            self.P.op("act", lambda e: e.copy(out=out, in_=in_), reads=reads, writes=writes)
        else:
            self.P.op(eng, lambda e: e.tensor_copy(out=out, in_=in_), reads=reads, writes=writes)

    def act(self, out, in_, func, reads, writes, **kw):
        out = self.R(out)
        self.P.op("act", lambda e: e.activation(out=out, in_=in_, func=func, **kw), reads=reads, writes=writes)

    def tt(self, eng, out, in0, in1, op, reads, writes):
        out = self.R(out)
        self.P.op(eng, lambda e: e.tensor_tensor(out=out, in0=in0, in1=in1, op=op), reads=reads, writes=writes)

    def ts(self, eng, out, in0, s1, s2, op0, op1, reads, writes):
        out = self.R(out)
        if s2 is None:
            self.P.op(eng, lambda e: e.tensor_scalar(out=out, in0=in0, scalar1=s1, scalar2=None, op0=op0), reads=reads, writes=writes)
        else:
            self.P.op(eng, lambda e: e.tensor_scalar(out=out, in0=in0, scalar1=s1, scalar2=s2, op0=op0, op1=op1), reads=reads, writes=writes)

    def stt(self, eng, out, in0, scalar, in1, op0, op1, reads, writes):
        out = self.R(out)
        self.P.op(eng, lambda e: e.scalar_tensor_tensor(out=out, in0=in0, scalar=scalar, in1=in1, op0=op0, op1=op1), reads=reads, writes=writes)

    def memset(self, eng, ap, val, writes):
        ap = self.R(ap)
        self.P.op(eng, lambda e: e.memset(ap, val), writes=writes)

    def rsqrt_cols(self, src, dst, scale, eps, key="sm"):
        self.ts("dve", dst, src, scale, eps, ALU.mult, ALU.add, [key], [key])
        self.P.op("act", lambda e: e.sqrt(out=dst, in_=dst), reads=[key], writes=[key])
        self.P.op("dve", lambda e: e.reciprocal(out=dst, in_=dst), reads=[key], writes=[key])

    def wload(self, src2d, n, kc=8, variants=None):
        P = self.P
        src = src2d.rearrange("(c p) n -> p c n", p=128)
        if variants is None:
            j = self.nxt("wb", 4)
            P.dma("pool", self.wbf[j][:, 0:kc, 0:n], src, writes=["wbf%d" % j])
            return [(self.wbf[j], "wbf%d" % j)]
        wst = self.wst[0]
        P.dma("sp", wst[:, 0:kc, 0:n], src, writes=["wst0"])
        res = []
        for vi, (vap, vkey) in enumerate(variants):
            j = self.nxt("wb", 4)
            self.tt("dve" if vi != 1 else "pool", self.wbf[j][:, 0:kc, 0:n], wst[:, 0:kc, 0:n], vap.unsqueeze(1).broadcast_to([128, kc, n]), ALU.mult,
                    ["wst0", vkey], ["wbf%d" % j])
            res.append((self.wbf[j], "wbf%d" % j))
        return res

    def proj_T(self, wl, n, consume, shifts=(0,), rhs_fn=None, rkey="hT", ntb=4, tbw=512):
        hT = self.hT
        for ct in range(n // 128):
            for tb in range(ntb):
                b = self.pbank()
                nmm = 8 * len(shifts)
                m = 0
                for (wap, wkey), s in zip(wl, shifts):
                    for c in range(8):
                        if rhs_fn is None:
                            lo = 1 + tb * 512 + s
                            rhs = hT[:, c, lo:lo + 512]
                        else:
                            rhs = rhs_fn(c, tb)
                        self.mm(self.bank[b][:, 0:tbw], wap[:, c, ct * 128:(ct + 1) * 128], rhs, m == 0, m == nmm - 1,
                                [wkey, rkey], [self.bk(b)])
                        m += 1
                consume(b, ct, tb)

    def proj_N(self, wl, n, consume, shifts=(0,), lhs_fn=None, lkey="hT", ntt=NT, kc=8):
        hT = self.hT
        for tt in range(ntt):
            b = self.pbank()
            nmm = kc * len(shifts)
            m = 0
            for (wap, wkey), s in zip(wl, shifts):
                for c in range(kc):
                    if lhs_fn is None:
                        lo = 1 + tt * 128 + s
                        lh = hT[:, c, lo:lo + 128]
                    else:
                        lh = lhs_fn(c, tt)
                    self.mm(self.bank[b][:, 0:n], lh, wap[:, c, 0:n], m == 0, m == nmm - 1, [wkey, lkey], [self.bk(b)])
                    m += 1
            consume(b, tt)

    def ln_inplace(self, xt, xkey, eps=1e-5):
        sm, junk = self.sm, self.junk
        P = self.P
        P.op("dve", lambda e: e.reduce_sum(out=sm[:, 0:1], in_=xt, axis=AX.X), reads=[xkey], writes=["sm"])
        self.ts("dve", sm[:, 1:2], sm[:, 0:1], -1.0 / D, None, ALU.mult, None, ["sm"], ["sm"])
        self.ts("dve", xt, xt, sm[:, 1:2], None, ALU.add, None, [xkey, "sm"], [xkey])
        self.memset("dve", sm[:, 2:3], 0.0, ["sm"])
        self.act(junk[:], xt, AF.Square, [xkey, "sm"], ["junk", "sm"], accum_out=sm[:, 2:3])
        self.rsqrt_cols(sm[:, 2:3], sm[:, 3:4], 1.0 / D, eps)
        self.stt("dve", xt, xt, sm[:, 3:4], self.lng[:], ALU.mult, ALU.mult, [xkey, "sm", "lng"], [xkey])
        self.tt("dve", xt, xt, self.lnb[:], ALU.add, [xkey, "lnb"], [xkey])

    def to_hT(self, src, skey, tt):
        hT = self.hT
        for half in range(2):
            b = self.pbank()
            for c4 in range(4):
                c = half * 4 + c4
                self.tr(self.bank[b][:, c4 * 128:(c4 + 1) * 128], src[:, c * 128:(c + 1) * 128], [skey], [self.bk(b)])
            self.cp("act" if half else "dve", hT[:, half * 4:half * 4 + 4, 1 + tt * 128:1 + (tt + 1) * 128],
                    self.bank[b][:, :].rearrange("p (c t) -> p c t", t=128), [self.bk(b)], ["hT"])

    def prelude(self):
        P, I = self.P, self.I
        P.dma("sp", self.ident[:], I["ident"], writes=["ident"])
        P.dma("sp", self.ropec[:], I["ropec"].rearrange("(t p) c -> p t c", p=128), writes=["ropec"])
        P.dma("sp", self.ropes[:], I["ropes"].rearrange("(t p) c -> p t c", p=128), writes=["ropes"])
        self.memset("pool", self.onesf[:], 1.0, ["onesf"])
        self.memset("pool", self.hT[:, :, 0:1], 0.0, ["hT"])
        self.memset("pool", self.hT[:, :, S + 1:S + 2], 0.0, ["hT"])
        tmpA = self.tmp[0]
        rb = tmpA[0:32, 0:12]
        P.dma("sp", rb, I["rel_bias"], writes=["tmp0"])
        ohs = self.BIG[0][:].bitcast(F32)
        P.dma("sp", ohs[0:32, 0:4096], I["onehot"], writes=["BIG0"])
        mts = self.BIG[1][:].bitcast(F32)
        P.dma("sp", mts[0:12, 0:4096], I["multab"], writes=["BIG1"])
        xts = self.BIG[2][:].bitcast(F32)
        for j in range(8):
            b = self.pbank()
            self.mm(self.bank[b][0:12, :], rb, ohs[0:32, j * 512:(j + 1) * 512], True, True, ["tmp0", "BIG0"], [self.bk(b)])
            self.act(xts[0:12, j * 512:(j + 1) * 512], self.bank[b][0:12, :], AF.Exp, [self.bk(b)], ["BIG2"])
        self.tt("dve", xts[0:12, 0:4096], xts[0:12, 0:4096], mts[0:12, 0:4096], ALU.mult, ["BIG2", "BIG1"], ["BIG2"])
        P.dma("sp", self.xtab, xts[0:12, 0:4096], reads=["BIG2"], writes=["xtab"])
        self.barrier()
        for mt_ in range(2):
            i = self.nxt("ln", 2)
            P.dma("sp", self.lnx[i][:], I["mem"][mt_ * 128:(mt_ + 1) * 128, :], writes=["lnx%d" % i])
            for half in range(2):
                b = self.pbank()
                for c4 in range(4):
                    c = half * 4 + c4
                    self.tr(self.bank[b][:, c4 * 128:(c4 + 1) * 128], self.lnx[i][:, c * 128:(c + 1) * 128], ["lnx%d" % i], [self.bk(b)])
                self.cp("dve", self.memT[:, half * 4:half * 4 + 4, mt_ * 128:(mt_ + 1) * 128],
                        self.bank[b][:, :].rearrange("p (c t) -> p c t", t=128), [self.bk(b)], ["memT"])
        P.dma("sp", self.lng[:], I["ln_in_g"].partition_broadcast(128), writes=["lng"])
        P.dma("sp", self.lnb[:], I["ln_in_b"].partition_broadcast(128), writes=["lnb"])
        for tt in range(NT):
            i = self.nxt("ln", 2)
            P.dma("sp", self.lnx[i][:], I["x"][tt * 128:(tt + 1) * 128, :], writes=["lnx%d" % i])
            self.ln_inplace(self.lnx[i][:], "lnx%d" % i)
            P.dma("sp", self.hres[0][tt * 128:(tt + 1) * 128, :], self.lnx[i][:], reads=["lnx%d" % i], writes=["hres0"])
        self.barrier()

    def layer(self, l, last):
        P, I = self.P, self.I
        hin = self.hres[l % 2]
        for tt in range(NT):
            i = self.nxt("ln", 2)
            P.dma("sp", self.lnx[i][:], hin[tt * 128:(tt + 1) * 128, :], reads=["hres%d" % (l % 2)], writes=["lnx%d" % i])
            self.to_hT(self.lnx[i], "lnx%d" % i, tt)
        self.win = I["w_in"][l * D:(l + 1) * D, :]
        for mx in "MCADB":
            if mx in self.mixers:
                getattr(self, "mixer_" + mx)(l)
            else:
                self.zero_o(mx)
            self.barrier()
        self.merge(l, last)
        self.barrier()

    def zero_o(self, mx):
        c0, w = {"M": (OM, 256), "C": (OC, 512), "A": (OA, 512), "D": (OD, 512), "B": (OB, 512)}[mx]
        t = self.tmp[2]
        self.memset("pool", t[:, :], 0.0, ["tmp2"])
        for tt in range(NT):
            self.P.dma("sp", self.o_scr[tt * 128:(tt + 1) * 128, c0:c0 + w], t[:, 0:w], reads=["tmp2"], writes=["o_scr"])

    def attn_head(self, maps, vfn, vkey, nkt, dv1, table, band, post):
        nm = len(maps)
        G = self.G

        nqt = 4 if nm == 1 else 2
        QB = nqt * 128

        def accap(m, qt):
            bi = 4 + m * nqt + qt
            return self.bank[bi][:, 0:dv1], bi

        for qb in range(S // QB):
            q0 = qb * QB
            kts = []
            for kt in range(nkt):
                dk = kt * 128 - q0
                if band and (dk - (QB - 1) > 1024 or dk + 127 < -1024):
                    continue
                kts.append(kt)
            steps = [(idx, kt, m) for idx, kt in enumerate(kts) for m in range(nm)]

            def stageA(si):
                idx, kt, m = steps[si]
                q_ap, kfn = maps[m]
                sb_ = si % 4
                self.mm(self.bank[sb_][:, 0:QB], kfn(kt), q_ap[:, q0:q0 + QB], True, True, ["BIG0", "BIG1"], [self.bk(sb_)])

            def stageBC(si):
                idx, kt, m = steps[si]
                sb_ = si % 4
                pti = self.nxt("pt", 4)
                ptile = self.pt[pti]
                if table:
                    pei = self.nxt("pe", 4)
                    self.act(self.pe_[pei][:, 0:QB], self.bank[sb_][:, 0:QB], AF.Exp, [self.bk(sb_)], ["pe%d" % pei], scale=0.125)
                    j0 = kt * 128 - q0 + XC
                    gs = G[:, j0 - (QB - 1):j0 + 1][:, ::-1]
                    self.tt("dve", ptile[:, 0:QB], self.pe_[pei][:, 0:QB], gs, ALU.mult, ["pe%d" % pei, "G"], ["pt%d" % pti])
                else:
                    self.act(ptile[:, 0:QB], self.bank[sb_][:, 0:QB], AF.Exp, [self.bk(sb_)], ["pt%d" % pti], scale=0.125)
                for qt in range(nqt):
                    acc, bi = accap(m, qt)
                    self.mm(acc, ptile[:, qt * 128:(qt + 1) * 128], vfn(kt), idx == 0, idx == len(kts) - 1,
                            ["pt%d" % pti, vkey], [self.bk(bi)])

            PF = 3
            for si in range(min(PF, len(steps))):
                stageA(si)
            for si in range(len(steps)):
                if si + PF < len(steps):
                    stageA(si + PF)
                stageBC(si)
            for qt in range(nqt):
                accs = [accap(m, qt) for m in range(nm)]
                post(qb * nqt + qt, [a for a, _ in accs], [self.bk(bi) for _, bi in accs])

    def post_simple(self, h, hd, ocol):
        def post(tt, accs, keys):
            acc = accs[0]
            sm = self.sm
            i = self.nxt("ost", 4)
            self.P.op("dve", lambda e: e.reciprocal(out=sm[:, 8:9], in_=acc[:, hd:hd + 1]), reads=keys, writes=["sm"])
            self.stt("dve", self.ost[i][:, 0:hd], acc[:, 0:hd], sm[:, 8:9], self.sg[:, tt, h * hd:(h + 1) * hd], ALU.mult, ALU.mult,
                     keys + ["sm", "BIG3"], ["ost%d" % i])
            self.P.dma("sp", self.o_scr[tt * 128:(tt + 1) * 128, ocol + h * hd:ocol + (h + 1) * hd], self.ost[i][:, 0:hd],
                       reads=["ost%d" % i], writes=["o_scr"])
        return post

    def cons_T(self, dst, dkey):
        def consume(b, ct, tb):
            self.cp("dve" if (ct + tb) % 2 else "act", dst[:, ct, tb * 512:(tb + 1) * 512], self.bank[b][:, :], [self.bk(b)], [dkey])
        return consume

    def cons_sg(self, c0, n):
        def consume(b, tt):
            self.act(self.sg[:, tt, c0:c0 + n], self.bank[b][:, 0:n], AF.Silu, [self.bk(b)], ["BIG3"])
        return consume

    def cons_v(self, V1, h0, nh, hd):
        def consume(b, tt):
            self.cp("dve", V1[:, tt, h0:h0 + nh, 0:hd], self.bank[b][:, 0:nh * hd].rearrange("p (h d) -> p h d", d=hd), [self.bk(b)], ["BIG2"])
        return consume

    def mixer_M(self, l):
        P, I = self.P, self.I
        wkv = I["w_mem_kv"][l * D:(l + 1) * D, :]
        V1 = self.BIG[2][:, 0:520].rearrange("p (k h d) -> p k h d", k=2, h=4, d=65)
        self.memset("pool", self.BIG[2][:, 0:520], 1.0, ["BIG2"])
        memT = self.memT
        wl = self.wload(wkv[:, 0:256], 256)
        self.proj_T(wl, 256, lambda b, ct, tb: self.cp("dve", self.kT[:, ct, 0:256], self.bank[b][:, 0:256], [self.bk(b)], ["BIG1"]),
                    rhs_fn=lambda c, tb: memT[:, c, 0:256], rkey="memT", ntb=1, tbw=256)
        wl = self.wload(wkv[:, 256:512], 256)
        self.proj_N(wl, 256, self.cons_v(V1, 0, 4, 64), lhs_fn=lambda c, tt: memT[:, c, tt * 128:(tt + 1) * 128], lkey="memT", ntt=2)
        wl = self.wload(self.win[:, M0:M0 + 256], 256)
        self.proj_T(wl, 256, self.cons_T(self.qT, "BIG0"))
        wl = self.wload(self.win[:, M0 + 256:M0 + 512], 256)
        self.proj_N(wl, 256, self.cons_sg(0, 256))
        if l == 0 and "m" in self.mixers:
            self.dbg("qT", self.BIG[0][:, 0:8192], ["BIG0"], BF16)
            self.dbg("kT", self.BIG[1][:, 0:8192], ["BIG1"], BF16)
            self.dbg("V1", self.BIG[2][:, 0:520], ["BIG2"], BF16)
            self.dbg("sg", self.BIG[3][:, 0:8192], ["BIG3"], BF16)
            self.dbg("hT", self.hT[:, :, :].rearrange("p c t -> p (c t)"), ["hT"], BF16)
        for h in range(4):
            ct, pb = h // 2, (h % 2) * 64
            q_ap = self.qT[pb:pb + 64, ct, :]
            self.attn_head([(q_ap, lambda kt, ct=ct, pb=pb: self.kT[pb:pb + 64, ct, kt * 128:(kt + 1) * 128])],
                           lambda kt, h=h: V1[:, kt, h, :], "BIG2", 2, 65, False, False, self.post_simple(h, 64, OM))

    def mixer_A(self, l):
        P = self.P
        V1 = self.BIG[2][:, 0:8320].rearrange("p (k h d) -> p k h d", k=16, h=8, d=65)
        self.memset("pool", self.BIG[2][:, 0:8320], 1.0, ["BIG2"])
        for j in range(2):
            wl = self.wload(self.win[:, A0 + j * 256:A0 + (j + 1) * 256], 256)
            self.proj_T(wl, 256, lambda b, ct, tb, j=j: self.cons_T(self.qT, "BIG0")(b, ct + 2 * j, tb))
        for j in range(2):
            wl = self.wload(self.win[:, A0 + 512 + j * 256:A0 + 512 + (j + 1) * 256], 256)
            self.proj_T(wl, 256, lambda b, ct, tb, j=j: self.cons_T(self.kT, "BIG1")(b, ct + 2 * j, tb))
        for j in range(2):
            wl = self.wload(self.win[:, A0 + 1024 + j * 256:A0 + 1024 + (j + 1) * 256], 256)
            self.proj_N(wl, 256, self.cons_v(V1, 4 * j, 4, 64))
        for j in range(2):
            wl = self.wload(self.win[:, A0 + 1536 + j * 256:A0 + 1536 + (j + 1) * 256], 256)
            self.proj_N(wl, 256, self.cons_sg(j * 256, 256))
        for h in range(8):
            ct, pb = h // 2, (h % 2) * 64
            P.dma("sp", self.G[:, 0:3968], AP(self.xtab.tensor, h * 4096, [[1, 128], [1, 3968]]), reads=["xtab"], writes=["G"])
            q_ap = self.qT[pb:pb + 64, ct, :]
            self.attn_head([(q_ap, lambda kt, ct=ct, pb=pb: self.kT[pb:pb + 64, ct, kt * 128:(kt + 1) * 128])],
                           lambda kt, h=h: V1[:, kt, h, :], "BIG2", 16, 65, True, True, self.post_simple(h, 64, OA))

    def normrope(self, b, tt, nh, gcol):
        n = nh * 64
        sm = self.sm
        t0, t1, t2 = self.tmp
        ps = self.bank[b][:, 0:n]
        self.act(t0[:, 0:n], ps, AF.Square, [self.bk(b)], ["tmp0"])
        self.P.op("dve", lambda e: e.reduce_sum(out=sm[:, 16:16 + nh], in_=t0[:, 0:n].rearrange("p (h d) -> p h d", d=64), axis=AX.X),
                  reads=["tmp0"], writes=["sm"])
        self.rsqrt_cols(sm[:, 16:16 + nh], sm[:, 24:24 + nh], 1.0 / 64, 1e-6)
        v3 = lambda ap: ap.rearrange("p (h d) -> p h d", d=64)
        self.tt("dve", v3(t0[:, 0:n]), v3(ps), sm[:, 24:24 + nh].unsqueeze(2).broadcast_to([128, nh, 64]), ALU.mult,
                [self.bk(b), "sm"], ["tmp0"])
        self.tt("dve", v3(t0[:, 0:n]), v3(t0[:, 0:n]), self.gq[:, gcol:gcol + 64].unsqueeze(1).broadcast_to([128, nh, 64]), ALU.mult,
                ["tmp0", "gq"], ["tmp0"])
        self.tt("pool", v3(t1[:, 0:n]), v3(t0[:, 0:n]), self.ropec[:, tt, :].unsqueeze(1).broadcast_to([128, nh, 64]), ALU.mult,
                ["tmp0", "ropec"], ["tmp1"])
        v5 = lambda ap: ap.rearrange("p (h a b c) -> p h a b c", a=2, b=2, c=16)
        rs = self.ropes[:, tt, :].rearrange("p (a b c) -> p a b c", a=2, b=2, c=16)
        for bb in range(2):
            self.tt("dve", v5(t2[:, 0:n])[:, :, :, bb, :], v5(t0[:, 0:n])[:, :, :, 1 - bb, :],
                    rs[:, :, bb, :].unsqueeze(1).broadcast_to([128, nh, 2, 16]), ALU.mult, ["tmp0", "ropes"], ["tmp2"])
        self.tt("dve", t1[:, 0:n], t1[:, 0:n], t2[:, 0:n], ALU.add, ["tmp1", "tmp2"], ["tmp1"])

    def mixer_C(self, l):
        P, I = self.P, self.I
        V1 = self.BIG[2][:, 0:2080].rearrange("p (k h d) -> p k h d", k=16, h=2, d=65)
        self.memset("pool", self.BIG[2][:, 0:2080], 1.0, ["BIG2"])
        P.dma("sp", self.gq[:, 0:64], I["c_qnorm_g"][l:l + 1, :].partition_broadcast(128), writes=["gq"])
        P.dma("sp", self.gq[:, 64:128], I["c_knorm_g"][l:l + 1, :].partition_broadcast(128), writes=["gq"])
        t1 = self.tmp[1]
        for j in range(2):
            wl = self.wload(self.win[:, C0 + j * 256:C0 + (j + 1) * 256], 256)

            def cons_q(b, tt, j=j):
                self.normrope(b, tt, 4, 0)
                b2 = self.pbank()
                for u in range(2):
                    self.tr(self.bank[b2][:, u * 128:(u + 1) * 128], t1[:, u * 128:(u + 1) * 128], ["tmp1"], [self.bk(b2)])
                self.cp("act", self.qT[:, 2 * j:2 * j + 2, tt * 128:(tt + 1) * 128],
                        self.bank[b2][:, 0:256].rearrange("p (c t) -> p c t", t=128), [self.bk(b2)], ["BIG0"])
            self.proj_N(wl, 256, cons_q)
        wl = self.wload(self.win[:, C0 + 512:C0 + 768], 256)

        def cons_kv(b, tt):
            self.cp("act", V1[:, tt, :, 0:64], self.bank[b][:, 128:256].rearrange("p (h d) -> p h d", d=64), [self.bk(b)], ["BIG2"])
            self.normrope(b, tt, 2, 64)
            t2 = self.tmp[2]
            self.cp("dve", t2[:, 0:256].rearrange("p (g r d) -> p g r d", g=2, r=2, d=64),
                    t1[:, 0:128].rearrange("p (g d) -> p g d", d=64).unsqueeze(2).broadcast_to([128, 2, 2, 64]), ["tmp1"], ["tmp2"])
            b2 = self.pbank()
            for u in range(2):
                self.tr(self.bank[b2][:, u * 128:(u + 1) * 128], t2[:, u * 128:(u + 1) * 128], ["tmp2"], [self.bk(b2)])
            self.cp("act", self.kT[:, 0:2, tt * 128:(tt + 1) * 128],
                    self.bank[b2][:, 0:256].rearrange("p (c t) -> p c t", t=128), [self.bk(b2)], ["BIG1"])
        self.proj_N(wl, 256, cons_kv)
        for j in range(2):
            wl = self.wload(self.win[:, C0 + 768 + j * 256:C0 + 768 + (j + 1) * 256], 256)
            self.proj_N(wl, 256, self.cons_sg(j * 256, 256))
        for h in range(8):
            ct, pb = h // 2, (h % 2) * 64
            g = h // 4
            q_ap = self.qT[pb:pb + 64, ct, :]
            self.attn_head([(q_ap, lambda kt, g=g, pb=pb: self.kT[pb:pb + 64, g, kt * 128:(kt + 1) * 128])],
                           lambda kt, g=g: V1[:, kt, g, :], "BIG2", 16, 65, False, False, self.post_simple(h, 64, OC))

    def mixer_D(self, l):
        P, I = self.P, self.I
        V1 = self.BIG[2][:, 0:8256].rearrange("p (k h d) -> p k h d", k=16, h=4, d=129)
        self.memset("pool", self.BIG[2][:, 0:8256], 1.0, ["BIG2"])
        lam_init = 0.8 - 0.6 * math.exp(-0.3 * l)
        lamt, sm = self.lamt, self.sm
        P.dma("sp", lamt[:, 0:256], I["d_lambda"][l:l + 1, :].partition_broadcast(128), writes=["lamt"])
        P.dma("sp", self.subg[:], I["d_subln_g"][l:l + 1, :].partition_broadcast(128), writes=["subg"])
        self.ts("pool", self.subg[:], self.subg[:], 1.0 - lam_init, None, ALU.mult, None, ["subg"], ["subg"])
        lv = lamt[:, 0:256].rearrange("p (a b c) -> p a b c", a=2, b=2, c=64)
        lp = self.tmp[2][:, 0:128].rearrange("p (a c) -> p a c", c=64)
        self.tt("dve", lp, lv[:, :, 0, :], lv[:, :, 1, :], ALU.mult, ["lamt"], ["tmp2"])
        P.op("dve", lambda e: e.reduce_sum(out=lamt[:, 256:258], in_=lp, axis=AX.X), reads=["tmp2"], writes=["lamt"])
        self.act(lamt[:, 258:260], lamt[:, 256:258], AF.Exp, ["lamt"], ["lamt"])
        self.tt("dve", lamt[:, 260:261], lamt[:, 259:260], lamt[:, 258:259], ALU.subtract, ["lamt"], ["lamt"])
        self.ts("dve", lamt[:, 260:261], lamt[:, 260:261], -lam_init, None, ALU.add, None, ["lamt"], ["lamt"])
        for j in range(2):
            wl = self.wload(self.win[:, D0 + j * 256:D0 + (j + 1) * 256], 256)
            self.proj_T(wl, 256, lambda b, ct, tb, j=j: self.cons_T(self.qT, "BIG0")(b, ct + 2 * j, tb))
        for j in range(2):
            wl = self.wload(self.win[:, D0 + 512 + j * 256:D0 + 512 + (j + 1) * 256], 256)
            self.proj_T(wl, 256, lambda b, ct, tb, j=j: self.cons_T(self.kT, "BIG1")(b, ct + 2 * j, tb))
        for j in range(2):
            wl = self.wload(self.win[:, D0 + 1024 + j * 256:D0 + 1024 + (j + 1) * 256], 256)
            self.proj_N(wl, 256, self.cons_v(V1, 2 * j, 2, 128))
        for j in range(2):
            wl = self.wload(self.win[:, D0 + 1536 + j * 256:D0 + 1536 + (j + 1) * 256], 256)
            self.proj_N(wl, 256, self.cons_sg(j * 256, 256))
        t0 = self.tmp[0]
        for h in range(4):
            P.dma("sp", self.G[:, 0:3968], AP(self.xtab.tensor, (8 + h) * 4096, [[1, 128], [1, 3968]]), reads=["xtab"], writes=["G"])

            def post(tt, accs, keys, h=h):
                a1, a2 = accs
                i = self.nxt("ost", 4)
                P.op("dve", lambda e: e.reciprocal(out=sm[:, 8:9], in_=a1[:, 128:129]), reads=keys, writes=["sm"])
                P.op("dve", lambda e: e.reciprocal(out=sm[:, 9:10], in_=a2[:, 128:129]), reads=keys, writes=["sm"])
                self.tt("dve", sm[:, 9:10], sm[:, 9:10], lamt[:, 260:261], ALU.mult, ["sm", "lamt"], ["sm"])
                self.ts("dve", t0[:, 0:128], a1[:, 0:128], sm[:, 8:9], None, ALU.mult, None, keys + ["sm"], ["tmp0"])
                self.stt("dve", t0[:, 128:256], a2[:, 0:128], sm[:, 9:10], t0[:, 0:128], ALU.mult, ALU.add, keys + ["sm", "tmp0"], ["tmp0"])
                self.memset("dve", sm[:, 10:11], 0.0, ["sm"])
                self.act(t0[:, 256:384], t0[:, 128:256], AF.Square, ["tmp0", "sm"], ["tmp0", "sm"], accum_out=sm[:, 10:11])
                self.rsqrt_cols(sm[:, 10:11], sm[:, 11:12], 1.0 / 128, 1e-5)
                self.stt("dve", t0[:, 128:256], t0[:, 128:256], sm[:, 11:12], self.subg[:], ALU.mult, ALU.mult, ["tmp0", "sm", "subg"], ["tmp0"])
                self.tt("dve", self.ost[i][:], t0[:, 128:256], self.sg[:, tt, h * 128:(h + 1) * 128], ALU.mult, ["tmp0", "BIG3"], ["ost%d" % i])
                P.dma("sp", self.o_scr[tt * 128:(tt + 1) * 128, OD + h * 128:OD + (h + 1) * 128], self.ost[i][:],
                      reads=["ost%d" % i], writes=["o_scr"])
            maps = [(self.qT[c * 64:(c + 1) * 64, h, :], (lambda kt, c=c, h=h: self.kT[c * 64:(c + 1) * 64, h, kt * 128:(kt + 1) * 128]))
                    for c in range(2)]
            self.attn_head(maps, lambda kt, h=h: V1[:, kt, h, :], "BIG2", 16, 129, True, False, post)

    def dbg(self, name, ap, reads, dt=F32):
        if not self.debug:
            return
        t = self.nc.dram_tensor("dbg_" + name, list(ap.shape), dt, kind="ExternalOutput").ap()
        self.P.dma("sp", t, ap, reads=reads, is_output=True)
        self.dbg_outs.append("dbg_" + name)

    def mixer_B(self, l):
        P, I = self.P, self.I
        CW = 0.6065306597126334
        t_ring = self.tmp
        mub = self.lnx[0][:, 0:768].rearrange("p (v n) -> p v n", n=256)

        def load_mu(c0):
            for v in range(2):
                P.dma("sp", mub[:, 1 + v, :], I["shift_mu"][l * 2 + v:l * 2 + v + 1, c0:c0 + 256].partition_broadcast(128), writes=["lnx0"])
            self.tt("dve", mub[:, 0, :], mub[:, 1, :], mub[:, 2, :], ALU.add, ["lnx0"], ["lnx0"])
            self.ts("dve", mub[:, 0, :], mub[:, 0, :], -1.0, 1.0, ALU.mult, ALU.add, ["lnx0"], ["lnx0"])
            return [(mub[:, 0, :], "lnx0"), (mub[:, 1, :], "lnx0"), (mub[:, 2, :], "lnx0")]

        def stage_out(dst_ap, dkey, func=None):
            def f(b, n_part=128, ncol=512):
                i = self.nxt("tmp", 3)
                if func is None:
                    self.cp("dve", t_ring[i][0:n_part, 0:ncol], self.bank[b][0:n_part, 0:ncol], [self.bk(b)], ["tmp%d" % i])
                else:
                    self.act(t_ring[i][0:n_part, 0:ncol], self.bank[b][0:n_part, 0:ncol], func, [self.bk(b)], ["tmp%d" % i])
                P.dma("sp", dst_ap, t_ring[i][0:n_part, 0:ncol], reads=["tmp%d" % i], writes=[dkey])
            return f

        for j in range(4):
            c0 = j * 256
            wl = self.wload(self.win[:, B0 + c0:B0 + c0 + 256], 256, variants=load_mu(c0))
            self.proj_T(wl, 256, lambda b, ct, tb, c0=c0: stage_out(self.rk_scr[c0 + ct * 128:c0 + (ct + 1) * 128, tb * 512:(tb + 1) * 512], "rk_scr")(b),
                        shifts=(0, -1, 1))
        for j in range(2):
            c0 = 1024 + j * 256
            wl = self.wload(self.win[:, B0 + c0:B0 + c0 + 256], 256, variants=load_mu(c0))
            self.proj_N(wl, 256, lambda b, tt, j=j: stage_out(self.v_scr[tt * 128:(tt + 1) * 128, j * 256:(j + 1) * 256], "v_scr")(b, 128, 256),
                        shifts=(0, -1, 1))
        wl = self.wload(self.win[:, B0 + 1536:B0 + 1792], 256, variants=load_mu(1536))
        self.proj_T(wl, 256, lambda b, ct, tb: stage_out(self.wa_scr[ct * 128:(ct + 1) * 128, tb * 512:(tb + 1) * 512], "wa_scr",
                                                         AF.Tanh if ct == 0 else AF.Copy)(b), shifts=(0, -1, 1))
        for j in range(2):
            wl = self.wload(self.win[:, B0 + 1792 + j * 256:B0 + 1792 + (j + 1) * 256], 256)
            self.proj_N(wl, 256, lambda b, tt, j=j: stage_out(self.sg_scr[tt * 128:(tt + 1) * 128, j * 256:(j + 1) * 256], "sg_scr", AF.Silu)(b, 128, 256))
        self.barrier()
        slots = []
        for bi in range(4):
            a = self.BIG[bi][:].bitcast(F32)
            for q in range(4):
                slots.append(a[:, q * 1024:(q + 1) * 1024])
        for q in range(4):
            slots.append(self.G[:, q * 1024:(q + 1) * 1024])
        for wi in range(1):
            a = self.wst[wi][:, :, :].rearrange("p c n -> p (c n)")
            for q in range(2):
                slots.append(a[:, q * 1024:(q + 1) * 1024])
        si = [0]

        def slot(full=True):
            if full:
                if si[0] % 2:
                    si[0] += 1
                a = slots[si[0] // 2]
                si[0] += 2
                return a
            a = slots[si[0] // 2][:, (si[0] % 2) * 512:(si[0] % 2) * 512 + 512]
            si[0] += 1
            return a

        def v3(ap, w):
            return ap[0:64, 0:8 * w].rearrange("p (h t) -> p h t", t=w)

        w_upS = slot()[0:64, :].rearrange("p (e c) -> p e c", c=512)
        a_upS = slot()[0:64, :].rearrange("p (e c) -> p e c", c=512)
        w0B = slot()[0:64, :].rearrange("p (e c) -> p e c", c=512)
        rkT = slot()[0:64, :].rearrange("p (g t) -> p g t", t=64)
        AR = slot()[0:64, :].rearrange("p (h t) -> p h t", t=128)
        NP = [self.RX[:, q * 1024:(q + 1) * 1024].rearrange("p (h t) -> p h t", t=128) for q in range(2)]
        ysb = slot()[0:64, 0:520]
        rmaskS = slot(False)[0:64, 0:384].rearrange("p (e n) -> p e n", n=192)
        waT = slot(False)[0:64, 0:256].rearrange("p (g t) -> p g t", t=64)
        vtok = slot(False)[0:64, :]
        sgw = slot(False)[0:64, :]
        asT, kkn, ke, be, tE0, tE1, bch, kch, z = [v3(slot(False), 64) for _ in range(9)]
        eLs, Bt, Kt = [slot(False)[0:64, :] for _ in range(3)]
        Mm = [self.RX[:, 2048 + q * 512:2048 + (q + 1) * 512].rearrange("p (h t) -> p h t", t=64) for q in range(2)]
        Mrb, Mak, Mrk, Xs, Us, tmpS = [v3(slot(False), 64) for _ in range(6)]
        Sst = [v3(slot(False), 64) for _ in range(2)]
        assert si[0] <= 2 * len(slots), si[0]
        rwp = self.gq[0:64, 0:40]
        omka = self.gq[0:64, 40:48]
        ident64 = self.ident[0:64, 0:64]
        ones64 = self.onesf[0:64, 0:64]
        self.r32 = True
        P.dma("sp", w_upS, I["rwkv_w_up"][l * 128:(l + 1) * 128, :].rearrange("(e r) c -> r e c", r=64), writes=["w_upS"])
        P.dma("sp", a_upS, I["rwkv_a_up"][l * 128:(l + 1) * 128, :].rearrange("(e r) c -> r e c", r=64), writes=["a_upS"])
        for e in range(2):
            P.dma("sp", w0B[:, e, :], I["rwkv_w0"][l * 2 + e:l * 2 + e + 1, :].partition_broadcast(64), writes=["w0B"])
        P.dma("sp", rmaskS, I["rmask"].rearrange("(e p) n -> p e n", p=64), writes=["rmaskS"])

        pm = self.tmp[0]
        P.dma("sp", pm[0:16, 0:64], I["rwkv_a0"][l * 2:(l + 1) * 2, :].rearrange("e (h c) -> (e h) c", c=64), writes=["tmp0"])
        P.dma("sp", pm[16:24, 0:64], I["rwkv_k_k"][l:l + 1, :].rearrange("e (h c) -> (e h) c", c=64), writes=["tmp0"])
        P.dma("sp", pm[24:32, 0:64], I["rwkv_k_a"][l:l + 1, :].rearrange("e (h c) -> (e h) c", c=64), writes=["tmp0"])
        P.dma("sp", pm[32:40, 0:64], I["rwkv_r_k"][l:l + 1, :].rearrange("e (h c) -> (e h) c", c=64), writes=["tmp0"])
        b = self.pbank()
        self.P.op("pe", lambda e_: e_.transpose(self.bank[b][0:64, 0:40], pm[0:40, 0:64], self.ident[0:40, 0:40]), reads=["tmp0", "ident"], writes=[self.bk(b)])
        self.cp("dve", rwp, self.bank[b][0:64, 0:40], [self.bk(b)], ["gq"])
        self.ts("dve", omka, rwp[:, 24:32], -1.0, 1.0, ALU.mult, ALU.add, ["gq"], ["gq"])
        bc3 = lambda ap: ap.unsqueeze(2).broadcast_to([64, 8, 64])
        hb = lambda b_, h, w=64: self.bank[b_][0:64, h * w:(h + 1) * w]
        b3 = lambda b_, w=64: self.bank[b_][0:64, 0:8 * w].rearrange("p (h t) -> p h t", t=w)

        for e in range(2):
            Scur = 0
            self.memset("dve", Sst[0], 0.0, ["S0"])
            order = range(32) if e == 0 else range(31, -1, -1)
            tl = 63 if e == 0 else 0
            mS, mI, mT = rmaskS[:, e, 0:64], rmaskS[:, e, 64:128], rmaskS[:, e, 128:192]
            for ch in order:
                t0 = ch * 64
                P.dma("sp", rkT, self.rk_scr.rearrange("(g p) t -> p g t", p=64)[:, :, t0:t0 + 64], reads=["rk_scr"], writes=["rkT"])
                P.dma("sp", waT, self.wa_scr.rearrange("(g p) t -> p g t", p=64)[:, :, t0:t0 + 64], reads=["wa_scr"], writes=["waT"])
                P.dma("sp", vtok, self.v_scr[t0:t0 + 64, :], reads=["v_scr"], writes=["vtok"])
                rT, kT_ = rkT[:, 0:8, :], rkT[:, 8:16, :]
                b = self.pbank()
                self.mm(self.bank[b][0:64, :], waT[:, e, :], w_upS[:, e, :], True, True, ["waT", "w_upS"], [self.bk(b)])
                self.tt("dve", sgw, self.bank[b][0:64, :], w0B[:, e, :], ALU.add, [self.bk(b), "w0B"], ["sgw"])
                self.act(sgw, sgw, AF.Sigmoid, ["sgw"], ["sgw"])
                b = self.pbank()
                for h in range(8):
                    self.mm(hb(b, h), a_upS[:, e, h * 64:(h + 1) * 64], waT[:, 2 + e, :], True, True, ["waT", "a_upS"], [self.bk(b)])
                self.tt("dve", asT, b3(b), bc3(rwp[:, e * 8:(e + 1) * 8]), ALU.add, [self.bk(b), "gq"], ["asT"])
                self.act(asT, asT, AF.Sigmoid, ["asT"], ["asT"])
                self.tt("dve", kkn, kT_, bc3(rwp[:, 16:24]), ALU.mult, ["rkT", "gq"], ["kkn"])
                self.act(tE0, kkn, AF.Square, ["kkn"], ["tE0"])
                b = self.pbank()
                self.mm(self.bank[b][0:64, :], ones64, tE0.rearrange("p h t -> p (h t)"), True, True, ["tE0", "onesf"], [self.bk(b)])
                self.act(tE0, b3(b), AF.Sqrt, [self.bk(b)], ["tE0"])
                self.ts("dve", tE0, tE0, 1e-12, None, ALU.max, None, ["tE0"], ["tE0"])
                self.P.op("dve", lambda e_: e_.reciprocal(out=tE0, in_=tE0), reads=["tE0"], writes=["tE0"])
                self.tt("dve", kkn, kkn, tE0, ALU.mult, ["kkn", "tE0"], ["kkn"])
                self.tt("pool", ke, asT, bc3(rwp[:, 24:32]), ALU.mult, ["asT", "gq"], ["ke"])
                self.tt("pool", ke, ke, bc3(omka), ALU.add, ["ke", "gq"], ["ke"])
                self.tt("pool", ke, ke, kT_, ALU.mult, ["ke", "rkT"], ["ke"])
                self.tt("pool", be, kkn, asT, ALU.mult, ["kkn", "asT"], ["be"])
                self.tt("pool", z, rT, ke, ALU.mult, ["rkT", "ke"], ["z"])
                bLi = self.pbank()
                for h in range(8):
                    self.mm(hb(bLi, h), sgw[:, h * 64:(h + 1) * 64], mI, True, True, ["sgw", "rmaskS"], [self.bk(bLi)])
                self.act(tE0, b3(bLi), AF.Exp, [self.bk(bLi)], ["tE0"], scale=-CW)
                self.act(tE1, b3(bLi), AF.Exp, [self.bk(bLi)], ["tE1"], scale=CW)
                self.tt("dve", AR[:, :, 64:128], rT, tE0, ALU.mult, ["rkT", "tE0"], ["AR"])
                self.cp("dve", self.sm[0:64, 32:40], tE0[:, :, tl], ["tE0"], ["sm"])
                self.tt("dve", bch, be, tE1, ALU.mult, ["be", "tE1"], ["bch"])
                self.tt("pool", kch, ke, tE1, ALU.mult, ["ke", "tE1"], ["kch"])
                bLe = self.pbank()
                for h in range(8):
                    self.mm(hb(bLe, h), sgw[:, h * 64:(h + 1) * 64], mS, True, True, ["sgw", "rmaskS"], [self.bk(bLe)])
                self.act(tE0, b3(bLe), AF.Exp, [self.bk(bLe)], ["tE0"], scale=-CW)
                self.stt("dve", AR[:, :, 0:64], kkn, -1.0, tE0, ALU.mult, ALU.mult, ["kkn", "tE0"], ["AR"])
                b = self.pbank()
                self.mm(self.bank[b][0:64, :], mT, sgw, True, True, ["sgw", "rmaskS"], [self.bk(b)])
                self.act(eLs, self.bank[b][0:64, :], AF.Exp, [self.bk(b)], ["eLs"], scale=-CW)
                for src, skey, dst, dkey in ((be, "be", Bt, "Bt"), (ke, "ke", Kt, "Kt")):
                    b = self.pbank()
                    for h in range(8):
                        self.P.op("pe", lambda e_, b=b, h=h, src=src: e_.transpose(hb(b, h), src[:, h, :], ident64), reads=[skey, "ident"], writes=[self.bk(b)])
                    self.tt("dve", dst, self.bank[b][0:64, :], eLs, ALU.mult, [self.bk(b), "eLs"], [dkey])
                b = self.pbank()
                for h in range(8):
                    self.mm(self.bank[b][0:64, h:h + 1], z[:, h, :], rwp[:, 32 + h:33 + h], True, True, ["z", "gq"], [self.bk(b)])
                self.cp("act", ysb[:, 512:520], self.bank[b][0:64, 0:8], [self.bk(b)], ["ysb"])
                for h in range(8):
                    self.mm(self.bank[h // 4][0:64, (h % 4) * 128:(h % 4 + 1) * 128], bch[:, h, :], AR[:, h, :], True, True, ["bch", "AR"], [self.bk(h // 4)])
                for h in range(8):
                    self.mm(self.bank[4 + h // 4][0:64, (h % 4) * 128:(h % 4 + 1) * 128], kch[:, h, :], AR[:, h, :], True, True, ["kch", "AR"], [self.bk(4 + h // 4)])
                for h in range(8):
                    self.mm(hb(6, h), AR[:, h, 0:64], bch[:, h, :], True, True, ["bch", "AR"], [self.bk(6)])
                m4 = lambda m_: m_.unsqueeze(1).broadcast_to([64, 4, 64])
                for g in range(2):
                    bb = self.bank[g][0:64, :].rearrange("p (h t) -> p h t", t=128)
                    kb = self.bank[4 + g][0:64, :].rearrange("p (h t) -> p h t", t=128)
                    self.tt("dve", NP[0][:, 4 * g:4 * g + 4, 0:64], bb[:, :, 0:64], m4(mS), ALU.mult, [self.bk(g), "rmaskS"], ["NP0"])
                    self.tt("dve", Mrb[:, 4 * g:4 * g + 4, :], bb[:, :, 64:128], m4(mI), ALU.mult, [self.bk(g), "rmaskS"], ["Mrb"])
                    self.tt("dve", Mak[:, 4 * g:4 * g + 4, :], kb[:, :, 0:64], m4(mS), ALU.mult, [self.bk(4 + g), "rmaskS"], ["Mak"])
                    self.tt("dve", Mrk[:, 4 * g:4 * g + 4, :], kb[:, :, 64:128], m4(mI), ALU.mult, [self.bk(4 + g), "rmaskS"], ["Mrk"])
                self.tt("dve", Mm[0], b3(6), mT.unsqueeze(1).broadcast_to([64, 8, 64]), ALU.mult, [self.bk(6), "rmaskS"], ["Mm0"])
                self.tt("pool", NP[0][:, :, 64:128], NP[0][:, :, 0:64], ident64.unsqueeze(1).broadcast_to([64, 8, 64]), ALU.add, ["NP0", "ident"], ["NP0"])
                cur = 0
                for step in range(6):
                    nx = 1 - cur
                    pbk = (0, 1) if step % 2 == 0 else (4, 5)
                    mbk = 6 if step % 2 else 7
                    ncur, nnx, mcur, mnx = "NP%d" % cur, "NP%d" % nx, "Mm%d" % cur, "Mm%d" % nx
                    if step == 0:
                        for h in range(8):
                            self.mm(self.bank[pbk[h // 4]][0:64, (h % 4) * 128:(h % 4) * 128 + 64], Mm[cur][:, h, :], NP[cur][:, h, 0:64], True, True,
                                    [mcur, ncur], [self.bk(pbk[h // 4])])
                    elif step < 5:
                        for h in range(8):
                            self.mm(self.bank[pbk[h // 4]][0:64, (h % 4) * 128:(h % 4 + 1) * 128], Mm[cur][:, h, :], NP[cur][:, h, :], True, True,
                                    [mcur, ncur], [self.bk(pbk[h // 4])])
                    else:
                        for h in range(8):
                            self.mm(self.bank[pbk[h // 4]][0:64, (h % 4) * 128 + 64:(h % 4 + 1) * 128], Mm[cur][:, h, :], NP[cur][:, h, 64:128], True, True,
                                    [mcur, ncur], [self.bk(pbk[h // 4])])
                    if step < 5:
                        for h in range(8):
                            self.mm(hb(mbk, h), NP[cur][:, h, 0:64], Mm[cur][:, h, :], True, True, [mcur, ncur], [self.bk(mbk)])
                    for g in range(2):
                        pv = self.bank[pbk[g]][0:64, :].rearrange("p (h t) -> p h t", t=128)
                        if step < 5:
                            self.cp("act", NP[nx][:, 4 * g:4 * g + 4, 0:64], pv[:, :, 0:64], [self.bk(pbk[g])], [nnx])
                        if step == 0:
                            self.cp("dve", NP[nx][:, 4 * g:4 * g + 4, 64:128], NP[cur][:, 4 * g:4 * g + 4, 64:128], [ncur], [nnx])
                        else:
                            self.tt("dve", NP[nx][:, 4 * g:4 * g + 4, 64:128], pv[:, :, 64:128], NP[cur][:, 4 * g:4 * g + 4, 64:128], ALU.add,
                                    [self.bk(pbk[g]), ncur], [nnx])
                    if step < 5:
                        self.cp("act", Mm[nx], b3(mbk), [self.bk(mbk)], [mnx])
                    cur = nx
                TT, tkey = NP[cur], "NP%d" % cur
                S0, skey = Sst[Scur], "S%d" % Scur
                S1, s1key = Sst[1 - Scur], "S%d" % (1 - Scur)
                bX = self.pbank()
                for h in range(8):
                    self.mm(hb(bX, h), AR[:, h, 0:64], S0[:, h, :], True, False, ["AR", skey], [self.bk(bX)])
                    self.mm(hb(bX, h), Mak[:, h, :], vtok[:, h * 64:(h + 1) * 64], False, True, ["Mak", "vtok"], [self.bk(bX)])
                self.cp("dve", Xs, b3(bX), [self.bk(bX)], ["Xs"])
                bU = self.pbank()
                for h in range(8):
                    self.mm(hb(bU, h), TT[:, h, 64:128], Xs[:, h, :], True, True, [tkey, "Xs"], [self.bk(bU)])
                self.cp("act", Us, b3(bU), [self.bk(bU)], ["Us"])
                bY = self.pbank()
                for h in range(8):
                    self.mm(hb(bY, h), AR[:, h, 64:128], S0[:, h, :], True, False, ["AR", skey], [self.bk(bY)])
                    self.mm(hb(bY, h), Mrb[:, h, :], Us[:, h, :], False, False, ["Mrb", "Us"], [self.bk(bY)])
                    self.mm(hb(bY, h), Mrk[:, h, :], vtok[:, h * 64:(h + 1) * 64], False, True, ["Mrk", "vtok"], [self.bk(bY)])
                self.cp("act", ysb[:, 0:512], self.bank[bY][0:64, :], [self.bk(bY)], ["ysb"])
                P.dma("sp", self.y_scr[e * S + t0:e * S + t0 + 64, :], ysb, reads=["ysb"], writes=["y_scr"])
                bS = self.pbank()
                for h in range(8):
                    self.mm(hb(bS, h), Bt[:, h * 64:(h + 1) * 64], Us[:, h, :], True, False, ["Bt", "Us"], [self.bk(bS)])
                    self.mm(hb(bS, h), Kt[:, h * 64:(h + 1) * 64], vtok[:, h * 64:(h + 1) * 64], False, True, ["Kt", "vtok"], [self.bk(bS)])
                self.tt("pool", tmpS, S0, bc3(self.sm[0:64, 32:40]), ALU.mult, [skey, "sm"], ["tmpS"])
                self.tt("dve", S1, tmpS, b3(bS), ALU.add, ["tmpS", self.bk(bS)], [s1key])
                Scur = 1 - Scur
        self.r32 = False
        self.barrier()
        P.dma("sp", self.lng[:, 0:512], I["rwkv_ln_g"][l:l + 1, :].partition_broadcast(128), writes=["lng"])
        P.dma("sp", self.lnb[:, 0:512], I["rwkv_ln_b"][l:l + 1, :].partition_broadcast(128), writes=["lnb"])
        yf, yb, vt = self.lnx[0], self.lnx[1], self.junk
        sm = self.sm
        t0_, t1_, t2_ = self.tmp
        for tt in range(NT):
            P.dma("sp", yf[:, 0:520], self.y_scr[tt * 128:(tt + 1) * 128, :], reads=["y_scr"], writes=["lnx0"])
            P.dma("sp", yb[:, 0:520], self.y_scr[S + tt * 128:S + (tt + 1) * 128, :], reads=["y_scr"], writes=["lnx1"])
            P.dma("sp", vt[:, 0:512], self.v_scr[tt * 128:(tt + 1) * 128, :], reads=["v_scr"], writes=["junk"])
            P.dma("sp", vt[:, 512:1024], self.sg_scr[tt * 128:(tt + 1) * 128, :], reads=["sg_scr"], writes=["junk"])
            self.tt("dve", yf[:, 0:520], yf[:, 0:520], yb[:, 0:520], ALU.add, ["lnx0", "lnx1"], ["lnx0"])
            y3 = yf[:, 0:512].rearrange("p (h d) -> p h d", d=64)
            P.op("dve", lambda e_, y3=y3: e_.reduce_sum(out=sm[:, 40:48], in_=y3, axis=AX.X), reads=["lnx0"], writes=["sm"])
            self.ts("dve", sm[:, 40:48], sm[:, 40:48], -1.0 / 64, None, ALU.mult, None, ["sm"], ["sm"])
            self.tt("dve", y3, y3, sm[:, 40:48].unsqueeze(2).broadcast_to([128, 8, 64]), ALU.add, ["lnx0", "sm"], ["lnx0"])
            self.act(t0_[:, 0:512], yf[:, 0:512], AF.Square, ["lnx0"], ["tmp0"])
            P.op("dve", lambda e_: e_.reduce_sum(out=sm[:, 48:56], in_=t0_[:, 0:512].rearrange("p (h d) -> p h d", d=64), axis=AX.X), reads=["tmp0"], writes=["sm"])
            self.rsqrt_cols(sm[:, 48:56], sm[:, 56:64], 1.0 / 64, 64e-5)
            self.tt("dve", y3, y3, sm[:, 56:64].unsqueeze(2).broadcast_to([128, 8, 64]), ALU.mult, ["lnx0", "sm"], ["lnx0"])
            self.tt("dve", yf[:, 0:512], yf[:, 0:512], self.lng[:, 0:512], ALU.mult, ["lnx0", "lng"], ["lnx0"])
            self.tt("pool", yf[:, 0:512], yf[:, 0:512], self.lnb[:, 0:512], ALU.add, ["lnx0", "lnb"], ["lnx0"])
            self.tt("pool", t1_[:, 0:512].rearrange("p (h d) -> p h d", d=64), vt[:, 0:512].rearrange("p (h d) -> p h d", d=64),
                    yf[:, 512:520].unsqueeze(2).broadcast_to([128, 8, 64]), ALU.mult, ["junk", "lnx0"], ["tmp1"])
            self.tt("dve", t1_[:, 0:512], t1_[:, 0:512], yf[:, 0:512], ALU.add, ["tmp1", "lnx0"], ["tmp1"])
            self.tt("dve", t2_[:, 0:512], t1_[:, 0:512], vt[:, 512:1024], ALU.mult, ["tmp1", "junk"], ["tmp2"])
            P.dma("sp", self.o_scr[tt * 128:(tt + 1) * 128, OB:OB + 512], t2_[:, 0:512], reads=["tmp2"], writes=["o_scr"])

    def merge(self, l, last):
        P, I = self.P, self.I
        wg_all = I["w_gate"][l * D:(l + 1) * D, :]
        wb_all = I["w_branch"][l * 2304:(l + 1) * 2304, :]
        wo_all = I["w_out"][l * D:(l + 1) * D, :]
        P.dma("sp", self.lng[:], I["ln_g"][l:l + 1, :].partition_broadcast(128), writes=["lng"])
        P.dma("sp", self.lnb[:], I["ln_b"][l:l + 1, :].partition_broadcast(128), writes=["lnb"])
        bgT = self.lamt[:, 0:40]
        for i5 in range(5):
            P.dma("sp", bgT[:, i5 * 8:(i5 + 1) * 8], I["b_gate"][l:l + 1, i5 * 1024:(i5 + 1) * 1024].rearrange("e (g c) -> c (e g)", c=128),
                  writes=["lamt"], allow_slow_non_contiguous=True)
        oT = self.BIG[0][:, 0:9216].rearrange("p (j t) -> p j t", t=512)
        yTf = self.BIG[1][:].bitcast(F32)[:, 0:4096].rearrange("p (c t) -> p c t", t=512)
        otile = self.BIG[2][:].bitcast(F32)[:, 0:2304]
        yTb = self.BIG[3][:, 0:4096].rearrange("p (c t) -> p c t", t=512)
        hgrp = self.G[:, 0:4096].rearrange("p (q c) -> p q c", c=1024)
        hin = self.hres[l % 2]
        hout = self.out if last else self.hres[(l + 1) % 2]
        t0, t1 = self.tmp[0], self.tmp[1]
        mb = [0]

        def mbank():
            mb[0] = (mb[0] + 1) % 8
            return mb[0]

        for grp in range(4):
            for tq in range(4):
                tt = grp * 4 + tq
                P.dma("sp", otile, self.o_scr[tt * 128:(tt + 1) * 128, :], reads=["o_scr"], writes=["BIG2"])
                P.dma("sp", hgrp[:, tq, :], hin[tt * 128:(tt + 1) * 128, :], reads=["hres%d" % (l % 2)], writes=["G"])
                for j4 in range(5):
                    nj = min(4, 18 - j4 * 4)
                    b = mbank()
                    for u in range(nj):
                        j = j4 * 4 + u
                        self.tr(self.bank[b][:, u * 128:(u + 1) * 128], otile[:, j * 128:(j + 1) * 128], ["BIG2"], [self.bk(b)])
                    self.cp("act" if j4 % 2 else "dve", oT[:, j4 * 4:j4 * 4 + nj, tq * 128:(tq + 1) * 128],
                            self.bank[b][:, 0:nj * 128].rearrange("p (c t) -> p c t", t=128), [self.bk(b)], ["BIG0"])
            hsl = lambda c: self.hT[:, c, 1 + grp * 512:1 + (grp + 1) * 512]
            for i, (r0, rw) in enumerate(BROWS):
                kci = rw // 128
                for cc in range(4):
                    wg, wgk = self.wload(wg_all[:, i * 1024 + cc * 256:i * 1024 + (cc + 1) * 256], 256)[0]
                    wb, wbk = self.wload(wb_all[r0:r0 + rw, cc * 256:(cc + 1) * 256], 256, kc=kci)[0]
                    for u in range(2):
                        ct = cc * 2 + u
                        b1 = mbank()
                        for c in range(8):
                            self.mm(self.bank[b1][:, :], wg[:, c, u * 128:(u + 1) * 128], hsl(c), c == 0, c == 7, [wgk, "hT"], [self.bk(b1)])
                        ti = self.nxt("mt", 2)
                        tg = self.tmp[ti]
                        self.act(tg[:, :], self.bank[b1][:, :], AF.Sigmoid, [self.bk(b1), "lamt"], ["tmp%d" % ti], bias=bgT[:, i * 8 + ct:i * 8 + ct + 1])
                        b2 = mbank()
                        for c in range(kci):
                            self.mm(self.bank[b2][:, :], wb[:, c, u * 128:(u + 1) * 128], oT[:, r0 // 128 + c, :], c == 0, c == kci - 1,
                                    [wbk, "BIG0"], [self.bk(b2)])
                        ysl = yTf[:, ct, :]
                        if i == 0:
                            self.tt("dve", ysl, self.bank[b2][:, :], tg[:, :], ALU.mult, [self.bk(b2), "tmp%d" % ti], ["BIG1"])
                        else:
                            self.tt("dve", tg[:, :], self.bank[b2][:, :], tg[:, :], ALU.mult, [self.bk(b2), "tmp%d" % ti], ["tmp%d" % ti])
                            self.tt("dve", ysl, ysl, tg[:, :], ALU.add, ["BIG1", "tmp%d" % ti], ["BIG1"])
            if l == 0 and grp == 0:
                self.dbg("yTf", self.BIG[1][:].bitcast(F32)[:, 0:4096], ["BIG1"])
                self.dbg("oT", self.BIG[0][:, 0:9216], ["BIG0"], BF16)
                self.dbg("bgT", self.lamt[:, 0:40], ["lamt"])
            for half in range(2):
                self.cp("act" if half else "dve", yTb[:, half * 4:half * 4 + 4, :], yTf[:, half * 4:half * 4 + 4, :], ["BIG1"], ["BIG3"])
            for cc in range(4):
                wo, wok = self.wload(wo_all[:, cc * 256:(cc + 1) * 256], 256)[0]
                for tq in range(4):
                    b = mbank()
                    for c in range(8):
                        self.mm(self.bank[b][:, 0:256], yTb[:, c, tq * 128:(tq + 1) * 128], wo[:, c, :], c == 0, c == 7, [wok, "BIG3"], [self.bk(b)])
                    hs = hgrp[:, tq, cc * 256:(cc + 1) * 256]
                    self.stt("dve", hs, hs, ALPHA, self.bank[b][:, 0:256], ALU.mult, ALU.add, ["G", self.bk(b)], ["G"])
            for tq in range(4):
                tt = grp * 4 + tq
                self.ln_inplace(hgrp[:, tq, :], "G")
                P.dma("sp", hout[tt * 128:(tt + 1) * 128, :], hgrp[:, tq, :], reads=["G"],
                      writes=["out" if last else "hres%d" % ((l + 1) % 2)], is_output=last)


def make_in_map(inputs, b, consts):
    m = {"x": np.ascontiguousarray(inputs["x"][b]), "mem": np.ascontiguousarray(inputs["mem"][b])}
    for nm, shp in IN_SPECS:
        if nm in consts:
            m[nm] = consts[nm]
        else:
            m[nm] = np.ascontiguousarray(np.asarray(inputs[nm], dtype=np.float32).reshape(shp))
    return m


def kernel(**inputs):
    consts = host_consts()
    kb = KB(debug=False)
    nb = inputs["x"].shape[0]
    in_maps = [make_in_map(inputs, b, consts) for b in range(nb)]
    res = run_bass_kernel_spmd(kb.nc, in_maps, core_ids=list(range(nb)))
    out = np.stack([np.asarray(r["out"], dtype=np.float32).reshape(S, D) for r in res.results], axis=0)
    return out
```

```python
import math
from concourse.ap import AP
import contextlib
import numpy as np
import concourse.bass as bass
import concourse.mybir as mybir
from concourse.bass_utils import run_bass_kernel_spmd

F32 = mybir.dt.float32
BF16 = mybir.dt.bfloat16
I32 = mybir.dt.int32
AF = mybir.ActivationFunctionType
ALU = mybir.AluOpType
AX = mybir.AxisListType

ENGS = ("pe", "act", "dve", "pool", "sp")
DMA_SEMS = 8


class Op:
    __slots__ = ("eng", "fn", "waits", "is_dma", "idx", "marked", "dma_slot", "dma_val", "prewait")

    def __init__(self, eng, fn, is_dma):
        self.eng = eng
        self.fn = fn
        self.is_dma = is_dma
        self.waits = []
        self.marked = False
        self.idx = None
        self.dma_slot = None
        self.dma_val = None
        self.prewait = None


class Prog:
    def __init__(self, nc, same_engine_sync=True):
        self.nc = nc
        self.ops = {e: [] for e in ENGS}
        self.last_write = {}
        self.readers = {}
        self.children = {}
        self.same_engine_sync = same_engine_sync
        self.dma_count = {e: 0 for e in ENGS}
        self.dma_hist = {e: [] for e in ENGS}
        self.all_dma_out = []
        self.stack = contextlib.ExitStack()
        self.n_ops = 0

    def sb(self, name, shape, dt):
        return self.stack.enter_context(self.nc.sbuf_tensor("s_" + name, list(shape), dt))

    def ps(self, name, shape, dt):
        return self.stack.enter_context(self.nc.psum_tensor("p_" + name, list(shape), dt))

    def _related(self, k):
        if "/" in k:
            p = k.split("/")[0]
            self.children.setdefault(p, set()).add(k)
            return (k, p)
        return (k,) + tuple(self.children.get(k, ()))

    def _deps(self, op, reads, writes):
        deps = []
        for k0 in reads:
            for k in self._related(k0):
                w = self.last_write.get(k)
                if w is not None:
                    deps.append(w)
        for k0 in writes:
            for k in self._related(k0):
                w = self.last_write.get(k)
                if w is not None:
                    deps.append(w)
                for r in self.readers.get(k, ()):
                    deps.append(r)
        best = {}
        for d in deps:
            if d is op:
                continue
            key = (d.eng, d.is_dma, d.dma_slot if d.is_dma else None)
            cur = best.get(key)
            if cur is None or d.idx > cur.idx:
                best[key] = d
        for d in best.values():
            if (not d.is_dma) and d.eng == op.eng and not op.is_dma:
                if op.eng == "pe" or not self.same_engine_sync:
                    continue
            op.waits.append(d)
            d.marked = True
        for k in reads:
            self.readers.setdefault(k, []).append(op)
        for k in writes:
            self.last_write[k] = op
            self.readers[k] = []

    def barrier(self, fn):
        o = Op("pool", fn, False)
        o.idx = len(self.ops["pool"])
        self.ops["pool"].append(o)
        self._deps(o, [], ["__phase__"])
        return o

    def op(self, eng, fn, reads=(), writes=()):
        reads = list(reads) + ["__phase__"]
        o = Op(eng, fn, False)
        o.idx = len(self.ops[eng])
        self.ops[eng].append(o)
        self._deps(o, reads, writes)
        self.n_ops += 1
        return o

    def dma(self, eng, out, in_, reads=(), writes=(), is_output=False, **kw):
        def fn(e, out=out, in_=in_, kw=kw):
            return e.dma_start(out=out, in_=in_, **kw)
        reads = list(reads) + ["__phase__"]
        o = Op(eng, fn, True)
        o.idx = len(self.ops[eng])
        n = self.dma_count[eng]
        self.dma_count[eng] += 1
        o.dma_slot = n % DMA_SEMS
        o.dma_val = 16 * (n // DMA_SEMS + 1)
        if n >= DMA_SEMS:
            o.prewait = self.dma_hist[eng][n - DMA_SEMS]
        self.dma_hist[eng].append(o)
        self.ops[eng].append(o)
        self._deps(o, reads, writes)
        if is_output:
            self.all_dma_out.append(o)
        self.n_ops += 1
        return o

    def emit(self):
        nc = self.nc
        st = self.stack
        fin = Op("sp", None, False)
        fin.idx = len(self.ops["sp"])
        for o in self.all_dma_out:
            fin.waits.append(o)
        self.ops["sp"].append(fin)
        csem = {e: st.enter_context(nc.semaphore("c_" + e)) for e in ENGS}
        dsem = {e: [st.enter_context(nc.semaphore("d_%s_%d" % (e, i))) for i in range(DMA_SEMS)]
                for e in ENGS if self.dma_count[e] > 0}
        for e in ENGS:
            c = 0
            for o in self.ops[e]:
                if o.is_dma:
                    continue
                if o.marked:
                    c += 1
                    o.dma_val = c
        block = st.enter_context(nc.Block())
        prog = self

        def run(e, eng):
            seen = {}
            for o in prog.ops[e]:
                ws = list(o.waits)
                if o.prewait is not None:
                    ws.append(o.prewait)
                for d in ws:
                    if d.is_dma:
                        sem, val = dsem[d.eng][d.dma_slot], d.dma_val
                    else:
                        sem, val = csem[d.eng], d.dma_val
                    k = id(sem)
                    if seen.get(k, 0) >= val:
                        continue
                    seen[k] = val
                    eng.wait_ge(sem, val)
                if o.fn is None:
                    continue
                ins = o.fn(eng)
                if o.is_dma:
                    ins.then_inc(dsem[e][o.dma_slot], 16)
                elif o.marked:
                    ins.then_inc(csem[e], 1)

        @block.tensor
        def _(eng):
            run("pe", eng)

        @block.scalar
        def _(eng):
            run("act", eng)

        @block.vector
        def _(eng):
            run("dve", eng)

        @block.gpsimd
        def _(eng):
            run("pool", eng)

        @block.sync
        def _(eng):
            run("sp", eng)

    def close(self):
        self.stack.close()


F32R = mybir.dt.float32r

S = 2048
D = 1024
NT = 16
DEPTH = 2
WC = 256
XC = 2047
GW = 4096
ALPHA = (2 * DEPTH) ** 0.25
A0, B0, C0, D0, M0 = 0, 2048, 4352, 5632, 7680
OA, OB, OC, OD, OM = 0, 512, 1024, 1536, 2048
BROWS = [(0, 512), (512, 512), (1024, 512), (1536, 512), (2048, 256)]


def rel_bucket_np(rel):
    nb = 16
    max_exact = 8
    n = np.abs(rel)
    nf = np.maximum(n, 1).astype(np.float32)
    large = max_exact + (np.log(nf / max_exact) / np.float32(math.log(1024 / max_exact)) * (nb - max_exact)).astype(np.int32)
    large = np.minimum(large, nb - 1)
    return np.where(rel > 0, nb, 0) + np.where(n < max_exact, n, large)


def host_consts():
    c = {}
    c["ident"] = np.eye(128, dtype=np.float32)
    rel = np.arange(4096) - XC
    bkt = rel_bucket_np(rel)
    oh = np.zeros((32, 4096), np.float32)
    oh[bkt, np.arange(4096)] = 1.0
    c["onehot"] = oh
    n = np.abs(rel)
    mA = (n <= 64).astype(np.float32) + ((rel % 4 == 0) & (n <= 256)) + ((rel % 16 == 0) & (n <= 1024))
    mt = np.ones((12, 4096), np.float32)
    mt[:8] = mA[None, :]
    c["multab"] = mt
    t = np.arange(S)
    row = (t // 64).astype(np.float32)
    col = (t % 64).astype(np.float32)
    freqs = (10000.0 ** (-(np.arange(16, dtype=np.float32) / 16))).astype(np.float32)
    ar = row[:, None] * freqs[None, :]
    ac = col[:, None] * freqs[None, :]
    c["ropec"] = np.concatenate([np.cos(ar), np.cos(ar), np.cos(ac), np.cos(ac)], 1).astype(np.float32)
    c["ropes"] = np.concatenate([-np.sin(ar), np.sin(ar), -np.sin(ac), np.sin(ac)], 1).astype(np.float32)
    tri = np.zeros((2, 3, 128, 128), np.float32)
    sg = np.arange(128)[:, None]
    tt = np.arange(128)[None, :]
    same = (sg // 64) == (tt // 64)
    tri[0, 0] = same & (sg <= tt)
    tri[0, 1] = same & (sg < tt)
    tri[0, 2] = same & (sg > tt)
    tri[1, 0] = same & (sg >= tt)
    tri[1, 1] = same & (sg > tt)
    tri[1, 2] = same & (sg < tt)
    c["tri"] = tri.reshape(6 * 128, 128)
    mk_ = np.zeros((2, 64, 192), np.float32)
    a = np.arange(64)[:, None]
    b = np.arange(64)[None, :]
    mk_[0, :, 0:64] = a < b
    mk_[0, :, 64:128] = a <= b
    mk_[0, :, 128:192] = b < a
    mk_[1, :, 0:64] = a > b
    mk_[1, :, 64:128] = a >= b
    mk_[1, :, 128:192] = b > a
    c["rmask"] = mk_.reshape(128, 192)
    return c


IN_SPECS = [("ln_in_g", [1, D]), ("ln_in_b", [1, D]), ("rel_bias", [32, 12]), ("w_in", [DEPTH * D, 8192]),
            ("shift_mu", [DEPTH * 2, 1792]), ("rwkv_w0", [DEPTH * 2, 512]), ("rwkv_w_up", [DEPTH * 2 * 64, 512]),
            ("rwkv_a0", [DEPTH * 2, 512]), ("rwkv_a_up", [DEPTH * 2 * 64, 512]), ("rwkv_k_k", [DEPTH, 512]),
            ("rwkv_k_a", [DEPTH, 512]), ("rwkv_r_k", [DEPTH, 512]), ("rwkv_ln_g", [DEPTH, 512]),
            ("rwkv_ln_b", [DEPTH, 512]), ("c_qnorm_g", [DEPTH, 64]), ("c_knorm_g", [DEPTH, 64]),
            ("d_lambda", [DEPTH, 256]), ("d_subln_g", [DEPTH, 128]), ("w_mem_kv", [DEPTH * D, 512]),
            ("w_branch", [DEPTH * 2304, D]), ("w_gate", [DEPTH * D, 5120]), ("b_gate", [DEPTH, 5120]),
            ("w_out", [DEPTH * D, D]), ("ln_g", [DEPTH, D]), ("ln_b", [DEPTH, D]),
            ("ident", [128, 128]), ("onehot", [32, 4096]), ("multab", [12, 4096]), ("ropec", [S, 64]),
            ("ropes", [S, 64]), ("tri", [768, 128]), ("rmask", [128, 192])]


class KB:
    def __init__(self, debug=False, mixers="MCADB", layers=DEPTH):
        self.debug = debug
        self.mixers = mixers
        nc = bass.Bass("TRN2", target_bir_lowering=False)
        self.nc = nc
        P = Prog(nc)
        self.P = P
        I = {}
        I["x"] = nc.dram_tensor("x", [S, D], F32, kind="ExternalInput").ap()
        I["mem"] = nc.dram_tensor("mem", [256, D], F32, kind="ExternalInput").ap()
        for nm, shp in IN_SPECS:
            I[nm] = nc.dram_tensor(nm, list(shp), F32, kind="ExternalInput").ap()
        self.I = I
        self.out = nc.dram_tensor("out", [S, D], F32, kind="ExternalOutput").ap()
        self.hres = [nc.dram_tensor("hres%d" % i, [S, D], F32, kind="ExternalOutput" if debug else "Internal").ap() for i in range(2)]
        self.dbg_outs = []
        self.o_scr = nc.dram_tensor("o_scr", [S, 2304], F32, kind="ExternalOutput" if debug else "Internal").ap()
        self.xtab = nc.dram_tensor("xtab", [12, 4096], F32).ap()
        self.rk_scr = nc.dram_tensor("rk_scr", [1024, S], F32).ap()
        self.wa_scr = nc.dram_tensor("wa_scr", [256, S], F32).ap()
        self.v_scr = nc.dram_tensor("v_scr", [S, 512], F32).ap()
        self.y_scr = nc.dram_tensor("y_scr", [2 * S, 520], F32, kind="ExternalOutput" if debug else "Internal").ap()
        self.sg_scr = nc.dram_tensor("sg_scr", [S, 512], F32).ap()
        self.ident = P.sb("ident", [128, 128], F32)
        self.hT = P.sb("hT", [128, 8, S + 2], BF16)
        self.BIG = [P.sb("BIG%d" % i, [128, 9216], BF16) for i in range(4)]
        self.G = P.sb("G", [128, GW], F32)
        self.wst = [P.sb("wst%d" % i, [128, 8, WC], F32) for i in range(1)]
        self.RX = P.sb("RX", [64, 3072], F32)
        self.wbf = [P.sb("wbf%d" % i, [128, 8, WC], BF16) for i in range(4)]
        self.ropec = P.sb("ropec", [128, NT, 64], F32)
        self.ropes = P.sb("ropes", [128, NT, 64], F32)
        self.lnx = [P.sb("lnx%d" % i, [128, D], F32) for i in range(2)]
        self.junk = P.sb("junk", [128, D], F32)
        self.lng = P.sb("lng", [128, D], F32)
        self.lnb = P.sb("lnb", [128, D], F32)
        self.pt = [P.sb("pt%d" % i, [128, 512], BF16) for i in range(4)]
        self.pe_ = [P.sb("pe%d" % i, [128, 512], BF16) for i in range(4)]
        self.ost = [P.sb("ost%d" % i, [128, 128], F32) for i in range(4)]
        self.sm = P.sb("sm", [128, 64], F32)
        self.tmp = [P.sb("tmp%d" % i, [128, 512], F32) for i in range(3)]
        self.onesf = P.sb("onesf", [128, 128], F32)
        self.gq = P.sb("gq", [128, 128], F32)
        self.subg = P.sb("subg", [128, 128], F32)
        self.lamt = P.sb("lamt", [128, 264], F32)
        self.pbar = P.sb("pbar", [1, 8], F32)
        self.memT = P.sb("memT", [128, 8, 256], BF16)
        self.bank = [P.ps("bank%d" % i, [128, 512], F32) for i in range(8)]
        self.cnt = {}
        self.pbi = 0
        B0_, B1_, B2_, B3_ = [b[:] for b in self.BIG]
        self.qT = B0_[:, 0:8192].rearrange("p (c t) -> p c t", t=S)
        self.kT = B1_[:, 0:8192].rearrange("p (c t) -> p c t", t=S)
        self.sg = B3_[:, 0:8192].rearrange("p (t c) -> p t c", c=512)
        self.prelude()
        for l in range(layers):
            self.layer(l, last=(l == layers - 1))
        P.emit()
        P.close()

    def nxt(self, name, n):
        v = self.cnt.get(name, 0)
        self.cnt[name] = (v + 1) % n
        return v

    def bk(self, i):
        return "bank%d" % i

    def pbank(self):
        self.pbi ^= 1
        return 2 + self.pbi

    def barrier(self):
        pbar = self.pbar
        self.P.barrier(lambda e: e.memset(pbar[:], 0.0))

    def R(self, ap):
        if ap.dtype == F32 and ap.name == "s_RX":
            return ap.bitcast(F32R)
        return ap

    def mm(self, out, lhsT, rhs, start, stop, reads, writes):
        if lhsT.name == "s_RX" and rhs.name == "s_RX":
            lhsT, rhs = self.R(lhsT), self.R(rhs)
        self.P.op("pe", lambda e: e.matmul(out, lhsT=lhsT, rhs=rhs, start=start, stop=stop), reads=reads, writes=writes)

    def tr(self, out, in_, reads, writes, np_=128):
        ident = self.ident
        self.P.op("pe", lambda e: e.transpose(out, in_, ident[0:np_, 0:np_]), reads=list(reads) + ["ident"], writes=writes)

    def cp(self, eng, out, in_, reads, writes):
        out = self.R(out)
        if eng == "act":
            self.P.op("act", lambda e: e.copy(out=out, in_=in_), reads=reads, writes=writes)
        else:
            self.P.op(eng, lambda e: e.tensor_copy(out=out, in_=in_), reads=reads, writes=writes)

    def act(self, out, in_, func, reads, writes, **kw):
        out = self.R(out)
        self.P.op("act", lambda e: e.activation(out=out, in_=in_, func=func, **kw), reads=reads, writes=writes)

    def tt(self, eng, out, in0, in1, op, reads, writes):
        out = self.R(out)
        self.P.op(eng, lambda e: e.tensor_tensor(out=out, in0=in0, in1=in1, op=op), reads=reads, writes=writes)

    def ts(self, eng, out, in0, s1, s2, op0, op1, reads, writes):
        out = self.R(out)
        if s2 is None:
            self.P.op(eng, lambda e: e.tensor_scalar(out=out, in0=in0, scalar1=s1, scalar2=None, op0=op0), reads=reads, writes=writes)
        else:
            self.P.op(eng, lambda e: e.tensor_scalar(out=out, in0=in0, scalar1=s1, scalar2=s2, op0=op0, op1=op1), reads=reads, writes=writes)

    def stt(self, eng, out, in0, scalar, in1, op0, op1, reads, writes):
        out = self.R(out)
        self.P.op(eng, lambda e: e.scalar_tensor_tensor(out=out, in0=in0, scalar=scalar, in1=in1, op0=op0, op1=op1), reads=reads, writes=writes)

    def memset(self, eng, ap, val, writes):
        ap = self.R(ap)
        self.P.op(eng, lambda e: e.memset(ap, val), writes=writes)

    def rsqrt_cols(self, src, dst, scale, eps, key="sm"):
        self.ts("dve", dst, src, scale, eps, ALU.mult, ALU.add, [key], [key])
        self.P.op("act", lambda e: e.sqrt(out=dst, in_=dst), reads=[key], writes=[key])
        self.P.op("dve", lambda e: e.reciprocal(out=dst, in_=dst), reads=[key], writes=[key])

    def wload(self, src2d, n, kc=8, variants=None):
        P = self.P
        src = src2d.rearrange("(c p) n -> p c n", p=128)
        if variants is None:
            j = self.nxt("wb", 4)
            P.dma("pool", self.wbf[j][:, 0:kc, 0:n], src, writes=["wbf%d" % j])
            return [(self.wbf[j], "wbf%d" % j)]
        wst = self.wst[0]
        P.dma("sp", wst[:, 0:kc, 0:n], src, writes=["wst0"])
        res = []
        for vi, (vap, vkey) in enumerate(variants):
            j = self.nxt("wb", 4)
            self.tt("dve" if vi != 1 else "pool", self.wbf[j][:, 0:kc, 0:n], wst[:, 0:kc, 0:n], vap.unsqueeze(1).broadcast_to([128, kc, n]), ALU.mult,
                    ["wst0", vkey], ["wbf%d" % j])
            res.append((self.wbf[j], "wbf%d" % j))
        return res

    def proj_T(self, wl, n, consume, shifts=(0,), rhs_fn=None, rkey="hT", ntb=4, tbw=512):
        hT = self.hT
        for ct in range(n // 128):
            for tb in range(ntb):
                b = self.pbank()
                nmm = 8 * len(shifts)
                m = 0
                for (wap, wkey), s in zip(wl, shifts):
                    for c in range(8):
                        if rhs_fn is None:
                            lo = 1 + tb * 512 + s
                            rhs = hT[:, c, lo:lo + 512]
                        else:
                            rhs = rhs_fn(c, tb)
                        self.mm(self.bank[b][:, 0:tbw], wap[:, c, ct * 128:(ct + 1) * 128], rhs, m == 0, m == nmm - 1,
                                [wkey, rkey], [self.bk(b)])
                        m += 1
                consume(b, ct, tb)

    def proj_N(self, wl, n, consume, shifts=(0,), lhs_fn=None, lkey="hT", ntt=NT, kc=8):
        hT = self.hT
        for tt in range(ntt):
            b = self.pbank()
            nmm = kc * len(shifts)
            m = 0
            for (wap, wkey), s in zip(wl, shifts):
                for c in range(kc):
                    if lhs_fn is None:
                        lo = 1 + tt * 128 + s
                        lh = hT[:, c, lo:lo + 128]
                    else:
                        lh = lhs_fn(c, tt)
                    self.mm(self.bank[b][:, 0:n], lh, wap[:, c, 0:n], m == 0, m == nmm - 1, [wkey, lkey], [self.bk(b)])
                    m += 1
            consume(b, tt)

    def ln_inplace(self, xt, xkey, eps=1e-5):
        sm, junk = self.sm, self.junk
        P = self.P
        P.op("dve", lambda e: e.reduce_sum(out=sm[:, 0:1], in_=xt, axis=AX.X), reads=[xkey], writes=["sm"])
        self.ts("dve", sm[:, 1:2], sm[:, 0:1], -1.0 / D, None, ALU.mult, None, ["sm"], ["sm"])
        self.ts("dve", xt, xt, sm[:, 1:2], None, ALU.add, None, [xkey, "sm"], [xkey])
        self.memset("dve", sm[:, 2:3], 0.0, ["sm"])
        self.act(junk[:], xt, AF.Square, [xkey, "sm"], ["junk", "sm"], accum_out=sm[:, 2:3])
        self.rsqrt_cols(sm[:, 2:3], sm[:, 3:4], 1.0 / D, eps)
        self.stt("dve", xt, xt, sm[:, 3:4], self.lng[:], ALU.mult, ALU.mult, [xkey, "sm", "lng"], [xkey])
        self.tt("dve", xt, xt, self.lnb[:], ALU.add, [xkey, "lnb"], [xkey])

    def to_hT(self, src, skey, tt):
        hT = self.hT
        for half in range(2):
            b = self.pbank()
            for c4 in range(4):
                c = half * 4 + c4
                self.tr(self.bank[b][:, c4 * 128:(c4 + 1) * 128], src[:, c * 128:(c + 1) * 128], [skey], [self.bk(b)])
            self.cp("act" if half else "dve", hT[:, half * 4:half * 4 + 4, 1 + tt * 128:1 + (tt + 1) * 128],
                    self.bank[b][:, :].rearrange("p (c t) -> p c t", t=128), [self.bk(b)], ["hT"])

    def prelude(self):
        P, I = self.P, self.I
        P.dma("sp", self.ident[:], I["ident"], writes=["ident"])
        P.dma("sp", self.ropec[:], I["ropec"].rearrange("(t p) c -> p t c", p=128), writes=["ropec"])
        P.dma("sp", self.ropes[:], I["ropes"].rearrange("(t p) c -> p t c", p=128), writes=["ropes"])
        self.memset("pool", self.onesf[:], 1.0, ["onesf"])
        self.memset("pool", self.hT[:, :, 0:1], 0.0, ["hT"])
        self.memset("pool", self.hT[:, :, S + 1:S + 2], 0.0, ["hT"])
        tmpA = self.tmp[0]
        rb = tmpA[0:32, 0:12]
        P.dma("sp", rb, I["rel_bias"], writes=["tmp0"])
        ohs = self.BIG[0][:].bitcast(F32)
        P.dma("sp", ohs[0:32, 0:4096], I["onehot"], writes=["BIG0"])
        mts = self.BIG[1][:].bitcast(F32)
        P.dma("sp", mts[0:12, 0:4096], I["multab"], writes=["BIG1"])
        xts = self.BIG[2][:].bitcast(F32)
        for j in range(8):
            b = self.pbank()
            self.mm(self.bank[b][0:12, :], rb, ohs[0:32, j * 512:(j + 1) * 512], True, True, ["tmp0", "BIG0"], [self.bk(b)])
            self.act(xts[0:12, j * 512:(j + 1) * 512], self.bank[b][0:12, :], AF.Exp, [self.bk(b)], ["BIG2"])
        self.tt("dve", xts[0:12, 0:4096], xts[0:12, 0:4096], mts[0:12, 0:4096], ALU.mult, ["BIG2", "BIG1"], ["BIG2"])
        P.dma("sp", self.xtab, xts[0:12, 0:4096], reads=["BIG2"], writes=["xtab"])
        self.barrier()
        for mt_ in range(2):
            i = self.nxt("ln", 2)
            P.dma("sp", self.lnx[i][:], I["mem"][mt_ * 128:(mt_ + 1) * 128, :], writes=["lnx%d" % i])
            for half in range(2):
                b = self.pbank()
                for c4 in range(4):
                    c = half * 4 + c4
                    self.tr(self.bank[b][:, c4 * 128:(c4 + 1) * 128], self.lnx[i][:, c * 128:(c + 1) * 128], ["lnx%d" % i], [self.bk(b)])
                self.cp("dve", self.memT[:, half * 4:half * 4 + 4, mt_ * 128:(mt_ + 1) * 128],
                        self.bank[b][:, :].rearrange("p (c t) -> p c t", t=128), [self.bk(b)], ["memT"])
        P.dma("sp", self.lng[:], I["ln_in_g"].partition_broadcast(128), writes=["lng"])
        P.dma("sp", self.lnb[:], I["ln_in_b"].partition_broadcast(128), writes=["lnb"])
        for tt in range(NT):
            i = self.nxt("ln", 2)
            P.dma("sp", self.lnx[i][:], I["x"][tt * 128:(tt + 1) * 128, :], writes=["lnx%d" % i])
            self.ln_inplace(self.lnx[i][:], "lnx%d" % i)
            P.dma("sp", self.hres[0][tt * 128:(tt + 1) * 128, :], self.lnx[i][:], reads=["lnx%d" % i], writes=["hres0"])
        self.barrier()

    def layer(self, l, last):
        P, I = self.P, self.I
        hin = self.hres[l % 2]
        for tt in range(NT):
            i = self.nxt("ln", 2)
            P.dma("sp", self.lnx[i][:], hin[tt * 128:(tt + 1) * 128, :], reads=["hres%d" % (l % 2)], writes=["lnx%d" % i])
            self.to_hT(self.lnx[i], "lnx%d" % i, tt)
        self.win = I["w_in"][l * D:(l + 1) * D, :]
        for mx in "MCADB":
            if mx in self.mixers:
                getattr(self, "mixer_" + mx)(l)
            else:
                self.zero_o(mx)
            self.barrier()
        self.merge(l, last)
        self.barrier()

    def zero_o(self, mx):
        c0, w = {"M": (OM, 256), "C": (OC, 512), "A": (OA, 512), "D": (OD, 512), "B": (OB, 512)}[mx]
        t = self.tmp[2]
        self.memset("pool", t[:, :], 0.0, ["tmp2"])
        for tt in range(NT):
            self.P.dma("sp", self.o_scr[tt * 128:(tt + 1) * 128, c0:c0 + w], t[:, 0:w], reads=["tmp2"], writes=["o_scr"])

    def attn_head(self, maps, vfn, vkey, nkt, dv1, table, band, post):
        nm = len(maps)
        G = self.G

        nqt = 4 if nm == 1 else 2
        QB = nqt * 128

        def accap(m, qt):
            bi = 4 + m * nqt + qt
            return self.bank[bi][:, 0:dv1], bi

        for qb in range(S // QB):
            q0 = qb * QB
            kts = []
            for kt in range(nkt):
                dk = kt * 128 - q0
                if band and (dk - (QB - 1) > 1024 or dk + 127 < -1024):
                    continue
                kts.append(kt)
            steps = [(idx, kt, m) for idx, kt in enumerate(kts) for m in range(nm)]

            def stageA(si):
                idx, kt, m = steps[si]
                q_ap, kfn = maps[m]
                sb_ = si % 4
                self.mm(self.bank[sb_][:, 0:QB], kfn(kt), q_ap[:, q0:q0 + QB], True, True, ["BIG0", "BIG1"], [self.bk(sb_)])

            def stageBC(si):
                idx, kt, m = steps[si]
                sb_ = si % 4
                pti = self.nxt("pt", 4)
                ptile = self.pt[pti]
                if table:
                    pei = self.nxt("pe", 4)
                    self.act(self.pe_[pei][:, 0:QB], self.bank[sb_][:, 0:QB], AF.Exp, [self.bk(sb_)], ["pe%d" % pei], scale=0.125)
                    j0 = kt * 128 - q0 + XC
                    gs = G[:, j0 - (QB - 1):j0 + 1][:, ::-1]
                    self.tt("dve", ptile[:, 0:QB], self.pe_[pei][:, 0:QB], gs, ALU.mult, ["pe%d" % pei, "G"], ["pt%d" % pti])
                else:
                    self.act(ptile[:, 0:QB], self.bank[sb_][:, 0:QB], AF.Exp, [self.bk(sb_)], ["pt%d" % pti], scale=0.125)
                for qt in range(nqt):
                    acc, bi = accap(m, qt)
                    self.mm(acc, ptile[:, qt * 128:(qt + 1) * 128], vfn(kt), idx == 0, idx == len(kts) - 1,
                            ["pt%d" % pti, vkey], [self.bk(bi)])

            PF = 3
            for si in range(min(PF, len(steps))):
                stageA(si)
            for si in range(len(steps)):
                if si + PF < len(steps):
                    stageA(si + PF)
                stageBC(si)
            for qt in range(nqt):
                accs = [accap(m, qt) for m in range(nm)]
                post(qb * nqt + qt, [a for a, _ in accs], [self.bk(bi) for _, bi in accs])

    def post_simple(self, h, hd, ocol):
        def post(tt, accs, keys):
            acc = accs[0]
            sm = self.sm
            i = self.nxt("ost", 4)
            self.P.op("dve", lambda e: e.reciprocal(out=sm[:, 8:9], in_=acc[:, hd:hd + 1]), reads=keys, writes=["sm"])
            self.stt("dve", self.ost[i][:, 0:hd], acc[:, 0:hd], sm[:, 8:9], self.sg[:, tt, h * hd:(h + 1) * hd], ALU.mult, ALU.mult,
                     keys + ["sm", "BIG3"], ["ost%d" % i])
            self.P.dma("sp", self.o_scr[tt * 128:(tt + 1) * 128, ocol + h * hd:ocol + (h + 1) * hd], self.ost[i][:, 0:hd],
                       reads=["ost%d" % i], writes=["o_scr"])
        return post

    def cons_T(self, dst, dkey):
        def consume(b, ct, tb):
            self.cp("dve" if (ct + tb) % 2 else "act", dst[:, ct, tb * 512:(tb + 1) * 512], self.bank[b][:, :], [self.bk(b)], [dkey])
        return consume

    def cons_sg(self, c0, n):
        def consume(b, tt):
            self.act(self.sg[:, tt, c0:c0 + n], self.bank[b][:, 0:n], AF.Silu, [self.bk(b)], ["BIG3"])
        return consume

    def cons_v(self, V1, h0, nh, hd):
        def consume(b, tt):
            self.cp("dve", V1[:, tt, h0:h0 + nh, 0:hd], self.bank[b][:, 0:nh * hd].rearrange("p (h d) -> p h d", d=hd), [self.bk(b)], ["BIG2"])
        return consume

    def mixer_M(self, l):
        P, I = self.P, self.I
        wkv = I["w_mem_kv"][l * D:(l + 1) * D, :]
        V1 = self.BIG[2][:, 0:520].rearrange("p (k h d) -> p k h d", k=2, h=4, d=65)
        self.memset("pool", self.BIG[2][:, 0:520], 1.0, ["BIG2"])
        memT = self.memT
        wl = self.wload(wkv[:, 0:256], 256)
        self.proj_T(wl, 256, lambda b, ct, tb: self.cp("dve", self.kT[:, ct, 0:256], self.bank[b][:, 0:256], [self.bk(b)], ["BIG1"]),
                    rhs_fn=lambda c, tb: memT[:, c, 0:256], rkey="memT", ntb=1, tbw=256)
        wl = self.wload(wkv[:, 256:512], 256)
        self.proj_N(wl, 256, self.cons_v(V1, 0, 4, 64), lhs_fn=lambda c, tt: memT[:, c, tt * 128:(tt + 1) * 128], lkey="memT", ntt=2)
        wl = self.wload(self.win[:, M0:M0 + 256], 256)
        self.proj_T(wl, 256, self.cons_T(self.qT, "BIG0"))
        wl = self.wload(self.win[:, M0 + 256:M0 + 512], 256)
        self.proj_N(wl, 256, self.cons_sg(0, 256))
        if l == 0 and "m" in self.mixers:
            self.dbg("qT", self.BIG[0][:, 0:8192], ["BIG0"], BF16)
            self.dbg("kT", self.BIG[1][:, 0:8192], ["BIG1"], BF16)
            self.dbg("V1", self.BIG[2][:, 0:520], ["BIG2"], BF16)
            self.dbg("sg", self.BIG[3][:, 0:8192], ["BIG3"], BF16)
            self.dbg("hT", self.hT[:, :, :].rearrange("p c t -> p (c t)"), ["hT"], BF16)
        for h in range(4):
            ct, pb = h // 2, (h % 2) * 64
            q_ap = self.qT[pb:pb + 64, ct, :]
            self.attn_head([(q_ap, lambda kt, ct=ct, pb=pb: self.kT[pb:pb + 64, ct, kt * 128:(kt + 1) * 128])],
                           lambda kt, h=h: V1[:, kt, h, :], "BIG2", 2, 65, False, False, self.post_simple(h, 64, OM))

    def mixer_A(self, l):
        P = self.P
        V1 = self.BIG[2][:, 0:8320].rearrange("p (k h d) -> p k h d", k=16, h=8, d=65)
        self.memset("pool", self.BIG[2][:, 0:8320], 1.0, ["BIG2"])
        for j in range(2):
            wl = self.wload(self.win[:, A0 + j * 256:A0 + (j + 1) * 256], 256)
            self.proj_T(wl, 256, lambda b, ct, tb, j=j: self.cons_T(self.qT, "BIG0")(b, ct + 2 * j, tb))
        for j in range(2):
            wl = self.wload(self.win[:, A0 + 512 + j * 256:A0 + 512 + (j + 1) * 256], 256)
            self.proj_T(wl, 256, lambda b, ct, tb, j=j: self.cons_T(self.kT, "BIG1")(b, ct + 2 * j, tb))
        for j in range(2):
            wl = self.wload(self.win[:, A0 + 1024 + j * 256:A0 + 1024 + (j + 1) * 256], 256)
            self.proj_N(wl, 256, self.cons_v(V1, 4 * j, 4, 64))
        for j in range(2):
            wl = self.wload(self.win[:, A0 + 1536 + j * 256:A0 + 1536 + (j + 1) * 256], 256)
            self.proj_N(wl, 256, self.cons_sg(j * 256, 256))
        for h in range(8):
            ct, pb = h // 2, (h % 2) * 64
            P.dma("sp", self.G[:, 0:3968], AP(self.xtab.tensor, h * 4096, [[1, 128], [1, 3968]]), reads=["xtab"], writes=["G"])
            q_ap = self.qT[pb:pb + 64, ct, :]
            self.attn_head([(q_ap, lambda kt, ct=ct, pb=pb: self.kT[pb:pb + 64, ct, kt * 128:(kt + 1) * 128])],
                           lambda kt, h=h: V1[:, kt, h, :], "BIG2", 16, 65, True, True, self.post_simple(h, 64, OA))

    def normrope(self, b, tt, nh, gcol):
        n = nh * 64
        sm = self.sm
        t0, t1, t2 = self.tmp
        ps = self.bank[b][:, 0:n]
        self.act(t0[:, 0:n], ps, AF.Square, [self.bk(b)], ["tmp0"])
        self.P.op("dve", lambda e: e.reduce_sum(out=sm[:, 16:16 + nh], in_=t0[:, 0:n].rearrange("p (h d) -> p h d", d=64), axis=AX.X),
                  reads=["tmp0"], writes=["sm"])
        self.rsqrt_cols(sm[:, 16:16 + nh], sm[:, 24:24 + nh], 1.0 / 64, 1e-6)
        v3 = lambda ap: ap.rearrange("p (h d) -> p h d", d=64)
        self.tt("dve", v3(t0[:, 0:n]), v3(ps), sm[:, 24:24 + nh].unsqueeze(2).broadcast_to([128, nh, 64]), ALU.mult,
                [self.bk(b), "sm"], ["tmp0"])
        self.tt("dve", v3(t0[:, 0:n]), v3(t0[:, 0:n]), self.gq[:, gcol:gcol + 64].unsqueeze(1).broadcast_to([128, nh, 64]), ALU.mult,
                ["tmp0", "gq"], ["tmp0"])
        self.tt("pool", v3(t1[:, 0:n]), v3(t0[:, 0:n]), self.ropec[:, tt, :].unsqueeze(1).broadcast_to([128, nh, 64]), ALU.mult,
                ["tmp0", "ropec"], ["tmp1"])
        v5 = lambda ap: ap.rearrange("p (h a b c) -> p h a b c", a=2, b=2, c=16)
        rs = self.ropes[:, tt, :].rearrange("p (a b c) -> p a b c", a=2, b=2, c=16)
        for bb in range(2):
            self.tt("dve", v5(t2[:, 0:n])[:, :, :, bb, :], v5(t0[:, 0:n])[:, :, :, 1 - bb, :],
                    rs[:, :, bb, :].unsqueeze(1).broadcast_to([128, nh, 2, 16]), ALU.mult, ["tmp0", "ropes"], ["tmp2"])
        self.tt("dve", t1[:, 0:n], t1[:, 0:n], t2[:, 0:n], ALU.add, ["tmp1", "tmp2"], ["tmp1"])

    def mixer_C(self, l):
        P, I = self.P, self.I
        V1 = self.BIG[2][:, 0:2080].rearrange("p (k h d) -> p k h d", k=16, h=2, d=65)
        self.memset("pool", self.BIG[2][:, 0:2080], 1.0, ["BIG2"])
        P.dma("sp", self.gq[:, 0:64], I["c_qnorm_g"][l:l + 1, :].partition_broadcast(128), writes=["gq"])
        P.dma("sp", self.gq[:, 64:128], I["c_knorm_g"][l:l + 1, :].partition_broadcast(128), writes=["gq"])
        t1 = self.tmp[1]
        for j in range(2):
            wl = self.wload(self.win[:, C0 + j * 256:C0 + (j + 1) * 256], 256)

            def cons_q(b, tt, j=j):
                self.normrope(b, tt, 4, 0)
                b2 = self.pbank()
                for u in range(2):
                    self.tr(self.bank[b2][:, u * 128:(u + 1) * 128], t1[:, u * 128:(u + 1) * 128], ["tmp1"], [self.bk(b2)])
                self.cp("act", self.qT[:, 2 * j:2 * j + 2, tt * 128:(tt + 1) * 128],
                        self.bank[b2][:, 0:256].rearrange("p (c t) -> p c t", t=128), [self.bk(b2)], ["BIG0"])
            self.proj_N(wl, 256, cons_q)
        wl = self.wload(self.win[:, C0 + 512:C0 + 768], 256)

        def cons_kv(b, tt):
            self.cp("act", V1[:, tt, :, 0:64], self.bank[b][:, 128:256].rearrange("p (h d) -> p h d", d=64), [self.bk(b)], ["BIG2"])
            self.normrope(b, tt, 2, 64)
            t2 = self.tmp[2]
            self.cp("dve", t2[:, 0:256].rearrange("p (g r d) -> p g r d", g=2, r=2, d=64),
                    t1[:, 0:128].rearrange("p (g d) -> p g d", d=64).unsqueeze(2).broadcast_to([128, 2, 2, 64]), ["tmp1"], ["tmp2"])
            b2 = self.pbank()
            for u in range(2):
                self.tr(self.bank[b2][:, u * 128:(u + 1) * 128], t2[:, u * 128:(u + 1) * 128], ["tmp2"], [self.bk(b2)])
            self.cp("act", self.kT[:, 0:2, tt * 128:(tt + 1) * 128],
                    self.bank[b2][:, 0:256].rearrange("p (c t) -> p c t", t=128), [self.bk(b2)], ["BIG1"])
        self.proj_N(wl, 256, cons_kv)
        for j in range(2):
            wl = self.wload(self.win[:, C0 + 768 + j * 256:C0 + 768 + (j + 1) * 256], 256)
            self.proj_N(wl, 256, self.cons_sg(j * 256, 256))
        for h in range(8):
            ct, pb = h // 2, (h % 2) * 64
            g = h // 4
            q_ap = self.qT[pb:pb + 64, ct, :]
            self.attn_head([(q_ap, lambda kt, g=g, pb=pb: self.kT[pb:pb + 64, g, kt * 128:(kt + 1) * 128])],
                           lambda kt, g=g: V1[:, kt, g, :], "BIG2", 16, 65, False, False, self.post_simple(h, 64, OC))

    def mixer_D(self, l):
        P, I = self.P, self.I
        V1 = self.BIG[2][:, 0:8256].rearrange("p (k h d) -> p k h d", k=16, h=4, d=129)
        self.memset("pool", self.BIG[2][:, 0:8256], 1.0, ["BIG2"])
        lam_init = 0.8 - 0.6 * math.exp(-0.3 * l)
        lamt, sm = self.lamt, self.sm
        P.dma("sp", lamt[:, 0:256], I["d_lambda"][l:l + 1, :].partition_broadcast(128), writes=["lamt"])
        P.dma("sp", self.subg[:], I["d_subln_g"][l:l + 1, :].partition_broadcast(128), writes=["subg"])
        self.ts("pool", self.subg[:], self.subg[:], 1.0 - lam_init, None, ALU.mult, None, ["subg"], ["subg"])
        lv = lamt[:, 0:256].rearrange("p (a b c) -> p a b c", a=2, b=2, c=64)
        lp = self.tmp[2][:, 0:128].rearrange("p (a c) -> p a c", c=64)
        self.tt("dve", lp, lv[:, :, 0, :], lv[:, :, 1, :], ALU.mult, ["lamt"], ["tmp2"])
        P.op("dve", lambda e: e.reduce_sum(out=lamt[:, 256:258], in_=lp, axis=AX.X), reads=["tmp2"], writes=["lamt"])
        self.act(lamt[:, 258:260], lamt[:, 256:258], AF.Exp, ["lamt"], ["lamt"])
        self.tt("dve", lamt[:, 260:261], lamt[:, 259:260], lamt[:, 258:259], ALU.subtract, ["lamt"], ["lamt"])
        self.ts("dve", lamt[:, 260:261], lamt[:, 260:261], -lam_init, None, ALU.add, None, ["lamt"], ["lamt"])
        for j in range(2):
            wl = self.wload(self.win[:, D0 + j * 256:D0 + (j + 1) * 256], 256)
            self.proj_T(wl, 256, lambda b, ct, tb, j=j: self.cons_T(self.qT, "BIG0")(b, ct + 2 * j, tb))
        for j in range(2):
            wl = self.wload(self.win[:, D0 + 512 + j * 256:D0 + 512 + (j + 1) * 256], 256)
            self.proj_T(wl, 256, lambda b, ct, tb, j=j: self.cons_T(self.kT, "BIG1")(b, ct + 2 * j, tb))
        for j in range(2):
            wl = self.wload(self.win[:, D0 + 1024 + j * 256:D0 + 1024 + (j + 1) * 256], 256)
            self.proj_N(wl, 256, self.cons_v(V1, 2 * j, 2, 128))
        for j in range(2):
            wl = self.wload(self.win[:, D0 + 1536 + j * 256:D0 + 1536 + (j + 1) * 256], 256)
            self.proj_N(wl, 256, self.cons_sg(j * 256, 256))
        t0 = self.tmp[0]
        for h in range(4):
            P.dma("sp", self.G[:, 0:3968], AP(self.xtab.tensor, (8 + h) * 4096, [[1, 128], [1, 3968]]), reads=["xtab"], writes=["G"])

            def post(tt, accs, keys, h=h):
                a1, a2 = accs
                i = self.nxt("ost", 4)
                P.op("dve", lambda e: e.reciprocal(out=sm[:, 8:9], in_=a1[:, 128:129]), reads=keys, writes=["sm"])
                P.op("dve", lambda e: e.reciprocal(out=sm[:, 9:10], in_=a2[:, 128:129]), reads=keys, writes=["sm"])
                self.tt("dve", sm[:, 9:10], sm[:, 9:10], lamt[:, 260:261], ALU.mult, ["sm", "lamt"], ["sm"])
                self.ts("dve", t0[:, 0:128], a1[:, 0:128], sm[:, 8:9], None, ALU.mult, None, keys + ["sm"], ["tmp0"])
                self.stt("dve", t0[:, 128:256], a2[:, 0:128], sm[:, 9:10], t0[:, 0:128], ALU.mult, ALU.add, keys + ["sm", "tmp0"], ["tmp0"])
                self.memset("dve", sm[:, 10:11], 0.0, ["sm"])
                self.act(t0[:, 256:384], t0[:, 128:256], AF.Square, ["tmp0", "sm"], ["tmp0", "sm"], accum_out=sm[:, 10:11])
                self.rsqrt_cols(sm[:, 10:11], sm[:, 11:12], 1.0 / 128, 1e-5)
                self.stt("dve", t0[:, 128:256], t0[:, 128:256], sm[:, 11:12], self.subg[:], ALU.mult, ALU.mult, ["tmp0", "sm", "subg"], ["tmp0"])
                self.tt("dve", self.ost[i][:], t0[:, 128:256], self.sg[:, tt, h * 128:(h + 1) * 128], ALU.mult, ["tmp0", "BIG3"], ["ost%d" % i])
                P.dma("sp", self.o_scr[tt * 128:(tt + 1) * 128, OD + h * 128:OD + (h + 1) * 128], self.ost[i][:],
                      reads=["ost%d" % i], writes=["o_scr"])
            maps = [(self.qT[c * 64:(c + 1) * 64, h, :], (lambda kt, c=c, h=h: self.kT[c * 64:(c + 1) * 64, h, kt * 128:(kt + 1) * 128]))
                    for c in range(2)]
            self.attn_head(maps, lambda kt, h=h: V1[:, kt, h, :], "BIG2", 16, 129, True, False, post)

    def dbg(self, name, ap, reads, dt=F32):
        if not self.debug:
            return
        t = self.nc.dram_tensor("dbg_" + name, list(ap.shape), dt, kind="ExternalOutput").ap()
        self.P.dma("sp", t, ap, reads=reads, is_output=True)
        self.dbg_outs.append("dbg_" + name)

    def mixer_B(self, l):
        P, I = self.P, self.I
        CW = 0.6065306597126334
        t_ring = self.tmp
        mub = self.lnx[0][:, 0:768].rearrange("p (v n) -> p v n", n=256)

        def load_mu(c0):
            for v in range(2):
                P.dma("sp", mub[:, 1 + v, :], I["shift_mu"][l * 2 + v:l * 2 + v + 1, c0:c0 + 256].partition_broadcast(128), writes=["lnx0"])
            self.tt("dve", mub[:, 0, :], mub[:, 1, :], mub[:, 2, :], ALU.add, ["lnx0"], ["lnx0"])
            self.ts("dve", mub[:, 0, :], mub[:, 0, :], -1.0, 1.0, ALU.mult, ALU.add, ["lnx0"], ["lnx0"])
            return [(mub[:, 0, :], "lnx0"), (mub[:, 1, :], "lnx0"), (mub[:, 2, :], "lnx0")]

        def stage_out(dst_ap, dkey, func=None):
            def f(b, n_part=128, ncol=512):
                i = self.nxt("tmp", 3)
                if func is None:
                    self.cp("dve", t_ring[i][0:n_part, 0:ncol], self.bank[b][0:n_part, 0:ncol], [self.bk(b)], ["tmp%d" % i])
                else:
                    self.act(t_ring[i][0:n_part, 0:ncol], self.bank[b][0:n_part, 0:ncol], func, [self.bk(b)], ["tmp%d" % i])
                P.dma("sp", dst_ap, t_ring[i][0:n_part, 0:ncol], reads=["tmp%d" % i], writes=[dkey])
            return f

        for j in range(4):
            c0 = j * 256
            wl = self.wload(self.win[:, B0 + c0:B0 + c0 + 256], 256, variants=load_mu(c0))
            self.proj_T(wl, 256, lambda b, ct, tb, c0=c0: stage_out(self.rk_scr[c0 + ct * 128:c0 + (ct + 1) * 128, tb * 512:(tb + 1) * 512], "rk_scr")(b),
                        shifts=(0, -1, 1))
        for j in range(2):
            c0 = 1024 + j * 256
            wl = self.wload(self.win[:, B0 + c0:B0 + c0 + 256], 256, variants=load_mu(c0))
            self.proj_N(wl, 256, lambda b, tt, j=j: stage_out(self.v_scr[tt * 128:(tt + 1) * 128, j * 256:(j + 1) * 256], "v_scr")(b, 128, 256),
                        shifts=(0, -1, 1))
        wl = self.wload(self.win[:, B0 + 1536:B0 + 1792], 256, variants=load_mu(1536))
        self.proj_T(wl, 256, lambda b, ct, tb: stage_out(self.wa_scr[ct * 128:(ct + 1) * 128, tb * 512:(tb + 1) * 512], "wa_scr",
                                                         AF.Tanh if ct == 0 else AF.Copy)(b), shifts=(0, -1, 1))
        for j in range(2):
            wl = self.wload(self.win[:, B0 + 1792 + j * 256:B0 + 1792 + (j + 1) * 256], 256)
            self.proj_N(wl, 256, lambda b, tt, j=j: stage_out(self.sg_scr[tt * 128:(tt + 1) * 128, j * 256:(j + 1) * 256], "sg_scr", AF.Silu)(b, 128, 256))
        self.barrier()
        slots = []
        for bi in range(4):
            a = self.BIG[bi][:].bitcast(F32)
            for q in range(4):
                slots.append(a[:, q * 1024:(q + 1) * 1024])
        for q in range(4):
            slots.append(self.G[:, q * 1024:(q + 1) * 1024])
        for wi in range(1):
            a = self.wst[wi][:, :, :].rearrange("p c n -> p (c n)")
            for q in range(2):
                slots.append(a[:, q * 1024:(q + 1) * 1024])
        si = [0]

        def slot(full=True):
            if full:
                if si[0] % 2:
                    si[0] += 1
                a = slots[si[0] // 2]
                si[0] += 2
                return a
            a = slots[si[0] // 2][:, (si[0] % 2) * 512:(si[0] % 2) * 512 + 512]
            si[0] += 1
            return a

        def v3(ap, w):
            return ap[0:64, 0:8 * w].rearrange("p (h t) -> p h t", t=w)

        w_upS = slot()[0:64, :].rearrange("p (e c) -> p e c", c=512)
        a_upS = slot()[0:64, :].rearrange("p (e c) -> p e c", c=512)
        w0B = slot()[0:64, :].rearrange("p (e c) -> p e c", c=512)
        rkT = slot()[0:64, :].rearrange("p (g t) -> p g t", t=64)
        AR = slot()[0:64, :].rearrange("p (h t) -> p h t", t=128)
        NP = [self.RX[:, q * 1024:(q + 1) * 1024].rearrange("p (h t) -> p h t", t=128) for q in range(2)]
        ysb = slot()[0:64, 0:520]
        rmaskS = slot(False)[0:64, 0:384].rearrange("p (e n) -> p e n", n=192)
        waT = slot(False)[0:64, 0:256].rearrange("p (g t) -> p g t", t=64)
        vtok = slot(False)[0:64, :]
        sgw = slot(False)[0:64, :]
        asT, kkn, ke, be, tE0, tE1, bch, kch, z = [v3(slot(False), 64) for _ in range(9)]
        eLs, Bt, Kt = [slot(False)[0:64, :] for _ in range(3)]
        Mm = [self.RX[:, 2048 + q * 512:2048 + (q + 1) * 512].rearrange("p (h t) -> p h t", t=64) for q in range(2)]
        Mrb, Mak, Mrk, Xs, Us, tmpS = [v3(slot(False), 64) for _ in range(6)]
        Sst = [v3(slot(False), 64) for _ in range(2)]
        assert si[0] <= 2 * len(slots), si[0]
        rwp = self.gq[0:64, 0:40]
        omka = self.gq[0:64, 40:48]
        ident64 = self.ident[0:64, 0:64]
        ones64 = self.onesf[0:64, 0:64]
        self.r32 = True
        P.dma("sp", w_upS, I["rwkv_w_up"][l * 128:(l + 1) * 128, :].rearrange("(e r) c -> r e c", r=64), writes=["w_upS"])
        P.dma("sp", a_upS, I["rwkv_a_up"][l * 128:(l + 1) * 128, :].rearrange("(e r) c -> r e c", r=64), writes=["a_upS"])
        for e in range(2):
            P.dma("sp", w0B[:, e, :], I["rwkv_w0"][l * 2 + e:l * 2 + e + 1, :].partition_broadcast(64), writes=["w0B"])
        P.dma("sp", rmaskS, I["rmask"].rearrange("(e p) n -> p e n", p=64), writes=["rmaskS"])

        pm = self.tmp[0]
        P.dma("sp", pm[0:16, 0:64], I["rwkv_a0"][l * 2:(l + 1) * 2, :].rearrange("e (h c) -> (e h) c", c=64), writes=["tmp0"])
        P.dma("sp", pm[16:24, 0:64], I["rwkv_k_k"][l:l + 1, :].rearrange("e (h c) -> (e h) c", c=64), writes=["tmp0"])
        P.dma("sp", pm[24:32, 0:64], I["rwkv_k_a"][l:l + 1, :].rearrange("e (h c) -> (e h) c", c=64), writes=["tmp0"])
        P.dma("sp", pm[32:40, 0:64], I["rwkv_r_k"][l:l + 1, :].rearrange("e (h c) -> (e h) c", c=64), writes=["tmp0"])
        b = self.pbank()
        self.P.op("pe", lambda e_: e_.transpose(self.bank[b][0:64, 0:40], pm[0:40, 0:64], self.ident[0:40, 0:40]), reads=["tmp0", "ident"], writes=[self.bk(b)])
        self.cp("dve", rwp, self.bank[b][0:64, 0:40], [self.bk(b)], ["gq"])
        self.ts("dve", omka, rwp[:, 24:32], -1.0, 1.0, ALU.mult, ALU.add, ["gq"], ["gq"])
        bc3 = lambda ap: ap.unsqueeze(2).broadcast_to([64, 8, 64])
        hb = lambda b_, h, w=64: self.bank[b_][0:64, h * w:(h + 1) * w]
        b3 = lambda b_, w=64: self.bank[b_][0:64, 0:8 * w].rearrange("p (h t) -> p h t", t=w)

        for e in range(2):
            Scur = 0
            self.memset("dve", Sst[0], 0.0, ["S0"])
            order = range(32) if e == 0 else range(31, -1, -1)
            tl = 63 if e == 0 else 0
            mS, mI, mT = rmaskS[:, e, 0:64], rmaskS[:, e, 64:128], rmaskS[:, e, 128:192]
            for ch in order:
                t0 = ch * 64
                P.dma("sp", rkT, self.rk_scr.rearrange("(g p) t -> p g t", p=64)[:, :, t0:t0 + 64], reads=["rk_scr"], writes=["rkT"])
                P.dma("sp", waT, self.wa_scr.rearrange("(g p) t -> p g t", p=64)[:, :, t0:t0 + 64], reads=["wa_scr"], writes=["waT"])
                P.dma("sp", vtok, self.v_scr[t0:t0 + 64, :], reads=["v_scr"], writes=["vtok"])
                rT, kT_ = rkT[:, 0:8, :], rkT[:, 8:16, :]
                b = self.pbank()
                self.mm(self.bank[b][0:64, :], waT[:, e, :], w_upS[:, e, :], True, True, ["waT", "w_upS"], [self.bk(b)])
                self.tt("dve", sgw, self.bank[b][0:64, :], w0B[:, e, :], ALU.add, [self.bk(b), "w0B"], ["sgw"])
                self.act(sgw, sgw, AF.Sigmoid, ["sgw"], ["sgw"])
                b = self.pbank()
                for h in range(8):
                    self.mm(hb(b, h), a_upS[:, e, h * 64:(h + 1) * 64], waT[:, 2 + e, :], True, True, ["waT", "a_upS"], [self.bk(b)])
                self.tt("dve", asT, b3(b), bc3(rwp[:, e * 8:(e + 1) * 8]), ALU.add, [self.bk(b), "gq"], ["asT"])
                self.act(asT, asT, AF.Sigmoid, ["asT"], ["asT"])
                self.tt("dve", kkn, kT_, bc3(rwp[:, 16:24]), ALU.mult, ["rkT", "gq"], ["kkn"])
                self.act(tE0, kkn, AF.Square, ["kkn"], ["tE0"])
                b = self.pbank()
                self.mm(self.bank[b][0:64, :], ones64, tE0.rearrange("p h t -> p (h t)"), True, True, ["tE0", "onesf"], [self.bk(b)])
                self.act(tE0, b3(b), AF.Sqrt, [self.bk(b)], ["tE0"])
                self.ts("dve", tE0, tE0, 1e-12, None, ALU.max, None, ["tE0"], ["tE0"])
                self.P.op("dve", lambda e_: e_.reciprocal(out=tE0, in_=tE0), reads=["tE0"], writes=["tE0"])
                self.tt("dve", kkn, kkn, tE0, ALU.mult, ["kkn", "tE0"], ["kkn"])
                self.tt("pool", ke, asT, bc3(rwp[:, 24:32]), ALU.mult, ["asT", "gq"], ["ke"])
                self.tt("pool", ke, ke, bc3(omka), ALU.add, ["ke", "gq"], ["ke"])
                self.tt("pool", ke, ke, kT_, ALU.mult, ["ke", "rkT"], ["ke"])
                self.tt("pool", be, kkn, asT, ALU.mult, ["kkn", "asT"], ["be"])
                self.tt("pool", z, rT, ke, ALU.mult, ["rkT", "ke"], ["z"])
                bLi = self.pbank()
                for h in range(8):
                    self.mm(hb(bLi, h), sgw[:, h * 64:(h + 1) * 64], mI, True, True, ["sgw", "rmaskS"], [self.bk(bLi)])
                self.act(tE0, b3(bLi), AF.Exp, [self.bk(bLi)], ["tE0"], scale=-CW)
                self.act(tE1, b3(bLi), AF.Exp, [self.bk(bLi)], ["tE1"], scale=CW)
                self.tt("dve", AR[:, :, 64:128], rT, tE0, ALU.mult, ["rkT", "tE0"], ["AR"])
                self.cp("dve", self.sm[0:64, 32:40], tE0[:, :, tl], ["tE0"], ["sm"])
                self.tt("dve", bch, be, tE1, ALU.mult, ["be", "tE1"], ["bch"])
                self.tt("pool", kch, ke, tE1, ALU.mult, ["ke", "tE1"], ["kch"])
                bLe = self.pbank()
                for h in range(8):
                    self.mm(hb(bLe, h), sgw[:, h * 64:(h + 1) * 64], mS, True, True, ["sgw", "rmaskS"], [self.bk(bLe)])
                self.act(tE0, b3(bLe), AF.Exp, [self.bk(bLe)], ["tE0"], scale=-CW)
                self.stt("dve", AR[:, :, 0:64], kkn, -1.0, tE0, ALU.mult, ALU.mult, ["kkn", "tE0"], ["AR"])
                b = self.pbank()
                self.mm(self.bank[b][0:64, :], mT, sgw, True, True, ["sgw", "rmaskS"], [self.bk(b)])
                self.act(eLs, self.bank[b][0:64, :], AF.Exp, [self.bk(b)], ["eLs"], scale=-CW)
                for src, skey, dst, dkey in ((be, "be", Bt, "Bt"), (ke, "ke", Kt, "Kt")):
                    b = self.pbank()
                    for h in range(8):
                        self.P.op("pe", lambda e_, b=b, h=h, src=src: e_.transpose(hb(b, h), src[:, h, :], ident64), reads=[skey, "ident"], writes=[self.bk(b)])
                    self.tt("dve", dst, self.bank[b][0:64, :], eLs, ALU.mult, [self.bk(b), "eLs"], [dkey])
                b = self.pbank()
                for h in range(8):
                    self.mm(self.bank[b][0:64, h:h + 1], z[:, h, :], rwp[:, 32 + h:33 + h], True, True, ["z", "gq"], [self.bk(b)])
                self.cp("act", ysb[:, 512:520], self.bank[b][0:64, 0:8], [self.bk(b)], ["ysb"])
                for h in range(8):
                    self.mm(self.bank[h // 4][0:64, (h % 4) * 128:(h % 4 + 1) * 128], bch[:, h, :], AR[:, h, :], True, True, ["bch", "AR"], [self.bk(h // 4)])
                for h in range(8):
                    self.mm(self.bank[4 + h // 4][0:64, (h % 4) * 128:(h % 4 + 1) * 128], kch[:, h, :], AR[:, h, :], True, True, ["kch", "AR"], [self.bk(4 + h // 4)])
                for h in range(8):
                    self.mm(hb(6, h), AR[:, h, 0:64], bch[:, h, :], True, True, ["bch", "AR"], [self.bk(6)])
                m4 = lambda m_: m_.unsqueeze(1).broadcast_to([64, 4, 64])
                for g in range(2):
                    bb = self.bank[g][0:64, :].rearrange("p (h t) -> p h t", t=128)
                    kb = self.bank[4 + g][0:64, :].rearrange("p (h t) -> p h t", t=128)
                    self.tt("dve", NP[0][:, 4 * g:4 * g + 4, 0:64], bb[:, :, 0:64], m4(mS), ALU.mult, [self.bk(g), "rmaskS"], ["NP0"])
                    self.tt("dve", Mrb[:, 4 * g:4 * g + 4, :], bb[:, :, 64:128], m4(mI), ALU.mult, [self.bk(g), "rmaskS"], ["Mrb"])
                    self.tt("dve", Mak[:, 4 * g:4 * g + 4, :], kb[:, :, 0:64], m4(mS), ALU.mult, [self.bk(4 + g), "rmaskS"], ["Mak"])
                    self.tt("dve", Mrk[:, 4 * g:4 * g + 4, :], kb[:, :, 64:128], m4(mI), ALU.mult, [self.bk(4 + g), "rmaskS"], ["Mrk"])
                self.tt("dve", Mm[0], b3(6), mT.unsqueeze(1).broadcast_to([64, 8, 64]), ALU.mult, [self.bk(6), "rmaskS"], ["Mm0"])
                self.tt("pool", NP[0][:, :, 64:128], NP[0][:, :, 0:64], ident64.unsqueeze(1).broadcast_to([64, 8, 64]), ALU.add, ["NP0", "ident"], ["NP0"])
                cur = 0
                for step in range(6):
                    nx = 1 - cur
                    pbk = (0, 1) if step % 2 == 0 else (4, 5)
                    mbk = 6 if step % 2 else 7
                    for g in range(2):
                        ncur, mcur = "NP%d/%d" % (cur, g), "Mm%d/%d" % (cur, g)
                        for h in range(4 * g, 4 * g + 4):
                            hh = h % 4
                            if step == 0:
                                o_, r_ = self.bank[pbk[g]][0:64, hh * 128:hh * 128 + 64], NP[cur][:, h, 0:64]
                            elif step < 5:
                                o_, r_ = self.bank[pbk[g]][0:64, hh * 128:(hh + 1) * 128], NP[cur][:, h, :]
                            else:
                                o_, r_ = self.bank[pbk[g]][0:64, hh * 128 + 64:(hh + 1) * 128], NP[cur][:, h, 64:128]
                            self.mm(o_, Mm[cur][:, h, :], r_, True, True, [mcur, ncur], [self.bk(pbk[g])])
                        if step < 5:
                            for h in range(4 * g, 4 * g + 4):
                                self.mm(self.bank[6 + g][0:64, (h % 4) * 64:(h % 4 + 1) * 64], NP[cur][:, h, 0:64], Mm[cur][:, h, :], True, True, [mcur, ncur], [self.bk(6 + g)])
                    for g in range(2):
                        ncur, nnx, mnx = "NP%d/%d" % (cur, g), "NP%d/%d" % (nx, g), "Mm%d/%d" % (nx, g)
                        pv = self.bank[pbk[g]][0:64, :].rearrange("p (h t) -> p h t", t=128)
                        if step < 5:
                            self.cp("act", NP[nx][:, 4 * g:4 * g + 4, 0:64], pv[:, :, 0:64], [self.bk(pbk[g])], [nnx])
                        if step == 0:
                            self.cp("dve", NP[nx][:, 4 * g:4 * g + 4, 64:128], NP[cur][:, 4 * g:4 * g + 4, 64:128], [ncur], [nnx])
                        else:
                            self.tt("dve", NP[nx][:, 4 * g:4 * g + 4, 64:128], pv[:, :, 64:128], NP[cur][:, 4 * g:4 * g + 4, 64:128], ALU.add,
                                    [self.bk(pbk[g]), ncur], [nnx])
                        if step < 5:
                            self.cp("act" if g else "dve", Mm[nx][:, 4 * g:4 * g + 4, :], self.bank[6 + g][0:64, 0:256].rearrange("p (h t) -> p h t", t=64), [self.bk(6 + g)], [mnx])
                    cur = nx
                TT = NP[cur]
                S0, skey = Sst[Scur], "S%d" % Scur
                S1, s1key = Sst[1 - Scur], "S%d" % (1 - Scur)
                bA, bB = (2, 0), (3, 1)
                G2 = range(2)
                hs = lambda g: range(4 * g, 4 * g + 4)
                gs = lambda ap, g: ap[:, 4 * g:4 * g + 4, :]
                hq = lambda b_, h: self.bank[b_][0:64, (h % 4) * 64:(h % 4 + 1) * 64]
                q3 = lambda b_: self.bank[b_][0:64, 0:256].rearrange("p (h t) -> p h t", t=64)
                for g in G2:
                    for h in hs(g):
                        self.mm(hq(bA[g], h), AR[:, h, 0:64], S0[:, h, :], True, False, ["AR", skey + "/%d" % g], [self.bk(bA[g])])
                        self.mm(hq(bA[g], h), Mak[:, h, :], vtok[:, h * 64:(h + 1) * 64], False, True, ["Mak", "vtok"], [self.bk(bA[g])])
                for g in G2:
                    self.cp("dve" if g else "act", gs(Xs, g), q3(bA[g]), [self.bk(bA[g])], ["Xs/%d" % g])
                for g in G2:
                    for h in hs(g):
                        self.mm(hq(bA[g], h), TT[:, h, 64:128], Xs[:, h, :], True, True, ["NP%d/%d" % (cur, g), "Xs/%d" % g], [self.bk(bA[g])])
                for g in G2:
                    self.cp("act" if g else "dve", gs(Us, g), q3(bA[g]), [self.bk(bA[g])], ["Us/%d" % g])
                for g in G2:
                    for h in hs(g):
                        self.mm(hq(bA[g], h), AR[:, h, 64:128], S0[:, h, :], True, False, ["AR", skey + "/%d" % g], [self.bk(bA[g])])
                        self.mm(hq(bA[g], h), Mrb[:, h, :], Us[:, h, :], False, False, ["Mrb", "Us/%d" % g], [self.bk(bA[g])])
                        self.mm(hq(bA[g], h), Mrk[:, h, :], vtok[:, h * 64:(h + 1) * 64], False, True, ["Mrk", "vtok"], [self.bk(bA[g])])
                    for h in hs(g):
                        self.mm(hq(bB[g], h), Bt[:, h * 64:(h + 1) * 64], Us[:, h, :], True, False, ["Bt", "Us/%d" % g], [self.bk(bB[g])])
                        self.mm(hq(bB[g], h), Kt[:, h * 64:(h + 1) * 64], vtok[:, h * 64:(h + 1) * 64], False, True, ["Kt", "vtok"], [self.bk(bB[g])])
                for g in G2:
                    self.cp("act", ysb[:, g * 256:(g + 1) * 256], self.bank[bA[g]][0:64, 0:256], [self.bk(bA[g])], ["ysb/%d" % g])
                    self.tt("pool", gs(tmpS, g), gs(S0, g), bc3(self.sm[0:64, 32:40])[:, 4 * g:4 * g + 4, :], ALU.mult, [skey + "/%d" % g, "sm"], ["tmpS/%d" % g])
                    self.tt("dve", gs(S1, g), gs(tmpS, g), q3(bB[g]), ALU.add, ["tmpS/%d" % g, self.bk(bB[g])], [s1key + "/%d" % g])
                P.dma("sp", self.y_scr[e * S + t0:e * S + t0 + 64, :], ysb, reads=["ysb"], writes=["y_scr"])
                Scur = 1 - Scur
        self.r32 = False
        self.barrier()
        P.dma("sp", self.lng[:, 0:512], I["rwkv_ln_g"][l:l + 1, :].partition_broadcast(128), writes=["lng"])
        P.dma("sp", self.lnb[:, 0:512], I["rwkv_ln_b"][l:l + 1, :].partition_broadcast(128), writes=["lnb"])
        yf, yb, vt = self.lnx[0], self.lnx[1], self.junk
        sm = self.sm
        t0_, t1_, t2_ = self.tmp
        for tt in range(NT):
            P.dma("sp", yf[:, 0:520], self.y_scr[tt * 128:(tt + 1) * 128, :], reads=["y_scr"], writes=["lnx0"])
            P.dma("sp", yb[:, 0:520], self.y_scr[S + tt * 128:S + (tt + 1) * 128, :], reads=["y_scr"], writes=["lnx1"])
            P.dma("sp", vt[:, 0:512], self.v_scr[tt * 128:(tt + 1) * 128, :], reads=["v_scr"], writes=["junk"])
            P.dma("sp", vt[:, 512:1024], self.sg_scr[tt * 128:(tt + 1) * 128, :], reads=["sg_scr"], writes=["junk"])
            self.tt("dve", yf[:, 0:520], yf[:, 0:520], yb[:, 0:520], ALU.add, ["lnx0", "lnx1"], ["lnx0"])
            y3 = yf[:, 0:512].rearrange("p (h d) -> p h d", d=64)
            P.op("dve", lambda e_, y3=y3: e_.reduce_sum(out=sm[:, 40:48], in_=y3, axis=AX.X), reads=["lnx0"], writes=["sm"])
            self.ts("dve", sm[:, 40:48], sm[:, 40:48], -1.0 / 64, None, ALU.mult, None, ["sm"], ["sm"])
            self.tt("dve", y3, y3, sm[:, 40:48].unsqueeze(2).broadcast_to([128, 8, 64]), ALU.add, ["lnx0", "sm"], ["lnx0"])
            self.act(t0_[:, 0:512], yf[:, 0:512], AF.Square, ["lnx0"], ["tmp0"])
            P.op("dve", lambda e_: e_.reduce_sum(out=sm[:, 48:56], in_=t0_[:, 0:512].rearrange("p (h d) -> p h d", d=64), axis=AX.X), reads=["tmp0"], writes=["sm"])
            self.rsqrt_cols(sm[:, 48:56], sm[:, 56:64], 1.0 / 64, 64e-5)
            self.tt("dve", y3, y3, sm[:, 56:64].unsqueeze(2).broadcast_to([128, 8, 64]), ALU.mult, ["lnx0", "sm"], ["lnx0"])
            self.tt("dve", yf[:, 0:512], yf[:, 0:512], self.lng[:, 0:512], ALU.mult, ["lnx0", "lng"], ["lnx0"])
            self.tt("pool", yf[:, 0:512], yf[:, 0:512], self.lnb[:, 0:512], ALU.add, ["lnx0", "lnb"], ["lnx0"])
            self.tt("pool", t1_[:, 0:512].rearrange("p (h d) -> p h d", d=64), vt[:, 0:512].rearrange("p (h d) -> p h d", d=64),
                    yf[:, 512:520].unsqueeze(2).broadcast_to([128, 8, 64]), ALU.mult, ["junk", "lnx0"], ["tmp1"])
            self.tt("dve", t1_[:, 0:512], t1_[:, 0:512], yf[:, 0:512], ALU.add, ["tmp1", "lnx0"], ["tmp1"])
            self.tt("dve", t2_[:, 0:512], t1_[:, 0:512], vt[:, 512:1024], ALU.mult, ["tmp1", "junk"], ["tmp2"])
            P.dma("sp", self.o_scr[tt * 128:(tt + 1) * 128, OB:OB + 512], t2_[:, 0:512], reads=["tmp2"], writes=["o_scr"])

    def merge(self, l, last):
        P, I = self.P, self.I
        wg_all = I["w_gate"][l * D:(l + 1) * D, :]
        wb_all = I["w_branch"][l * 2304:(l + 1) * 2304, :]
        wo_all = I["w_out"][l * D:(l + 1) * D, :]
        P.dma("sp", self.lng[:], I["ln_g"][l:l + 1, :].partition_broadcast(128), writes=["lng"])
        P.dma("sp", self.lnb[:], I["ln_b"][l:l + 1, :].partition_broadcast(128), writes=["lnb"])
        bgT = self.lamt[:, 0:40]
        for i5 in range(5):
            P.dma("sp", bgT[:, i5 * 8:(i5 + 1) * 8], I["b_gate"][l:l + 1, i5 * 1024:(i5 + 1) * 1024].rearrange("e (g c) -> c (e g)", c=128),
                  writes=["lamt"], allow_slow_non_contiguous=True)
        oT = self.BIG[0][:, 0:9216].rearrange("p (j t) -> p j t", t=512)
        yTf = self.BIG[1][:].bitcast(F32)[:, 0:4096].rearrange("p (c t) -> p c t", t=512)
        otile = self.BIG[2][:].bitcast(F32)[:, 0:2304]
        yTb = self.BIG[3][:, 0:4096].rearrange("p (c t) -> p c t", t=512)
        hgrp = self.G[:, 0:4096].rearrange("p (q c) -> p q c", c=1024)
        hin = self.hres[l % 2]
        hout = self.out if last else self.hres[(l + 1) % 2]
        t0, t1 = self.tmp[0], self.tmp[1]
        mb = [0]

        def mbank():
            mb[0] = (mb[0] + 1) % 8
            return mb[0]

        for grp in range(4):
            for tq in range(4):
                tt = grp * 4 + tq
                P.dma("sp", otile, self.o_scr[tt * 128:(tt + 1) * 128, :], reads=["o_scr"], writes=["BIG2"])
                P.dma("sp", hgrp[:, tq, :], hin[tt * 128:(tt + 1) * 128, :], reads=["hres%d" % (l % 2)], writes=["G"])
                for j4 in range(5):
                    nj = min(4, 18 - j4 * 4)
                    b = mbank()
                    for u in range(nj):
                        j = j4 * 4 + u
                        self.tr(self.bank[b][:, u * 128:(u + 1) * 128], otile[:, j * 128:(j + 1) * 128], ["BIG2"], [self.bk(b)])
                    self.cp("act" if j4 % 2 else "dve", oT[:, j4 * 4:j4 * 4 + nj, tq * 128:(tq + 1) * 128],
                            self.bank[b][:, 0:nj * 128].rearrange("p (c t) -> p c t", t=128), [self.bk(b)], ["BIG0"])
            hsl = lambda c: self.hT[:, c, 1 + grp * 512:1 + (grp + 1) * 512]
            for i, (r0, rw) in enumerate(BROWS):
                kci = rw // 128
                for cc in range(4):
                    wg, wgk = self.wload(wg_all[:, i * 1024 + cc * 256:i * 1024 + (cc + 1) * 256], 256)[0]
                    wb, wbk = self.wload(wb_all[r0:r0 + rw, cc * 256:(cc + 1) * 256], 256, kc=kci)[0]
                    for u in range(2):
                        ct = cc * 2 + u
                        b1 = mbank()
                        for c in range(8):
                            self.mm(self.bank[b1][:, :], wg[:, c, u * 128:(u + 1) * 128], hsl(c), c == 0, c == 7, [wgk, "hT"], [self.bk(b1)])
                        ti = self.nxt("mt", 2)
                        tg = self.tmp[ti]
                        self.act(tg[:, :], self.bank[b1][:, :], AF.Sigmoid, [self.bk(b1), "lamt"], ["tmp%d" % ti], bias=bgT[:, i * 8 + ct:i * 8 + ct + 1])
                        b2 = mbank()
                        for c in range(kci):
                            self.mm(self.bank[b2][:, :], wb[:, c, u * 128:(u + 1) * 128], oT[:, r0 // 128 + c, :], c == 0, c == kci - 1,
                                    [wbk, "BIG0"], [self.bk(b2)])
                        ysl = yTf[:, ct, :]
                        if i == 0:
                            self.tt("dve", ysl, self.bank[b2][:, :], tg[:, :], ALU.mult, [self.bk(b2), "tmp%d" % ti], ["BIG1"])
                        else:
                            self.tt("dve", tg[:, :], self.bank[b2][:, :], tg[:, :], ALU.mult, [self.bk(b2), "tmp%d" % ti], ["tmp%d" % ti])
                            self.tt("dve", ysl, ysl, tg[:, :], ALU.add, ["BIG1", "tmp%d" % ti], ["BIG1"])
            if l == 0 and grp == 0:
                self.dbg("yTf", self.BIG[1][:].bitcast(F32)[:, 0:4096], ["BIG1"])
                self.dbg("oT", self.BIG[0][:, 0:9216], ["BIG0"], BF16)
                self.dbg("bgT", self.lamt[:, 0:40], ["lamt"])
            for half in range(2):
                self.cp("act" if half else "dve", yTb[:, half * 4:half * 4 + 4, :], yTf[:, half * 4:half * 4 + 4, :], ["BIG1"], ["BIG3"])
            for cc in range(4):
                wo, wok = self.wload(wo_all[:, cc * 256:(cc + 1) * 256], 256)[0]
                for tq in range(4):
                    b = mbank()
                    for c in range(8):
                        self.mm(self.bank[b][:, 0:256], yTb[:, c, tq * 128:(tq + 1) * 128], wo[:, c, :], c == 0, c == 7, [wok, "BIG3"], [self.bk(b)])
                    hs = hgrp[:, tq, cc * 256:(cc + 1) * 256]
                    self.stt("dve", hs, hs, ALPHA, self.bank[b][:, 0:256], ALU.mult, ALU.add, ["G", self.bk(b)], ["G"])
            for tq in range(4):
                tt = grp * 4 + tq
                self.ln_inplace(hgrp[:, tq, :], "G")
                P.dma("sp", hout[tt * 128:(tt + 1) * 128, :], hgrp[:, tq, :], reads=["G"],
                      writes=["out" if last else "hres%d" % ((l + 1) % 2)], is_output=last)


def make_in_map(inputs, b, consts):
    m = {"x": np.ascontiguousarray(inputs["x"][b]), "mem": np.ascontiguousarray(inputs["mem"][b])}
    for nm, shp in IN_SPECS:
        if nm in consts:
            m[nm] = consts[nm]
        else:
            m[nm] = np.ascontiguousarray(np.asarray(inputs[nm], dtype=np.float32).reshape(shp))
    return m


def kernel(**inputs):
    consts = host_consts()
    kb = KB(debug=False)
    nb = inputs["x"].shape[0]
    in_maps = [make_in_map(inputs, b, consts) for b in range(nb)]
    res = run_bass_kernel_spmd(kb.nc, in_maps, core_ids=list(range(nb)))
    out = np.stack([np.asarray(r["out"], dtype=np.float32).reshape(S, D) for r in res.results], axis=0)
    return out
```

```python
import math
from concourse.ap import AP
import contextlib
import numpy as np
import concourse.bass as bass
import concourse.mybir as mybir
from concourse.bass_utils import run_bass_kernel_spmd

F32 = mybir.dt.float32
BF16 = mybir.dt.bfloat16
I32 = mybir.dt.int32
AF = mybir.ActivationFunctionType
ALU = mybir.AluOpType
AX = mybir.AxisListType

ENGS = ("pe", "act", "dve", "pool", "sp")
DMA_SEMS = 8


class Op:
    __slots__ = ("eng", "fn", "waits", "is_dma", "idx", "marked", "dma_slot", "dma_val", "prewait")

    def __init__(self, eng, fn, is_dma):
        self.eng = eng
        self.fn = fn
        self.is_dma = is_dma
        self.waits = []
        self.marked = False
        self.idx = None
        self.dma_slot = None
        self.dma_val = None
        self.prewait = None


class Prog:
    def __init__(self, nc, same_engine_sync=True):
        self.nc = nc
        self.ops = {e: [] for e in ENGS}
        self.last_write = {}
        self.readers = {}
        self.children = {}
        self.same_engine_sync = same_engine_sync
        self.dma_count = {e: 0 for e in ENGS}
        self.dma_hist = {e: [] for e in ENGS}
        self.all_dma_out = []
        self.stack = contextlib.ExitStack()
        self.n_ops = 0

    def sb(self, name, shape, dt):
        return self.stack.enter_context(self.nc.sbuf_tensor("s_" + name, list(shape), dt))

    def ps(self, name, shape, dt):
        return self.stack.enter_context(self.nc.psum_tensor("p_" + name, list(shape), dt))

    def _related(self, k):
        if "/" in k:
            p = k.split("/")[0]
            self.children.setdefault(p, set()).add(k)
            return (k, p)
        return (k,) + tuple(self.children.get(k, ()))

    def _deps(self, op, reads, writes):
        deps = []
        for k0 in reads:
            for k in self._related(k0):
                w = self.last_write.get(k)
                if w is not None:
                    deps.append(w)
        for k0 in writes:
            for k in self._related(k0):
                w = self.last_write.get(k)
                if w is not None:
                    deps.append(w)
                for r in self.readers.get(k, ()):
                    deps.append(r)
        best = {}
        for d in deps:
            if d is op:
                continue
            key = (d.eng, d.is_dma, d.dma_slot if d.is_dma else None)
            cur = best.get(key)
            if cur is None or d.idx > cur.idx:
                best[key] = d
        for d in best.values():
            if (not d.is_dma) and d.eng == op.eng and not op.is_dma:
                if op.eng == "pe" or not self.same_engine_sync:
                    continue
            op.waits.append(d)
            d.marked = True
        for k in reads:
            self.readers.setdefault(k, []).append(op)
        for k in writes:
            self.last_write[k] = op
            self.readers[k] = []

    def barrier(self, fn):
        o = Op("pool", fn, False)
        o.idx = len(self.ops["pool"])
        self.ops["pool"].append(o)
        self._deps(o, [], ["__phase__"])
        return o

    def op(self, eng, fn, reads=(), writes=()):
        reads = list(reads) + ["__phase__"]
        o = Op(eng, fn, False)
        o.idx = len(self.ops[eng])
        self.ops[eng].append(o)
        self._deps(o, reads, writes)
        self.n_ops += 1
        return o

    def dma(self, eng, out, in_, reads=(), writes=(), is_output=False, **kw):
        def fn(e, out=out, in_=in_, kw=kw):
            return e.dma_start(out=out, in_=in_, **kw)
        reads = list(reads) + ["__phase__"]
        o = Op(eng, fn, True)
        o.idx = len(self.ops[eng])
        n = self.dma_count[eng]
        self.dma_count[eng] += 1
        o.dma_slot = n % DMA_SEMS
        o.dma_val = 16 * (n // DMA_SEMS + 1)
        if n >= DMA_SEMS:
            o.prewait = self.dma_hist[eng][n - DMA_SEMS]
        self.dma_hist[eng].append(o)
        self.ops[eng].append(o)
        self._deps(o, reads, writes)
        if is_output:
            self.all_dma_out.append(o)
        self.n_ops += 1
        return o

    def emit(self):
        nc = self.nc
        st = self.stack
        fin = Op("sp", None, False)
        fin.idx = len(self.ops["sp"])
        for o in self.all_dma_out:
            fin.waits.append(o)
        self.ops["sp"].append(fin)
        csem = {e: st.enter_context(nc.semaphore("c_" + e)) for e in ENGS}
        dsem = {e: [st.enter_context(nc.semaphore("d_%s_%d" % (e, i))) for i in range(DMA_SEMS)]
                for e in ENGS if self.dma_count[e] > 0}
        for e in ENGS:
            c = 0
            for o in self.ops[e]:
                if o.is_dma:
                    continue
                if o.marked:
                    c += 1
                    o.dma_val = c
        block = st.enter_context(nc.Block())
        prog = self

        def run(e, eng):
            seen = {}
            for o in prog.ops[e]:
                ws = list(o.waits)
                if o.prewait is not None:
                    ws.append(o.prewait)
                for d in ws:
                    if d.is_dma:
                        sem, val = dsem[d.eng][d.dma_slot], d.dma_val
                    else:
                        sem, val = csem[d.eng], d.dma_val
                    k = id(sem)
                    if seen.get(k, 0) >= val:
                        continue
                    seen[k] = val
                    eng.wait_ge(sem, val)
                if o.fn is None:
                    continue
                ins = o.fn(eng)
                if o.is_dma:
                    ins.then_inc(dsem[e][o.dma_slot], 16)
                elif o.marked:
                    ins.then_inc(csem[e], 1)

        @block.tensor
        def _(eng):
            run("pe", eng)

        @block.scalar
        def _(eng):
            run("act", eng)

        @block.vector
        def _(eng):
            run("dve", eng)

        @block.gpsimd
        def _(eng):
            run("pool", eng)

        @block.sync
        def _(eng):
            run("sp", eng)

    def close(self):
        self.stack.close()


F32R = mybir.dt.float32r

S = 2048
D = 1024
NT = 16
DEPTH = 2
WC = 256
XC = 2047
GW = 4096
ALPHA = (2 * DEPTH) ** 0.25
A0, B0, C0, D0, M0 = 0, 2048, 4352, 5632, 7680
OA, OB, OC, OD, OM = 0, 512, 1024, 1536, 2048
BROWS = [(0, 512), (512, 512), (1024, 512), (1536, 512), (2048, 256)]


def rel_bucket_np(rel):
    nb = 16
    max_exact = 8
    n = np.abs(rel)
    nf = np.maximum(n, 1).astype(np.float32)
    large = max_exact + (np.log(nf / max_exact) / np.float32(math.log(1024 / max_exact)) * (nb - max_exact)).astype(np.int32)
    large = np.minimum(large, nb - 1)
    return np.where(rel > 0, nb, 0) + np.where(n < max_exact, n, large)


def host_consts():
    c = {}
    c["ident"] = np.eye(128, dtype=np.float32)
    rel = np.arange(4096) - XC
    bkt = rel_bucket_np(rel)
    oh = np.zeros((32, 4096), np.float32)
    oh[bkt, np.arange(4096)] = 1.0
    c["onehot"] = oh
    n = np.abs(rel)
    mA = (n <= 64).astype(np.float32) + ((rel % 4 == 0) & (n <= 256)) + ((rel % 16 == 0) & (n <= 1024))
    mt = np.ones((12, 4096), np.float32)
    mt[:8] = mA[None, :]
    c["multab"] = mt
    t = np.arange(S)
    row = (t // 64).astype(np.float32)
    col = (t % 64).astype(np.float32)
    freqs = (10000.0 ** (-(np.arange(16, dtype=np.float32) / 16))).astype(np.float32)
    ar = row[:, None] * freqs[None, :]
    ac = col[:, None] * freqs[None, :]
    c["ropec"] = np.concatenate([np.cos(ar), np.cos(ar), np.cos(ac), np.cos(ac)], 1).astype(np.float32)
    c["ropes"] = np.concatenate([-np.sin(ar), np.sin(ar), -np.sin(ac), np.sin(ac)], 1).astype(np.float32)
    tri = np.zeros((2, 3, 128, 128), np.float32)
    sg = np.arange(128)[:, None]
    tt = np.arange(128)[None, :]
    same = (sg // 64) == (tt // 64)
    tri[0, 0] = same & (sg <= tt)
    tri[0, 1] = same & (sg < tt)
    tri[0, 2] = same & (sg > tt)
    tri[1, 0] = same & (sg >= tt)
    tri[1, 1] = same & (sg > tt)
    tri[1, 2] = same & (sg < tt)
    c["tri"] = tri.reshape(6 * 128, 128)
    mk_ = np.zeros((2, 64, 192), np.float32)
    a = np.arange(64)[:, None]
    b = np.arange(64)[None, :]
    mk_[0, :, 0:64] = a < b
    mk_[0, :, 64:128] = a <= b
    mk_[0, :, 128:192] = b < a
    mk_[1, :, 0:64] = a > b
    mk_[1, :, 64:128] = a >= b
    mk_[1, :, 128:192] = b > a
    c["rmask"] = mk_.reshape(128, 192)
    return c


IN_SPECS = [("ln_in_g", [1, D]), ("ln_in_b", [1, D]), ("rel_bias", [32, 12]), ("w_in", [DEPTH * D, 8192]),
            ("shift_mu", [DEPTH * 2, 1792]), ("rwkv_w0", [DEPTH * 2, 512]), ("rwkv_w_up", [DEPTH * 2 * 64, 512]),
            ("rwkv_a0", [DEPTH * 2, 512]), ("rwkv_a_up", [DEPTH * 2 * 64, 512]), ("rwkv_k_k", [DEPTH, 512]),
            ("rwkv_k_a", [DEPTH, 512]), ("rwkv_r_k", [DEPTH, 512]), ("rwkv_ln_g", [DEPTH, 512]),
            ("rwkv_ln_b", [DEPTH, 512]), ("c_qnorm_g", [DEPTH, 64]), ("c_knorm_g", [DEPTH, 64]),
            ("d_lambda", [DEPTH, 256]), ("d_subln_g", [DEPTH, 128]), ("w_mem_kv", [DEPTH * D, 512]),
            ("w_branch", [DEPTH * 2304, D]), ("w_gate", [DEPTH * D, 5120]), ("b_gate", [DEPTH, 5120]),
            ("w_out", [DEPTH * D, D]), ("ln_g", [DEPTH, D]), ("ln_b", [DEPTH, D]),
            ("ident", [128, 128]), ("onehot", [32, 4096]), ("multab", [12, 4096]), ("ropec", [S, 64]),
            ("ropes", [S, 64]), ("tri", [768, 128]), ("rmask", [128, 192])]


class KB:
    def __init__(self, debug=False, mixers="MCADB", layers=DEPTH):
        self.debug = debug
        self.mixers = mixers
        nc = bass.Bass("TRN2", target_bir_lowering=False)
        self.nc = nc
        P = Prog(nc)
        self.P = P
        I = {}
        I["x"] = nc.dram_tensor("x", [S, D], F32, kind="ExternalInput").ap()
        I["mem"] = nc.dram_tensor("mem", [256, D], F32, kind="ExternalInput").ap()
        for nm, shp in IN_SPECS:
            I[nm] = nc.dram_tensor(nm, list(shp), F32, kind="ExternalInput").ap()
        self.I = I
        self.out = nc.dram_tensor("out", [S, D], F32, kind="ExternalOutput").ap()
        self.hres = [nc.dram_tensor("hres%d" % i, [S, D], F32, kind="ExternalOutput" if debug else "Internal").ap() for i in range(2)]
        self.dbg_outs = []
        self.o_scr = nc.dram_tensor("o_scr", [S, 2304], F32, kind="ExternalOutput" if debug else "Internal").ap()
        self.xtab = nc.dram_tensor("xtab", [12, 4096], F32).ap()
        self.rk_scr = nc.dram_tensor("rk_scr", [1024, S], F32).ap()
        self.wa_scr = nc.dram_tensor("wa_scr", [256, S], F32).ap()
        self.v_scr = nc.dram_tensor("v_scr", [S, 512], F32).ap()
        self.y_scr = nc.dram_tensor("y_scr", [2 * S, 520], F32, kind="ExternalOutput" if debug else "Internal").ap()
        self.sg_scr = nc.dram_tensor("sg_scr", [S, 512], F32).ap()
        self.ident = P.sb("ident", [128, 128], F32)
        self.hT = P.sb("hT", [128, 8, S + 2], BF16)
        self.BIG = [P.sb("BIG%d" % i, [128, 9216], BF16) for i in range(4)]
        self.G = P.sb("G", [128, GW], F32)
        self.wst = [P.sb("wst%d" % i, [128, 8, WC], F32) for i in range(1)]
        self.RX = P.sb("RX", [64, 3072], F32)
        self.wbf = [P.sb("wbf%d" % i, [128, 8, WC], BF16) for i in range(4)]
        self.ropec = P.sb("ropec", [128, NT, 64], F32)
        self.ropes = P.sb("ropes", [128, NT, 64], F32)
        self.lnx = [P.sb("lnx%d" % i, [128, D], F32) for i in range(2)]
        self.junk = P.sb("junk", [128, D], F32)
        self.lng = P.sb("lng", [128, D], F32)
        self.lnb = P.sb("lnb", [128, D], F32)
        self.pt = [P.sb("pt%d" % i, [128, 512], BF16) for i in range(4)]
        self.pe_ = [P.sb("pe%d" % i, [128, 512], BF16) for i in range(4)]
        self.ost = [P.sb("ost%d" % i, [128, 128], F32) for i in range(4)]
        self.sm = P.sb("sm", [128, 64], F32)
        self.tmp = [P.sb("tmp%d" % i, [128, 512], F32) for i in range(3)]
        self.onesf = P.sb("onesf", [128, 128], F32)
        self.gq = P.sb("gq", [128, 128], F32)
        self.subg = P.sb("subg", [128, 128], F32)
        self.lamt = P.sb("lamt", [128, 264], F32)
        self.pbar = P.sb("pbar", [1, 8], F32)
        self.memT = P.sb("memT", [128, 8, 256], BF16)
        self.bank = [P.ps("bank%d" % i, [128, 512], F32) for i in range(8)]
        self.cnt = {}
        self.pbi = 0
        B0_, B1_, B2_, B3_ = [b[:] for b in self.BIG]
        self.qT = B0_[:, 0:8192].rearrange("p (c t) -> p c t", t=S)
        self.kT = B1_[:, 0:8192].rearrange("p (c t) -> p c t", t=S)
        self.sg = B3_[:, 0:8192].rearrange("p (t c) -> p t c", c=512)
        self.prelude()
        for l in range(layers):
            self.layer(l, last=(l == layers - 1))
        P.emit()
        P.close()

    def nxt(self, name, n):
        v = self.cnt.get(name, 0)
        self.cnt[name] = (v + 1) % n
        return v

    def bk(self, i):
        return "bank%d" % i

    def pbank(self):
        self.pbi ^= 1
        return 2 + self.pbi

    def barrier(self):
        pbar = self.pbar
        self.P.barrier(lambda e: e.memset(pbar[:], 0.0))

    def R(self, ap):
        if ap.dtype == F32 and ap.name == "s_RX":
            return ap.bitcast(F32R)
        return ap

    def mm(self, out, lhsT, rhs, start, stop, reads, writes):
        if lhsT.name == "s_RX" and rhs.name == "s_RX":
            lhsT, rhs = self.R(lhsT), self.R(rhs)
        self.P.op("pe", lambda e: e.matmul(out, lhsT=lhsT, rhs=rhs, start=start, stop=stop), reads=reads, writes=writes)

    def tr(self, out, in_, reads, writes, np_=128):
        ident = self.ident
        self.P.op("pe", lambda e: e.transpose(out, in_, ident[0:np_, 0:np_]), reads=list(reads) + ["ident"], writes=writes)

    def cp(self, eng, out, in_, reads, writes):
        out = self.R(out)
        if eng == "act":
            self.P.op("act", lambda e: e.copy(out=out, in_=in_), reads=reads, writes=writes)
        else:
            self.P.op(eng, lambda e: e.tensor_copy(out=out, in_=in_), reads=reads, writes=writes)

    def act(self, out, in_, func, reads, writes, **kw):
        out = self.R(out)
        self.P.op("act", lambda e: e.activation(out=out, in_=in_, func=func, **kw), reads=reads, writes=writes)

    def tt(self, eng, out, in0, in1, op, reads, writes):
        out = self.R(out)
        self.P.op(eng, lambda e: e.tensor_tensor(out=out, in0=in0, in1=in1, op=op), reads=reads, writes=writes)

    def ts(self, eng, out, in0, s1, s2, op0, op1, reads, writes):
        out = self.R(out)
        if s2 is None:
            self.P.op(eng, lambda e: e.tensor_scalar(out=out, in0=in0, scalar1=s1, scalar2=None, op0=op0), reads=reads, writes=writes)
        else:
            self.P.op(eng, lambda e: e.tensor_scalar(out=out, in0=in0, scalar1=s1, scalar2=s2, op0=op0, op1=op1), reads=reads, writes=writes)

    def stt(self, eng, out, in0, scalar, in1, op0, op1, reads, writes):
        out = self.R(out)
        self.P.op(eng, lambda e: e.scalar_tensor_tensor(out=out, in0=in0, scalar=scalar, in1=in1, op0=op0, op1=op1), reads=reads, writes=writes)

    def memset(self, eng, ap, val, writes):
        ap = self.R(ap)
        self.P.op(eng, lambda e: e.memset(ap, val), writes=writes)

    def rsqrt_cols(self, src, dst, scale, eps, key="sm"):
        self.ts("dve", dst, src, scale, eps, ALU.mult, ALU.add, [key], [key])
        self.P.op("act", lambda e: e.sqrt(out=dst, in_=dst), reads=[key], writes=[key])
        self.P.op("dve", lambda e: e.reciprocal(out=dst, in_=dst), reads=[key], writes=[key])

    def wload(self, src2d, n, kc=8, variants=None):
        P = self.P
        src = src2d.rearrange("(c p) n -> p c n", p=128)
        if variants is None:
            j = self.nxt("wb", 4)
            P.dma("pool", self.wbf[j][:, 0:kc, 0:n], src, writes=["wbf%d" % j])
            return [(self.wbf[j], "wbf%d" % j)]
        wst = self.wst[0]
        P.dma("sp", wst[:, 0:kc, 0:n], src, writes=["wst0"])
        res = []
        for vi, (vap, vkey) in enumerate(variants):
            j = self.nxt("wb", 4)
            self.tt("dve" if vi != 1 else "pool", self.wbf[j][:, 0:kc, 0:n], wst[:, 0:kc, 0:n], vap.unsqueeze(1).broadcast_to([128, kc, n]), ALU.mult,
                    ["wst0", vkey], ["wbf%d" % j])
            res.append((self.wbf[j], "wbf%d" % j))
        return res

    def proj_T(self, wl, n, consume, shifts=(0,), rhs_fn=None, rkey="hT", ntb=4, tbw=512):
        hT = self.hT
        for ct in range(n // 128):
            for tb in range(ntb):
                b = self.pbank()
                nmm = 8 * len(shifts)
                m = 0
                for (wap, wkey), s in zip(wl, shifts):
                    for c in range(8):
                        if rhs_fn is None:
                            lo = 1 + tb * 512 + s
                            rhs = hT[:, c, lo:lo + 512]
                        else:
                            rhs = rhs_fn(c, tb)
                        self.mm(self.bank[b][:, 0:tbw], wap[:, c, ct * 128:(ct + 1) * 128], rhs, m == 0, m == nmm - 1,
                                [wkey, rkey], [self.bk(b)])
                        m += 1
                consume(b, ct, tb)

    def proj_N(self, wl, n, consume, shifts=(0,), lhs_fn=None, lkey="hT", ntt=NT, kc=8):
        hT = self.hT
        for tt in range(ntt):
            b = self.pbank()
            nmm = kc * len(shifts)
            m = 0
            for (wap, wkey), s in zip(wl, shifts):
                for c in range(kc):
                    if lhs_fn is None:
                        lo = 1 + tt * 128 + s
                        lh = hT[:, c, lo:lo + 128]
                    else:
                        lh = lhs_fn(c, tt)
                    self.mm(self.bank[b][:, 0:n], lh, wap[:, c, 0:n], m == 0, m == nmm - 1, [wkey, lkey], [self.bk(b)])
                    m += 1
            consume(b, tt)

    def ln_inplace(self, xt, xkey, eps=1e-5):
        sm, junk = self.sm, self.junk
        P = self.P
        P.op("dve", lambda e: e.reduce_sum(out=sm[:, 0:1], in_=xt, axis=AX.X), reads=[xkey], writes=["sm"])
        self.ts("dve", sm[:, 1:2], sm[:, 0:1], -1.0 / D, None, ALU.mult, None, ["sm"], ["sm"])
        self.ts("dve", xt, xt, sm[:, 1:2], None, ALU.add, None, [xkey, "sm"], [xkey])
        self.memset("dve", sm[:, 2:3], 0.0, ["sm"])
        self.act(junk[:], xt, AF.Square, [xkey, "sm"], ["junk", "sm"], accum_out=sm[:, 2:3])
        self.rsqrt_cols(sm[:, 2:3], sm[:, 3:4], 1.0 / D, eps)
        self.stt("dve", xt, xt, sm[:, 3:4], self.lng[:], ALU.mult, ALU.mult, [xkey, "sm", "lng"], [xkey])
        self.tt("dve", xt, xt, self.lnb[:], ALU.add, [xkey, "lnb"], [xkey])

    def to_hT(self, src, skey, tt):
        hT = self.hT
        for half in range(2):
            b = self.pbank()
            for c4 in range(4):
                c = half * 4 + c4
                self.tr(self.bank[b][:, c4 * 128:(c4 + 1) * 128], src[:, c * 128:(c + 1) * 128], [skey], [self.bk(b)])
            self.cp("act" if half else "dve", hT[:, half * 4:half * 4 + 4, 1 + tt * 128:1 + (tt + 1) * 128],
                    self.bank[b][:, :].rearrange("p (c t) -> p c t", t=128), [self.bk(b)], ["hT"])

    def prelude(self):
        P, I = self.P, self.I
        P.dma("sp", self.ident[:], I["ident"], writes=["ident"])
        P.dma("sp", self.ropec[:], I["ropec"].rearrange("(t p) c -> p t c", p=128), writes=["ropec"])
        P.dma("sp", self.ropes[:], I["ropes"].rearrange("(t p) c -> p t c", p=128), writes=["ropes"])
        self.memset("pool", self.onesf[:], 1.0, ["onesf"])
        self.memset("pool", self.hT[:, :, 0:1], 0.0, ["hT"])
        self.memset("pool", self.hT[:, :, S + 1:S + 2], 0.0, ["hT"])
        tmpA = self.tmp[0]
        rb = tmpA[0:32, 0:12]
        P.dma("sp", rb, I["rel_bias"], writes=["tmp0"])
        ohs = self.BIG[0][:].bitcast(F32)
        P.dma("sp", ohs[0:32, 0:4096], I["onehot"], writes=["BIG0"])
        mts = self.BIG[1][:].bitcast(F32)
        P.dma("sp", mts[0:12, 0:4096], I["multab"], writes=["BIG1"])
        xts = self.BIG[2][:].bitcast(F32)
        for j in range(8):
            b = self.pbank()
            self.mm(self.bank[b][0:12, :], rb, ohs[0:32, j * 512:(j + 1) * 512], True, True, ["tmp0", "BIG0"], [self.bk(b)])
            self.act(xts[0:12, j * 512:(j + 1) * 512], self.bank[b][0:12, :], AF.Exp, [self.bk(b)], ["BIG2"])
        self.tt("dve", xts[0:12, 0:4096], xts[0:12, 0:4096], mts[0:12, 0:4096], ALU.mult, ["BIG2", "BIG1"], ["BIG2"])
        P.dma("sp", self.xtab, xts[0:12, 0:4096], reads=["BIG2"], writes=["xtab"])
        self.barrier()
        for mt_ in range(2):
            i = self.nxt("ln", 2)
            P.dma("sp", self.lnx[i][:], I["mem"][mt_ * 128:(mt_ + 1) * 128, :], writes=["lnx%d" % i])
            for half in range(2):
                b = self.pbank()
                for c4 in range(4):
                    c = half * 4 + c4
                    self.tr(self.bank[b][:, c4 * 128:(c4 + 1) * 128], self.lnx[i][:, c * 128:(c + 1) * 128], ["lnx%d" % i], [self.bk(b)])
                self.cp("dve", self.memT[:, half * 4:half * 4 + 4, mt_ * 128:(mt_ + 1) * 128],
                        self.bank[b][:, :].rearrange("p (c t) -> p c t", t=128), [self.bk(b)], ["memT"])
        P.dma("sp", self.lng[:], I["ln_in_g"].partition_broadcast(128), writes=["lng"])
        P.dma("sp", self.lnb[:], I["ln_in_b"].partition_broadcast(128), writes=["lnb"])
        for tt in range(NT):
            i = self.nxt("ln", 2)
            P.dma("sp", self.lnx[i][:], I["x"][tt * 128:(tt + 1) * 128, :], writes=["lnx%d" % i])
            self.ln_inplace(self.lnx[i][:], "lnx%d" % i)
            P.dma("sp", self.hres[0][tt * 128:(tt + 1) * 128, :], self.lnx[i][:], reads=["lnx%d" % i], writes=["hres0"])
        self.barrier()

    def layer(self, l, last):
        P, I = self.P, self.I
        hin = self.hres[l % 2]
        for tt in range(NT):
            i = self.nxt("ln", 2)
            P.dma("sp", self.lnx[i][:], hin[tt * 128:(tt + 1) * 128, :], reads=["hres%d" % (l % 2)], writes=["lnx%d" % i])
            self.to_hT(self.lnx[i], "lnx%d" % i, tt)
        self.win = I["w_in"][l * D:(l + 1) * D, :]
        for mx in "MCADB":
            if mx in self.mixers:
                getattr(self, "mixer_" + mx)(l)
            else:
                self.zero_o(mx)
            self.barrier()
        self.merge(l, last)
        self.barrier()

    def zero_o(self, mx):
        c0, w = {"M": (OM, 256), "C": (OC, 512), "A": (OA, 512), "D": (OD, 512), "B": (OB, 512)}[mx]
        t = self.tmp[2]
        self.memset("pool", t[:, :], 0.0, ["tmp2"])
        for tt in range(NT):
            self.P.dma("sp", self.o_scr[tt * 128:(tt + 1) * 128, c0:c0 + w], t[:, 0:w], reads=["tmp2"], writes=["o_scr"])

    def attn_head(self, maps, vfn, vkey, nkt, dv1, table, band, post):
        nm = len(maps)
        G = self.G

        nqt = 4 if nm == 1 else 2
        QB = nqt * 128

        def accap(m, qt):
            bi = 4 + m * nqt + qt
            return self.bank[bi][:, 0:dv1], bi

        for qb in range(S // QB):
            q0 = qb * QB
            kts = []
            for kt in range(nkt):
                dk = kt * 128 - q0
                if band and (dk - (QB - 1) > 1024 or dk + 127 < -1024):
                    continue
                kts.append(kt)
            steps = [(idx, kt, m) for idx, kt in enumerate(kts) for m in range(nm)]

            def stageA(si):
                idx, kt, m = steps[si]
                q_ap, kfn = maps[m]
                sb_ = si % 4
                self.mm(self.bank[sb_][:, 0:QB], kfn(kt), q_ap[:, q0:q0 + QB], True, True, ["BIG0", "BIG1"], [self.bk(sb_)])

            def stageBC(si):
                idx, kt, m = steps[si]
                sb_ = si % 4
                pti = self.nxt("pt", 4)
                ptile = self.pt[pti]
                if table:
                    pei = self.nxt("pe", 4)
                    self.act(self.pe_[pei][:, 0:QB], self.bank[sb_][:, 0:QB], AF.Exp, [self.bk(sb_)], ["pe%d" % pei], scale=0.125)
                    j0 = kt * 128 - q0 + XC
                    gs = G[:, j0 - (QB - 1):j0 + 1][:, ::-1]
                    self.tt("dve", ptile[:, 0:QB], self.pe_[pei][:, 0:QB], gs, ALU.mult, ["pe%d" % pei, "G"], ["pt%d" % pti])
                else:
                    self.act(ptile[:, 0:QB], self.bank[sb_][:, 0:QB], AF.Exp, [self.bk(sb_)], ["pt%d" % pti], scale=0.125)
                for qt in range(nqt):
                    acc, bi = accap(m, qt)
                    self.mm(acc, ptile[:, qt * 128:(qt + 1) * 128], vfn(kt), idx == 0, idx == len(kts) - 1,
                            ["pt%d" % pti, vkey], [self.bk(bi)])

            PF = 3
            for si in range(min(PF, len(steps))):
                stageA(si)
            for si in range(len(steps)):
                if si + PF < len(steps):
                    stageA(si + PF)
                stageBC(si)
            for qt in range(nqt):
                accs = [accap(m, qt) for m in range(nm)]
                post(qb * nqt + qt, [a for a, _ in accs], [self.bk(bi) for _, bi in accs])

    def post_simple(self, h, hd, ocol):
        def post(tt, accs, keys):
            acc = accs[0]
            sm = self.sm
            i = self.nxt("ost", 4)
            self.P.op("dve", lambda e: e.reciprocal(out=sm[:, 8:9], in_=acc[:, hd:hd + 1]), reads=keys, writes=["sm"])
            self.stt("dve", self.ost[i][:, 0:hd], acc[:, 0:hd], sm[:, 8:9], self.sg[:, tt, h * hd:(h + 1) * hd], ALU.mult, ALU.mult,
                     keys + ["sm", "BIG3"], ["ost%d" % i])
            self.P.dma("sp", self.o_scr[tt * 128:(tt + 1) * 128, ocol + h * hd:ocol + (h + 1) * hd], self.ost[i][:, 0:hd],
                       reads=["ost%d" % i], writes=["o_scr"])
        return post

    def cons_T(self, dst, dkey):
        def consume(b, ct, tb):
            self.cp("dve" if (ct + tb) % 2 else "act", dst[:, ct, tb * 512:(tb + 1) * 512], self.bank[b][:, :], [self.bk(b)], [dkey])
        return consume

    def cons_sg(self, c0, n):
        def consume(b, tt):
            self.act(self.sg[:, tt, c0:c0 + n], self.bank[b][:, 0:n], AF.Silu, [self.bk(b)], ["BIG3"])
        return consume

    def cons_v(self, V1, h0, nh, hd):
        def consume(b, tt):
            self.cp("dve", V1[:, tt, h0:h0 + nh, 0:hd], self.bank[b][:, 0:nh * hd].rearrange("p (h d) -> p h d", d=hd), [self.bk(b)], ["BIG2"])
        return consume

    def mixer_M(self, l):
        P, I = self.P, self.I
        wkv = I["w_mem_kv"][l * D:(l + 1) * D, :]
        V1 = self.BIG[2][:, 0:520].rearrange("p (k h d) -> p k h d", k=2, h=4, d=65)
        self.memset("pool", self.BIG[2][:, 0:520], 1.0, ["BIG2"])
        memT = self.memT
        wl = self.wload(wkv[:, 0:256], 256)
        self.proj_T(wl, 256, lambda b, ct, tb: self.cp("dve", self.kT[:, ct, 0:256], self.bank[b][:, 0:256], [self.bk(b)], ["BIG1"]),
                    rhs_fn=lambda c, tb: memT[:, c, 0:256], rkey="memT", ntb=1, tbw=256)
        wl = self.wload(wkv[:, 256:512], 256)
        self.proj_N(wl, 256, self.cons_v(V1, 0, 4, 64), lhs_fn=lambda c, tt: memT[:, c, tt * 128:(tt + 1) * 128], lkey="memT", ntt=2)
        wl = self.wload(self.win[:, M0:M0 + 256], 256)
        self.proj_T(wl, 256, self.cons_T(self.qT, "BIG0"))
        wl = self.wload(self.win[:, M0 + 256:M0 + 512], 256)
        self.proj_N(wl, 256, self.cons_sg(0, 256))
        if l == 0 and "m" in self.mixers:
            self.dbg("qT", self.BIG[0][:, 0:8192], ["BIG0"], BF16)
            self.dbg("kT", self.BIG[1][:, 0:8192], ["BIG1"], BF16)
            self.dbg("V1", self.BIG[2][:, 0:520], ["BIG2"], BF16)
            self.dbg("sg", self.BIG[3][:, 0:8192], ["BIG3"], BF16)
            self.dbg("hT", self.hT[:, :, :].rearrange("p c t -> p (c t)"), ["hT"], BF16)
        for h in range(4):
            ct, pb = h // 2, (h % 2) * 64
            q_ap = self.qT[pb:pb + 64, ct, :]
            self.attn_head([(q_ap, lambda kt, ct=ct, pb=pb: self.kT[pb:pb + 64, ct, kt * 128:(kt + 1) * 128])],
                           lambda kt, h=h: V1[:, kt, h, :], "BIG2", 2, 65, False, False, self.post_simple(h, 64, OM))

    def mixer_A(self, l):
        P = self.P
        V1 = self.BIG[2][:, 0:8320].rearrange("p (k h d) -> p k h d", k=16, h=8, d=65)
        self.memset("pool", self.BIG[2][:, 0:8320], 1.0, ["BIG2"])
        for j in range(2):
            wl = self.wload(self.win[:, A0 + j * 256:A0 + (j + 1) * 256], 256)
            self.proj_T(wl, 256, lambda b, ct, tb, j=j: self.cons_T(self.qT, "BIG0")(b, ct + 2 * j, tb))
        for j in range(2):
            wl = self.wload(self.win[:, A0 + 512 + j * 256:A0 + 512 + (j + 1) * 256], 256)
            self.proj_T(wl, 256, lambda b, ct, tb, j=j: self.cons_T(self.kT, "BIG1")(b, ct + 2 * j, tb))
        for j in range(2):
            wl = self.wload(self.win[:, A0 + 1024 + j * 256:A0 + 1024 + (j + 1) * 256], 256)
            self.proj_N(wl, 256, self.cons_v(V1, 4 * j, 4, 64))
        for j in range(2):
            wl = self.wload(self.win[:, A0 + 1536 + j * 256:A0 + 1536 + (j + 1) * 256], 256)
            self.proj_N(wl, 256, self.cons_sg(j * 256, 256))
        for h in range(8):
            ct, pb = h // 2, (h % 2) * 64
            P.dma("sp", self.G[:, 0:3968], AP(self.xtab.tensor, h * 4096, [[1, 128], [1, 3968]]), reads=["xtab"], writes=["G"])
            q_ap = self.qT[pb:pb + 64, ct, :]
            self.attn_head([(q_ap, lambda kt, ct=ct, pb=pb: self.kT[pb:pb + 64, ct, kt * 128:(kt + 1) * 128])],
                           lambda kt, h=h: V1[:, kt, h, :], "BIG2", 16, 65, True, True, self.post_simple(h, 64, OA))

    def normrope(self, b, tt, nh, gcol):
        n = nh * 64
        sm = self.sm
        t0, t1, t2 = self.tmp
        ps = self.bank[b][:, 0:n]
        self.act(t0[:, 0:n], ps, AF.Square, [self.bk(b)], ["tmp0"])
        self.P.op("dve", lambda e: e.reduce_sum(out=sm[:, 16:16 + nh], in_=t0[:, 0:n].rearrange("p (h d) -> p h d", d=64), axis=AX.X),
                  reads=["tmp0"], writes=["sm"])
        self.rsqrt_cols(sm[:, 16:16 + nh], sm[:, 24:24 + nh], 1.0 / 64, 1e-6)
        v3 = lambda ap: ap.rearrange("p (h d) -> p h d", d=64)
        self.tt("dve", v3(t0[:, 0:n]), v3(ps), sm[:, 24:24 + nh].unsqueeze(2).broadcast_to([128, nh, 64]), ALU.mult,
                [self.bk(b), "sm"], ["tmp0"])
        self.tt("dve", v3(t0[:, 0:n]), v3(t0[:, 0:n]), self.gq[:, gcol:gcol + 64].unsqueeze(1).broadcast_to([128, nh, 64]), ALU.mult,
                ["tmp0", "gq"], ["tmp0"])
        self.tt("pool", v3(t1[:, 0:n]), v3(t0[:, 0:n]), self.ropec[:, tt, :].unsqueeze(1).broadcast_to([128, nh, 64]), ALU.mult,
                ["tmp0", "ropec"], ["tmp1"])
        v5 = lambda ap: ap.rearrange("p (h a b c) -> p h a b c", a=2, b=2, c=16)
        rs = self.ropes[:, tt, :].rearrange("p (a b c) -> p a b c", a=2, b=2, c=16)
        for bb in range(2):
            self.tt("dve", v5(t2[:, 0:n])[:, :, :, bb, :], v5(t0[:, 0:n])[:, :, :, 1 - bb, :],
                    rs[:, :, bb, :].unsqueeze(1).broadcast_to([128, nh, 2, 16]), ALU.mult, ["tmp0", "ropes"], ["tmp2"])
        self.tt("dve", t1[:, 0:n], t1[:, 0:n], t2[:, 0:n], ALU.add, ["tmp1", "tmp2"], ["tmp1"])

    def mixer_C(self, l):
        P, I = self.P, self.I
        V1 = self.BIG[2][:, 0:2080].rearrange("p (k h d) -> p k h d", k=16, h=2, d=65)
        self.memset("pool", self.BIG[2][:, 0:2080], 1.0, ["BIG2"])
        P.dma("sp", self.gq[:, 0:64], I["c_qnorm_g"][l:l + 1, :].partition_broadcast(128), writes=["gq"])
        P.dma("sp", self.gq[:, 64:128], I["c_knorm_g"][l:l + 1, :].partition_broadcast(128), writes=["gq"])
        t1 = self.tmp[1]
        for j in range(2):
            wl = self.wload(self.win[:, C0 + j * 256:C0 + (j + 1) * 256], 256)

            def cons_q(b, tt, j=j):
                self.normrope(b, tt, 4, 0)
                b2 = self.pbank()
                for u in range(2):
                    self.tr(self.bank[b2][:, u * 128:(u + 1) * 128], t1[:, u * 128:(u + 1) * 128], ["tmp1"], [self.bk(b2)])
                self.cp("act", self.qT[:, 2 * j:2 * j + 2, tt * 128:(tt + 1) * 128],
                        self.bank[b2][:, 0:256].rearrange("p (c t) -> p c t", t=128), [self.bk(b2)], ["BIG0"])
            self.proj_N(wl, 256, cons_q)
        wl = self.wload(self.win[:, C0 + 512:C0 + 768], 256)

        def cons_kv(b, tt):
            self.cp("act", V1[:, tt, :, 0:64], self.bank[b][:, 128:256].rearrange("p (h d) -> p h d", d=64), [self.bk(b)], ["BIG2"])
            self.normrope(b, tt, 2, 64)
            t2 = self.tmp[2]
            self.cp("dve", t2[:, 0:256].rearrange("p (g r d) -> p g r d", g=2, r=2, d=64),
                    t1[:, 0:128].rearrange("p (g d) -> p g d", d=64).unsqueeze(2).broadcast_to([128, 2, 2, 64]), ["tmp1"], ["tmp2"])
            b2 = self.pbank()
            for u in range(2):
                self.tr(self.bank[b2][:, u * 128:(u + 1) * 128], t2[:, u * 128:(u + 1) * 128], ["tmp2"], [self.bk(b2)])
            self.cp("act", self.kT[:, 0:2, tt * 128:(tt + 1) * 128],
                    self.bank[b2][:, 0:256].rearrange("p (c t) -> p c t", t=128), [self.bk(b2)], ["BIG1"])
        self.proj_N(wl, 256, cons_kv)
        for j in range(2):
            wl = self.wload(self.win[:, C0 + 768 + j * 256:C0 + 768 + (j + 1) * 256], 256)
            self.proj_N(wl, 256, self.cons_sg(j * 256, 256))
        for h in range(8):
            ct, pb = h // 2, (h % 2) * 64
            g = h // 4
            q_ap = self.qT[pb:pb + 64, ct, :]
            self.attn_head([(q_ap, lambda kt, g=g, pb=pb: self.kT[pb:pb + 64, g, kt * 128:(kt + 1) * 128])],
                           lambda kt, g=g: V1[:, kt, g, :], "BIG2", 16, 65, False, False, self.post_simple(h, 64, OC))

    def mixer_D(self, l):
        P, I = self.P, self.I
        V1 = self.BIG[2][:, 0:8256].rearrange("p (k h d) -> p k h d", k=16, h=4, d=129)
        self.memset("pool", self.BIG[2][:, 0:8256], 1.0, ["BIG2"])
        lam_init = 0.8 - 0.6 * math.exp(-0.3 * l)
        lamt, sm = self.lamt, self.sm
        P.dma("sp", lamt[:, 0:256], I["d_lambda"][l:l + 1, :].partition_broadcast(128), writes=["lamt"])
        P.dma("sp", self.subg[:], I["d_subln_g"][l:l + 1, :].partition_broadcast(128), writes=["subg"])
        self.ts("pool", self.subg[:], self.subg[:], 1.0 - lam_init, None, ALU.mult, None, ["subg"], ["subg"])
        lv = lamt[:, 0:256].rearrange("p (a b c) -> p a b c", a=2, b=2, c=64)
        lp = self.tmp[2][:, 0:128].rearrange("p (a c) -> p a c", c=64)
        self.tt("dve", lp, lv[:, :, 0, :], lv[:, :, 1, :], ALU.mult, ["lamt"], ["tmp2"])
        P.op("dve", lambda e: e.reduce_sum(out=lamt[:, 256:258], in_=lp, axis=AX.X), reads=["tmp2"], writes=["lamt"])
        self.act(lamt[:, 258:260], lamt[:, 256:258], AF.Exp, ["lamt"], ["lamt"])
        self.tt("dve", lamt[:, 260:261], lamt[:, 259:260], lamt[:, 258:259], ALU.subtract, ["lamt"], ["lamt"])
        self.ts("dve", lamt[:, 260:261], lamt[:, 260:261], -lam_init, None, ALU.add, None, ["lamt"], ["lamt"])
        for j in range(2):
            wl = self.wload(self.win[:, D0 + j * 256:D0 + (j + 1) * 256], 256)
            self.proj_T(wl, 256, lambda b, ct, tb, j=j: self.cons_T(self.qT, "BIG0")(b, ct + 2 * j, tb))
        for j in range(2):
            wl = self.wload(self.win[:, D0 + 512 + j * 256:D0 + 512 + (j + 1) * 256], 256)
            self.proj_T(wl, 256, lambda b, ct, tb, j=j: self.cons_T(self.kT, "BIG1")(b, ct + 2 * j, tb))
        for j in range(2):
            wl = self.wload(self.win[:, D0 + 1024 + j * 256:D0 + 1024 + (j + 1) * 256], 256)
            self.proj_N(wl, 256, self.cons_v(V1, 2 * j, 2, 128))
        for j in range(2):
            wl = self.wload(self.win[:, D0 + 1536 + j * 256:D0 + 1536 + (j + 1) * 256], 256)
            self.proj_N(wl, 256, self.cons_sg(j * 256, 256))
        t0 = self.tmp[0]
        for h in range(4):
            P.dma("sp", self.G[:, 0:3968], AP(self.xtab.tensor, (8 + h) * 4096, [[1, 128], [1, 3968]]), reads=["xtab"], writes=["G"])

            def post(tt, accs, keys, h=h):
                a1, a2 = accs
                i = self.nxt("ost", 4)
                P.op("dve", lambda e: e.reciprocal(out=sm[:, 8:9], in_=a1[:, 128:129]), reads=keys, writes=["sm"])
                P.op("dve", lambda e: e.reciprocal(out=sm[:, 9:10], in_=a2[:, 128:129]), reads=keys, writes=["sm"])
                self.tt("dve", sm[:, 9:10], sm[:, 9:10], lamt[:, 260:261], ALU.mult, ["sm", "lamt"], ["sm"])
                self.ts("dve", t0[:, 0:128], a1[:, 0:128], sm[:, 8:9], None, ALU.mult, None, keys + ["sm"], ["tmp0"])
                self.stt("dve", t0[:, 128:256], a2[:, 0:128], sm[:, 9:10], t0[:, 0:128], ALU.mult, ALU.add, keys + ["sm", "tmp0"], ["tmp0"])
                self.memset("dve", sm[:, 10:11], 0.0, ["sm"])
                self.act(t0[:, 256:384], t0[:, 128:256], AF.Square, ["tmp0", "sm"], ["tmp0", "sm"], accum_out=sm[:, 10:11])
                self.rsqrt_cols(sm[:, 10:11], sm[:, 11:12], 1.0 / 128, 1e-5)
                self.stt("dve", t0[:, 128:256], t0[:, 128:256], sm[:, 11:12], self.subg[:], ALU.mult, ALU.mult, ["tmp0", "sm", "subg"], ["tmp0"])
                self.tt("dve", self.ost[i][:], t0[:, 128:256], self.sg[:, tt, h * 128:(h + 1) * 128], ALU.mult, ["tmp0", "BIG3"], ["ost%d" % i])
                P.dma("sp", self.o_scr[tt * 128:(tt + 1) * 128, OD + h * 128:OD + (h + 1) * 128], self.ost[i][:],
                      reads=["ost%d" % i], writes=["o_scr"])
            maps = [(self.qT[c * 64:(c + 1) * 64, h, :], (lambda kt, c=c, h=h: self.kT[c * 64:(c + 1) * 64, h, kt * 128:(kt + 1) * 128]))
                    for c in range(2)]
            self.attn_head(maps, lambda kt, h=h: V1[:, kt, h, :], "BIG2", 16, 129, True, False, post)

    def dbg(self, name, ap, reads, dt=F32):
        if not self.debug:
            return
        t = self.nc.dram_tensor("dbg_" + name, list(ap.shape), dt, kind="ExternalOutput").ap()
        self.P.dma("sp", t, ap, reads=reads, is_output=True)
        self.dbg_outs.append("dbg_" + name)

    def mixer_B(self, l):
        P, I = self.P, self.I
        CW = 0.6065306597126334
        t_ring = self.tmp
        mub = self.lnx[0][:, 0:768].rearrange("p (v n) -> p v n", n=256)

        def load_mu(c0):
            for v in range(2):
                P.dma("sp", mub[:, 1 + v, :], I["shift_mu"][l * 2 + v:l * 2 + v + 1, c0:c0 + 256].partition_broadcast(128), writes=["lnx0"])
            self.tt("dve", mub[:, 0, :], mub[:, 1, :], mub[:, 2, :], ALU.add, ["lnx0"], ["lnx0"])
            self.ts("dve", mub[:, 0, :], mub[:, 0, :], -1.0, 1.0, ALU.mult, ALU.add, ["lnx0"], ["lnx0"])
            return [(mub[:, 0, :], "lnx0"), (mub[:, 1, :], "lnx0"), (mub[:, 2, :], "lnx0")]

        def stage_out(dst_ap, dkey, func=None):
            def f(b, n_part=128, ncol=512):
                i = self.nxt("tmp", 3)
                if func is None:
                    self.cp("dve", t_ring[i][0:n_part, 0:ncol], self.bank[b][0:n_part, 0:ncol], [self.bk(b)], ["tmp%d" % i])
                else:
                    self.act(t_ring[i][0:n_part, 0:ncol], self.bank[b][0:n_part, 0:ncol], func, [self.bk(b)], ["tmp%d" % i])
                P.dma("sp", dst_ap, t_ring[i][0:n_part, 0:ncol], reads=["tmp%d" % i], writes=[dkey])
            return f

        for j in range(4):
            c0 = j * 256
            wl = self.wload(self.win[:, B0 + c0:B0 + c0 + 256], 256, variants=load_mu(c0))
            self.proj_T(wl, 256, lambda b, ct, tb, c0=c0: stage_out(self.rk_scr[c0 + ct * 128:c0 + (ct + 1) * 128, tb * 512:(tb + 1) * 512], "rk_scr")(b),
                        shifts=(0, -1, 1))
        for j in range(2):
            c0 = 1024 + j * 256
            wl = self.wload(self.win[:, B0 + c0:B0 + c0 + 256], 256, variants=load_mu(c0))
            self.proj_N(wl, 256, lambda b, tt, j=j: stage_out(self.v_scr[tt * 128:(tt + 1) * 128, j * 256:(j + 1) * 256], "v_scr")(b, 128, 256),
                        shifts=(0, -1, 1))
        wl = self.wload(self.win[:, B0 + 1536:B0 + 1792], 256, variants=load_mu(1536))
        self.proj_T(wl, 256, lambda b, ct, tb: stage_out(self.wa_scr[ct * 128:(ct + 1) * 128, tb * 512:(tb + 1) * 512], "wa_scr",
                                                         AF.Tanh if ct == 0 else AF.Copy)(b), shifts=(0, -1, 1))
        for j in range(2):
            wl = self.wload(self.win[:, B0 + 1792 + j * 256:B0 + 1792 + (j + 1) * 256], 256)
            self.proj_N(wl, 256, lambda b, tt, j=j: stage_out(self.sg_scr[tt * 128:(tt + 1) * 128, j * 256:(j + 1) * 256], "sg_scr", AF.Silu)(b, 128, 256))
        self.barrier()
        slots = []
        for bi in range(4):
            a = self.BIG[bi][:].bitcast(F32)
            for q in range(4):
                slots.append(a[:, q * 1024:(q + 1) * 1024])
        for q in range(4):
            slots.append(self.G[:, q * 1024:(q + 1) * 1024])
        for wi in range(1):
            a = self.wst[wi][:, :, :].rearrange("p c n -> p (c n)")
            for q in range(2):
                slots.append(a[:, q * 1024:(q + 1) * 1024])
        si = [0]

        def slot(full=True):
            if full:
                if si[0] % 2:
                    si[0] += 1
                a = slots[si[0] // 2]
                si[0] += 2
                return a
            a = slots[si[0] // 2][:, (si[0] % 2) * 512:(si[0] % 2) * 512 + 512]
            si[0] += 1
            return a

        def v3(ap, w):
            return ap[0:64, 0:8 * w].rearrange("p (h t) -> p h t", t=w)

        w_upS = slot()[0:64, :].rearrange("p (e c) -> p e c", c=512)
        a_upS = slot()[0:64, :].rearrange("p (e c) -> p e c", c=512)
        w0B = slot()[0:64, :].rearrange("p (e c) -> p e c", c=512)
        rkT = slot()[0:64, :].rearrange("p (g t) -> p g t", t=64)
        AR = slot()[0:64, :].rearrange("p (h t) -> p h t", t=128)
        NP = [self.RX[:, q * 1024:(q + 1) * 1024].rearrange("p (h t) -> p h t", t=128) for q in range(2)]
        ysb = slot()[0:64, 0:520]
        rmaskS = slot(False)[0:64, 0:384].rearrange("p (e n) -> p e n", n=192)
        waT = slot(False)[0:64, 0:256].rearrange("p (g t) -> p g t", t=64)
        vtok = slot(False)[0:64, :]
        sgw = slot(False)[0:64, :]
        asT, kkn, ke, be, tE0, tE1, bch, kch, z = [v3(slot(False), 64) for _ in range(9)]
        eLs, Bt, Kt = [slot(False)[0:64, :] for _ in range(3)]
        Mm = [self.RX[:, 2048 + q * 512:2048 + (q + 1) * 512].rearrange("p (h t) -> p h t", t=64) for q in range(2)]
        Mrb, Mak, Mrk, Xs, Us, tmpS = [v3(slot(False), 64) for _ in range(6)]
        Sst = [v3(slot(False), 64) for _ in range(2)]
        hb16 = slot(False).bitcast(BF16)
        w_upSb = hb16[0:64, 0:1024].rearrange("p (e c) -> p e c", c=512)
        hb16b = slot(False).bitcast(BF16)
        a_upSb = hb16b[0:64, 0:1024].rearrange("p (e c) -> p e c", c=512)
        hb16c = slot(False).bitcast(BF16)
        waTb = hb16c[0:64, 0:256].rearrange("p (g t) -> p g t", t=64)
        sqb = hb16c[0:64, 256:768].rearrange("p (h t) -> p h t", t=64)
        onesb = hb16c[0:64, 768:832]
        assert si[0] <= 2 * len(slots), si[0]
        rwp = self.gq[0:64, 0:40]
        omka = self.gq[0:64, 40:48]
        ident64 = self.ident[0:64, 0:64]
        ones64 = self.onesf[0:64, 0:64]
        self.r32 = True
        P.dma("sp", w_upS, I["rwkv_w_up"][l * 128:(l + 1) * 128, :].rearrange("(e r) c -> r e c", r=64), writes=["w_upS"])
        P.dma("sp", a_upS, I["rwkv_a_up"][l * 128:(l + 1) * 128, :].rearrange("(e r) c -> r e c", r=64), writes=["a_upS"])
        for e in range(2):
            P.dma("sp", w0B[:, e, :], I["rwkv_w0"][l * 2 + e:l * 2 + e + 1, :].partition_broadcast(64), writes=["w0B"])
        P.dma("sp", rmaskS, I["rmask"].rearrange("(e p) n -> p e n", p=64), writes=["rmaskS"])
        self.cp("dve", w_upSb, w_upS, ["w_upS"], ["w_upSb"])
        self.cp("act", a_upSb, a_upS, ["a_upS"], ["a_upSb"])
        self.memset("dve", onesb, 1.0, ["onesb"])

        pm = self.tmp[0]
        P.dma("sp", pm[0:16, 0:64], I["rwkv_a0"][l * 2:(l + 1) * 2, :].rearrange("e (h c) -> (e h) c", c=64), writes=["tmp0"])
        P.dma("sp", pm[16:24, 0:64], I["rwkv_k_k"][l:l + 1, :].rearrange("e (h c) -> (e h) c", c=64), writes=["tmp0"])
        P.dma("sp", pm[24:32, 0:64], I["rwkv_k_a"][l:l + 1, :].rearrange("e (h c) -> (e h) c", c=64), writes=["tmp0"])
        P.dma("sp", pm[32:40, 0:64], I["rwkv_r_k"][l:l + 1, :].rearrange("e (h c) -> (e h) c", c=64), writes=["tmp0"])
        b = self.pbank()
        self.P.op("pe", lambda e_: e_.transpose(self.bank[b][0:64, 0:40], pm[0:40, 0:64], self.ident[0:40, 0:40]), reads=["tmp0", "ident"], writes=[self.bk(b)])
        self.cp("dve", rwp, self.bank[b][0:64, 0:40], [self.bk(b)], ["gq"])
        self.ts("dve", omka, rwp[:, 24:32], -1.0, 1.0, ALU.mult, ALU.add, ["gq"], ["gq"])
        bc3 = lambda ap: ap.unsqueeze(2).broadcast_to([64, 8, 64])
        hb = lambda b_, h, w=64: self.bank[b_][0:64, h * w:(h + 1) * w]
        b3 = lambda b_, w=64: self.bank[b_][0:64, 0:8 * w].rearrange("p (h t) -> p h t", t=w)

        for e in range(2):
            Scur = 0
            self.memset("dve", Sst[0], 0.0, ["S0"])
            order = range(32) if e == 0 else range(31, -1, -1)
            tl = 63 if e == 0 else 0
            mS, mI, mT = rmaskS[:, e, 0:64], rmaskS[:, e, 64:128], rmaskS[:, e, 128:192]
            for ch in order:
                t0 = ch * 64
                P.dma("sp", rkT, self.rk_scr.rearrange("(g p) t -> p g t", p=64)[:, :, t0:t0 + 64], reads=["rk_scr"], writes=["rkT"])
                P.dma("sp", waT, self.wa_scr.rearrange("(g p) t -> p g t", p=64)[:, :, t0:t0 + 64], reads=["wa_scr"], writes=["waT"])
                P.dma("sp", vtok, self.v_scr[t0:t0 + 64, :], reads=["v_scr"], writes=["vtok"])
                rT, kT_ = rkT[:, 0:8, :], rkT[:, 8:16, :]
                b = self.pbank()
                self.cp("dve", waTb, waT, ["waT"], ["waTb"])
                self.mm(self.bank[b][0:64, :], waTb[:, e, :], w_upSb[:, e, :], True, True, ["waTb", "w_upSb"], [self.bk(b)])
                self.tt("dve", sgw, self.bank[b][0:64, :], w0B[:, e, :], ALU.add, [self.bk(b), "w0B"], ["sgw"])
                self.act(sgw, sgw, AF.Sigmoid, ["sgw"], ["sgw"])
                b = self.pbank()
                for h in range(8):
                    self.mm(hb(b, h), a_upSb[:, e, h * 64:(h + 1) * 64], waTb[:, 2 + e, :], True, True, ["waTb", "a_upSb"], [self.bk(b)])
                self.tt("dve", asT, b3(b), bc3(rwp[:, e * 8:(e + 1) * 8]), ALU.add, [self.bk(b), "gq"], ["asT"])
                self.act(asT, asT, AF.Sigmoid, ["asT"], ["asT"])
                self.tt("dve", kkn, kT_, bc3(rwp[:, 16:24]), ALU.mult, ["rkT", "gq"], ["kkn"])
                self.act(sqb, kkn, AF.Square, ["kkn"], ["sqb"])
                b = self.pbank()
                self.mm(self.bank[b][0:64, :], onesb, sqb.rearrange("p h t -> p (h t)"), True, True, ["sqb", "onesb"], [self.bk(b)])
                self.act(tE0, b3(b), AF.Sqrt, [self.bk(b)], ["tE0"])
                self.ts("dve", tE0, tE0, 1e-12, None, ALU.max, None, ["tE0"], ["tE0"])
                self.P.op("dve", lambda e_: e_.reciprocal(out=tE0, in_=tE0), reads=["tE0"], writes=["tE0"])
                self.tt("dve", kkn, kkn, tE0, ALU.mult, ["kkn", "tE0"], ["kkn"])
                self.tt("pool", ke, asT, bc3(rwp[:, 24:32]), ALU.mult, ["asT", "gq"], ["ke"])
                self.tt("pool", ke, ke, bc3(omka), ALU.add, ["ke", "gq"], ["ke"])
                self.tt("pool", ke, ke, kT_, ALU.mult, ["ke", "rkT"], ["ke"])
                self.tt("pool", be, kkn, asT, ALU.mult, ["kkn", "asT"], ["be"])
                self.tt("pool", z, rT, ke, ALU.mult, ["rkT", "ke"], ["z"])
                bLi = self.pbank()
                for h in range(8):
                    self.mm(hb(bLi, h), sgw[:, h * 64:(h + 1) * 64], mI, True, True, ["sgw", "rmaskS"], [self.bk(bLi)])
                self.act(tE0, b3(bLi), AF.Exp, [self.bk(bLi)], ["tE0"], scale=-CW)
                self.act(tE1, b3(bLi), AF.Exp, [self.bk(bLi)], ["tE1"], scale=CW)
                self.tt("dve", AR[:, :, 64:128], rT, tE0, ALU.mult, ["rkT", "tE0"], ["AR"])
                self.cp("dve", self.sm[0:64, 32:40], tE0[:, :, tl], ["tE0"], ["sm"])
                self.tt("dve", bch, be, tE1, ALU.mult, ["be", "tE1"], ["bch"])
                self.tt("pool", kch, ke, tE1, ALU.mult, ["ke", "tE1"], ["kch"])
                bLe = self.pbank()
                for h in range(8):
                    self.mm(hb(bLe, h), sgw[:, h * 64:(h + 1) * 64], mS, True, True, ["sgw", "rmaskS"], [self.bk(bLe)])
                self.act(tE0, b3(bLe), AF.Exp, [self.bk(bLe)], ["tE0"], scale=-CW)
                self.stt("dve", AR[:, :, 0:64], kkn, -1.0, tE0, ALU.mult, ALU.mult, ["kkn", "tE0"], ["AR"])
                b = self.pbank()
                self.mm(self.bank[b][0:64, :], mT, sgw, True, True, ["sgw", "rmaskS"], [self.bk(b)])
                self.act(eLs, self.bank[b][0:64, :], AF.Exp, [self.bk(b)], ["eLs"], scale=-CW)
                for src, skey, dst, dkey in ((be, "be", Bt, "Bt"), (ke, "ke", Kt, "Kt")):
                    b = self.pbank()
                    for h in range(8):
                        self.P.op("pe", lambda e_, b=b, h=h, src=src: e_.transpose(hb(b, h), src[:, h, :], ident64), reads=[skey, "ident"], writes=[self.bk(b)])
                    self.tt("dve", dst, self.bank[b][0:64, :], eLs, ALU.mult, [self.bk(b), "eLs"], [dkey])
                b = self.pbank()
                for h in range(8):
                    self.mm(self.bank[b][0:64, h:h + 1], z[:, h, :], rwp[:, 32 + h:33 + h], True, True, ["z", "gq"], [self.bk(b)])
                self.cp("act", ysb[:, 512:520], self.bank[b][0:64, 0:8], [self.bk(b)], ["ysb"])
                for h in range(8):
                    self.mm(self.bank[h // 4][0:64, (h % 4) * 128:(h % 4 + 1) * 128], bch[:, h, :], AR[:, h, :], True, True, ["bch", "AR"], [self.bk(h // 4)])
                for h in range(8):
                    self.mm(self.bank[4 + h // 4][0:64, (h % 4) * 128:(h % 4 + 1) * 128], kch[:, h, :], AR[:, h, :], True, True, ["kch", "AR"], [self.bk(4 + h // 4)])
                for h in range(8):
                    self.mm(hb(6, h), AR[:, h, 0:64], bch[:, h, :], True, True, ["bch", "AR"], [self.bk(6)])
                m4 = lambda m_: m_.unsqueeze(1).broadcast_to([64, 4, 64])
                for g in range(2):
                    bb = self.bank[g][0:64, :].rearrange("p (h t) -> p h t", t=128)
                    kb = self.bank[4 + g][0:64, :].rearrange("p (h t) -> p h t", t=128)
                    self.tt("dve", NP[0][:, 4 * g:4 * g + 4, 0:64], bb[:, :, 0:64], m4(mS), ALU.mult, [self.bk(g), "rmaskS"], ["NP0"])
                    self.tt("dve", Mrb[:, 4 * g:4 * g + 4, :], bb[:, :, 64:128], m4(mI), ALU.mult, [self.bk(g), "rmaskS"], ["Mrb"])
                    self.tt("dve", Mak[:, 4 * g:4 * g + 4, :], kb[:, :, 0:64], m4(mS), ALU.mult, [self.bk(4 + g), "rmaskS"], ["Mak"])
                    self.tt("dve", Mrk[:, 4 * g:4 * g + 4, :], kb[:, :, 64:128], m4(mI), ALU.mult, [self.bk(4 + g), "rmaskS"], ["Mrk"])
                self.tt("dve", Mm[0], b3(6), mT.unsqueeze(1).broadcast_to([64, 8, 64]), ALU.mult, [self.bk(6), "rmaskS"], ["Mm0"])
                self.tt("pool", NP[0][:, :, 64:128], NP[0][:, :, 0:64], ident64.unsqueeze(1).broadcast_to([64, 8, 64]), ALU.add, ["NP0", "ident"], ["NP0"])
                cur = 0
                for step in range(6):
                    nx = 1 - cur
                    pbk = (0, 1) if step % 2 == 0 else (4, 5)
                    mbk = 6 if step % 2 else 7
                    for g in range(2):
                        ncur, mcur = "NP%d/%d" % (cur, g), "Mm%d/%d" % (cur, g)
                        for h in range(4 * g, 4 * g + 4):
                            hh = h % 4
                            if step == 0:
                                o_, r_ = self.bank[pbk[g]][0:64, hh * 128:hh * 128 + 64], NP[cur][:, h, 0:64]
                            elif step < 5:
                                o_, r_ = self.bank[pbk[g]][0:64, hh * 128:(hh + 1) * 128], NP[cur][:, h, :]
                            else:
                                o_, r_ = self.bank[pbk[g]][0:64, hh * 128 + 64:(hh + 1) * 128], NP[cur][:, h, 64:128]
                            self.mm(o_, Mm[cur][:, h, :], r_, True, True, [mcur, ncur], [self.bk(pbk[g])])
                        if step < 5:
                            for h in range(4 * g, 4 * g + 4):
                                self.mm(self.bank[6 + g][0:64, (h % 4) * 64:(h % 4 + 1) * 64], NP[cur][:, h, 0:64], Mm[cur][:, h, :], True, True, [mcur, ncur], [self.bk(6 + g)])
                    for g in range(2):
                        ncur, nnx, mnx = "NP%d/%d" % (cur, g), "NP%d/%d" % (nx, g), "Mm%d/%d" % (nx, g)
                        pv = self.bank[pbk[g]][0:64, :].rearrange("p (h t) -> p h t", t=128)
                        if step < 5:
                            self.cp("act", NP[nx][:, 4 * g:4 * g + 4, 0:64], pv[:, :, 0:64], [self.bk(pbk[g])], [nnx])
                        if step == 0:
                            self.cp("dve", NP[nx][:, 4 * g:4 * g + 4, 64:128], NP[cur][:, 4 * g:4 * g + 4, 64:128], [ncur], [nnx])
                        else:
                            self.tt("dve", NP[nx][:, 4 * g:4 * g + 4, 64:128], pv[:, :, 64:128], NP[cur][:, 4 * g:4 * g + 4, 64:128], ALU.add,
                                    [self.bk(pbk[g]), ncur], [nnx])
                        if step < 5:
                            self.cp("act" if g else "dve", Mm[nx][:, 4 * g:4 * g + 4, :], self.bank[6 + g][0:64, 0:256].rearrange("p (h t) -> p h t", t=64), [self.bk(6 + g)], [mnx])
                    cur = nx
                TT = NP[cur]
                S0, skey = Sst[Scur], "S%d" % Scur
                S1, s1key = Sst[1 - Scur], "S%d" % (1 - Scur)
                bA, bB = (2, 0), (3, 1)
                G2 = range(2)
                hs = lambda g: range(4 * g, 4 * g + 4)
                gs = lambda ap, g: ap[:, 4 * g:4 * g + 4, :]
                hq = lambda b_, h: self.bank[b_][0:64, (h % 4) * 64:(h % 4 + 1) * 64]
                q3 = lambda b_: self.bank[b_][0:64, 0:256].rearrange("p (h t) -> p h t", t=64)
                for g in G2:
                    for h in hs(g):
                        self.mm(hq(bA[g], h), AR[:, h, 0:64], S0[:, h, :], True, False, ["AR", skey + "/%d" % g], [self.bk(bA[g])])
                        self.mm(hq(bA[g], h), Mak[:, h, :], vtok[:, h * 64:(h + 1) * 64], False, True, ["Mak", "vtok"], [self.bk(bA[g])])
                for g in G2:
                    self.cp("dve" if g else "act", gs(Xs, g), q3(bA[g]), [self.bk(bA[g])], ["Xs/%d" % g])
                for g in G2:
                    for h in hs(g):
                        self.mm(hq(bA[g], h), TT[:, h, 64:128], Xs[:, h, :], True, True, ["NP%d/%d" % (cur, g), "Xs/%d" % g], [self.bk(bA[g])])
                for g in G2:
                    self.cp("act" if g else "dve", gs(Us, g), q3(bA[g]), [self.bk(bA[g])], ["Us/%d" % g])
                for g in G2:
                    for h in hs(g):
                        self.mm(hq(bA[g], h), AR[:, h, 64:128], S0[:, h, :], True, False, ["AR", skey + "/%d" % g], [self.bk(bA[g])])
                        self.mm(hq(bA[g], h), Mrb[:, h, :], Us[:, h, :], False, False, ["Mrb", "Us/%d" % g], [self.bk(bA[g])])
                        self.mm(hq(bA[g], h), Mrk[:, h, :], vtok[:, h * 64:(h + 1) * 64], False, True, ["Mrk", "vtok"], [self.bk(bA[g])])
                    for h in hs(g):
                        self.mm(hq(bB[g], h), Bt[:, h * 64:(h + 1) * 64], Us[:, h, :], True, False, ["Bt", "Us/%d" % g], [self.bk(bB[g])])
                        self.mm(hq(bB[g], h), Kt[:, h * 64:(h + 1) * 64], vtok[:, h * 64:(h + 1) * 64], False, True, ["Kt", "vtok"], [self.bk(bB[g])])
                for g in G2:
                    self.cp("act", ysb[:, g * 256:(g + 1) * 256], self.bank[bA[g]][0:64, 0:256], [self.bk(bA[g])], ["ysb/%d" % g])
                    self.tt("pool", gs(tmpS, g), gs(S0, g), bc3(self.sm[0:64, 32:40])[:, 4 * g:4 * g + 4, :], ALU.mult, [skey + "/%d" % g, "sm"], ["tmpS/%d" % g])
                    self.tt("dve", gs(S1, g), gs(tmpS, g), q3(bB[g]), ALU.add, ["tmpS/%d" % g, self.bk(bB[g])], [s1key + "/%d" % g])
                P.dma("sp", self.y_scr[e * S + t0:e * S + t0 + 64, :], ysb, reads=["ysb"], writes=["y_scr"])
                Scur = 1 - Scur
        self.r32 = False
        self.barrier()
        P.dma("sp", self.lng[:, 0:512], I["rwkv_ln_g"][l:l + 1, :].partition_broadcast(128), writes=["lng"])
        P.dma("sp", self.lnb[:, 0:512], I["rwkv_ln_b"][l:l + 1, :].partition_broadcast(128), writes=["lnb"])
        yf, yb, vt = self.lnx[0], self.lnx[1], self.junk
        sm = self.sm
        t0_, t1_, t2_ = self.tmp
        for tt in range(NT):
            P.dma("sp", yf[:, 0:520], self.y_scr[tt * 128:(tt + 1) * 128, :], reads=["y_scr"], writes=["lnx0"])
            P.dma("sp", yb[:, 0:520], self.y_scr[S + tt * 128:S + (tt + 1) * 128, :], reads=["y_scr"], writes=["lnx1"])
            P.dma("sp", vt[:, 0:512], self.v_scr[tt * 128:(tt + 1) * 128, :], reads=["v_scr"], writes=["junk"])
            P.dma("sp", vt[:, 512:1024], self.sg_scr[tt * 128:(tt + 1) * 128, :], reads=["sg_scr"], writes=["junk"])
            self.tt("dve", yf[:, 0:520], yf[:, 0:520], yb[:, 0:520], ALU.add, ["lnx0", "lnx1"], ["lnx0"])
            y3 = yf[:, 0:512].rearrange("p (h d) -> p h d", d=64)
            P.op("dve", lambda e_, y3=y3: e_.reduce_sum(out=sm[:, 40:48], in_=y3, axis=AX.X), reads=["lnx0"], writes=["sm"])
            self.ts("dve", sm[:, 40:48], sm[:, 40:48], -1.0 / 64, None, ALU.mult, None, ["sm"], ["sm"])
            self.tt("dve", y3, y3, sm[:, 40:48].unsqueeze(2).broadcast_to([128, 8, 64]), ALU.add, ["lnx0", "sm"], ["lnx0"])
            self.act(t0_[:, 0:512], yf[:, 0:512], AF.Square, ["lnx0"], ["tmp0"])
            P.op("dve", lambda e_: e_.reduce_sum(out=sm[:, 48:56], in_=t0_[:, 0:512].rearrange("p (h d) -> p h d", d=64), axis=AX.X), reads=["tmp0"], writes=["sm"])
            self.rsqrt_cols(sm[:, 48:56], sm[:, 56:64], 1.0 / 64, 64e-5)
            self.tt("dve", y3, y3, sm[:, 56:64].unsqueeze(2).broadcast_to([128, 8, 64]), ALU.mult, ["lnx0", "sm"], ["lnx0"])
            self.tt("dve", yf[:, 0:512], yf[:, 0:512], self.lng[:, 0:512], ALU.mult, ["lnx0", "lng"], ["lnx0"])
            self.tt("pool", yf[:, 0:512], yf[:, 0:512], self.lnb[:, 0:512], ALU.add, ["lnx0", "lnb"], ["lnx0"])
            self.tt("pool", t1_[:, 0:512].rearrange("p (h d) -> p h d", d=64), vt[:, 0:512].rearrange("p (h d) -> p h d", d=64),
                    yf[:, 512:520].unsqueeze(2).broadcast_to([128, 8, 64]), ALU.mult, ["junk", "lnx0"], ["tmp1"])
            self.tt("dve", t1_[:, 0:512], t1_[:, 0:512], yf[:, 0:512], ALU.add, ["tmp1", "lnx0"], ["tmp1"])
            self.tt("dve", t2_[:, 0:512], t1_[:, 0:512], vt[:, 512:1024], ALU.mult, ["tmp1", "junk"], ["tmp2"])
            P.dma("sp", self.o_scr[tt * 128:(tt + 1) * 128, OB:OB + 512], t2_[:, 0:512], reads=["tmp2"], writes=["o_scr"])

    def merge(self, l, last):
        P, I = self.P, self.I
        wg_all = I["w_gate"][l * D:(l + 1) * D, :]
        wb_all = I["w_branch"][l * 2304:(l + 1) * 2304, :]
        wo_all = I["w_out"][l * D:(l + 1) * D, :]
        P.dma("sp", self.lng[:], I["ln_g"][l:l + 1, :].partition_broadcast(128), writes=["lng"])
        P.dma("sp", self.lnb[:], I["ln_b"][l:l + 1, :].partition_broadcast(128), writes=["lnb"])
        bgT = self.lamt[:, 0:40]
        for i5 in range(5):
            P.dma("sp", bgT[:, i5 * 8:(i5 + 1) * 8], I["b_gate"][l:l + 1, i5 * 1024:(i5 + 1) * 1024].rearrange("e (g c) -> c (e g)", c=128),
                  writes=["lamt"], allow_slow_non_contiguous=True)
        oT = self.BIG[0][:, 0:9216].rearrange("p (j t) -> p j t", t=512)
        yTf = self.BIG[1][:].bitcast(F32)[:, 0:4096].rearrange("p (c t) -> p c t", t=512)
        otile = self.BIG[2][:].bitcast(F32)[:, 0:2304]
        yTb = self.BIG[3][:, 0:4096].rearrange("p (c t) -> p c t", t=512)
        hgrp = self.G[:, 0:4096].rearrange("p (q c) -> p q c", c=1024)
        hin = self.hres[l % 2]
        hout = self.out if last else self.hres[(l + 1) % 2]
        t0, t1 = self.tmp[0], self.tmp[1]
        mb = [0]

        def mbank():
            mb[0] = (mb[0] + 1) % 8
            return mb[0]

        for grp in range(4):
            for tq in range(4):
                tt = grp * 4 + tq
                P.dma("sp", otile, self.o_scr[tt * 128:(tt + 1) * 128, :], reads=["o_scr"], writes=["BIG2"])
                P.dma("sp", hgrp[:, tq, :], hin[tt * 128:(tt + 1) * 128, :], reads=["hres%d" % (l % 2)], writes=["G"])
                for j4 in range(5):
                    nj = min(4, 18 - j4 * 4)
                    b = mbank()
                    for u in range(nj):
                        j = j4 * 4 + u
                        self.tr(self.bank[b][:, u * 128:(u + 1) * 128], otile[:, j * 128:(j + 1) * 128], ["BIG2"], [self.bk(b)])
                    self.cp("act" if j4 % 2 else "dve", oT[:, j4 * 4:j4 * 4 + nj, tq * 128:(tq + 1) * 128],
                            self.bank[b][:, 0:nj * 128].rearrange("p (c t) -> p c t", t=128), [self.bk(b)], ["BIG0"])
            hsl = lambda c: self.hT[:, c, 1 + grp * 512:1 + (grp + 1) * 512]
            for i, (r0, rw) in enumerate(BROWS):
                kci = rw // 128
                for cc in range(4):
                    wg, wgk = self.wload(wg_all[:, i * 1024 + cc * 256:i * 1024 + (cc + 1) * 256], 256)[0]
                    wb, wbk = self.wload(wb_all[r0:r0 + rw, cc * 256:(cc + 1) * 256], 256, kc=kci)[0]
                    for u in range(2):
                        ct = cc * 2 + u
                        b1 = mbank()
                        for c in range(8):
                            self.mm(self.bank[b1][:, :], wg[:, c, u * 128:(u + 1) * 128], hsl(c), c == 0, c == 7, [wgk, "hT"], [self.bk(b1)])
                        ti = self.nxt("mt", 2)
                        tg = self.tmp[ti]
                        self.act(tg[:, :], self.bank[b1][:, :], AF.Sigmoid, [self.bk(b1), "lamt"], ["tmp%d" % ti], bias=bgT[:, i * 8 + ct:i * 8 + ct + 1])
                        b2 = mbank()
                        for c in range(kci):
                            self.mm(self.bank[b2][:, :], wb[:, c, u * 128:(u + 1) * 128], oT[:, r0 // 128 + c, :], c == 0, c == kci - 1,
                                    [wbk, "BIG0"], [self.bk(b2)])
                        ysl = yTf[:, ct, :]
                        if i == 0:
                            self.tt("dve", ysl, self.bank[b2][:, :], tg[:, :], ALU.mult, [self.bk(b2), "tmp%d" % ti], ["BIG1"])
                        else:
                            self.tt("dve", tg[:, :], self.bank[b2][:, :], tg[:, :], ALU.mult, [self.bk(b2), "tmp%d" % ti], ["tmp%d" % ti])
                            self.tt("dve", ysl, ysl, tg[:, :], ALU.add, ["BIG1", "tmp%d" % ti], ["BIG1"])
            if l == 0 and grp == 0:
                self.dbg("yTf", self.BIG[1][:].bitcast(F32)[:, 0:4096], ["BIG1"])
                self.dbg("oT", self.BIG[0][:, 0:9216], ["BIG0"], BF16)
                self.dbg("bgT", self.lamt[:, 0:40], ["lamt"])
            for half in range(2):
                self.cp("act" if half else "dve", yTb[:, half * 4:half * 4 + 4, :], yTf[:, half * 4:half * 4 + 4, :], ["BIG1"], ["BIG3"])
            for cc in range(4):
                wo, wok = self.wload(wo_all[:, cc * 256:(cc + 1) * 256], 256)[0]
                for tq in range(4):
                    b = mbank()
                    for c in range(8):
                        self.mm(self.bank[b][:, 0:256], yTb[:, c, tq * 128:(tq + 1) * 128], wo[:, c, :], c == 0, c == 7, [wok, "BIG3"], [self.bk(b)])
                    hs = hgrp[:, tq, cc * 256:(cc + 1) * 256]
                    self.stt("dve", hs, hs, ALPHA, self.bank[b][:, 0:256], ALU.mult, ALU.add, ["G", self.bk(b)], ["G"])
            for tq in range(4):
                tt = grp * 4 + tq
                self.ln_inplace(hgrp[:, tq, :], "G")
                P.dma("sp", hout[tt * 128:(tt + 1) * 128, :], hgrp[:, tq, :], reads=["G"],
                      writes=["out" if last else "hres%d" % ((l + 1) % 2)], is_output=last)


def make_in_map(inputs, b, consts):
    m = {"x": np.ascontiguousarray(inputs["x"][b]), "mem": np.ascontiguousarray(inputs["mem"][b])}
    for nm, shp in IN_SPECS:
        if nm in consts:
            m[nm] = consts[nm]
        else:
            m[nm] = np.ascontiguousarray(np.asarray(inputs[nm], dtype=np.float32).reshape(shp))
    return m


def kernel(**inputs):
    consts = host_consts()
    kb = KB(debug=False)
    nb = inputs["x"].shape[0]
    in_maps = [make_in_map(inputs, b, consts) for b in range(nb)]
    res = run_bass_kernel_spmd(kb.nc, in_maps, core_ids=list(range(nb)))
    out = np.stack([np.asarray(r["out"], dtype=np.float32).reshape(S, D) for r in res.results], axis=0)
    return out
```

```python
import math
from concourse.ap import AP
import contextlib
import numpy as np
import concourse.bass as bass
import concourse.mybir as mybir
from concourse.bass_utils import run_bass_kernel_spmd

F32 = mybir.dt.float32
BF16 = mybir.dt.bfloat16
I32 = mybir.dt.int32
AF = mybir.ActivationFunctionType
ALU = mybir.AluOpType
AX = mybir.AxisListType

ENGS = ("pe", "act", "dve", "pool", "sp")
DMA_SEMS = 8


class Op:
    __slots__ = ("eng", "fn", "waits", "is_dma", "idx", "marked", "dma_slot", "dma_val", "prewait")

    def __init__(self, eng, fn, is_dma):
        self.eng = eng
        self.fn = fn
        self.is_dma = is_dma
        self.waits = []
        self.marked = False
        self.idx = None
        self.dma_slot = None
        self.dma_val = None
        self.prewait = None


class Prog:
    def __init__(self, nc, same_engine_sync=True):
        self.nc = nc
        self.ops = {e: [] for e in ENGS}
        self.last_write = {}
        self.readers = {}
        self.children = {}
        self.same_engine_sync = same_engine_sync
        self.dma_count = {e: 0 for e in ENGS}
        self.dma_hist = {e: [] for e in ENGS}
        self.all_dma_out = []
        self.stack = contextlib.ExitStack()
        self.n_ops = 0

    def sb(self, name, shape, dt):
        return self.stack.enter_context(self.nc.sbuf_tensor("s_" + name, list(shape), dt))

    def ps(self, name, shape, dt):
        return self.stack.enter_context(self.nc.psum_tensor("p_" + name, list(shape), dt))

    def _related(self, k):
        if "/" in k:
            p = k.split("/")[0]
            self.children.setdefault(p, set()).add(k)
            return (k, p)
        return (k,) + tuple(self.children.get(k, ()))

    def _deps(self, op, reads, writes):
        deps = []
        for k0 in reads:
            for k in self._related(k0):
                w = self.last_write.get(k)
                if w is not None:
                    deps.append(w)
        for k0 in writes:
            for k in self._related(k0):
                w = self.last_write.get(k)
                if w is not None:
                    deps.append(w)
                for r in self.readers.get(k, ()):
                    deps.append(r)
        best = {}
        for d in deps:
            if d is op:
                continue
            key = (d.eng, d.is_dma, d.dma_slot if d.is_dma else None)
            cur = best.get(key)
            if cur is None or d.idx > cur.idx:
                best[key] = d
        for d in best.values():
            if (not d.is_dma) and d.eng == op.eng and not op.is_dma:
                if op.eng == "pe" or not self.same_engine_sync:
                    continue
            op.waits.append(d)
            d.marked = True
        for k in reads:
            self.readers.setdefault(k, []).append(op)
        for k in writes:
            self.last_write[k] = op
            self.readers[k] = []

    def barrier(self, fn):
        o = Op("pool", fn, False)
        o.idx = len(self.ops["pool"])
        self.ops["pool"].append(o)
        self._deps(o, [], ["__phase__"])
        return o

    def op(self, eng, fn, reads=(), writes=()):
        reads = list(reads) + ["__phase__"]
        o = Op(eng, fn, False)
        o.idx = len(self.ops[eng])
        self.ops[eng].append(o)
        self._deps(o, reads, writes)
        self.n_ops += 1
        return o

    def dma(self, eng, out, in_, reads=(), writes=(), is_output=False, **kw):
        def fn(e, out=out, in_=in_, kw=kw):
            return e.dma_start(out=out, in_=in_, **kw)
        reads = list(reads) + ["__phase__"]
        o = Op(eng, fn, True)
        o.idx = len(self.ops[eng])
        n = self.dma_count[eng]
        self.dma_count[eng] += 1
        o.dma_slot = n % DMA_SEMS
        o.dma_val = 16 * (n // DMA_SEMS + 1)
        if n >= DMA_SEMS:
            o.prewait = self.dma_hist[eng][n - DMA_SEMS]
        self.dma_hist[eng].append(o)
        self.ops[eng].append(o)
        self._deps(o, reads, writes)
        if is_output:
            self.all_dma_out.append(o)
        self.n_ops += 1
        return o

    def emit(self):
        nc = self.nc
        st = self.stack
        fin = Op("sp", None, False)
        fin.idx = len(self.ops["sp"])
        for o in self.all_dma_out:
            fin.waits.append(o)
        self.ops["sp"].append(fin)
        csem = {e: st.enter_context(nc.semaphore("c_" + e)) for e in ENGS}
        dsem = {e: [st.enter_context(nc.semaphore("d_%s_%d" % (e, i))) for i in range(DMA_SEMS)]
                for e in ENGS if self.dma_count[e] > 0}
        for e in ENGS:
            c = 0
            for o in self.ops[e]:
                if o.is_dma:
                    continue
                if o.marked:
                    c += 1
                    o.dma_val = c
        block = st.enter_context(nc.Block())
        prog = self

        def run(e, eng):
            seen = {}
            for o in prog.ops[e]:
                ws = list(o.waits)
                if o.prewait is not None:
                    ws.append(o.prewait)
                for d in ws:
                    if d.is_dma:
                        sem, val = dsem[d.eng][d.dma_slot], d.dma_val
                    else:
                        sem, val = csem[d.eng], d.dma_val
                    k = id(sem)
                    if seen.get(k, 0) >= val:
                        continue
                    seen[k] = val
                    eng.wait_ge(sem, val)
                if o.fn is None:
                    continue
                ins = o.fn(eng)
                if o.is_dma:
                    ins.then_inc(dsem[e][o.dma_slot], 16)
                elif o.marked:
                    ins.then_inc(csem[e], 1)

        @block.tensor
        def _(eng):
            run("pe", eng)

        @block.scalar
        def _(eng):
            run("act", eng)

        @block.vector
        def _(eng):
            run("dve", eng)

        @block.gpsimd
        def _(eng):
            run("pool", eng)

        @block.sync
        def _(eng):
            run("sp", eng)

    def close(self):
        self.stack.close()


F32R = mybir.dt.float32r

S = 2048
D = 1024
NT = 16
DEPTH = 2
WC = 256
XC = 2047
GW = 4096
ALPHA = (2 * DEPTH) ** 0.25
A0, B0, C0, D0, M0 = 0, 2048, 4352, 5632, 7680
OA, OB, OC, OD, OM = 0, 512, 1024, 1536, 2048
BROWS = [(0, 512), (512, 512), (1024, 512), (1536, 512), (2048, 256)]


def rel_bucket_np(rel):
    nb = 16
    max_exact = 8
    n = np.abs(rel)
    nf = np.maximum(n, 1).astype(np.float32)
    large = max_exact + (np.log(nf / max_exact) / np.float32(math.log(1024 / max_exact)) * (nb - max_exact)).astype(np.int32)
    large = np.minimum(large, nb - 1)
    return np.where(rel > 0, nb, 0) + np.where(n < max_exact, n, large)


def host_consts():
    c = {}
    c["ident"] = np.eye(128, dtype=np.float32)
    rel = np.arange(4096) - XC
    bkt = rel_bucket_np(rel)
    oh = np.zeros((32, 4096), np.float32)
    oh[bkt, np.arange(4096)] = 1.0
    c["onehot"] = oh
    n = np.abs(rel)
    mA = (n <= 64).astype(np.float32) + ((rel % 4 == 0) & (n <= 256)) + ((rel % 16 == 0) & (n <= 1024))
    mt = np.ones((12, 4096), np.float32)
    mt[:8] = mA[None, :]
    c["multab"] = mt
    t = np.arange(S)
    row = (t // 64).astype(np.float32)
    col = (t % 64).astype(np.float32)
    freqs = (10000.0 ** (-(np.arange(16, dtype=np.float32) / 16))).astype(np.float32)
    ar = row[:, None] * freqs[None, :]
    ac = col[:, None] * freqs[None, :]
    c["ropec"] = np.concatenate([np.cos(ar), np.cos(ar), np.cos(ac), np.cos(ac)], 1).astype(np.float32)
    c["ropes"] = np.concatenate([-np.sin(ar), np.sin(ar), -np.sin(ac), np.sin(ac)], 1).astype(np.float32)
    tri = np.zeros((2, 3, 128, 128), np.float32)
    sg = np.arange(128)[:, None]
    tt = np.arange(128)[None, :]
    same = (sg // 64) == (tt // 64)
    tri[0, 0] = same & (sg <= tt)
    tri[0, 1] = same & (sg < tt)
    tri[0, 2] = same & (sg > tt)
    tri[1, 0] = same & (sg >= tt)
    tri[1, 1] = same & (sg > tt)
    tri[1, 2] = same & (sg < tt)
    c["tri"] = tri.reshape(6 * 128, 128)
    mk_ = np.zeros((2, 64, 192), np.float32)
    a = np.arange(64)[:, None]
    b = np.arange(64)[None, :]
    mk_[0, :, 0:64] = a < b
    mk_[0, :, 64:128] = a <= b
    mk_[0, :, 128:192] = b < a
    mk_[1, :, 0:64] = a > b
    mk_[1, :, 64:128] = a >= b
    mk_[1, :, 128:192] = b > a
    c["rmask"] = mk_.reshape(128, 192)
    return c


IN_SPECS = [("ln_in_g", [1, D]), ("ln_in_b", [1, D]), ("rel_bias", [32, 12]), ("w_in", [DEPTH * D, 8192]),
            ("shift_mu", [DEPTH * 2, 1792]), ("rwkv_w0", [DEPTH * 2, 512]), ("rwkv_w_up", [DEPTH * 2 * 64, 512]),
            ("rwkv_a0", [DEPTH * 2, 512]), ("rwkv_a_up", [DEPTH * 2 * 64, 512]), ("rwkv_k_k", [DEPTH, 512]),
            ("rwkv_k_a", [DEPTH, 512]), ("rwkv_r_k", [DEPTH, 512]), ("rwkv_ln_g", [DEPTH, 512]),
            ("rwkv_ln_b", [DEPTH, 512]), ("c_qnorm_g", [DEPTH, 64]), ("c_knorm_g", [DEPTH, 64]),
            ("d_lambda", [DEPTH, 256]), ("d_subln_g", [DEPTH, 128]), ("w_mem_kv", [DEPTH * D, 512]),
            ("w_branch", [DEPTH * 2304, D]), ("w_gate", [DEPTH * D, 5120]), ("b_gate", [DEPTH, 5120]),
            ("w_out", [DEPTH * D, D]), ("ln_g", [DEPTH, D]), ("ln_b", [DEPTH, D]),
            ("ident", [128, 128]), ("onehot", [32, 4096]), ("multab", [12, 4096]), ("ropec", [S, 64]),
            ("ropes", [S, 64]), ("tri", [768, 128]), ("rmask", [128, 192])]


class KB:
    def __init__(self, debug=False, mixers="MCADB", layers=DEPTH):
        self.debug = debug
        self.mixers = mixers
        nc = bass.Bass("TRN2", target_bir_lowering=False)
        self.nc = nc
        P = Prog(nc)
        self.P = P
        I = {}
        I["x"] = nc.dram_tensor("x", [S, D], F32, kind="ExternalInput").ap()
        I["mem"] = nc.dram_tensor("mem", [256, D], F32, kind="ExternalInput").ap()
        for nm, shp in IN_SPECS:
            I[nm] = nc.dram_tensor(nm, list(shp), F32, kind="ExternalInput").ap()
        self.I = I
        self.out = nc.dram_tensor("out", [S, D], F32, kind="ExternalOutput").ap()
        self.hres = [nc.dram_tensor("hres%d" % i, [S, D], F32, kind="ExternalOutput" if debug else "Internal").ap() for i in range(2)]
        self.dbg_outs = []
        self.o_scr = nc.dram_tensor("o_scr", [S, 2304], F32, kind="ExternalOutput" if debug else "Internal").ap()
        self.xtab = nc.dram_tensor("xtab", [12, 4096], F32).ap()
        self.rk_scr = nc.dram_tensor("rk_scr", [1024, S], F32).ap()
        self.wa_scr = nc.dram_tensor("wa_scr", [256, S], F32).ap()
        self.v_scr = nc.dram_tensor("v_scr", [S, 512], F32).ap()
        self.y_scr = nc.dram_tensor("y_scr", [2 * S, 520], F32, kind="ExternalOutput" if debug else "Internal").ap()
        self.sg_scr = nc.dram_tensor("sg_scr", [S, 512], F32).ap()
        self.ident = P.sb("ident", [128, 128], F32)
        self.hT = P.sb("hT", [128, 8, S + 2], BF16)
        self.BIG = [P.sb("BIG%d" % i, [128, 9216], BF16) for i in range(4)]
        self.G = P.sb("G", [128, GW], F32)
        self.wst = [P.sb("wst%d" % i, [128, 8, WC], F32) for i in range(1)]
        self.RX = P.sb("RX", [64, 5120], F32)
        self.wbf = [P.sb("wbf%d" % i, [128, 8, WC], BF16) for i in range(4)]
        self.ropet = P.sb("ropet", [128, 128], F32)
        self.lnx = [P.sb("lnx%d" % i, [128, D], F32) for i in range(2)]
        self.junk = P.sb("junk", [128, D], F32)
        self.lng = P.sb("lng", [128, D], F32)
        self.lnb = P.sb("lnb", [128, D], F32)
        self.pt = [P.sb("pt%d" % i, [128, 512], BF16) for i in range(4)]
        self.pe_ = [P.sb("pe%d" % i, [128, 512], BF16) for i in range(4)]
        self.ost = [P.sb("ost%d" % i, [128, 128], F32) for i in range(3)]
        self.sm = P.sb("sm", [128, 64], F32)
        self.tmp = [P.sb("tmp%d" % i, [128, 512], F32) for i in range(3)]
        self.onesf = P.sb("onesf", [128, 128], F32)
        self.gq = P.sb("gq", [128, 128], F32)
        self.subg = P.sb("subg", [128, 128], F32)
        self.lamt = P.sb("lamt", [128, 264], F32)
        self.pbar = P.sb("pbar", [1, 8], F32)
        self.memT = P.sb("memT", [128, 8, 256], BF16)
        self.bank = [P.ps("bank%d" % i, [128, 512], F32) for i in range(8)]
        self.cnt = {}
        self.pbi = 0
        B0_, B1_, B2_, B3_ = [b[:] for b in self.BIG]
        self.qT = B0_[:, 0:8192].rearrange("p (c t) -> p c t", t=S)
        self.kT = B1_[:, 0:8192].rearrange("p (c t) -> p c t", t=S)
        self.sg = B3_[:, 0:8192].rearrange("p (t c) -> p t c", c=512)
        self.prelude()
        for l in range(layers):
            self.layer(l, last=(l == layers - 1))
        P.emit()
        P.close()

    def nxt(self, name, n):
        v = self.cnt.get(name, 0)
        self.cnt[name] = (v + 1) % n
        return v

    def bk(self, i):
        return "bank%d" % i

    def pbank(self):
        self.pbi ^= 1
        return 2 + self.pbi

    def barrier(self):
        pbar = self.pbar
        self.P.barrier(lambda e: e.memset(pbar[:], 0.0))

    def R(self, ap):
        if ap.dtype == F32 and ap.name == "s_RX":
            return ap.bitcast(F32R)
        return ap

    def mm(self, out, lhsT, rhs, start, stop, reads, writes):
        if lhsT.name == "s_RX" and rhs.name == "s_RX":
            lhsT, rhs = self.R(lhsT), self.R(rhs)
        self.P.op("pe", lambda e: e.matmul(out, lhsT=lhsT, rhs=rhs, start=start, stop=stop), reads=reads, writes=writes)

    def tr(self, out, in_, reads, writes, np_=128):
        ident = self.ident
        self.P.op("pe", lambda e: e.transpose(out, in_, ident[0:np_, 0:np_]), reads=list(reads) + ["ident"], writes=writes)

    def cp(self, eng, out, in_, reads, writes):
        out = self.R(out)
        if eng == "act":
            self.P.op("act", lambda e: e.copy(out=out, in_=in_), reads=reads, writes=writes)
        else:
            self.P.op(eng, lambda e: e.tensor_copy(out=out, in_=in_), reads=reads, writes=writes)

    def act(self, out, in_, func, reads, writes, **kw):
        out = self.R(out)
        self.P.op("act", lambda e: e.activation(out=out, in_=in_, func=func, **kw), reads=reads, writes=writes)

    def tt(self, eng, out, in0, in1, op, reads, writes):
        out = self.R(out)
        self.P.op(eng, lambda e: e.tensor_tensor(out=out, in0=in0, in1=in1, op=op), reads=reads, writes=writes)

    def ts(self, eng, out, in0, s1, s2, op0, op1, reads, writes):
        out = self.R(out)
        if s2 is None:
            self.P.op(eng, lambda e: e.tensor_scalar(out=out, in0=in0, scalar1=s1, scalar2=None, op0=op0), reads=reads, writes=writes)
        else:
            self.P.op(eng, lambda e: e.tensor_scalar(out=out, in0=in0, scalar1=s1, scalar2=s2, op0=op0, op1=op1), reads=reads, writes=writes)

    def stt(self, eng, out, in0, scalar, in1, op0, op1, reads, writes):
        out = self.R(out)
        self.P.op(eng, lambda e: e.scalar_tensor_tensor(out=out, in0=in0, scalar=scalar, in1=in1, op0=op0, op1=op1), reads=reads, writes=writes)

    def memset(self, eng, ap, val, writes):
        ap = self.R(ap)
        self.P.op(eng, lambda e: e.memset(ap, val), writes=writes)

    def rsqrt_cols(self, src, dst, scale, eps, key="sm"):
        self.ts("dve", dst, src, scale, eps, ALU.mult, ALU.add, [key], [key])
        self.P.op("act", lambda e: e.sqrt(out=dst, in_=dst), reads=[key], writes=[key])
        self.P.op("dve", lambda e: e.reciprocal(out=dst, in_=dst), reads=[key], writes=[key])

    def wload(self, src2d, n, kc=8, variants=None):
        P = self.P
        src = src2d.rearrange("(c p) n -> p c n", p=128)
        if variants is None:
            j = self.nxt("wb", 4)
            P.dma("pool", self.wbf[j][:, 0:kc, 0:n], src, writes=["wbf%d" % j])
            return [(self.wbf[j], "wbf%d" % j)]
        wst = self.wst[0]
        P.dma("sp", wst[:, 0:kc, 0:n], src, writes=["wst0"])
        res = []
        for vi, (vap, vkey) in enumerate(variants):
            j = self.nxt("wb", 4)
            self.tt("dve" if vi != 1 else "pool", self.wbf[j][:, 0:kc, 0:n], wst[:, 0:kc, 0:n], vap.unsqueeze(1).broadcast_to([128, kc, n]), ALU.mult,
                    ["wst0", vkey], ["wbf%d" % j])
            res.append((self.wbf[j], "wbf%d" % j))
        return res

    def proj_T(self, wl, n, consume, shifts=(0,), rhs_fn=None, rkey="hT", ntb=4, tbw=512):
        hT = self.hT
        for ct in range(n // 128):
            for tb in range(ntb):
                b = self.pbank()
                nmm = 8 * len(shifts)
                m = 0
                for (wap, wkey), s in zip(wl, shifts):
                    for c in range(8):
                        if rhs_fn is None:
                            lo = 1 + tb * 512 + s
                            rhs = hT[:, c, lo:lo + 512]
                        else:
                            rhs = rhs_fn(c, tb)
                        self.mm(self.bank[b][:, 0:tbw], wap[:, c, ct * 128:(ct + 1) * 128], rhs, m == 0, m == nmm - 1,
                                [wkey, rkey], [self.bk(b)])
                        m += 1
                consume(b, ct, tb)

    def proj_N(self, wl, n, consume, shifts=(0,), lhs_fn=None, lkey="hT", ntt=NT, kc=8):
        hT = self.hT
        for tt in range(ntt):
            b = self.pbank()
            nmm = kc * len(shifts)
            m = 0
            for (wap, wkey), s in zip(wl, shifts):
                for c in range(kc):
                    if lhs_fn is None:
                        lo = 1 + tt * 128 + s
                        lh = hT[:, c, lo:lo + 128]
                    else:
                        lh = lhs_fn(c, tt)
                    self.mm(self.bank[b][:, 0:n], lh, wap[:, c, 0:n], m == 0, m == nmm - 1, [wkey, lkey], [self.bk(b)])
                    m += 1
            consume(b, tt)

    def ln_inplace(self, xt, xkey, eps=1e-5):
        sm, junk = self.sm, self.junk
        P = self.P
        P.op("dve", lambda e: e.reduce_sum(out=sm[:, 0:1], in_=xt, axis=AX.X), reads=[xkey], writes=["sm"])
        self.ts("dve", sm[:, 1:2], sm[:, 0:1], -1.0 / D, None, ALU.mult, None, ["sm"], ["sm"])
        self.ts("dve", xt, xt, sm[:, 1:2], None, ALU.add, None, [xkey, "sm"], [xkey])
        self.memset("dve", sm[:, 2:3], 0.0, ["sm"])
        self.act(junk[:], xt, AF.Square, [xkey, "sm"], ["junk", "sm"], accum_out=sm[:, 2:3])
        self.rsqrt_cols(sm[:, 2:3], sm[:, 3:4], 1.0 / D, eps)
        self.stt("dve", xt, xt, sm[:, 3:4], self.lng[:], ALU.mult, ALU.mult, [xkey, "sm", "lng"], [xkey])
        self.tt("dve", xt, xt, self.lnb[:], ALU.add, [xkey, "lnb"], [xkey])

    def to_hT(self, src, skey, tt):
        hT = self.hT
        for half in range(2):
            b = self.pbank()
            for c4 in range(4):
                c = half * 4 + c4
                self.tr(self.bank[b][:, c4 * 128:(c4 + 1) * 128], src[:, c * 128:(c + 1) * 128], [skey], [self.bk(b)])
            self.cp("act" if half else "dve", hT[:, half * 4:half * 4 + 4, 1 + tt * 128:1 + (tt + 1) * 128],
                    self.bank[b][:, :].rearrange("p (c t) -> p c t", t=128), [self.bk(b)], ["hT"])

    def prelude(self):
        P, I = self.P, self.I
        P.dma("sp", self.ident[:], I["ident"], writes=["ident"])
        self.memset("pool", self.onesf[:], 1.0, ["onesf"])
        self.memset("pool", self.hT[:, :, 0:1], 0.0, ["hT"])
        self.memset("pool", self.hT[:, :, S + 1:S + 2], 0.0, ["hT"])
        tmpA = self.tmp[0]
        rb = tmpA[0:32, 0:12]
        P.dma("sp", rb, I["rel_bias"], writes=["tmp0"])
        ohs = self.BIG[0][:].bitcast(F32)
        P.dma("sp", ohs[0:32, 0:4096], I["onehot"], writes=["BIG0"])
        mts = self.BIG[1][:].bitcast(F32)
        P.dma("sp", mts[0:12, 0:4096], I["multab"], writes=["BIG1"])
        xts = self.BIG[2][:].bitcast(F32)
        for j in range(8):
            b = self.pbank()
            self.mm(self.bank[b][0:12, :], rb, ohs[0:32, j * 512:(j + 1) * 512], True, True, ["tmp0", "BIG0"], [self.bk(b)])
            self.act(xts[0:12, j * 512:(j + 1) * 512], self.bank[b][0:12, :], AF.Exp, [self.bk(b)], ["BIG2"])
        self.tt("dve", xts[0:12, 0:4096], xts[0:12, 0:4096], mts[0:12, 0:4096], ALU.mult, ["BIG2", "BIG1"], ["BIG2"])
        P.dma("sp", self.xtab, xts[0:12, 0:4096], reads=["BIG2"], writes=["xtab"])
        self.barrier()
        for mt_ in range(2):
            i = self.nxt("ln", 2)
            P.dma("sp", self.lnx[i][:], I["mem"][mt_ * 128:(mt_ + 1) * 128, :], writes=["lnx%d" % i])
            for half in range(2):
                b = self.pbank()
                for c4 in range(4):
                    c = half * 4 + c4
                    self.tr(self.bank[b][:, c4 * 128:(c4 + 1) * 128], self.lnx[i][:, c * 128:(c + 1) * 128], ["lnx%d" % i], [self.bk(b)])
                self.cp("dve", self.memT[:, half * 4:half * 4 + 4, mt_ * 128:(mt_ + 1) * 128],
                        self.bank[b][:, :].rearrange("p (c t) -> p c t", t=128), [self.bk(b)], ["memT"])
        P.dma("sp", self.lng[:], I["ln_in_g"].partition_broadcast(128), writes=["lng"])
        P.dma("sp", self.lnb[:], I["ln_in_b"].partition_broadcast(128), writes=["lnb"])
        for tt in range(NT):
            i = self.nxt("ln", 2)
            P.dma("sp", self.lnx[i][:], I["x"][tt * 128:(tt + 1) * 128, :], writes=["lnx%d" % i])
            self.ln_inplace(self.lnx[i][:], "lnx%d" % i)
            P.dma("sp", self.hres[0][tt * 128:(tt + 1) * 128, :], self.lnx[i][:], reads=["lnx%d" % i], writes=["hres0"])
        self.barrier()

    def layer(self, l, last):
        P, I = self.P, self.I
        hin = self.hres[l % 2]
        for tt in range(NT):
            i = self.nxt("ln", 2)
            P.dma("sp", self.lnx[i][:], hin[tt * 128:(tt + 1) * 128, :], reads=["hres%d" % (l % 2)], writes=["lnx%d" % i])
            self.to_hT(self.lnx[i], "lnx%d" % i, tt)
        self.win = I["w_in"][l * D:(l + 1) * D, :]
        for mx in "MCADB":
            if mx in self.mixers:
                getattr(self, "mixer_" + mx)(l)
            else:
                self.zero_o(mx)
            self.barrier()
        self.merge(l, last)
        self.barrier()

    def zero_o(self, mx):
        c0, w = {"M": (OM, 256), "C": (OC, 512), "A": (OA, 512), "D": (OD, 512), "B": (OB, 512)}[mx]
        t = self.tmp[2]
        self.memset("pool", t[:, :], 0.0, ["tmp2"])
        for tt in range(NT):
            self.P.dma("sp", self.o_scr[tt * 128:(tt + 1) * 128, c0:c0 + w], t[:, 0:w], reads=["tmp2"], writes=["o_scr"])

    def attn_head(self, maps, vfn, vkey, nkt, dv1, table, band, post):
        nm = len(maps)
        G = self.G

        nqt = 4 if nm == 1 else 2
        QB = nqt * 128

        def accap(m, qt):
            bi = 4 + m * nqt + qt
            return self.bank[bi][:, 0:dv1], bi

        for qb in range(S // QB):
            q0 = qb * QB
            kts = []
            for kt in range(nkt):
                dk = kt * 128 - q0
                if band and (dk - (QB - 1) > 1024 or dk + 127 < -1024):
                    continue
                kts.append(kt)
            steps = [(idx, kt, m) for idx, kt in enumerate(kts) for m in range(nm)]

            def stageA(si):
                idx, kt, m = steps[si]
                q_ap, kfn = maps[m]
                sb_ = si % 4
                self.mm(self.bank[sb_][:, 0:QB], kfn(kt), q_ap[:, q0:q0 + QB], True, True, ["BIG0", "BIG1"], [self.bk(sb_)])

            def stageBC(si):
                idx, kt, m = steps[si]
                sb_ = si % 4
                pti = self.nxt("pt", 4)
                ptile = self.pt[pti]
                if table:
                    pei = self.nxt("pe", 4)
                    self.act(self.pe_[pei][:, 0:QB], self.bank[sb_][:, 0:QB], AF.Exp, [self.bk(sb_)], ["pe%d" % pei], scale=0.125)
                    j0 = kt * 128 - q0 + XC
                    gs = G[:, j0 - (QB - 1):j0 + 1][:, ::-1]
                    self.tt("dve", ptile[:, 0:QB], self.pe_[pei][:, 0:QB], gs, ALU.mult, ["pe%d" % pei, "G"], ["pt%d" % pti])
                else:
                    self.act(ptile[:, 0:QB], self.bank[sb_][:, 0:QB], AF.Exp, [self.bk(sb_)], ["pt%d" % pti], scale=0.125)
                for qt in range(nqt):
                    acc, bi = accap(m, qt)
                    self.mm(acc, ptile[:, qt * 128:(qt + 1) * 128], vfn(kt), idx == 0, idx == len(kts) - 1,
                            ["pt%d" % pti, vkey], [self.bk(bi)])

            PF = 3
            for si in range(min(PF, len(steps))):
                stageA(si)
            for si in range(len(steps)):
                if si + PF < len(steps):
                    stageA(si + PF)
                stageBC(si)
            for qt in range(nqt):
                accs = [accap(m, qt) for m in range(nm)]
                post(qb * nqt + qt, [a for a, _ in accs], [self.bk(bi) for _, bi in accs])

    def post_simple(self, h, hd, ocol):
        def post(tt, accs, keys):
            acc = accs[0]
            sm = self.sm
            i = self.nxt("ost", 3)
            self.P.op("dve", lambda e: e.reciprocal(out=sm[:, 8:9], in_=acc[:, hd:hd + 1]), reads=keys, writes=["sm"])
            self.stt("dve", self.ost[i][:, 0:hd], acc[:, 0:hd], sm[:, 8:9], self.sg[:, tt, h * hd:(h + 1) * hd], ALU.mult, ALU.mult,
                     keys + ["sm", "BIG3"], ["ost%d" % i])
            self.P.dma("sp", self.o_scr[tt * 128:(tt + 1) * 128, ocol + h * hd:ocol + (h + 1) * hd], self.ost[i][:, 0:hd],
                       reads=["ost%d" % i], writes=["o_scr"])
        return post

    def cons_T(self, dst, dkey):
        def consume(b, ct, tb):
            self.cp("dve" if (ct + tb) % 2 else "act", dst[:, ct, tb * 512:(tb + 1) * 512], self.bank[b][:, :], [self.bk(b)], [dkey])
        return consume

    def cons_sg(self, c0, n):
        def consume(b, tt):
            self.act(self.sg[:, tt, c0:c0 + n], self.bank[b][:, 0:n], AF.Silu, [self.bk(b)], ["BIG3"])
        return consume

    def cons_v(self, V1, h0, nh, hd):
        def consume(b, tt):
            self.cp("dve", V1[:, tt, h0:h0 + nh, 0:hd], self.bank[b][:, 0:nh * hd].rearrange("p (h d) -> p h d", d=hd), [self.bk(b)], ["BIG2"])
        return consume

    def mixer_M(self, l):
        P, I = self.P, self.I
        wkv = I["w_mem_kv"][l * D:(l + 1) * D, :]
        V1 = self.BIG[2][:, 0:520].rearrange("p (k h d) -> p k h d", k=2, h=4, d=65)
        self.memset("pool", self.BIG[2][:, 0:520], 1.0, ["BIG2"])
        memT = self.memT
        wl = self.wload(wkv[:, 0:256], 256)
        self.proj_T(wl, 256, lambda b, ct, tb: self.cp("dve", self.kT[:, ct, 0:256], self.bank[b][:, 0:256], [self.bk(b)], ["BIG1"]),
                    rhs_fn=lambda c, tb: memT[:, c, 0:256], rkey="memT", ntb=1, tbw=256)
        wl = self.wload(wkv[:, 256:512], 256)
        self.proj_N(wl, 256, self.cons_v(V1, 0, 4, 64), lhs_fn=lambda c, tt: memT[:, c, tt * 128:(tt + 1) * 128], lkey="memT", ntt=2)
        wl = self.wload(self.win[:, M0:M0 + 256], 256)
        self.proj_T(wl, 256, self.cons_T(self.qT, "BIG0"))
        wl = self.wload(self.win[:, M0 + 256:M0 + 512], 256)
        self.proj_N(wl, 256, self.cons_sg(0, 256))
        if l == 0 and "m" in self.mixers:
            self.dbg("qT", self.BIG[0][:, 0:8192], ["BIG0"], BF16)
            self.dbg("kT", self.BIG[1][:, 0:8192], ["BIG1"], BF16)
            self.dbg("V1", self.BIG[2][:, 0:520], ["BIG2"], BF16)
            self.dbg("sg", self.BIG[3][:, 0:8192], ["BIG3"], BF16)
            self.dbg("hT", self.hT[:, :, :].rearrange("p c t -> p (c t)"), ["hT"], BF16)
        for h in range(4):
            ct, pb = h // 2, (h % 2) * 64
            q_ap = self.qT[pb:pb + 64, ct, :]
            self.attn_head([(q_ap, lambda kt, ct=ct, pb=pb: self.kT[pb:pb + 64, ct, kt * 128:(kt + 1) * 128])],
                           lambda kt, h=h: V1[:, kt, h, :], "BIG2", 2, 65, False, False, self.post_simple(h, 64, OM))

    def mixer_A(self, l):
        P = self.P
        V1 = self.BIG[2][:, 0:8320].rearrange("p (k h d) -> p k h d", k=16, h=8, d=65)
        self.memset("pool", self.BIG[2][:, 0:8320], 1.0, ["BIG2"])
        for j in range(2):
            wl = self.wload(self.win[:, A0 + j * 256:A0 + (j + 1) * 256], 256)
            self.proj_T(wl, 256, lambda b, ct, tb, j=j: self.cons_T(self.qT, "BIG0")(b, ct + 2 * j, tb))
        for j in range(2):
            wl = self.wload(self.win[:, A0 + 512 + j * 256:A0 + 512 + (j + 1) * 256], 256)
            self.proj_T(wl, 256, lambda b, ct, tb, j=j: self.cons_T(self.kT, "BIG1")(b, ct + 2 * j, tb))
        for j in range(2):
            wl = self.wload(self.win[:, A0 + 1024 + j * 256:A0 + 1024 + (j + 1) * 256], 256)
            self.proj_N(wl, 256, self.cons_v(V1, 4 * j, 4, 64))
        for j in range(2):
            wl = self.wload(self.win[:, A0 + 1536 + j * 256:A0 + 1536 + (j + 1) * 256], 256)
            self.proj_N(wl, 256, self.cons_sg(j * 256, 256))
        for h in range(8):
            ct, pb = h // 2, (h % 2) * 64
            P.dma("sp", self.G[:, 0:3968], AP(self.xtab.tensor, h * 4096, [[1, 128], [1, 3968]]), reads=["xtab"], writes=["G"])
            q_ap = self.qT[pb:pb + 64, ct, :]
            self.attn_head([(q_ap, lambda kt, ct=ct, pb=pb: self.kT[pb:pb + 64, ct, kt * 128:(kt + 1) * 128])],
                           lambda kt, h=h: V1[:, kt, h, :], "BIG2", 16, 65, True, True, self.post_simple(h, 64, OA))

    def normrope(self, b, tt, nh, gcol):
        n = nh * 64
        sm = self.sm
        t0, t1, t2 = self.tmp
        ps = self.bank[b][:, 0:n]
        self.act(t0[:, 0:n], ps, AF.Square, [self.bk(b)], ["tmp0"])
        self.P.op("dve", lambda e: e.reduce_sum(out=sm[:, 16:16 + nh], in_=t0[:, 0:n].rearrange("p (h d) -> p h d", d=64), axis=AX.X),
                  reads=["tmp0"], writes=["sm"])
        self.rsqrt_cols(sm[:, 16:16 + nh], sm[:, 24:24 + nh], 1.0 / 64, 1e-6)
        v3 = lambda ap: ap.rearrange("p (h d) -> p h d", d=64)
        self.tt("dve", v3(t0[:, 0:n]), v3(ps), sm[:, 24:24 + nh].unsqueeze(2).broadcast_to([128, nh, 64]), ALU.mult,
                [self.bk(b), "sm"], ["tmp0"])
        self.tt("dve", v3(t0[:, 0:n]), v3(t0[:, 0:n]), self.gq[:, gcol:gcol + 64].unsqueeze(1).broadcast_to([128, nh, 64]), ALU.mult,
                ["tmp0", "gq"], ["tmp0"])
        self.P.dma("sp", self.ropet[:, 0:64], self.I["ropec"][tt * 128:(tt + 1) * 128, :], writes=["ropet"])
        self.P.dma("sp", self.ropet[:, 64:128], self.I["ropes"][tt * 128:(tt + 1) * 128, :], writes=["ropet"])
        self.tt("pool", v3(t1[:, 0:n]), v3(t0[:, 0:n]), self.ropet[:, 0:64].unsqueeze(1).broadcast_to([128, nh, 64]), ALU.mult,
                ["tmp0", "ropet"], ["tmp1"])
        v5 = lambda ap: ap.rearrange("p (h a b c) -> p h a b c", a=2, b=2, c=16)
        rs = self.ropet[:, 64:128].rearrange("p (a b c) -> p a b c", a=2, b=2, c=16)
        for bb in range(2):
            self.tt("dve", v5(t2[:, 0:n])[:, :, :, bb, :], v5(t0[:, 0:n])[:, :, :, 1 - bb, :],
                    rs[:, :, bb, :].unsqueeze(1).broadcast_to([128, nh, 2, 16]), ALU.mult, ["tmp0", "ropet"], ["tmp2"])
        self.tt("dve", t1[:, 0:n], t1[:, 0:n], t2[:, 0:n], ALU.add, ["tmp1", "tmp2"], ["tmp1"])

    def mixer_C(self, l):
        P, I = self.P, self.I
        V1 = self.BIG[2][:, 0:2080].rearrange("p (k h d) -> p k h d", k=16, h=2, d=65)
        self.memset("pool", self.BIG[2][:, 0:2080], 1.0, ["BIG2"])
        P.dma("sp", self.gq[:, 0:64], I["c_qnorm_g"][l:l + 1, :].partition_broadcast(128), writes=["gq"])
        P.dma("sp", self.gq[:, 64:128], I["c_knorm_g"][l:l + 1, :].partition_broadcast(128), writes=["gq"])
        t1 = self.tmp[1]
        for j in range(2):
            wl = self.wload(self.win[:, C0 + j * 256:C0 + (j + 1) * 256], 256)

            def cons_q(b, tt, j=j):
                self.normrope(b, tt, 4, 0)
                b2 = self.pbank()
                for u in range(2):
                    self.tr(self.bank[b2][:, u * 128:(u + 1) * 128], t1[:, u * 128:(u + 1) * 128], ["tmp1"], [self.bk(b2)])
                self.cp("act", self.qT[:, 2 * j:2 * j + 2, tt * 128:(tt + 1) * 128],
                        self.bank[b2][:, 0:256].rearrange("p (c t) -> p c t", t=128), [self.bk(b2)], ["BIG0"])
            self.proj_N(wl, 256, cons_q)
        wl = self.wload(self.win[:, C0 + 512:C0 + 768], 256)

        def cons_kv(b, tt):
            self.cp("act", V1[:, tt, :, 0:64], self.bank[b][:, 128:256].rearrange("p (h d) -> p h d", d=64), [self.bk(b)], ["BIG2"])
            self.normrope(b, tt, 2, 64)
            t2 = self.tmp[2]
            self.cp("dve", t2[:, 0:256].rearrange("p (g r d) -> p g r d", g=2, r=2, d=64),
                    t1[:, 0:128].rearrange("p (g d) -> p g d", d=64).unsqueeze(2).broadcast_to([128, 2, 2, 64]), ["tmp1"], ["tmp2"])
            b2 = self.pbank()
            for u in range(2):
                self.tr(self.bank[b2][:, u * 128:(u + 1) * 128], t2[:, u * 128:(u + 1) * 128], ["tmp2"], [self.bk(b2)])
            self.cp("act", self.kT[:, 0:2, tt * 128:(tt + 1) * 128],
                    self.bank[b2][:, 0:256].rearrange("p (c t) -> p c t", t=128), [self.bk(b2)], ["BIG1"])
        self.proj_N(wl, 256, cons_kv)
        for j in range(2):
            wl = self.wload(self.win[:, C0 + 768 + j * 256:C0 + 768 + (j + 1) * 256], 256)
            self.proj_N(wl, 256, self.cons_sg(j * 256, 256))
        for h in range(8):
            ct, pb = h // 2, (h % 2) * 64
            g = h // 4
            q_ap = self.qT[pb:pb + 64, ct, :]
            self.attn_head([(q_ap, lambda kt, g=g, pb=pb: self.kT[pb:pb + 64, g, kt * 128:(kt + 1) * 128])],
                           lambda kt, g=g: V1[:, kt, g, :], "BIG2", 16, 65, False, False, self.post_simple(h, 64, OC))

    def mixer_D(self, l):
        P, I = self.P, self.I
        V1 = self.BIG[2][:, 0:8256].rearrange("p (k h d) -> p k h d", k=16, h=4, d=129)
        self.memset("pool", self.BIG[2][:, 0:8256], 1.0, ["BIG2"])
        lam_init = 0.8 - 0.6 * math.exp(-0.3 * l)
        lamt, sm = self.lamt, self.sm
        P.dma("sp", lamt[:, 0:256], I["d_lambda"][l:l + 1, :].partition_broadcast(128), writes=["lamt"])
        P.dma("sp", self.subg[:], I["d_subln_g"][l:l + 1, :].partition_broadcast(128), writes=["subg"])
        self.ts("pool", self.subg[:], self.subg[:], 1.0 - lam_init, None, ALU.mult, None, ["subg"], ["subg"])
        lv = lamt[:, 0:256].rearrange("p (a b c) -> p a b c", a=2, b=2, c=64)
        lp = self.tmp[2][:, 0:128].rearrange("p (a c) -> p a c", c=64)
        self.tt("dve", lp, lv[:, :, 0, :], lv[:, :, 1, :], ALU.mult, ["lamt"], ["tmp2"])
        P.op("dve", lambda e: e.reduce_sum(out=lamt[:, 256:258], in_=lp, axis=AX.X), reads=["tmp2"], writes=["lamt"])
        self.act(lamt[:, 258:260], lamt[:, 256:258], AF.Exp, ["lamt"], ["lamt"])
        self.tt("dve", lamt[:, 260:261], lamt[:, 259:260], lamt[:, 258:259], ALU.subtract, ["lamt"], ["lamt"])
        self.ts("dve", lamt[:, 260:261], lamt[:, 260:261], -lam_init, None, ALU.add, None, ["lamt"], ["lamt"])
        for j in range(2):
            wl = self.wload(self.win[:, D0 + j * 256:D0 + (j + 1) * 256], 256)
            self.proj_T(wl, 256, lambda b, ct, tb, j=j: self.cons_T(self.qT, "BIG0")(b, ct + 2 * j, tb))
        for j in range(2):
            wl = self.wload(self.win[:, D0 + 512 + j * 256:D0 + 512 + (j + 1) * 256], 256)
            self.proj_T(wl, 256, lambda b, ct, tb, j=j: self.cons_T(self.kT, "BIG1")(b, ct + 2 * j, tb))
        for j in range(2):
            wl = self.wload(self.win[:, D0 + 1024 + j * 256:D0 + 1024 + (j + 1) * 256], 256)
            self.proj_N(wl, 256, self.cons_v(V1, 2 * j, 2, 128))
        for j in range(2):
            wl = self.wload(self.win[:, D0 + 1536 + j * 256:D0 + 1536 + (j + 1) * 256], 256)
            self.proj_N(wl, 256, self.cons_sg(j * 256, 256))
        t0 = self.tmp[0]
        for h in range(4):
            P.dma("sp", self.G[:, 0:3968], AP(self.xtab.tensor, (8 + h) * 4096, [[1, 128], [1, 3968]]), reads=["xtab"], writes=["G"])

            def post(tt, accs, keys, h=h):
                a1, a2 = accs
                i = self.nxt("ost", 3)
                P.op("dve", lambda e: e.reciprocal(out=sm[:, 8:9], in_=a1[:, 128:129]), reads=keys, writes=["sm"])
                P.op("dve", lambda e: e.reciprocal(out=sm[:, 9:10], in_=a2[:, 128:129]), reads=keys, writes=["sm"])
                self.tt("dve", sm[:, 9:10], sm[:, 9:10], lamt[:, 260:261], ALU.mult, ["sm", "lamt"], ["sm"])
                self.ts("dve", t0[:, 0:128], a1[:, 0:128], sm[:, 8:9], None, ALU.mult, None, keys + ["sm"], ["tmp0"])
                self.stt("dve", t0[:, 128:256], a2[:, 0:128], sm[:, 9:10], t0[:, 0:128], ALU.mult, ALU.add, keys + ["sm", "tmp0"], ["tmp0"])
                self.memset("dve", sm[:, 10:11], 0.0, ["sm"])
                self.act(t0[:, 256:384], t0[:, 128:256], AF.Square, ["tmp0", "sm"], ["tmp0", "sm"], accum_out=sm[:, 10:11])
                self.rsqrt_cols(sm[:, 10:11], sm[:, 11:12], 1.0 / 128, 1e-5)
                self.stt("dve", t0[:, 128:256], t0[:, 128:256], sm[:, 11:12], self.subg[:], ALU.mult, ALU.mult, ["tmp0", "sm", "subg"], ["tmp0"])
                self.tt("dve", self.ost[i][:], t0[:, 128:256], self.sg[:, tt, h * 128:(h + 1) * 128], ALU.mult, ["tmp0", "BIG3"], ["ost%d" % i])
                P.dma("sp", self.o_scr[tt * 128:(tt + 1) * 128, OD + h * 128:OD + (h + 1) * 128], self.ost[i][:],
                      reads=["ost%d" % i], writes=["o_scr"])
            maps = [(self.qT[c * 64:(c + 1) * 64, h, :], (lambda kt, c=c, h=h: self.kT[c * 64:(c + 1) * 64, h, kt * 128:(kt + 1) * 128]))
                    for c in range(2)]
            self.attn_head(maps, lambda kt, h=h: V1[:, kt, h, :], "BIG2", 16, 129, True, False, post)

    def dbg(self, name, ap, reads, dt=F32):
        if not self.debug:
            return
        t = self.nc.dram_tensor("dbg_" + name, list(ap.shape), dt, kind="ExternalOutput").ap()
        self.P.dma("sp", t, ap, reads=reads, is_output=True)
        self.dbg_outs.append("dbg_" + name)

    def mixer_B(self, l):
        P, I = self.P, self.I
        CW = 0.6065306597126334
        t_ring = self.tmp
        mub = self.lnx[0][:, 0:768].rearrange("p (v n) -> p v n", n=256)

        def load_mu(c0):
            for v in range(2):
                P.dma("sp", mub[:, 1 + v, :], I["shift_mu"][l * 2 + v:l * 2 + v + 1, c0:c0 + 256].partition_broadcast(128), writes=["lnx0"])
            self.tt("dve", mub[:, 0, :], mub[:, 1, :], mub[:, 2, :], ALU.add, ["lnx0"], ["lnx0"])
            self.ts("dve", mub[:, 0, :], mub[:, 0, :], -1.0, 1.0, ALU.mult, ALU.add, ["lnx0"], ["lnx0"])
            return [(mub[:, 0, :], "lnx0"), (mub[:, 1, :], "lnx0"), (mub[:, 2, :], "lnx0")]

        def stage_out(dst_ap, dkey, func=None):
            def f(b, n_part=128, ncol=512):
                i = self.nxt("tmp", 3)
                if func is None:
                    self.cp("dve", t_ring[i][0:n_part, 0:ncol], self.bank[b][0:n_part, 0:ncol], [self.bk(b)], ["tmp%d" % i])
                else:
                    self.act(t_ring[i][0:n_part, 0:ncol], self.bank[b][0:n_part, 0:ncol], func, [self.bk(b)], ["tmp%d" % i])
                P.dma("sp", dst_ap, t_ring[i][0:n_part, 0:ncol], reads=["tmp%d" % i], writes=[dkey])
            return f

        for j in range(4):
            c0 = j * 256
            wl = self.wload(self.win[:, B0 + c0:B0 + c0 + 256], 256, variants=load_mu(c0))
            self.proj_T(wl, 256, lambda b, ct, tb, c0=c0: stage_out(self.rk_scr[c0 + ct * 128:c0 + (ct + 1) * 128, tb * 512:(tb + 1) * 512], "rk_scr")(b),
                        shifts=(0, -1, 1))
        for j in range(2):
            c0 = 1024 + j * 256
            wl = self.wload(self.win[:, B0 + c0:B0 + c0 + 256], 256, variants=load_mu(c0))
            self.proj_N(wl, 256, lambda b, tt, j=j: stage_out(self.v_scr[tt * 128:(tt + 1) * 128, j * 256:(j + 1) * 256], "v_scr")(b, 128, 256),
                        shifts=(0, -1, 1))
        wl = self.wload(self.win[:, B0 + 1536:B0 + 1792], 256, variants=load_mu(1536))
        self.proj_T(wl, 256, lambda b, ct, tb: stage_out(self.wa_scr[ct * 128:(ct + 1) * 128, tb * 512:(tb + 1) * 512], "wa_scr",
                                                         AF.Tanh if ct == 0 else AF.Copy)(b), shifts=(0, -1, 1))
        for j in range(2):
            wl = self.wload(self.win[:, B0 + 1792 + j * 256:B0 + 1792 + (j + 1) * 256], 256)
            self.proj_N(wl, 256, lambda b, tt, j=j: stage_out(self.sg_scr[tt * 128:(tt + 1) * 128, j * 256:(j + 1) * 256], "sg_scr", AF.Silu)(b, 128, 256))
        self.barrier()
        slots = []
        for bi in range(4):
            a = self.BIG[bi][:].bitcast(F32)
            for q in range(4):
                slots.append(a[:, q * 1024:(q + 1) * 1024])
        for q in range(4):
            slots.append(self.G[:, q * 1024:(q + 1) * 1024])
        for wi in range(1):
            a = self.wst[wi][:, :, :].rearrange("p c n -> p (c n)")
            for q in range(2):
                slots.append(a[:, q * 1024:(q + 1) * 1024])
        si = [0]

        def slot(full=True):
            if full:
                if si[0] % 2:
                    si[0] += 1
                a = slots[si[0] // 2]
                si[0] += 2
                return a
            a = slots[si[0] // 2][:, (si[0] % 2) * 512:(si[0] % 2) * 512 + 512]
            si[0] += 1
            return a

        def v3(ap, w):
            return ap[0:64, 0:8 * w].rearrange("p (h t) -> p h t", t=w)

        w_upS = slot()[0:64, :].rearrange("p (e c) -> p e c", c=512)
        a_upS = slot()[0:64, :].rearrange("p (e c) -> p e c", c=512)
        w0B = slot()[0:64, :].rearrange("p (e c) -> p e c", c=512)
        rkT = slot()[0:64, :].rearrange("p (g t) -> p g t", t=64)
        AR = self.RX[:, 3072:4096].rearrange("p (h t) -> p h t", t=128)
        NP = [self.RX[:, q * 1024:(q + 1) * 1024].rearrange("p (h t) -> p h t", t=128) for q in range(2)]
        ysb = slot()[0:64, 0:520]
        rmaskS = slot(False)[0:64, 0:384].rearrange("p (e n) -> p e n", n=192)
        waT = slot(False)[0:64, 0:256].rearrange("p (g t) -> p g t", t=64)
        vtok = slot(False)[0:64, :]
        sgw = slot(False)[0:64, :]
        asT, kkn, ke, be, tE0, tE1, z = [v3(slot(False), 64) for _ in range(7)]
        bch = self.RX[:, 4096:4608].rearrange("p (h t) -> p h t", t=64)
        kch = self.RX[:, 4608:5120].rearrange("p (h t) -> p h t", t=64)
        eLs, Bt, Kt = [slot(False)[0:64, :] for _ in range(3)]
        Mm = [self.RX[:, 2048 + q * 512:2048 + (q + 1) * 512].rearrange("p (h t) -> p h t", t=64) for q in range(2)]
        Mrb, Mak, Mrk, Xs, Us, tmpS = [v3(slot(False), 64) for _ in range(6)]
        Sst = [v3(slot(False), 64) for _ in range(2)]
        hb16 = slot(False).bitcast(BF16)
        w_upSb = hb16[0:64, 0:1024].rearrange("p (e c) -> p e c", c=512)
        hb16b = slot(False).bitcast(BF16)
        a_upSb = hb16b[0:64, 0:1024].rearrange("p (e c) -> p e c", c=512)
        hb16c = slot(False).bitcast(BF16)
        waTb = hb16c[0:64, 0:256].rearrange("p (g t) -> p g t", t=64)
        sqb = hb16c[0:64, 256:768].rearrange("p (h t) -> p h t", t=64)
        onesb = hb16c[0:64, 768:832]
        assert si[0] <= 2 * len(slots), si[0]
        rwp = self.gq[0:64, 0:40]
        omka = self.gq[0:64, 40:48]
        ident64 = self.ident[0:64, 0:64]
        ones64 = self.onesf[0:64, 0:64]
        self.r32 = True
        P.dma("sp", w_upS, I["rwkv_w_up"][l * 128:(l + 1) * 128, :].rearrange("(e r) c -> r e c", r=64), writes=["w_upS"])
        P.dma("sp", a_upS, I["rwkv_a_up"][l * 128:(l + 1) * 128, :].rearrange("(e r) c -> r e c", r=64), writes=["a_upS"])
        for e in range(2):
            P.dma("sp", w0B[:, e, :], I["rwkv_w0"][l * 2 + e:l * 2 + e + 1, :].partition_broadcast(64), writes=["w0B"])
        P.dma("sp", rmaskS, I["rmask"].rearrange("(e p) n -> p e n", p=64), writes=["rmaskS"])
        self.cp("dve", w_upSb, w_upS, ["w_upS"], ["w_upSb"])
        self.cp("act", a_upSb, a_upS, ["a_upS"], ["a_upSb"])
        self.memset("dve", onesb, 1.0, ["onesb"])

        pm = self.tmp[0]
        P.dma("sp", pm[0:16, 0:64], I["rwkv_a0"][l * 2:(l + 1) * 2, :].rearrange("e (h c) -> (e h) c", c=64), writes=["tmp0"])
        P.dma("sp", pm[16:24, 0:64], I["rwkv_k_k"][l:l + 1, :].rearrange("e (h c) -> (e h) c", c=64), writes=["tmp0"])
        P.dma("sp", pm[24:32, 0:64], I["rwkv_k_a"][l:l + 1, :].rearrange("e (h c) -> (e h) c", c=64), writes=["tmp0"])
        P.dma("sp", pm[32:40, 0:64], I["rwkv_r_k"][l:l + 1, :].rearrange("e (h c) -> (e h) c", c=64), writes=["tmp0"])
        b = self.pbank()
        self.P.op("pe", lambda e_: e_.transpose(self.bank[b][0:64, 0:40], pm[0:40, 0:64], self.ident[0:40, 0:40]), reads=["tmp0", "ident"], writes=[self.bk(b)])
        self.cp("dve", rwp, self.bank[b][0:64, 0:40], [self.bk(b)], ["gq"])
        self.ts("dve", omka, rwp[:, 24:32], -1.0, 1.0, ALU.mult, ALU.add, ["gq"], ["gq"])
        bc3 = lambda ap: ap.unsqueeze(2).broadcast_to([64, 8, 64])
        hb = lambda b_, h, w=64: self.bank[b_][0:64, h * w:(h + 1) * w]
        b3 = lambda b_, w=64: self.bank[b_][0:64, 0:8 * w].rearrange("p (h t) -> p h t", t=w)

        for e in range(2):
            Scur = 0
            self.memset("dve", Sst[0], 0.0, ["S0"])
            order = range(32) if e == 0 else range(31, -1, -1)
            tl = 63 if e == 0 else 0
            mS, mI, mT = rmaskS[:, e, 0:64], rmaskS[:, e, 64:128], rmaskS[:, e, 128:192]
            for ch in order:
                t0 = ch * 64
                P.dma("sp", rkT, self.rk_scr.rearrange("(g p) t -> p g t", p=64)[:, :, t0:t0 + 64], reads=["rk_scr"], writes=["rkT"])
                P.dma("sp", waT, self.wa_scr.rearrange("(g p) t -> p g t", p=64)[:, :, t0:t0 + 64], reads=["wa_scr"], writes=["waT"])
                P.dma("sp", vtok, self.v_scr[t0:t0 + 64, :], reads=["v_scr"], writes=["vtok"])
                rT, kT_ = rkT[:, 0:8, :], rkT[:, 8:16, :]
                b = self.pbank()
                self.cp("dve", waTb, waT, ["waT"], ["waTb"])
                self.mm(self.bank[b][0:64, :], waTb[:, e, :], w_upSb[:, e, :], True, True, ["waTb", "w_upSb"], [self.bk(b)])
                self.tt("dve", sgw, self.bank[b][0:64, :], w0B[:, e, :], ALU.add, [self.bk(b), "w0B"], ["sgw"])
                self.act(sgw, sgw, AF.Sigmoid, ["sgw"], ["sgw"])
                b = self.pbank()
                for h in range(8):
                    self.mm(hb(b, h), a_upSb[:, e, h * 64:(h + 1) * 64], waTb[:, 2 + e, :], True, True, ["waTb", "a_upSb"], [self.bk(b)])
                self.tt("dve", asT, b3(b), bc3(rwp[:, e * 8:(e + 1) * 8]), ALU.add, [self.bk(b), "gq"], ["asT"])
                self.act(asT, asT, AF.Sigmoid, ["asT"], ["asT"])
                self.tt("dve", kkn, kT_, bc3(rwp[:, 16:24]), ALU.mult, ["rkT", "gq"], ["kkn"])
                self.act(sqb, kkn, AF.Square, ["kkn"], ["sqb"])
                b = self.pbank()
                self.mm(self.bank[b][0:64, :], onesb, sqb.rearrange("p h t -> p (h t)"), True, True, ["sqb", "onesb"], [self.bk(b)])
                self.act(tE0, b3(b), AF.Sqrt, [self.bk(b)], ["tE0"])
                self.ts("dve", tE0, tE0, 1e-12, None, ALU.max, None, ["tE0"], ["tE0"])
                self.P.op("dve", lambda e_: e_.reciprocal(out=tE0, in_=tE0), reads=["tE0"], writes=["tE0"])
                self.stt("dve", kkn, kkn, -1.0, tE0, ALU.mult, ALU.mult, ["kkn", "tE0"], ["kkn"])
                self.tt("pool", ke, asT, bc3(rwp[:, 24:32]), ALU.mult, ["asT", "gq"], ["ke"])
                self.tt("pool", ke, ke, bc3(omka), ALU.add, ["ke", "gq"], ["ke"])
                self.tt("pool", ke, ke, kT_, ALU.mult, ["ke", "rkT"], ["ke"])
                self.stt("dve", be, kkn, -1.0, asT, ALU.mult, ALU.mult, ["kkn", "asT"], ["be"])
                self.tt("pool", z, rT, ke, ALU.mult, ["rkT", "ke"], ["z"])
                bLi = self.pbank()
                for h in range(8):
                    self.mm(hb(bLi, h), sgw[:, h * 64:(h + 1) * 64], mI, True, True, ["sgw", "rmaskS"], [self.bk(bLi)])
                self.act(tE0, b3(bLi), AF.Exp, [self.bk(bLi)], ["tE0"], scale=-CW)
                self.act(tE1, b3(bLi), AF.Exp, [self.bk(bLi)], ["tE1"], scale=CW)
                self.tt("dve", AR[:, :, 64:128], rT, tE0, ALU.mult, ["rkT", "tE0"], ["AR"])
                self.cp("dve", self.sm[0:64, 32:40], tE0[:, :, tl], ["tE0"], ["sm"])
                self.tt("dve", bch, be, tE1, ALU.mult, ["be", "tE1"], ["bch"])
                self.tt("pool", kch, ke, tE1, ALU.mult, ["ke", "tE1"], ["kch"])
                bLe = self.pbank()
                for h in range(8):
                    self.mm(hb(bLe, h), sgw[:, h * 64:(h + 1) * 64], mS, True, True, ["sgw", "rmaskS"], [self.bk(bLe)])
                self.act(tE0, b3(bLe), AF.Exp, [self.bk(bLe)], ["tE0"], scale=-CW)
                self.tt("dve", AR[:, :, 0:64], kkn, tE0, ALU.mult, ["kkn", "tE0"], ["AR"])
                b = self.pbank()
                self.mm(self.bank[b][0:64, :], mT, sgw, True, True, ["sgw", "rmaskS"], [self.bk(b)])
                self.act(eLs, self.bank[b][0:64, :], AF.Exp, [self.bk(b)], ["eLs"], scale=-CW)
                for src, skey, dst, dkey in ((be, "be", Bt, "Bt"), (ke, "ke", Kt, "Kt")):
                    b = self.pbank()
                    for h in range(8):
                        self.P.op("pe", lambda e_, b=b, h=h, src=src: e_.transpose(hb(b, h), src[:, h, :], ident64), reads=[skey, "ident"], writes=[self.bk(b)])
                    self.tt("dve", dst, self.bank[b][0:64, :], eLs, ALU.mult, [self.bk(b), "eLs"], [dkey])
                b = self.pbank()
                for h in range(8):
                    self.mm(self.bank[b][0:64, h:h + 1], z[:, h, :], rwp[:, 32 + h:33 + h], True, True, ["z", "gq"], [self.bk(b)])
                self.cp("act", ysb[:, 512:520], self.bank[b][0:64, 0:8], [self.bk(b)], ["ysb"])
                for h in range(8):
                    self.mm(self.bank[h // 4][0:64, (h % 4) * 128:(h % 4 + 1) * 128], bch[:, h, :], AR[:, h, :], True, True, ["bch", "AR"], [self.bk(h // 4)])
                for h in range(8):
                    self.mm(self.bank[4 + h // 4][0:64, (h % 4) * 128:(h % 4 + 1) * 128], kch[:, h, :], AR[:, h, :], True, True, ["kch", "AR"], [self.bk(4 + h // 4)])
                for h in range(8):
                    self.mm(hb(6, h), AR[:, h, 0:64], bch[:, h, :], True, True, ["bch", "AR"], [self.bk(6)])
                m4 = lambda m_: m_.unsqueeze(1).broadcast_to([64, 4, 64])
                for g in range(2):
                    bb = self.bank[g][0:64, :].rearrange("p (h t) -> p h t", t=128)
                    kb = self.bank[4 + g][0:64, :].rearrange("p (h t) -> p h t", t=128)
                    self.tt("dve", NP[0][:, 4 * g:4 * g + 4, 0:64], bb[:, :, 0:64], m4(mS), ALU.mult, [self.bk(g), "rmaskS"], ["NP0"])
                    self.tt("dve", Mrb[:, 4 * g:4 * g + 4, :], bb[:, :, 64:128], m4(mI), ALU.mult, [self.bk(g), "rmaskS"], ["Mrb"])
                    self.tt("dve", Mak[:, 4 * g:4 * g + 4, :], kb[:, :, 0:64], m4(mS), ALU.mult, [self.bk(4 + g), "rmaskS"], ["Mak"])
                    self.tt("dve", Mrk[:, 4 * g:4 * g + 4, :], kb[:, :, 64:128], m4(mI), ALU.mult, [self.bk(4 + g), "rmaskS"], ["Mrk"])
                self.tt("dve", Mm[0], b3(6), mT.unsqueeze(1).broadcast_to([64, 8, 64]), ALU.mult, [self.bk(6), "rmaskS"], ["Mm0"])
                self.tt("pool", NP[0][:, :, 64:128], NP[0][:, :, 0:64], ident64.unsqueeze(1).broadcast_to([64, 8, 64]), ALU.add, ["NP0", "ident"], ["NP0"])
                cur = 0
                for step in range(6):
                    nx = 1 - cur
                    pbk = (0, 1) if step % 2 == 0 else (4, 5)
                    mbk = 6 if step % 2 else 7
                    for g in range(2):
                        ncur, mcur = "NP%d/%d" % (cur, g), "Mm%d/%d" % (cur, g)
                        for h in range(4 * g, 4 * g + 4):
                            hh = h % 4
                            if step == 0:
                                o_, r_ = self.bank[pbk[g]][0:64, hh * 128:hh * 128 + 64], NP[cur][:, h, 0:64]
                            elif step < 5:
                                o_, r_ = self.bank[pbk[g]][0:64, hh * 128:(hh + 1) * 128], NP[cur][:, h, :]
                            else:
                                o_, r_ = self.bank[pbk[g]][0:64, hh * 128 + 64:(hh + 1) * 128], NP[cur][:, h, 64:128]
                            self.mm(o_, Mm[cur][:, h, :], r_, True, True, [mcur, ncur], [self.bk(pbk[g])])
                        if step < 5:
                            for h in range(4 * g, 4 * g + 4):
                                self.mm(self.bank[6 + g][0:64, (h % 4) * 64:(h % 4 + 1) * 64], NP[cur][:, h, 0:64], Mm[cur][:, h, :], True, True, [mcur, ncur], [self.bk(6 + g)])
                    for g in range(2):
                        ncur, nnx, mnx = "NP%d/%d" % (cur, g), "NP%d/%d" % (nx, g), "Mm%d/%d" % (nx, g)
                        pv = self.bank[pbk[g]][0:64, :].rearrange("p (h t) -> p h t", t=128)
                        if step < 5:
                            self.cp("act", NP[nx][:, 4 * g:4 * g + 4, 0:64], pv[:, :, 0:64], [self.bk(pbk[g])], [nnx])
                        if step == 0:
                            self.cp("dve", NP[nx][:, 4 * g:4 * g + 4, 64:128], NP[cur][:, 4 * g:4 * g + 4, 64:128], [ncur], [nnx])
                        else:
                            self.tt("dve", NP[nx][:, 4 * g:4 * g + 4, 64:128], pv[:, :, 64:128], NP[cur][:, 4 * g:4 * g + 4, 64:128], ALU.add,
                                    [self.bk(pbk[g]), ncur], [nnx])
                        if step < 5:
                            self.cp("act" if g else "dve", Mm[nx][:, 4 * g:4 * g + 4, :], self.bank[6 + g][0:64, 0:256].rearrange("p (h t) -> p h t", t=64), [self.bk(6 + g)], [mnx])
                    cur = nx
                TT = NP[cur]
                S0, skey = Sst[Scur], "S%d" % Scur
                S1, s1key = Sst[1 - Scur], "S%d" % (1 - Scur)
                bA, bB = (2, 0), (3, 1)
                G2 = range(2)
                hs = lambda g: range(4 * g, 4 * g + 4)
                gs = lambda ap, g: ap[:, 4 * g:4 * g + 4, :]
                hq = lambda b_, h: self.bank[b_][0:64, (h % 4) * 64:(h % 4 + 1) * 64]
                q3 = lambda b_: self.bank[b_][0:64, 0:256].rearrange("p (h t) -> p h t", t=64)
                for g in G2:
                    for h in hs(g):
                        self.mm(hq(bA[g], h), AR[:, h, 0:64], S0[:, h, :], True, False, ["AR", skey + "/%d" % g], [self.bk(bA[g])])
                        self.mm(hq(bA[g], h), Mak[:, h, :], vtok[:, h * 64:(h + 1) * 64], False, True, ["Mak", "vtok"], [self.bk(bA[g])])
                for g in G2:
                    self.cp("dve" if g else "act", gs(Xs, g), q3(bA[g]), [self.bk(bA[g])], ["Xs/%d" % g])
                for g in G2:
                    for h in hs(g):
                        self.mm(hq(bA[g], h), TT[:, h, 64:128], Xs[:, h, :], True, True, ["NP%d/%d" % (cur, g), "Xs/%d" % g], [self.bk(bA[g])])
                for g in G2:
                    self.cp("act" if g else "dve", gs(Us, g), q3(bA[g]), [self.bk(bA[g])], ["Us/%d" % g])
                for g in G2:
                    for h in hs(g):
                        self.mm(hq(bA[g], h), AR[:, h, 64:128], S0[:, h, :], True, False, ["AR", skey + "/%d" % g], [self.bk(bA[g])])
                        self.mm(hq(bA[g], h), Mrb[:, h, :], Us[:, h, :], False, False, ["Mrb", "Us/%d" % g], [self.bk(bA[g])])
                        self.mm(hq(bA[g], h), Mrk[:, h, :], vtok[:, h * 64:(h + 1) * 64], False, True, ["Mrk", "vtok"], [self.bk(bA[g])])
                    for h in hs(g):
                        self.mm(hq(bB[g], h), Bt[:, h * 64:(h + 1) * 64], Us[:, h, :], True, False, ["Bt", "Us/%d" % g], [self.bk(bB[g])])
                        self.mm(hq(bB[g], h), Kt[:, h * 64:(h + 1) * 64], vtok[:, h * 64:(h + 1) * 64], False, True, ["Kt", "vtok"], [self.bk(bB[g])])
                for g in G2:
                    self.cp("act", ysb[:, g * 256:(g + 1) * 256], self.bank[bA[g]][0:64, 0:256], [self.bk(bA[g])], ["ysb/%d" % g])
                    self.tt("pool", gs(tmpS, g), gs(S0, g), bc3(self.sm[0:64, 32:40])[:, 4 * g:4 * g + 4, :], ALU.mult, [skey + "/%d" % g, "sm"], ["tmpS/%d" % g])
                    self.tt("dve", gs(S1, g), gs(tmpS, g), q3(bB[g]), ALU.add, ["tmpS/%d" % g, self.bk(bB[g])], [s1key + "/%d" % g])
                P.dma("sp", self.y_scr[e * S + t0:e * S + t0 + 64, :], ysb, reads=["ysb"], writes=["y_scr"])
                Scur = 1 - Scur
        self.r32 = False
        self.barrier()
        P.dma("sp", self.lng[:, 0:512], I["rwkv_ln_g"][l:l + 1, :].partition_broadcast(128), writes=["lng"])
        P.dma("sp", self.lnb[:, 0:512], I["rwkv_ln_b"][l:l + 1, :].partition_broadcast(128), writes=["lnb"])
        yf, yb, vt = self.lnx[0], self.lnx[1], self.junk
        sm = self.sm
        t0_, t1_, t2_ = self.tmp
        for tt in range(NT):
            P.dma("sp", yf[:, 0:520], self.y_scr[tt * 128:(tt + 1) * 128, :], reads=["y_scr"], writes=["lnx0"])
            P.dma("sp", yb[:, 0:520], self.y_scr[S + tt * 128:S + (tt + 1) * 128, :], reads=["y_scr"], writes=["lnx1"])
            P.dma("sp", vt[:, 0:512], self.v_scr[tt * 128:(tt + 1) * 128, :], reads=["v_scr"], writes=["junk"])
            P.dma("sp", vt[:, 512:1024], self.sg_scr[tt * 128:(tt + 1) * 128, :], reads=["sg_scr"], writes=["junk"])
            self.tt("dve", yf[:, 0:520], yf[:, 0:520], yb[:, 0:520], ALU.add, ["lnx0", "lnx1"], ["lnx0"])
            y3 = yf[:, 0:512].rearrange("p (h d) -> p h d", d=64)
            P.op("dve", lambda e_, y3=y3: e_.reduce_sum(out=sm[:, 40:48], in_=y3, axis=AX.X), reads=["lnx0"], writes=["sm"])
            self.ts("dve", sm[:, 40:48], sm[:, 40:48], -1.0 / 64, None, ALU.mult, None, ["sm"], ["sm"])
            self.tt("dve", y3, y3, sm[:, 40:48].unsqueeze(2).broadcast_to([128, 8, 64]), ALU.add, ["lnx0", "sm"], ["lnx0"])
            self.act(t0_[:, 0:512], yf[:, 0:512], AF.Square, ["lnx0"], ["tmp0"])
            P.op("dve", lambda e_: e_.reduce_sum(out=sm[:, 48:56], in_=t0_[:, 0:512].rearrange("p (h d) -> p h d", d=64), axis=AX.X), reads=["tmp0"], writes=["sm"])
            self.rsqrt_cols(sm[:, 48:56], sm[:, 56:64], 1.0 / 64, 64e-5)
            self.tt("dve", y3, y3, sm[:, 56:64].unsqueeze(2).broadcast_to([128, 8, 64]), ALU.mult, ["lnx0", "sm"], ["lnx0"])
            self.tt("dve", yf[:, 0:512], yf[:, 0:512], self.lng[:, 0:512], ALU.mult, ["lnx0", "lng"], ["lnx0"])
            self.tt("pool", yf[:, 0:512], yf[:, 0:512], self.lnb[:, 0:512], ALU.add, ["lnx0", "lnb"], ["lnx0"])
            self.tt("pool", t1_[:, 0:512].rearrange("p (h d) -> p h d", d=64), vt[:, 0:512].rearrange("p (h d) -> p h d", d=64),
                    yf[:, 512:520].unsqueeze(2).broadcast_to([128, 8, 64]), ALU.mult, ["junk", "lnx0"], ["tmp1"])
            self.tt("dve", t1_[:, 0:512], t1_[:, 0:512], yf[:, 0:512], ALU.add, ["tmp1", "lnx0"], ["tmp1"])
            self.tt("dve", t2_[:, 0:512], t1_[:, 0:512], vt[:, 512:1024], ALU.mult, ["tmp1", "junk"], ["tmp2"])
            P.dma("sp", self.o_scr[tt * 128:(tt + 1) * 128, OB:OB + 512], t2_[:, 0:512], reads=["tmp2"], writes=["o_scr"])

    def merge(self, l, last):
        P, I = self.P, self.I
        wg_all = I["w_gate"][l * D:(l + 1) * D, :]
        wb_all = I["w_branch"][l * 2304:(l + 1) * 2304, :]
        wo_all = I["w_out"][l * D:(l + 1) * D, :]
        P.dma("sp", self.lng[:], I["ln_g"][l:l + 1, :].partition_broadcast(128), writes=["lng"])
        P.dma("sp", self.lnb[:], I["ln_b"][l:l + 1, :].partition_broadcast(128), writes=["lnb"])
        bgT = self.lamt[:, 0:40]
        for i5 in range(5):
            P.dma("sp", bgT[:, i5 * 8:(i5 + 1) * 8], I["b_gate"][l:l + 1, i5 * 1024:(i5 + 1) * 1024].rearrange("e (g c) -> c (e g)", c=128),
                  writes=["lamt"], allow_slow_non_contiguous=True)
        oT = self.BIG[0][:, 0:9216].rearrange("p (j t) -> p j t", t=512)
        yTf = self.BIG[1][:].bitcast(F32)[:, 0:4096].rearrange("p (c t) -> p c t", t=512)
        otile = self.BIG[2][:].bitcast(F32)[:, 0:2304]
        yTb = self.BIG[3][:, 0:4096].rearrange("p (c t) -> p c t", t=512)
        hgrp = self.G[:, 0:4096].rearrange("p (q c) -> p q c", c=1024)
        hin = self.hres[l % 2]
        hout = self.out if last else self.hres[(l + 1) % 2]
        t0, t1 = self.tmp[0], self.tmp[1]
        mb = [0]

        def mbank():
            mb[0] = (mb[0] + 1) % 8
            return mb[0]

        for grp in range(4):
            for tq in range(4):
                tt = grp * 4 + tq
                P.dma("sp", otile, self.o_scr[tt * 128:(tt + 1) * 128, :], reads=["o_scr"], writes=["BIG2"])
                P.dma("sp", hgrp[:, tq, :], hin[tt * 128:(tt + 1) * 128, :], reads=["hres%d" % (l % 2)], writes=["G"])
                for j4 in range(5):
                    nj = min(4, 18 - j4 * 4)
                    b = mbank()
                    for u in range(nj):
                        j = j4 * 4 + u
                        self.tr(self.bank[b][:, u * 128:(u + 1) * 128], otile[:, j * 128:(j + 1) * 128], ["BIG2"], [self.bk(b)])
                    self.cp("act" if j4 % 2 else "dve", oT[:, j4 * 4:j4 * 4 + nj, tq * 128:(tq + 1) * 128],
                            self.bank[b][:, 0:nj * 128].rearrange("p (c t) -> p c t", t=128), [self.bk(b)], ["BIG0"])
            hsl = lambda c: self.hT[:, c, 1 + grp * 512:1 + (grp + 1) * 512]
            for i, (r0, rw) in enumerate(BROWS):
                kci = rw // 128
                for cc in range(4):
                    wg, wgk = self.wload(wg_all[:, i * 1024 + cc * 256:i * 1024 + (cc + 1) * 256], 256)[0]
                    wb, wbk = self.wload(wb_all[r0:r0 + rw, cc * 256:(cc + 1) * 256], 256, kc=kci)[0]
                    for u in range(2):
                        ct = cc * 2 + u
                        b1 = mbank()
                        for c in range(8):
                            self.mm(self.bank[b1][:, :], wg[:, c, u * 128:(u + 1) * 128], hsl(c), c == 0, c == 7, [wgk, "hT"], [self.bk(b1)])
                        ti = self.nxt("mt", 2)
                        tg = self.tmp[ti]
                        self.act(tg[:, :], self.bank[b1][:, :], AF.Sigmoid, [self.bk(b1), "lamt"], ["tmp%d" % ti], bias=bgT[:, i * 8 + ct:i * 8 + ct + 1])
                        b2 = mbank()
                        for c in range(kci):
                            self.mm(self.bank[b2][:, :], wb[:, c, u * 128:(u + 1) * 128], oT[:, r0 // 128 + c, :], c == 0, c == kci - 1,
                                    [wbk, "BIG0"], [self.bk(b2)])
                        ysl = yTf[:, ct, :]
                        if i == 0:
                            self.tt("dve", ysl, self.bank[b2][:, :], tg[:, :], ALU.mult, [self.bk(b2), "tmp%d" % ti], ["BIG1"])
                        else:
                            self.tt("dve", tg[:, :], self.bank[b2][:, :], tg[:, :], ALU.mult, [self.bk(b2), "tmp%d" % ti], ["tmp%d" % ti])
                            self.tt("dve", ysl, ysl, tg[:, :], ALU.add, ["BIG1", "tmp%d" % ti], ["BIG1"])
            if l == 0 and grp == 0:
                self.dbg("yTf", self.BIG[1][:].bitcast(F32)[:, 0:4096], ["BIG1"])
                self.dbg("oT", self.BIG[0][:, 0:9216], ["BIG0"], BF16)
                self.dbg("bgT", self.lamt[:, 0:40], ["lamt"])
            for half in range(2):
                self.cp("act" if half else "dve", yTb[:, half * 4:half * 4 + 4, :], yTf[:, half * 4:half * 4 + 4, :], ["BIG1"], ["BIG3"])
            for cc in range(4):
                wo, wok = self.wload(wo_all[:, cc * 256:(cc + 1) * 256], 256)[0]
                for tq in range(4):
                    b = mbank()
                    for c in range(8):
                        self.mm(self.bank[b][:, 0:256], yTb[:, c, tq * 128:(tq + 1) * 128], wo[:, c, :], c == 0, c == 7, [wok, "BIG3"], [self.bk(b)])
                    hs = hgrp[:, tq, cc * 256:(cc + 1) * 256]
                    self.stt("dve", hs, hs, ALPHA, self.bank[b][:, 0:256], ALU.mult, ALU.add, ["G", self.bk(b)], ["G"])
            for tq in range(4):
                tt = grp * 4 + tq
                self.ln_inplace(hgrp[:, tq, :], "G")
                P.dma("sp", hout[tt * 128:(tt + 1) * 128, :], hgrp[:, tq, :], reads=["G"],
                      writes=["out" if last else "hres%d" % ((l + 1) % 2)], is_output=last)


def make_in_map(inputs, b, consts):
    m = {"x": np.ascontiguousarray(inputs["x"][b]), "mem": np.ascontiguousarray(inputs["mem"][b])}
    for nm, shp in IN_SPECS:
        if nm in consts:
            m[nm] = consts[nm]
        else:
            m[nm] = np.ascontiguousarray(np.asarray(inputs[nm], dtype=np.float32).reshape(shp))
    return m


def kernel(**inputs):
    consts = host_consts()
    kb = KB(debug=False)
    nb = inputs["x"].shape[0]
    in_maps = [make_in_map(inputs, b, consts) for b in range(nb)]
    res = run_bass_kernel_spmd(kb.nc, in_maps, core_ids=list(range(nb)))
    out = np.stack([np.asarray(r["out"], dtype=np.float32).reshape(S, D) for r in res.results], axis=0)
    return out
```
